# Optimizing a Trainium2 kernel written in Bass

```python
import math
import jax
import jax.numpy as jnp
from jax import lax
import numpy as np

D_MODEL = 1024
BATCH = 8
SEQ = 4096
DEPTH = 4

HEAD_DIM = 64
MOBA_HEADS = 8
SB_HEADS = 8
MOBA_WIDTH = MOBA_HEADS * HEAD_DIM
SB_WIDTH = SB_HEADS * HEAD_DIM
ATTN_IN = 3 * (MOBA_WIDTH + SB_WIDTH)
ATTN_OUT = MOBA_WIDTH + SB_WIDTH
MOBA_BLOCK = 256
MOBA_TOPK = 3
MOBA_Q_CHUNK = 32
SB_Q_BLOCK = 128
ROPE_THETA = 500000.0
ROPE_DIMS = HEAD_DIM // 4
GDN_HEADS = 8
GDN_HEAD_DIM = 128
GDN_WIDTH = GDN_HEADS * GDN_HEAD_DIM
GDN_IN = 4 * GDN_WIDTH + 2 * GDN_HEADS
GDN_CONV = 4
GDN_CHUNK = 64
FFN_HIDDEN = -(-8 * D_MODEL // (3 * 256)) * 256
N_EVEN = (DEPTH + 1) // 2
N_ODD = DEPTH // 2
EPS = 1e-6

kernel_name = 'hybrid_moba_stickbreak_gdn_adaln'


def rmsnorm(x, g):
    xf = x.astype(jnp.float32)
    y = xf * lax.rsqrt(jnp.mean(xf * xf, axis=-1, keepdims=True) + EPS)
    return (y * g.astype(jnp.float32)).astype(x.dtype)


def l2norm(x):
    xf = x.astype(jnp.float32)
    return xf * lax.rsqrt(jnp.sum(xf * xf, axis=-1, keepdims=True) + EPS)


def partial_rope(x, positions):
    half = ROPE_DIMS // 2
    inv_freq = ROPE_THETA ** (-jnp.arange(half, dtype=jnp.float32) * 2.0 / ROPE_DIMS)
    ang = positions.astype(jnp.float32)[:, :, None] * inv_freq
    cos = jnp.cos(ang)[:, :, None, :].astype(x.dtype)
    sin = jnp.sin(ang)[:, :, None, :].astype(x.dtype)
    x1 = x[..., :half]
    x2 = x[..., half:ROPE_DIMS]
    return jnp.concatenate([x1 * cos - x2 * sin, x2 * cos + x1 * sin, x[..., ROPE_DIMS:]], axis=-1)


def moba_attention(q, k, v):
    B, H, T, dh = q.shape
    scale = dh ** -0.5
    n_blk = -(-T // MOBA_BLOCK)
    pad = n_blk * MOBA_BLOCK - T
    kp = jnp.pad(k, ((0, 0), (0, 0), (0, pad), (0, 0)))
    vp = jnp.pad(v, ((0, 0), (0, 0), (0, pad), (0, 0)))
    k_blocks = kp.reshape(B, H, n_blk, MOBA_BLOCK, dh)
    v_blocks = vp.reshape(B, H, n_blk, MOBA_BLOCK, dh)
    k_mean = jnp.mean(k_blocks.astype(jnp.float32), axis=3)
    q_blk = jnp.arange(T) // MOBA_BLOCK
    past = jnp.arange(n_blk)[None, :] < q_blk[:, None]
    gate = jnp.einsum('bhtd,bhnd->bhtn', q.astype(jnp.float32), k_mean)
    gate = jnp.where(past, gate, -jnp.inf)
    n_sel = min(MOBA_TOPK, max(n_blk - 1, 1))
    _, sel = lax.top_k(gate, n_sel)
    sel_valid = sel < q_blk[:, None]

    n_qc = T // MOBA_Q_CHUNK
    def to_chunks(t):
        return jnp.moveaxis(t.reshape((B, H, n_qc, MOBA_Q_CHUNK) + t.shape[3:]), 2, 0)
    b_idx = jnp.arange(B)[:, None, None, None]
    h_idx = jnp.arange(H)[None, :, None, None]
    q_offs = jnp.arange(MOBA_Q_CHUNK)
    k_offs = jnp.arange(MOBA_BLOCK)

    def chunk(args):
        ci, q_c, sel_c, valid_c = args
        t0 = ci * MOBA_Q_CHUNK
        k_sel = k_blocks[b_idx, h_idx, sel_c]
        v_sel = v_blocks[b_idx, h_idx, sel_c]
        s_sel = jnp.einsum('bhqd,bhqnkd->bhqnk', q_c, k_sel).astype(jnp.float32) * scale
        s_sel = jnp.where(valid_c[..., None], s_sel, -jnp.inf)
        s_sel = s_sel.reshape(B, H, MOBA_Q_CHUNK, n_sel * MOBA_BLOCK)
        blk_start = (t0 // MOBA_BLOCK) * MOBA_BLOCK
        k_own = lax.dynamic_slice_in_dim(kp, blk_start, MOBA_BLOCK, axis=2)
        v_own = lax.dynamic_slice_in_dim(vp, blk_start, MOBA_BLOCK, axis=2)
        s_own = jnp.einsum('bhqd,bhkd->bhqk', q_c, k_own).astype(jnp.float32) * scale
        causal = (blk_start + k_offs)[None, :] <= (t0 + q_offs)[:, None]
        s_own = jnp.where(causal, s_own, -jnp.inf)
        p = jax.nn.softmax(jnp.concatenate([s_sel, s_own], axis=-1), axis=-1).astype(v.dtype)
        p_sel = p[..., :n_sel * MOBA_BLOCK].reshape(B, H, MOBA_Q_CHUNK, n_sel, MOBA_BLOCK)
        p_own = p[..., n_sel * MOBA_BLOCK:]
        return (jnp.einsum('bhqnk,bhqnkd->bhqd', p_sel, v_sel)
                + jnp.einsum('bhqk,bhkd->bhqd', p_own, v_own))

    out = lax.map(chunk, (jnp.arange(n_qc), to_chunks(q), to_chunks(sel), to_chunks(sel_valid)))
    return jnp.moveaxis(out, 0, 2).reshape(B, H, T, dh)


def stick_breaking_attention(q, k, v):
    B, H, T, dh = q.shape
    scale = dh ** -0.5
    outs = []
    for t0 in range(0, T, SB_Q_BLOCK):
        t1 = t0 + SB_Q_BLOCK
        z = jnp.einsum('bhqd,bhkd->bhqk', q[:, :, t0:t1], k[:, :, :t1]).astype(jnp.float32) * scale
        causal = jnp.arange(t1)[None, :] < jnp.arange(t0, t1)[:, None]
        log_1m = jnp.where(causal, jax.nn.log_sigmoid(-z), 0.0)
        log_after = lax.cumsum(log_1m, axis=3, reverse=True) - log_1m
        w = jnp.where(causal, jnp.exp(jax.nn.log_sigmoid(z) + log_after), 0.0)
        outs.append(jnp.einsum('bhqk,bhkd->bhqd', w.astype(v.dtype), v[:, :, :t1]))
    return jnp.concatenate(outs, axis=2)


def attention_head_groups(h, positions, w_in, w_out):
    B, T, _ = h.shape
    proj = h @ w_in
    cuts = [MOBA_WIDTH, 2 * MOBA_WIDTH, 3 * MOBA_WIDTH,
            3 * MOBA_WIDTH + SB_WIDTH, 3 * MOBA_WIDTH + 2 * SB_WIDTH]
    qa, ka, va, qb, kb, vb = (t.reshape(B, T, -1, HEAD_DIM) for t in jnp.split(proj, cuts, axis=-1))
    qa = partial_rope(qa, positions)
    ka = partial_rope(ka, positions)
    def bhtd(t):
        return t.transpose(0, 2, 1, 3)
    oa = moba_attention(bhtd(qa), bhtd(ka), bhtd(va))
    ob = stick_breaking_attention(bhtd(qb), bhtd(kb), bhtd(vb))
    o = jnp.concatenate([oa, ob], axis=1).transpose(0, 2, 1, 3).reshape(B, T, ATTN_OUT)
    return o @ w_out


def causal_depthwise_conv(x, w):
    ch = x.shape[-1]
    return lax.conv_general_dilated(
        x, w[:, None, :].astype(x.dtype), window_strides=(1,), padding=[(GDN_CONV - 1, 0)],
        dimension_numbers=('NWC', 'WIO', 'NWC'), feature_group_count=ch)


def chunk_gated_delta_rule(q, k, v, g, beta):
    B, T, H, dk = q.shape
    dv = v.shape[-1]
    C = GDN_CHUNK
    N = T // C
    def chunks(t):
        t = t.astype(jnp.float32).reshape((B, N, C, H) + t.shape[3:])
        return t.transpose((0, 3, 1, 2) + tuple(range(4, t.ndim)))
    q, k, v, g, beta = (chunks(t) for t in (q, k, v, g, beta))
    g = jnp.cumsum(g, axis=-1)
    idx = jnp.arange(C)
    incl = idx[:, None] >= idx[None, :]
    strict = idx[:, None] > idx[None, :]
    decay = jnp.exp(jnp.where(incl, g[..., :, None] - g[..., None, :], -jnp.inf))
    k_beta = k * beta[..., None]
    lower = jnp.where(strict, jnp.einsum('bhncd,bhnsd->bhncs', k_beta, k) * decay, 0.0)
    eye = jnp.eye(C, dtype=jnp.float32)
    rhs = jnp.concatenate([v * beta[..., None], k_beta * jnp.exp(g)[..., None]], axis=-1)
    sol = lax.linalg.triangular_solve(lower + eye, rhs, left_side=True, lower=True, unit_diagonal=True)
    u, w = sol[..., :dv], sol[..., dv:]
    attn = jnp.einsum('bhncd,bhnsd->bhncs', q, k) * decay

    def step(state, xs):
        q_c, k_c, u_c, w_c, g_c, a_c = xs
        v_new = u_c - jnp.einsum('bhcd,bhde->bhce', w_c, state)
        o = (jnp.einsum('bhcd,bhde->bhce', q_c * jnp.exp(g_c)[..., None], state)
             + jnp.einsum('bhcs,bhse->bhce', a_c, v_new))
        g_last = g_c[..., -1]
        k_dec = k_c * jnp.exp(g_last[..., None] - g_c)[..., None]
        state = state * jnp.exp(g_last)[..., None, None] + jnp.einsum('bhcd,bhce->bhde', k_dec, v_new)
        return state, o

    xs = tuple(jnp.moveaxis(t, 2, 0) for t in (q, k, u, w, g, attn))
    state0 = jnp.zeros((B, H, dk, dv), jnp.float32)
    _, o = lax.scan(step, state0, xs)
    return o.transpose(1, 0, 3, 2, 4).reshape(B, T, H, dv)


def gated_deltanet(h, w_in, conv_w, a_log, dt_bias, norm_g, w_out):
    B, T, _ = h.shape
    proj = h @ w_in
    qkv = jax.nn.silu(causal_depthwise_conv(proj[..., :3 * GDN_WIDTH], conv_w))
    z = proj[..., 3 * GDN_WIDTH:4 * GDN_WIDTH].reshape(B, T, GDN_HEADS, GDN_HEAD_DIM)
    a = proj[..., 4 * GDN_WIDTH:4 * GDN_WIDTH + GDN_HEADS].astype(jnp.float32)
    b = proj[..., 4 * GDN_WIDTH + GDN_HEADS:].astype(jnp.float32)
    q, k, v = (t.reshape(B, T, GDN_HEADS, GDN_HEAD_DIM) for t in jnp.split(qkv, 3, axis=-1))
    q = l2norm(q) * GDN_HEAD_DIM ** -0.5
    k = l2norm(k)
    beta = jax.nn.sigmoid(b)
    g = -jnp.exp(a_log.astype(jnp.float32)) * jax.nn.softplus(a + dt_bias.astype(jnp.float32))
    o = chunk_gated_delta_rule(q, k, v, g, beta)
    o = rmsnorm(o, norm_g) * jax.nn.silu(z.astype(jnp.float32))
    return o.reshape(B, T, GDN_WIDTH).astype(h.dtype) @ w_out


def swiglu(h, w_in, w_out):
    gate, up = jnp.split(h @ w_in, 2, axis=-1)
    return (jax.nn.silu(gate) * up) @ w_out


def setup_inputs(seed: int = 0) -> dict:
    key = jax.random.key(seed)
    ks = jax.random.split(key, 18)
    f32 = jnp.float32
    def dense(k, shape, fan_in, s=1.0):
        return jax.random.normal(k, shape, f32) * (s * fan_in ** -0.5)
    def gain(k, shape):
        return 1.0 + 0.05 * jax.random.normal(k, shape, f32)
    x = jax.random.normal(ks[0], (BATCH, SEQ, D_MODEL), f32)
    c = jax.random.normal(ks[1], (BATCH, D_MODEL), f32)
    offset = jax.random.randint(ks[2], (BATCH, 1), 0, 1024, dtype=jnp.int32)
    positions = offset + jnp.arange(SEQ, dtype=jnp.int32)[None, :]
    ada_w = dense(ks[3], (DEPTH, D_MODEL, 6 * D_MODEL), D_MODEL, 0.5)
    ada_b = 0.02 * jax.random.normal(ks[4], (DEPTH, 6 * D_MODEL), f32)
    norm_mix_g = gain(ks[5], (DEPTH, D_MODEL))
    norm_ffn_g = gain(ks[6], (DEPTH, D_MODEL))
    attn_w_in = dense(ks[7], (N_EVEN, D_MODEL, ATTN_IN), D_MODEL)
    attn_w_out = dense(ks[8], (N_EVEN, ATTN_OUT, D_MODEL), ATTN_OUT)
    gdn_w_in = dense(ks[9], (N_ODD, D_MODEL, GDN_IN), D_MODEL)
    gdn_conv_w = dense(ks[10], (N_ODD, GDN_CONV, 3 * GDN_WIDTH), GDN_CONV)
    gdn_a_log = jnp.log(jax.random.uniform(ks[11], (N_ODD, GDN_HEADS), f32, 1.0, 16.0))
    dt = jnp.exp(jax.random.uniform(ks[12], (N_ODD, GDN_HEADS), f32, math.log(1e-3), math.log(1e-1)))
    gdn_dt_bias = dt + jnp.log(-jnp.expm1(-dt))
    gdn_norm_g = gain(ks[13], (N_ODD, GDN_HEAD_DIM))
    gdn_w_out = dense(ks[14], (N_ODD, GDN_WIDTH, D_MODEL), GDN_WIDTH)
    ffn_w_in = dense(ks[15], (DEPTH, D_MODEL, 2 * FFN_HIDDEN), D_MODEL)
    ffn_w_out = dense(ks[16], (DEPTH, FFN_HIDDEN, D_MODEL), FFN_HIDDEN)
    final_norm_g = gain(ks[17], (D_MODEL,))
    return {'x': x, 'c': c, 'positions': positions, 'ada_w': ada_w, 'ada_b': ada_b,
            'norm_mix_g': norm_mix_g, 'norm_ffn_g': norm_ffn_g,
            'attn_w_in': attn_w_in, 'attn_w_out': attn_w_out,
            'gdn_w_in': gdn_w_in, 'gdn_conv_w': gdn_conv_w, 'gdn_a_log': gdn_a_log,
            'gdn_dt_bias': gdn_dt_bias, 'gdn_norm_g': gdn_norm_g, 'gdn_w_out': gdn_w_out,
            'ffn_w_in': ffn_w_in, 'ffn_w_out': ffn_w_out, 'final_norm_g': final_norm_g}


def reference(x, c, positions, ada_w, ada_b, norm_mix_g, norm_ffn_g, attn_w_in, attn_w_out,
              gdn_w_in, gdn_conv_w, gdn_a_log, gdn_dt_bias, gdn_norm_g, gdn_w_out,
              ffn_w_in, ffn_w_out, final_norm_g):
    cond = jax.nn.silu(c)
    for layer in range(DEPTH):
        mod = (cond @ ada_w[layer] + ada_b[layer])[:, None, :]
        shift_m, scale_m, gate_m, shift_f, scale_f, gate_f = jnp.split(mod, 6, axis=-1)
        h = rmsnorm(x, norm_mix_g[layer]) * (1.0 + scale_m) + shift_m
        i = layer // 2
        if layer % 2 == 0:
            mixed = attention_head_groups(h, positions, attn_w_in[i], attn_w_out[i])
        else:
            mixed = gated_deltanet(h, gdn_w_in[i], gdn_conv_w[i], gdn_a_log[i], gdn_dt_bias[i],
                                   gdn_norm_g[i], gdn_w_out[i])
        x = x + gate_m * mixed
        h = rmsnorm(x, norm_ffn_g[layer]) * (1.0 + scale_f) + shift_f
        x = x + gate_f * swiglu(h, ffn_w_in[layer], ffn_w_out[layer])
    return rmsnorm(x, final_norm_g)
```

```python
import numpy as np
from contextlib import ExitStack
import concourse.bass as bass
import concourse.mybir as mybir
from concourse.bass_utils import run_bass_kernel_spmd

F32 = mybir.dt.float32
BF16 = mybir.dt.bfloat16
I32 = mybir.dt.int32
AF = mybir.ActivationFunctionType
ALU = mybir.AluOpType
AX = mybir.AxisListType

D = 1024
T = 4096
NL = 4
KC = 8
FH = 2816
NJ = 22
EPS = 1e-6
NDMA = 48
BIG = 30000.0


class Buf:
    __slots__ = ("t", "w", "r", "name")

    def __init__(self, t, name=""):
        self.t = t
        self.w = None
        self.r = {}
        self.name = name

    def __getitem__(self, idx):
        return self.t[idx]


class Sched:
    def __init__(self, nc):
        self.nc = nc
        self.E = {"pe": nc.tensor, "act": nc.scalar, "dve": nc.vector, "pool": nc.gpsimd, "sp": nc.sync}
        self.sem = {k: nc.alloc_semaphore("s_" + k) for k in ("pe", "act", "dve", "pool")}
        self.cnt = {k: 0 for k in self.sem}
        self.seen = {k: {} for k in self.E}
        self.dsem = [nc.alloc_semaphore("d%d" % i) for i in range(NDMA)]
        self.dcnt = [0] * NDMA
        self.drr = 0
        self.bufs = []
        self.ps_rr = 0

    def sbuf(self, name, shape, dtype):
        b = Buf(self.nc.alloc_sbuf_tensor(name, list(shape), dtype), name)
        self.bufs.append(b)
        return b

    def psum(self, name, shape, dtype=F32):
        b = Buf(self.nc.alloc_psum_tensor(name, list(shape), dtype), name)
        self.bufs.append(b)
        return b

    def dram(self, t, name=""):
        b = Buf(t, name)
        self.bufs.append(b)
        return b

    def _wait(self, eng, ev):
        if ev is None:
            return
        sem, val, key = ev
        if self.seen[eng].get(key, 0) >= val:
            return
        self.E[eng].wait_ge(sem, val)
        self.seen[eng][key] = val

    def _deps(self, eng, reads, writes):
        for b in reads:
            self._wait(eng, b.w)
        for b in writes:
            self._wait(eng, b.w)
            for ev in b.r.values():
                self._wait(eng, ev)

    def _done(self, ev, reads, writes):
        for b in reads:
            b.r[ev[2]] = ev
        for b in writes:
            b.w = ev
            b.r = {}

    def op(self, eng, reads, writes, fn):
        self._deps(eng, reads, writes)
        ins = fn(self.E[eng])
        self.cnt[eng] += 1
        ins.then_inc(self.sem[eng], 1)
        ev = (self.sem[eng], self.cnt[eng], eng)
        self._done(ev, reads, writes)
        return ev

    def dma(self, reads, writes, out, in_, eng="sp"):
        k = self.drr
        self.drr = (k + 1) % NDMA
        key = ("d", k)
        if self.dcnt[k] > 0:
            self._wait(eng, (self.dsem[k], self.dcnt[k], key))
        self._deps(eng, reads, writes)
        ins = self.E[eng].dma_start(out=out, in_=in_)
        self.dcnt[k] += 16
        ins.then_inc(self.dsem[k], 16)
        ev = (self.dsem[k], self.dcnt[k], key)
        self._done(ev, reads, writes)
        return ev

    def barrier(self):
        evs = []
        for k in self.sem:
            if self.cnt[k] > 0:
                evs.append((self.sem[k], self.cnt[k], k))
        for k in range(NDMA):
            if self.dcnt[k] > 0:
                evs.append((self.dsem[k], self.dcnt[k], ("d", k)))
        for eng in self.E:
            for ev in evs:
                self._wait(eng, ev)
        for b in self.bufs:
            b.w = None
            b.r = {}


class Phase:
    _uid = [0]

    def __init__(self, S):
        Phase._uid[0] += 1
        self.uid = Phase._uid[0]
        self.S = S
        self.nc = S.nc
        self.st = ExitStack()
        self.nbufs0 = len(S.bufs)

    def sbuf(self, name, shape, dtype):
        name = "%s_p%d" % (name, self.uid)
        t = self.st.enter_context(self.nc.sbuf_tensor(name, list(shape), dtype))
        b = Buf(t, name)
        self.S.bufs.append(b)
        return b

    def psum(self, name, shape, dtype=F32):
        name = "%s_p%d" % (name, self.uid)
        t = self.st.enter_context(self.nc.psum_tensor(name, list(shape), dtype))
        b = Buf(t, name)
        self.S.bufs.append(b)
        return b

    def close(self):
        self.S.barrier()
        self.st.close()
        del self.S.bufs[self.nbufs0:]


VOFF = {}


def _vec_layout():
    off = 0
    for name, n in [("gm", 32), ("gf", 32), ("fin", 8), ("adab", 192), ("conv", 192), ("gngb", 256),
                    ("alog", 16), ("dtb", 16), ("invf", 1), ("sinsign", 1), ("cT", 8)]:
        VOFF[name] = off
        off += n
    return off


NV = _vec_layout()


def build_vecs(inp, b):
    v = np.zeros((128, NV), np.float32)
    def chunked(a):
        a = np.asarray(a, np.float32).reshape(-1, 128)
        return a.T
    v[:, VOFF["gm"]:VOFF["gm"] + 32] = chunked(inp["norm_mix_g"])
    v[:, VOFF["gf"]:VOFF["gf"] + 32] = chunked(inp["norm_ffn_g"])
    v[:, VOFF["fin"]:VOFF["fin"] + 8] = chunked(inp["final_norm_g"])
    v[:, VOFF["adab"]:VOFF["adab"] + 192] = chunked(inp["ada_b"])
    v[:, VOFF["conv"]:VOFF["conv"] + 192] = chunked(inp["gdn_conv_w"])
    v[:, VOFF["gngb"]:VOFF["gngb"] + 256] = np.broadcast_to(np.asarray(inp["gdn_norm_g"], np.float32).reshape(1, 256), (128, 256))
    v[:, VOFF["alog"]:VOFF["alog"] + 16] = np.broadcast_to(np.asarray(inp["gdn_a_log"], np.float32).reshape(1, 16), (128, 16))
    v[:, VOFF["dtb"]:VOFF["dtb"] + 16] = np.broadcast_to(np.asarray(inp["gdn_dt_bias"], np.float32).reshape(1, 16), (128, 16))
    p = np.arange(128) % 64
    inv = (500000.0 ** (-np.arange(8, dtype=np.float32) * 2.0 / 16)).astype(np.float32)
    v[:, VOFF["invf"]] = np.where(p < 16, inv[p % 8], 0.0)
    v[:, VOFF["sinsign"]] = np.where(p < 8, -1.0, np.where(p < 16, 1.0, 0.0))
    v[:, VOFF["cT"]:VOFF["cT"] + 8] = chunked(inp["c"][b])
    return v


def rope_partner_perm():
    idx = np.arange(512)
    d = idx % 64
    return np.where(d < 8, idx + 8, np.where(d < 16, idx - 8, idx))


class LazyDram(dict):
    def __init__(self, prog):
        super().__init__()
        self.prog = prog

    def __missing__(self, name):
        pg = self.prog
        if name in pg.in_shapes:
            shape, dt = pg.in_shapes[name]
            kind = "ExternalInput"
            pg.used_inputs.append(name)
        else:
            shape, dt = pg.scr_shapes[name]
            kind = "Internal"
            if name in pg.ext_out or name == "outT":
                kind = "ExternalOutput"
            if name in pg.ext_in:
                kind = "ExternalInput"
                pg.used_inputs.append(name)
        ap = pg.nc.dram_tensor(name, list(shape), dt, kind=kind).ap()
        self[name] = ap
        return ap


class Prog:
    def __init__(self, ext_out=(), ext_in=()):
        self.nc = bass.Bass("TRN2", target_bir_lowering=False)
        nc = self.nc
        self.ext_out = set(ext_out)
        self.ext_in = set(ext_in)
        self.S = Sched(nc)
        S = self.S
        self.in_shapes = {
            "xT": ([D, T], F32), "vecs": ([128, NV], F32), "pos": ([1, T], I32),
            "ada_w": ([NL, D, 6 * D], F32), "attn_w_in": ([2, D, 4096], F32), "attn_w_out": ([2, D, D], F32),
            "gdn_w_in": ([2, D, 4112], F32), "gdn_w_out": ([2, D, D], F32),
            "ffn_w_in": ([NL, D, 2 * FH], F32), "ffn_w_out": ([NL, FH, D], F32),
        }
        self.used_inputs = []
        self.dr = LazyDram(self)
        self.dbg_outs = []
        self.scr_shapes = {
            "XT": ([D, T], F32), "COS": ([128, T], F32), "SIN": ([128, T], F32),
            "QT": ([2048, T], BF16),
            "VTOK": ([T, 1024], BF16),
            "OT": ([D, T], BF16),
            "GT": ([3072, T], BF16),
            "AB": ([T, 16], F32),
            "outT": ([D, T], F32),
        }
        self.VEC = S.sbuf("VEC", [128, NV], F32)
        self.MOD = S.sbuf("MOD", [128, NL * 48], F32)
        self.AM = S.sbuf("AMc", [128, NL * 8], F32)
        self.AFc = S.sbuf("AFc", [128, NL * 8], F32)
        self.ones32 = S.sbuf("ones32", [128, 128], F32)
        self.ident32 = S.sbuf("ident32", [128, 128], F32)
        self.PS = [S.psum("ps%d" % i, [128, 512], F32) for i in range(8)]
        self.ps_i = 0
        S.op("pool", [], [self.ones32], lambda e: e.memset(self.ones32[:], 1.0))
        def mk_ident(e):
            e.memset(self.ident32[:], 0.0)
            return e.affine_select(out=self.ident32[:], in_=self.ident32[:], compare_op=ALU.not_equal,
                                   fill=1.0, base=0, pattern=[[-1, 128]], channel_multiplier=1)
        S.op("pool", [], [self.ident32], mk_ident)

    def dbg(self, name, buf, ap):
        shape = [int(x) for x in ap.shape]
        d = self.nc.dram_tensor(name, shape, buf.t.dtype, kind="ExternalOutput").ap()
        self.S.dma([buf], [], d, ap)
        self.dbg_outs.append(name)

    def ps(self):
        b = self.PS[self.ps_i]
        self.ps_i = (self.ps_i + 1) % 8
        return b

    def vcol(self, name, i=0, n=1):
        o = VOFF[name] + i
        return self.VEC[:, o:o + n]

    def phase0(self):
        S, nc = self.S, self.nc
        P = Phase(S)
        VEC, MOD = self.VEC, self.MOD
        S.dma([], [VEC], VEC[:], self.dr["vecs"])
        condT = P.sbuf("condT", [128, 8], F32)
        S.op("act", [VEC], [condT], lambda e: e.activation(out=condT[:], in_=self.vcol("cT", 0, 8), func=AF.Silu))
        CB = 1536
        wst = [P.sbuf("adaw%d" % i, [128, KC, CB], F32) for i in range(2)]
        k = 0
        for l in range(NL):
            pm = self.ps()
            for cb in range(4):
                w = wst[k % 2]
                k += 1
                src = self.dr["ada_w"][l, :, cb * CB:(cb + 1) * CB].rearrange("(c p) n -> p c n", p=128)
                S.dma([], [w], w[:], src)
                def mm(e, w=w, cb=cb, pm=pm):
                    last = None
                    for j in range(12):
                        col = cb * 12 + j
                        for kc in range(KC):
                            last = e.matmul(pm[:, col:col + 1], lhsT=w[:, kc, j * 128:(j + 1) * 128],
                                            rhs=condT[:, kc:kc + 1], start=(kc == 0), stop=(kc == KC - 1))
                    return last
                S.op("pe", [w, condT], [pm], mm)
            S.op("dve", [pm, VEC], [MOD], lambda e, l=l, pm=pm: e.tensor_tensor(
                out=MOD[:, l * 48:(l + 1) * 48], in0=pm[:, 0:48], in1=self.vcol("adab", l * 48, 48), op=ALU.add))
            S.op("dve", [MOD, VEC], [self.AM], lambda e, l=l: e.scalar_tensor_tensor(
                out=self.AM[:, l * 8:(l + 1) * 8], in0=MOD[:, l * 48 + 8:l * 48 + 16], scalar=1.0,
                in1=self.vcol("gm", l * 8, 8), op0=ALU.add, op1=ALU.mult))
            S.op("dve", [MOD, VEC], [self.AFc], lambda e, l=l: e.scalar_tensor_tensor(
                out=self.AFc[:, l * 8:(l + 1) * 8], in0=MOD[:, l * 48 + 32:l * 48 + 40], scalar=1.0,
                in1=self.vcol("gf", l * 8, 8), op0=ALU.add, op1=ALU.mult))
        posi = P.sbuf("posi", [1, T], I32)
        posf = P.sbuf("posf", [1, T], F32)
        S.dma([], [posi], posi[:], self.dr["pos"])
        S.op("dve", [posi], [posf], lambda e: e.tensor_copy(out=posf[:], in_=posi[:]))
        angs = [P.sbuf("ang%d" % k, [128, 512], F32) for k in range(2)]
        rbuf = {}
        for wh in ("SIN", "COS"):
            for k in range(2):
                rbuf[(wh, k)] = (P.sbuf("r%s%d" % (wh, k), [128, 512], F32), P.sbuf("xs%s%d" % (wh, k), [128, 512], F32),
                                 P.sbuf("ki%s%d" % (wh, k), [128, 512], I32))
        for tt in range(8):
            sl = slice(tt * 512, (tt + 1) * 512)
            pb = self.ps()
            S.op("pe", [posf, self.ones32], [pb], lambda e, pb=pb, sl=sl: e.matmul(
                pb[:, :], lhsT=self.ones32[0:1, :], rhs=posf[0:1, sl], start=True, stop=True))
            ang = angs[tt % 2]
            S.op("dve", [pb, VEC], [ang], lambda e, pb=pb, ang=ang: e.tensor_scalar(
                out=ang[:], in0=pb[:, :], scalar1=self.vcol("invf"), scalar2=None, op0=ALU.mult))
            for which, shift in (("SIN", 0.0), ("COS", 0.5 * np.pi)):
                r, xs, ki = rbuf[(which, tt % 2)]
                C1 = 6.28125
                C2 = float(2 * np.pi - 6.28125)
                S.op("dve", [ang], [xs], lambda e, xs=xs, ang=ang, shift=shift: e.tensor_scalar(
                    out=xs[:], in0=ang[:], scalar1=float(shift), scalar2=None, op0=ALU.add))
                S.op("dve", [xs], [r], lambda e, r=r, xs=xs: e.tensor_scalar(
                    out=r[:], in0=xs[:], scalar1=float(1.0 / (2 * np.pi)), scalar2=None, op0=ALU.mult))
                S.op("dve", [r], [ki], lambda e, r=r, ki=ki: e.tensor_copy(out=ki[:], in_=r[:]))
                S.op("dve", [ki], [r], lambda e, r=r, ki=ki: e.tensor_copy(out=r[:], in_=ki[:]))
                S.op("dve", [r, xs], [xs], lambda e, r=r, xs=xs: e.scalar_tensor_tensor(
                    out=xs[:], in0=r[:], scalar=-C1, in1=xs[:], op0=ALU.mult, op1=ALU.add))
                S.op("dve", [r, xs], [xs], lambda e, r=r, xs=xs: e.scalar_tensor_tensor(
                    out=xs[:], in0=r[:], scalar=-C2, in1=xs[:], op0=ALU.mult, op1=ALU.add))
                S.op("dve", [xs], [xs], lambda e, xs=xs: e.tensor_scalar(
                    out=xs[:], in0=xs[:], scalar1=-3.14159, scalar2=3.14159, op0=ALU.max, op1=ALU.min))
                S.op("act", [xs], [r], lambda e, r=r, xs=xs: e.activation(out=r[:], in_=xs[:], func=AF.Sin))
                if which == "SIN":
                    S.op("dve", [r, VEC], [r], lambda e, r=r: e.tensor_scalar(
                        out=r[:], in0=r[:], scalar1=self.vcol("sinsign"), scalar2=None, op0=ALU.mult))
                S.dma([r], [], self.dr[which][:, sl], r[:])
        P.close()

    def load_weight_bf16(self, P, Wb, src3, ncols, stages, col0=0):
        S = self.S
        nk = src3.shape[1]
        CB = stages[0].t.shape[2]
        c = 0
        i = 0
        while c < ncols:
            n = min(CB, ncols - c)
            st = stages[i % len(stages)]
            i += 1
            S.dma([], [st], st[:, 0:nk, 0:n], src3[:, :, c:c + n])
            S.op("pool", [st], [Wb], lambda e, st=st, c=c, n=n: e.tensor_copy(
                out=Wb[:, 0:nk, col0 + c:col0 + c + n], in_=st[:, 0:nk, 0:n]))
            c += n

    def norm_tile(self, P, X, A, B, h, sq, tmps, rstd, n=512):
        S = self.S
        S.op("act", [X], [sq], lambda e: e.activation(out=sq[:, :, 0:n], in_=X[:, :, 0:n], func=AF.Square))
        pss = self.ps()
        def mm(e):
            last = None
            for kc in range(KC):
                last = e.matmul(pss[:, 0:n], lhsT=self.ones32[:, :], rhs=sq[:, kc, 0:n], start=(kc == 0), stop=(kc == KC - 1))
            return last
        S.op("pe", [sq, self.ones32], [pss], mm)
        S.op("dve", [pss], [rstd], lambda e: e.tensor_scalar(
            out=rstd[:, 0:n], in0=pss[:, 0:n], scalar1=1.0 / D, scalar2=EPS, op0=ALU.mult, op1=ALU.add))
        S.op("act", [rstd], [rstd], lambda e: e.activation(out=rstd[:, 0:n], in_=rstd[:, 0:n], func=AF.Sqrt))
        S.op("dve", [rstd], [rstd], lambda e: e.reciprocal(out=rstd[:, 0:n], in_=rstd[:, 0:n]))
        for kc in range(KC):
            tmp = tmps[kc % len(tmps)]
            S.op("dve", [X, rstd, self.AM, self.AFc], [tmp], lambda e, kc=kc, tmp=tmp: e.scalar_tensor_tensor(
                out=tmp[:, 0:n], in0=X[:, kc, 0:n], scalar=A[:, kc:kc + 1], in1=rstd[:, 0:n], op0=ALU.mult, op1=ALU.mult))
            S.op("act", [tmp, self.MOD], [h], lambda e, kc=kc, tmp=tmp: e.activation(
                out=h[:, kc, 0:n], in_=tmp[:, 0:n], func=AF.Identity, bias=B[:, kc:kc + 1], scale=1.0))

    def phase1(self, l):
        S = self.S
        P = Phase(S)
        attn = (l % 2 == 0)
        i = l // 2
        NC_ = 4096 if attn else 4112
        Wb = P.sbuf("W1", [128, KC, NC_], BF16)
        stages = [P.sbuf("wst%d" % k, [128, KC, 512], F32) for k in range(2)]
        wsrc = self.dr["attn_w_in" if attn else "gdn_w_in"][i].rearrange("(c p) n -> p c n", p=128)
        self.load_weight_bf16(P, Wb, wsrc, NC_, stages)
        Xb = [P.sbuf("X%d" % k, [128, KC, 512], F32) for k in range(2)]
        sq = P.sbuf("sq", [128, KC, 512], F32)
        tmps = [P.sbuf("tmp%d" % k, [128, 512], F32) for k in range(2)]
        rstd = P.sbuf("rstd", [128, 512], F32)
        h = P.sbuf("h", [128, KC, 512], BF16)
        stg = [P.sbuf("stg%d" % k, [128, 512], BF16) for k in range(4)]
        stg32 = [P.sbuf("stgf%d" % k, [128, 16], F32) for k in range(2)]
        rt = [P.sbuf("rt%d" % k, [128, 512], F32) for k in range(4)]
        cs = [[P.sbuf("cs%d_%d" % (a, k), [128, 512], F32) for k in range(2)] for a in range(2)]
        A = self.AM[:, l * 8:(l + 1) * 8]
        B = self.MOD[:, l * 48:l * 48 + 8]
        XT3 = self.dr["xT" if l == 0 else "XT"].rearrange("(c p) t -> p c t", p=128)
        NT = T // 512
        S.dma([], [Xb[0]], Xb[0][:], XT3[:, :, 0:512])
        sk = 0
        ev = 0
        for tt in range(NT):
            sl = slice(tt * 512, (tt + 1) * 512)
            X = Xb[tt % 2]
            if tt + 1 < NT:
                S.dma([], [Xb[(tt + 1) % 2]], Xb[(tt + 1) % 2][:], XT3[:, :, (tt + 1) * 512:(tt + 2) * 512])
            if attn:
                cosb, sinb = cs[0][tt % 2], cs[1][tt % 2]
                S.dma([], [cosb], cosb[:], self.dr["COS"][:, sl])
                S.dma([], [sinb], sinb[:], self.dr["SIN"][:, sl])
            self.norm_tile(P, X, A, B, h, sq, tmps, rstd)

            def proj_fm(col0):
                pp = self.ps()
                def mm(e, pp=pp, col0=col0):
                    last = None
                    for kc in range(KC):
                        last = e.matmul(pp[:, :], lhsT=Wb[:, kc, col0:col0 + 128], rhs=h[:, kc, :],
                                        start=(kc == 0), stop=(kc == KC - 1))
                    return last
                S.op("pe", [Wb, h], [pp], mm)
                return pp

            def evac(pp, dst_ap, scale, width=512):
                nonlocal sk, ev
                st = stg[sk % 4]
                sk += 1
                if ev % 2 == 0:
                    S.op("act", [pp], [st], lambda e: e.activation(out=st[:, 0:width], in_=pp[:, 0:width], func=AF.Copy, scale=float(scale)))
                else:
                    S.op("dve", [pp], [st], lambda e: e.tensor_scalar(out=st[:, 0:width], in0=pp[:, 0:width], scalar1=float(scale), scalar2=None, op0=ALU.mult))
                ev += 1
                S.dma([st], [], dst_ap, st[:, 0:width])

            def proj_tm(col0, ncols, dst, dcol0, f32out=False):
                nonlocal sk
                for s in range(4):
                    pp = self.ps()
                    def mm(e, pp=pp, s=s):
                        last = None
                        for kc in range(KC):
                            last = e.matmul(pp[:, 0:ncols], lhsT=h[:, kc, s * 128:(s + 1) * 128], rhs=Wb[:, kc, col0:col0 + ncols],
                                            start=(kc == 0), stop=(kc == KC - 1))
                        return last
                    S.op("pe", [Wb, h], [pp], mm)
                    r0 = tt * 512 + s * 128
                    if f32out:
                        st = stg32[s % 2]
                        S.op("dve", [pp], [st], lambda e, st=st, pp=pp: e.tensor_copy(out=st[:, 0:ncols], in_=pp[:, 0:ncols]))
                        S.dma([st], [], dst[r0:r0 + 128, dcol0:dcol0 + ncols], st[:, 0:ncols])
                    else:
                        evac(pp, dst[r0:r0 + 128, dcol0:dcol0 + ncols], 1.0, ncols)

            if attn:
                QT = self.dr["QT"]
                for grp, (c0, pc0, d0, scale) in enumerate([(0, 3072, 0, 0.125), (512, 3584, 512, 1.0)]):
                    for n in range(4):
                        pq = proj_fm(c0 + n * 128)
                        pp = proj_fm(pc0 + n * 128)
                        t1, t2 = rt[(2 * n) % 4], rt[(2 * n + 1) % 4]
                        st = stg[sk % 4]
                        sk += 1
                        S.op("dve", [pq, cosb], [t1], lambda e, pq=pq, t1=t1, scale=scale: e.scalar_tensor_tensor(
                            out=t1[:], in0=pq[:, :], scalar=float(scale), in1=cosb[:], op0=ALU.mult, op1=ALU.mult))
                        S.op("dve", [pp, sinb], [t2], lambda e, pp=pp, t2=t2, scale=scale: e.scalar_tensor_tensor(
                            out=t2[:], in0=pp[:, :], scalar=float(scale), in1=sinb[:], op0=ALU.mult, op1=ALU.mult))
                        S.op("pool", [t1, t2], [st], lambda e, st=st, t1=t1, t2=t2: e.tensor_tensor(
                            out=st[:], in0=t1[:], in1=t2[:], op=ALU.add))
                        S.dma([st], [], QT[d0 + n * 128:d0 + (n + 1) * 128, sl], st[:])
                for n in range(4):
                    evac(proj_fm(1536 + n * 128), QT[1024 + n * 128:1024 + (n + 1) * 128, sl], 0.125)
                for n in range(4):
                    evac(proj_fm(2048 + n * 128), QT[1536 + n * 128:1536 + (n + 1) * 128, sl], 1.0)
                proj_tm(1024, 512, self.dr["VTOK"], 0)
                proj_tm(2560, 512, self.dr["VTOK"], 512)
            else:
                GT = self.dr["GT"]
                for n in range(24):
                    evac(proj_fm(n * 128), GT[n * 128:(n + 1) * 128, sl], 1.0)
                proj_tm(3072, 512, self.dr["VTOK"], 0)
                proj_tm(3584, 512, self.dr["VTOK"], 512)
                proj_tm(4096, 16, self.dr["AB"], 0, f32out=True)
        P.close()

    def phase34(self, l, final=False, tiles=None):
        S = self.S
        P = Phase(S)
        i = l // 2
        TT = 256
        Wo = P.sbuf("Wo", [128, KC, D], BF16)
        Wi = P.sbuf("Wi", [128, KC, 2 * FH], BF16)
        W2 = P.sbuf("W2", [128, NJ, D], BF16)
        PW = Phase(S)
        stA = [PW.sbuf("stA%d" % k, [128, KC, 512], F32) for k in range(2)]
        stB = [PW.sbuf("stB%d" % k, [128, NJ, 128], F32) for k in range(2)]
        wo_src = self.dr["attn_w_out" if l % 2 == 0 else "gdn_w_out"][i].rearrange("(c p) n -> p c n", p=128)
        self.load_weight_bf16(PW, Wo, wo_src, D, stA)
        self.load_weight_bf16(PW, Wi, self.dr["ffn_w_in"][l].rearrange("(c p) n -> p c n", p=128), 2 * FH, stA)
        self.load_weight_bf16(PW, W2, self.dr["ffn_w_out"][l].rearrange("(c p) n -> p c n", p=128), D, stB)
        PW.close()
        Xb = [P.sbuf("X%d" % k, [128, KC, TT], F32) for k in range(1)]
        ob = [P.sbuf("oT%d" % k, [128, KC, TT], BF16) for k in range(2)]
        sq = P.sbuf("sq", [128, KC, TT], F32)
        tmps = [P.sbuf("tmp%d" % k, [128, TT], F32) for k in range(2)]
        sgs = [P.sbuf("sg%d" % k, [128, TT], F32) for k in range(2)]
        rstd = P.sbuf("rstd", [128, TT], F32)
        h = P.sbuf("h", [128, KC, TT], BF16)
        hid = P.sbuf("hid", [128, NJ, TT], BF16)
        GM = self.MOD[:, l * 48 + 16:l * 48 + 24]
        GF = self.MOD[:, l * 48 + 40:l * 48 + 48]
        A = self.AFc[:, l * 8:(l + 1) * 8]
        B = self.MOD[:, l * 48 + 24:l * 48 + 32]
        XS3 = self.dr["xT" if l == 0 else "XT"].rearrange("(c p) t -> p c t", p=128)
        XD3 = self.dr["XT"].rearrange("(c p) t -> p c t", p=128)
        OT3 = self.dr["OT"].rearrange("(c p) t -> p c t", p=128)
        if final:
            OUT3 = self.dr["outT"].rearrange("(c p) t -> p c t", p=128)
            yout = P.sbuf("yout", [128, KC, TT], F32)
        NT = T // TT
        tl = list(range(NT)) if tiles is None else list(tiles)
        S.dma([], [ob[0]], ob[0][:], OT3[:, :, tl[0] * TT:(tl[0] + 1) * TT])
        for ti, tt in enumerate(tl):
            sl = slice(tt * TT, (tt + 1) * TT)
            X = Xb[0]
            oT = ob[ti % 2]
            S.dma([], [X], X[:], XS3[:, :, sl])
            if ti + 1 < len(tl):
                nt = tl[ti + 1]
                S.dma([], [ob[(ti + 1) % 2]], ob[(ti + 1) % 2][:], OT3[:, :, nt * TT:(nt + 1) * TT])
            for n in range(KC):
                pp = self.ps()
                def mm(e, pp=pp, n=n):
                    last = None
                    for kc in range(KC):
                        last = e.matmul(pp[:, 0:TT], lhsT=Wo[:, kc, n * 128:(n + 1) * 128], rhs=oT[:, kc, :],
                                        start=(kc == 0), stop=(kc == KC - 1))
                    return last
                S.op("pe", [Wo, oT], [pp], mm)
                S.op("dve", [pp, X, self.MOD], [X], lambda e, pp=pp, n=n: e.scalar_tensor_tensor(
                    out=X[:, n, :], in0=pp[:, 0:TT], scalar=GM[:, n:n + 1], in1=X[:, n, :], op0=ALU.mult, op1=ALU.add))
            self.norm_tile(P, X, A, B, h, sq, tmps, rstd, n=TT)
            for j in range(NJ):
                pg = self.ps()
                pu = self.ps()
                def mmg(e, pg=pg, j=j):
                    last = None
                    for kc in range(KC):
                        last = e.matmul(pg[:, 0:TT], lhsT=Wi[:, kc, j * 128:(j + 1) * 128], rhs=h[:, kc, :],
                                        start=(kc == 0), stop=(kc == KC - 1))
                    return last
                def mmu(e, pu=pu, j=j):
                    last = None
                    for kc in range(KC):
                        last = e.matmul(pu[:, 0:TT], lhsT=Wi[:, kc, FH + j * 128:FH + (j + 1) * 128], rhs=h[:, kc, :],
                                        start=(kc == 0), stop=(kc == KC - 1))
                    return last
                S.op("pe", [Wi, h], [pg], mmg)
                S.op("pe", [Wi, h], [pu], mmu)
                sg = sgs[j % 2]
                S.op("act", [pg], [sg], lambda e, pg=pg, sg=sg: e.activation(out=sg[:], in_=pg[:, 0:TT], func=AF.Silu))
                S.op("dve", [sg, pu], [hid], lambda e, pu=pu, sg=sg, j=j: e.tensor_tensor(
                    out=hid[:, j, :], in0=pu[:, 0:TT], in1=sg[:], op=ALU.mult))
            for n in range(KC):
                pp = self.ps()
                def mm2(e, pp=pp, n=n):
                    last = None
                    for j in range(NJ):
                        last = e.matmul(pp[:, 0:TT], lhsT=W2[:, j, n * 128:(n + 1) * 128], rhs=hid[:, j, :],
                                        start=(j == 0), stop=(j == NJ - 1))
                    return last
                S.op("pe", [W2, hid], [pp], mm2)
                S.op("dve", [pp, X, self.MOD], [X], lambda e, pp=pp, n=n: e.scalar_tensor_tensor(
                    out=X[:, n, :], in0=pp[:, 0:TT], scalar=GF[:, n:n + 1], in1=X[:, n, :], op0=ALU.mult, op1=ALU.add))
            if final:
                self.final_norm_tile(X, yout, sq, rstd, TT)
                S.dma([yout], [], OUT3[:, :, sl], yout[:])
            else:
                S.dma([X], [], XD3[:, :, sl], X[:])
        P.close()

    def final_norm_tile(self, X, yout, sq, rstd, n):
        S = self.S
        S.op("act", [X], [sq], lambda e: e.activation(out=sq[:, :, 0:n], in_=X[:, :, 0:n], func=AF.Square))
        pss = self.ps()
        def mm(e):
            last = None
            for kc in range(KC):
                last = e.matmul(pss[:, 0:n], lhsT=self.ones32[:, :], rhs=sq[:, kc, 0:n], start=(kc == 0), stop=(kc == KC - 1))
            return last
        S.op("pe", [sq, self.ones32], [pss], mm)
        S.op("dve", [pss], [rstd], lambda e: e.tensor_scalar(
            out=rstd[:, 0:n], in0=pss[:, 0:n], scalar1=1.0 / D, scalar2=EPS, op0=ALU.mult, op1=ALU.add))
        S.op("act", [rstd], [rstd], lambda e: e.activation(out=rstd[:, 0:n], in_=rstd[:, 0:n], func=AF.Sqrt))
        S.op("dve", [rstd], [rstd], lambda e: e.reciprocal(out=rstd[:, 0:n], in_=rstd[:, 0:n]))
        for kc in range(KC):
            S.op("dve", [X, rstd, self.VEC], [yout], lambda e, kc=kc: e.scalar_tensor_tensor(
                out=yout[:, kc, 0:n], in0=X[:, kc, 0:n], scalar=self.vcol("fin", kc), in1=rstd[:, 0:n], op0=ALU.mult, op1=ALU.mult))

    def phase_attn(self, heads_moba=range(8), heads_sb=range(8), qtiles=range(8)):
        S = self.S
        P = Phase(S)
        QT, VTOK, OT = self.dr["QT"], self.dr["VTOK"], self.dr["OT"]
        cmask = [P.sbuf("cmask%d" % d, [128, 512], BF16) for d in range(4)]
        smask = [P.sbuf("smask%d" % d, [128, 512], BF16) for d in range(4)]
        for d in range(4):
            for mk, op in ((cmask[d], ALU.is_ge), (smask[d], ALU.is_gt)):
                def f(e, mk=mk, op=op, d=d):
                    e.memset(mk[:], 1.0)
                    return e.affine_select(out=mk[:], in_=mk[:], compare_op=op, fill=0.0, base=-128 * d,
                                           pattern=[[1, 512]], channel_multiplier=-1)
                S.op("pool", [], [mk], f)
        NUI = P.sbuf("NUI", [128, 128], BF16)
        def f(e):
            e.memset(NUI[:], -1.0)
            return e.affine_select(out=NUI[:], in_=NUI[:], compare_op=ALU.is_ge, fill=0.0, base=0,
                                   pattern=[[-1, 128]], channel_multiplier=1)
        S.op("pool", [], [NUI], f)
        onesb = P.sbuf("onesb", [128, 128], BF16)
        S.op("pool", [], [onesb], lambda e: e.memset(onesb[:], 1.0))
        KA = P.sbuf("KA", [33, T], BF16)
        def f(e):
            e.memset(KA[:], 0.0)
            e.memset(KA[0:16, :], BIG)
            e.affine_select(out=KA[0:16, :], in_=KA[0:16, :], compare_op=ALU.is_ge, fill=0.0, base=0,
                            pattern=[[1, T]], channel_multiplier=-256)
            e.affine_select(out=KA[0:16, :], in_=KA[0:16, :], compare_op=ALU.is_ge, fill=0.0, base=255,
                            pattern=[[-1, T]], channel_multiplier=256)
            return e.memset(KA[32:33, :], -1.0)
        S.op("pool", [], [KA], f)
        Esel = P.sbuf("Esel", [64, 33], BF16)
        def f(e):
            e.memset(Esel[:], 0.0)
            return e.memset(Esel[:, 32:33], 1.0)
        S.op("pool", [], [Esel], f)
        past01 = P.sbuf("past01", [128, 32, 16], F32)
        pastb = P.sbuf("pastb", [128, 32, 16], F32)
        ownm1 = P.sbuf("ownm1", [128, 32, 16], F32)
        def f(e):
            e.memset(past01[:], 1.0)
            return e.affine_select(out=past01[:], in_=past01[:], compare_op=ALU.is_ge, fill=0.0, base=-2,
                                   pattern=[[1, 32], [-2, 16]], channel_multiplier=0)
        S.op("pool", [], [past01], f)
        def f(e):
            e.memset(pastb[:], 0.0)
            return e.affine_select(out=pastb[:], in_=pastb[:], compare_op=ALU.is_ge, fill=-1e30, base=-2,
                                   pattern=[[1, 32], [-2, 16]], channel_multiplier=0)
        S.op("pool", [], [pastb], f)
        def f(e):
            e.memset(ownm1[:], 0.0)
            e.affine_select(out=ownm1[:], in_=ownm1[:], compare_op=ALU.is_ge, fill=-1.0, base=0,
                            pattern=[[1, 32], [-2, 16]], channel_multiplier=0)
            return e.affine_select(out=ownm1[:], in_=ownm1[:], compare_op=ALU.is_ge, fill=-1.0, base=1,
                                   pattern=[[-1, 32], [2, 16]], channel_multiplier=0)
        S.op("pool", [], [ownm1], f)
        Qb = [P.sbuf("Q%d" % k, [64, T], BF16) for k in range(2)]
        Kb = [P.sbuf("K%d" % k, [64, T], BF16) for k in range(2)]
        Vb = [P.sbuf("V%d" % k, [128, 32, 65], BF16) for k in range(2)]
        for k in range(2):
            S.op("pool", [], [Vb[k]], lambda e, k=k: e.memset(Vb[k][:], 1.0))
        QA = P.sbuf("QA", [33, T], BF16)
        S.op("pool", [], [QA], lambda e: e.memset(QA[:], 0.0))
        sqQ = P.sbuf("sqQ", [64, T], BF16)
        kmean = P.sbuf("kmean", [64, 16], F32)
        kmeanb = P.sbuf("kmeanb", [64, 16], BF16)
        gm = P.sbuf("gm", [128, 32, 16], F32)
        m8 = P.sbuf("m8", [128, 32, 8], F32)
        MB = P.sbuf("MB", [128, 32, 16], F32)
        kmx = P.sbuf("kmx", [33, 8], F32)
        kmax2 = P.sbuf("kmax2", [33, 1], F32)
        Pt = [P.sbuf("Pt%d" % k, [128, 512], BF16) for k in range(3)]
        Eb = [P.sbuf("E%d" % k, [128, 512], F32) for k in range(2)]
        Lp = [P.sbuf("Lp%d" % k, [128, 512], BF16) for k in range(3)]
        exb = [P.sbuf("ex%d" % k, [128, 512], F32) for k in range(2)]
        rs = P.sbuf("rs", [128, 512], F32)
        Osb = P.sbuf("Osb", [65, 512], F32)
        rden = P.sbuf("rden", [128, 512], F32)
        S.op("pool", [], [rden], lambda e: e.memset(rden[:], 0.0))
        sel64 = P.sbuf("sel64", [128, 128], F32)
        def f(e):
            e.memset(sel64[:], 0.0)
            return e.memset(sel64[64:65, :], 1.0)
        S.op("pool", [], [sel64], f)
        ost = [P.sbuf("ost%d" % k, [64, 512], BF16) for k in range(2)]
        PSr = self.PS[0:6]
        PSo = self.PS[6:8]
        rr = [0]
        def psr():
            b = PSr[rr[0] % 6]
            rr[0] += 1
            return b
        jobs = [("m", h) for h in heads_moba] + [("s", h) for h in heads_sb]
        def load_head(idx):
            kind, h = jobs[idx]
            k = idx % 2
            qrow = (0 if kind == "m" else 1024) + h * 64
            krow = (512 if kind == "m" else 1536) + h * 64
            vcol = (0 if kind == "m" else 512) + h * 64
            S.dma([], [Qb[k]], Qb[k][:], QT[qrow:qrow + 64, :])
            S.dma([], [Kb[k]], Kb[k][:], QT[krow:krow + 64, :])
            S.dma([], [Vb[k]], Vb[k][:, :, 0:64], VTOK[:, vcol:vcol + 64].rearrange("(j p) d -> p j d", p=128))
        load_head(0)
        oi = 0
        for idx, (kind, h) in enumerate(jobs):
            Q, Kt, V = Qb[idx % 2], Kb[idx % 2], Vb[idx % 2]
            if idx + 1 < len(jobs):
                load_head(idx + 1)
            if kind == "m":
                S.op("dve", [Kt], [kmean], lambda e: e.tensor_reduce(
                    out=kmean[:], in_=Kt[:].rearrange("p (n k) -> p n k", k=256), axis=AX.X, op=ALU.add))
                S.op("dve", [kmean], [kmeanb], lambda e: e.tensor_scalar(
                    out=kmeanb[:], in0=kmean[:], scalar1=1.0 / 256, scalar2=None, op0=ALU.mult))
                GP = psr()
                def mm(e):
                    last = None
                    for tt in range(32):
                        last = e.matmul(GP[:, tt * 16:(tt + 1) * 16], lhsT=Q[:, tt * 128:(tt + 1) * 128], rhs=kmeanb[:, :],
                                        start=True, stop=True)
                    return last
                S.op("pe", [Q, kmeanb], [GP], mm)
                S.op("dve", [GP, pastb], [gm], lambda e: e.tensor_tensor(
                    out=gm[:].rearrange("p a b -> p (a b)"), in0=GP[:, :], in1=pastb[:].rearrange("p a b -> p (a b)"), op=ALU.add))
                for tt in range(32):
                    S.op("dve", [gm], [m8], lambda e, tt=tt: e.max(out=m8[:, tt, :], in_=gm[:, tt, :]))
                for tt in range(32):
                    S.op("dve", [gm, m8, past01], [MB], lambda e, tt=tt: e.scalar_tensor_tensor(
                        out=MB[:, tt, :], in0=gm[:, tt, :], scalar=m8[:, tt, 2:3], in1=past01[:, tt, :], op0=ALU.is_ge, op1=ALU.mult))
                S.op("dve", [MB, ownm1], [MB], lambda e: e.tensor_tensor(
                    out=MB[:].rearrange("p a b -> p (a b)"), in0=MB[:].rearrange("p a b -> p (a b)"),
                    in1=ownm1[:].rearrange("p a b -> p (a b)"), op=ALU.add))
                for g in range(8):
                    TP = psr()
                    def tr(e, TP=TP, g=g):
                        last = None
                        for k in range(4):
                            last = e.transpose(out=TP[0:16, k * 128:(k + 1) * 128], in_=MB[:, 4 * g + k, :], identity=self.ident32[:, :])
                        return last
                    S.op("pe", [MB, self.ident32], [TP], tr)
                    S.op("dve", [TP], [QA], lambda e, TP=TP, g=g: e.tensor_copy(out=QA[0:16, g * 512:(g + 1) * 512], in_=TP[0:16, :]))
                S.op("pool", [Kt], [sqQ], lambda e: e.tensor_tensor(out=sqQ[:], in0=Kt[:], in1=Kt[:], op=ALU.mult))
                for g in range(8):
                    KP = psr()
                    S.op("pe", [sqQ, Esel], [KP], lambda e, KP=KP, g=g: e.matmul(
                        KP[0:33, :], lhsT=Esel[:, :], rhs=sqQ[:, g * 512:(g + 1) * 512], start=True, stop=True))
                    S.op("dve", [KP], [kmx], lambda e, KP=KP, g=g: e.tensor_reduce(
                        out=kmx[32:33, g:g + 1], in_=KP[32:33, :], axis=AX.X, op=ALU.max))
                S.op("dve", [kmx], [kmax2], lambda e: e.tensor_reduce(out=kmax2[32:33, :], in_=kmx[32:33, :], axis=AX.X, op=ALU.max))
                S.op("pool", [Q], [sqQ], lambda e: e.tensor_tensor(out=sqQ[:], in0=Q[:], in1=Q[:], op=ALU.mult))
                for g in range(8):
                    SP = psr()
                    S.op("pe", [sqQ, Esel], [SP], lambda e, SP=SP, g=g: e.matmul(
                        SP[0:33, :], lhsT=Esel[:, :], rhs=sqQ[:, g * 512:(g + 1) * 512], start=True, stop=True))
                    S.op("act", [SP, kmax2], [QA], lambda e, SP=SP, g=g: e.activation(
                        out=QA[32:33, g * 512:(g + 1) * 512], in_=SP[32:33, :], func=AF.Sqrt, scale=kmax2[32:33, 0:1]))
                if getattr(self, "debug", False):
                    self.dbg("dbgQA", QA, QA[:])
                    self.dbg("dbgkmax2", kmax2, kmax2[:])
                    self.dbg("dbgMB", MB, MB[:])
                    self.dbg("dbggm", gm, gm[:])
                pairs = [(i, j) for i in qtiles for j in range(4 * i + 4)]
                st = {}
                def stageA(p):
                    i, j = pairs[p]
                    sp = psr()
                    def mm(e):
                        e.matmul(sp[:, :], lhsT=Kt[:, j * 128:(j + 1) * 128], rhs=Q[:, i * 512:(i + 1) * 512], start=True, stop=False)
                        return e.matmul(sp[:, :], lhsT=KA[0:33, j * 128:(j + 1) * 128], rhs=QA[0:33, i * 512:(i + 1) * 512], start=False, stop=True)
                    S.op("pe", [Kt, Q, KA, QA], [sp], mm)
                    pt = Pt[p % 3]
                    S.op("act", [sp], [pt], lambda e: e.activation(out=pt[:], in_=sp[:, :], func=AF.Exp))
                    if j >= 4 * i:
                        S.op("pool", [pt, cmask[j - 4 * i]], [pt], lambda e: e.tensor_tensor(out=pt[:], in0=pt[:], in1=cmask[j - 4 * i][:], op=ALU.mult))
                    st[p] = pt
                def stageC(p):
                    nonlocal oi
                    i, j = pairs[p]
                    pt = st.pop(p)
                    last = 4 * i + 3
                    if j == 0:
                        oi += 1
                    OP = PSo[oi % 2]
                    S.op("pe", [V, pt], [OP], lambda e: e.matmul(OP[0:65, :], lhsT=V[:, j, :], rhs=pt[:], start=(j == 0), stop=(j == last)))
                    if j == last:
                        S.op("act", [OP], [Osb], lambda e: e.activation(out=Osb[:], in_=OP[0:65, :], func=AF.Copy))
                        if getattr(self, "debug", False) and i == 0:
                            self.dbg("dbgOsb", Osb, Osb[:])
                        S.op("dve", [Osb], [rden], lambda e: e.reciprocal(out=rden[64:65, :], in_=Osb[64:65, :]))
                        BP = psr()
                        S.op("pe", [rden, sel64], [BP], lambda e: e.matmul(
                            BP[:, :], lhsT=sel64[:, :], rhs=rden[:, :], start=True, stop=True))
                        o = ost[i % 2]
                        S.op("dve", [Osb, BP], [o], lambda e: e.tensor_tensor(out=o[:], in0=Osb[0:64, :], in1=BP[0:64, :], op=ALU.mult))
                        S.dma([o], [], OT[h * 64:(h + 1) * 64, i * 512:(i + 1) * 512], o[:])
                for p in range(len(pairs) + 1):
                    if p < len(pairs):
                        stageA(p)
                    if p >= 1:
                        stageC(p - 1)
            else:
                pairs = [(i, j) for i in qtiles for j in range(4 * i + 3, -1, -1)]
                st = {}
                def stageA(p):
                    i, j = pairs[p]
                    z = psr()
                    S.op("pe", [Kt, Q], [z], lambda e: e.matmul(
                        z[:, :], lhsT=Kt[:, j * 128:(j + 1) * 128], rhs=Q[:, i * 512:(i + 1) * 512], start=True, stop=False))
                    E = Eb[p % 2]
                    lp = Lp[p % 3]
                    S.op("act", [z], [E], lambda e: e.activation(out=E[:], in_=z[:, :], func=AF.Exp))
                    S.op("act", [E], [lp], lambda e: e.activation(out=lp[:], in_=E[:], func=AF.Ln, bias=1.0, scale=1.0))
                    if j >= 4 * i:
                        S.op("pool", [lp, smask[j - 4 * i]], [lp], lambda e: e.tensor_tensor(out=lp[:], in0=lp[:], in1=smask[j - 4 * i][:], op=ALU.mult))
                    st[p] = (z, lp)
                def stageB(p):
                    i, j = pairs[p]
                    z, lp = st[p]
                    first = (j == 4 * i + 3)
                    S.op("pe", [NUI, lp], [z], lambda e: e.matmul(z[:, :], lhsT=NUI[:, :], rhs=lp[:], start=False, stop=True))
                    cs = psr()
                    S.op("pe", [onesb, lp], [cs], lambda e: e.matmul(cs[:, :], lhsT=onesb[:, :], rhs=lp[:], start=True, stop=True))
                    ex = exb[p % 2]
                    if first:
                        S.op("dve", [z], [ex], lambda e: e.tensor_copy(out=ex[:], in_=z[:, :]))
                        S.op("dve", [cs], [rs], lambda e: e.tensor_copy(out=rs[:], in_=cs[:, :]))
                    else:
                        S.op("dve", [z, rs], [ex], lambda e: e.tensor_tensor(out=ex[:], in0=z[:, :], in1=rs[:], op=ALU.subtract))
                        S.op("dve", [cs, rs], [rs], lambda e: e.tensor_tensor(out=rs[:], in0=cs[:, :], in1=rs[:], op=ALU.add))
                    w = Pt[p % 3]
                    S.op("act", [ex], [w], lambda e: e.activation(out=w[:], in_=ex[:], func=AF.Exp))
                    if j >= 4 * i:
                        S.op("pool", [w, smask[j - 4 * i]], [w], lambda e: e.tensor_tensor(out=w[:], in0=w[:], in1=smask[j - 4 * i][:], op=ALU.mult))
                    st[p] = w
                def stageC(p):
                    nonlocal oi
                    i, j = pairs[p]
                    w = st.pop(p)
                    first = (j == 4 * i + 3)
                    if first:
                        oi += 1
                    OP = PSo[oi % 2]
                    S.op("pe", [V, w], [OP], lambda e: e.matmul(OP[0:64, :], lhsT=V[:, j, 0:64], rhs=w[:], start=first, stop=(j == 0)))
                    if j == 0:
                        o = ost[i % 2]
                        S.op("act", [OP], [o], lambda e: e.activation(out=o[:], in_=OP[0:64, :], func=AF.Copy))
                        S.dma([o], [], OT[512 + h * 64:512 + (h + 1) * 64, i * 512:(i + 1) * 512], o[:])
                n = len(pairs)
                for p in range(n + 2):
                    if p < n:
                        stageA(p)
                    if 1 <= p <= n:
                        stageB(p - 1)
                    if p >= 2:
                        stageC(p - 2)
        P.close()

    def phase_gdn(self, li, heads=range(8), nchunks=32):
        S = self.S
        P = Phase(S)
        GT, ZT, AB, OT = self.dr["GT"], self.dr["VTOK"], self.dr["AB"], self.dr["OT"]
        NCH = 32
        C = 128
        UT = P.sbuf("UT", [128, 128], F32)
        SU = P.sbuf("SU", [128, 128], F32)
        def f(e):
            e.memset(UT[:], 1.0)
            return e.affine_select(out=UT[:], in_=UT[:], compare_op=ALU.is_ge, fill=0.0, base=0,
                                   pattern=[[1, 128]], channel_multiplier=-1)
        S.op("pool", [], [UT], f)
        def f(e):
            e.memset(SU[:], 1.0)
            return e.affine_select(out=SU[:], in_=SU[:], compare_op=ALU.is_gt, fill=0.0, base=0,
                                   pattern=[[-1, 128]], channel_multiplier=1)
        S.op("pool", [], [SU], f)
        I32m = self.ident32
        ab = P.sbuf("ab", [128, NCH, 16], F32)
        S.dma([], [ab], ab[:], AB.rearrange("(c p) n -> p c n", p=128))
        graw = P.sbuf("graw", [128, NCH, 8], F32)
        beta = P.sbuf("beta", [128, NCH, 8], F32)
        nbeta = P.sbuf("nbeta", [128, NCH, 8], F32)
        negA = P.sbuf("negA", [128, 8], F32)
        tmpa = P.sbuf("tmpa", [128, NCH, 8], F32)
        S.op("act", [self.VEC], [negA], lambda e: e.activation(out=negA[:], in_=self.vcol("alog", li * 8, 8), func=AF.Exp))
        S.op("dve", [negA], [negA], lambda e: e.tensor_scalar(out=negA[:], in0=negA[:], scalar1=-1.0, scalar2=None, op0=ALU.mult))
        for c in range(NCH):
            S.op("dve", [ab, self.VEC], [tmpa], lambda e, c=c: e.tensor_tensor(
                out=tmpa[:, c, :], in0=ab[:, c, 0:8], in1=self.vcol("dtb", li * 8, 8), op=ALU.add))
        S.op("act", [tmpa], [tmpa], lambda e: e.activation(out=tmpa[:], in_=tmpa[:], func=AF.Exp))
        S.op("act", [tmpa], [tmpa], lambda e: e.activation(out=tmpa[:], in_=tmpa[:], func=AF.Ln, bias=1.0, scale=1.0))
        for c in range(NCH):
            S.op("dve", [tmpa, negA], [graw], lambda e, c=c: e.tensor_tensor(
                out=graw[:, c, :], in0=tmpa[:, c, :], in1=negA[:], op=ALU.mult))
        S.op("act", [ab], [beta], lambda e: e.activation(out=beta[:], in_=ab[:, :, 8:16], func=AF.Sigmoid))
        S.op("dve", [beta], [nbeta], lambda e: e.tensor_scalar(out=nbeta[:], in0=beta[:], scalar1=-1.0, scalar2=None, op0=ALU.mult))
        gc = P.sbuf("gc", [128, NCH, 8], F32)
        egc = P.sbuf("egc", [128, NCH, 8], F32)
        egl = P.sbuf("egl", [128, NCH, 8], F32)
        edec = P.sbuf("edec", [128, NCH, 8], F32)
        bgc = P.sbuf("bgc", [128, NCH, 8], F32)
        pgc = self.ps()
        S.op("pe", [UT, graw], [pgc], lambda e: e.matmul(pgc[:, 0:256], lhsT=UT[:, :], rhs=graw[:].rearrange("p a b -> p (a b)"), start=True, stop=True))
        pgl = self.ps()
        S.op("pe", [self.ones32, graw], [pgl], lambda e: e.matmul(pgl[:, 0:256], lhsT=self.ones32[:, :], rhs=graw[:].rearrange("p a b -> p (a b)"), start=True, stop=True))
        fl = lambda b: b[:].rearrange("p a b -> p (a b)")
        S.op("dve", [pgc], [gc], lambda e: e.tensor_copy(out=fl(gc), in_=pgc[:, 0:256]))
        S.op("act", [pgc], [egc], lambda e: e.activation(out=fl(egc), in_=pgc[:, 0:256], func=AF.Exp))
        S.op("act", [pgl], [egl], lambda e: e.activation(out=fl(egl), in_=pgl[:, 0:256], func=AF.Exp))
        S.op("dve", [pgl, gc], [edec], lambda e: e.tensor_tensor(out=fl(edec), in0=pgl[:, 0:256], in1=fl(gc), op=ALU.subtract))
        S.op("act", [edec], [edec], lambda e: e.activation(out=fl(edec), in_=fl(edec), func=AF.Exp))
        S.op("dve", [beta, egc], [bgc], lambda e: e.tensor_tensor(out=fl(bgc), in0=fl(beta), in1=fl(egc), op=ALU.mult))
        xin = [P.sbuf("xin%d" % k, [128, T], BF16) for k in range(2)]
        acc = P.sbuf("acc", [128, T], F32)
        ybuf = P.sbuf("ybuf", [128, T], F32)
        qTb = P.sbuf("qTb", [128, T], BF16)
        kTf = P.sbuf("kTf", [128, T], F32)
        kTb = P.sbuf("kTb", [128, T], BF16)
        vTf = P.sbuf("vTf", [128, T], F32)
        rn = P.sbuf("rn", [128, 512], F32)
        zt = P.sbuf("zt", [128, NCH, 128], BF16)
        zg = P.sbuf("zg", [128, NCH, 128], BF16)
        U_ = P.sbuf("U_", [128, NCH, 128], F32)
        WT = P.sbuf("WT", [128, NCH, 128], BF16)
        AT = P.sbuf("AT", [128, NCH, 128], BF16)
        KD = P.sbuf("KD", [128, NCH, 128], BF16)
        ot = P.sbuf("ot", [128, T], BF16)
        Sst = P.sbuf("Sst", [128, 128], F32)
        Sbf = P.sbuf("Sbf", [128, 128], BF16)
        NR = 3
        def rot(name, dt=F32, n=NR):
            return [P.sbuf("%s%d" % (name, k), [128, 128], dt) for k in range(n)]
        ktok, vb, kbg, G1, dcs, dsc, Nm, Am, Pm, Nn, An = (rot(x) for x in ("ktok", "vb", "kbg", "G1", "dcs", "dsc", "Nm", "Am", "Pm", "Nn", "An"))
        vnew = rot("vnew", BF16, 2)
        o2b = rot("o2b", F32, 2)
        osb = rot("osb", F32, 2)
        ysb = rot("ysb", F32, 2)
        ssq = P.sbuf("ssq", [128, 2], F32)
        junk = P.sbuf("junk", [128, 128], F32)
        ew = [0]
        def evac_copy(src_ps, dst, dst_ap, src_ap, extra_reads=()):
            ew[0] += 1
            if ew[0] % 2 == 0:
                S.op("act", [src_ps] + list(extra_reads), [dst], lambda e: e.activation(out=dst_ap, in_=src_ap, func=AF.Copy))
            else:
                S.op("dve", [src_ps] + list(extra_reads), [dst], lambda e: e.tensor_copy(out=dst_ap, in_=src_ap))
        for h in heads:
            for wi, which in enumerate(("q", "k", "v")):
                xi = xin[wi % 2]
                row = wi * 1024 + h * 128
                S.dma([], [xi], xi[:], GT[row:row + 128, :])
                def wcol(j, wi=wi):
                    return self.vcol("conv", (li * 4 + j) * 24 + wi * 8 + h)
                S.op("dve", [xi, self.VEC], [acc], lambda e, xi=xi: e.tensor_scalar(out=acc[:], in0=xi[:], scalar1=wcol(3), scalar2=None, op0=ALU.mult))
                for sh in (1, 2, 3):
                    S.op("dve", [xi, self.VEC, acc], [acc], lambda e, xi=xi, sh=sh: e.scalar_tensor_tensor(
                        out=acc[:, sh:], in0=xi[:, 0:T - sh], scalar=wcol(3 - sh), in1=acc[:, sh:], op0=ALU.mult, op1=ALU.add))
                dst = vTf if which == "v" else ybuf
                S.op("act", [acc], [dst], lambda e, dst=dst: e.activation(out=dst[:], in_=acc[:], func=AF.Silu))
                if which == "v":
                    continue
                S.op("act", [ybuf], [acc], lambda e: e.activation(out=acc[:], in_=ybuf[:], func=AF.Square))
                for g in range(8):
                    sl = slice(g * 512, (g + 1) * 512)
                    pss = self.ps()
                    S.op("pe", [acc, self.ones32], [pss], lambda e, pss=pss, sl=sl: e.matmul(pss[:, :], lhsT=self.ones32[:, :], rhs=acc[:, sl], start=True, stop=True))
                    S.op("dve", [pss], [rn], lambda e, pss=pss: e.tensor_scalar(out=rn[:], in0=pss[:, :], scalar1=EPS, scalar2=None, op0=ALU.add))
                    S.op("act", [rn], [rn], lambda e: e.activation(out=rn[:], in_=rn[:], func=AF.Sqrt))
                    S.op("dve", [rn], [rn], lambda e: e.reciprocal(out=rn[:], in_=rn[:]))
                    if which == "q":
                        S.op("dve", [ybuf, rn], [qTb], lambda e, sl=sl: e.scalar_tensor_tensor(
                            out=qTb[:, sl], in0=ybuf[:, sl], scalar=float(128 ** -0.5), in1=rn[:], op0=ALU.mult, op1=ALU.mult))
                    else:
                        S.op("dve", [ybuf, rn], [kTf], lambda e, sl=sl: e.tensor_tensor(out=kTf[:, sl], in0=ybuf[:, sl], in1=rn[:], op=ALU.mult))
                        S.op("pool", [kTf], [kTb], lambda e, sl=sl: e.tensor_copy(out=kTb[:, sl], in_=kTf[:, sl]))
            S.dma([], [zt], zt[:], ZT[:, h * 128:(h + 1) * 128].rearrange("(c p) d -> p c d", p=128))
            S.op("act", [zt], [zt], lambda e: e.activation(out=zt[:], in_=zt[:], func=AF.Silu))
            for c in range(NCH):
                S.op("pool", [zt, self.VEC], [zg], lambda e, c=c: e.tensor_tensor(
                    out=zg[:, c, :], in0=zt[:, c, :], in1=self.vcol("gngb", li * 128, 128), op=ALU.mult))
            for c in range(nchunks):
                cs = slice(c * C, (c + 1) * C)
                col = slice(h, h + 1)
                r = c % NR
                pk = self.ps()
                S.op("pe", [kTf, I32m], [pk], lambda e, pk=pk: e.transpose(out=pk[:, 0:128], in_=kTf[:, cs], identity=I32m[:, :]))
                evac_copy(pk, ktok[r], ktok[r][:], pk[:, 0:128])
                pv = self.ps()
                S.op("pe", [vTf, I32m], [pv], lambda e, pv=pv: e.transpose(out=pv[:, 0:128], in_=vTf[:, cs], identity=I32m[:, :]))
                S.op("dve", [pv, beta], [vb[r]], lambda e, pv=pv: e.tensor_scalar(out=vb[r][:], in0=pv[:, 0:128], scalar1=beta[:, c, col], scalar2=None, op0=ALU.mult))
                S.op("act", [ktok[r], bgc], [kbg[r]], lambda e: e.activation(out=kbg[r][:], in_=ktok[r][:], func=AF.Copy, scale=bgc[:, c, col]))
                S.op("act", [ktok[r], edec], [KD], lambda e: e.activation(out=KD[:, c, :], in_=ktok[r][:], func=AF.Copy, scale=edec[:, c, col]))
                S.op("dve", [SU, graw], [G1[r]], lambda e: e.tensor_scalar(out=G1[r][:], in0=SU[:], scalar1=graw[:, c, col], scalar2=None, op0=ALU.mult))
                pd1 = self.ps()
                S.op("pe", [UT, G1[r]], [pd1], lambda e, pd1=pd1: e.matmul(pd1[:, 0:128], lhsT=UT[:, :], rhs=G1[r][:], start=True, stop=True))
                pd2 = self.ps()
                S.op("pe", [UT, G1[r]], [pd2], lambda e, pd2=pd2: e.matmul(pd2[:, 0:128], lhsT=G1[r][:], rhs=UT[:, :], start=True, stop=True))
                S.op("act", [pd1], [dcs[r]], lambda e, pd1=pd1: e.activation(out=dcs[r][:], in_=pd1[:, 0:128], func=AF.Exp))
                S.op("act", [pd2], [dsc[r]], lambda e, pd2=pd2: e.activation(out=dsc[r][:], in_=pd2[:, 0:128], func=AF.Exp))
                S.op("pool", [dcs[r], SU], [dcs[r]], lambda e: e.tensor_tensor(out=dcs[r][:], in0=dcs[r][:], in1=SU[:], op=ALU.mult))
                S.op("pool", [dsc[r], UT], [dsc[r]], lambda e: e.tensor_tensor(out=dsc[r][:], in0=dsc[r][:], in1=UT[:], op=ALU.mult))
                pkk = self.ps()
                S.op("pe", [kTf], [pkk], lambda e, pkk=pkk: e.matmul(pkk[:, 0:128], lhsT=kTf[:, cs], rhs=kTf[:, cs], start=True, stop=True))
                S.op("dve", [pkk, nbeta, dcs[r]], [Nm[r]], lambda e, pkk=pkk: e.scalar_tensor_tensor(
                    out=Nm[r][:], in0=pkk[:, 0:128], scalar=nbeta[:, c, col], in1=dcs[r][:], op0=ALU.mult, op1=ALU.mult))
                pa = self.ps()
                S.op("pe", [Nm[r], I32m], [pa], lambda e, pa=pa: e.transpose(out=pa[:, 0:128], in_=Nm[r][:], identity=I32m[:, :]))
                evac_copy(pa, Am[r], Am[r][:], pa[:, 0:128])
                S.op("pool", [Am[r], I32m], [Pm[r]], lambda e: e.tensor_tensor(out=Pm[r][:], in0=Am[r][:], in1=I32m[:], op=ALU.add))
                pqk = self.ps()
                S.op("pe", [kTb, qTb], [pqk], lambda e, pqk=pqk: e.matmul(pqk[:, 0:128], lhsT=kTb[:, cs], rhs=qTb[:, cs], start=True, stop=True))
                S.op("dve", [pqk, dsc[r]], [AT], lambda e, pqk=pqk: e.tensor_tensor(out=AT[:, c, :], in0=pqk[:, 0:128], in1=dsc[r][:], op=ALU.mult))
                Ncur, Acur = Nm[r], Am[r]
                for lvl in range(1, 7):
                    Nnx = Nn[r] if lvl % 2 == 1 else Nm[r]
                    Anx = An[r] if lvl % 2 == 1 else Am[r]
                    pn = self.ps()
                    S.op("pe", [Acur, Ncur], [pn], lambda e, pn=pn, Acur=Acur, Ncur=Ncur: e.matmul(pn[:, 0:128], lhsT=Acur[:], rhs=Ncur[:], start=True, stop=True))
                    if lvl < 6:
                        pa2 = self.ps()
                        S.op("pe", [Acur, Ncur], [pa2], lambda e, pa2=pa2, Acur=Acur, Ncur=Ncur: e.matmul(pa2[:, 0:128], lhsT=Ncur[:], rhs=Acur[:], start=True, stop=True))
                    evac_copy(pn, Nnx, Nnx[:], pn[:, 0:128])
                    if lvl < 6:
                        evac_copy(pa2, Anx, Anx[:], pa2[:, 0:128])
                    pp = self.ps()
                    S.op("pe", [Nnx, Pm[r]], [pp], lambda e, pp=pp, Nnx=Nnx: e.matmul(pp[:, 0:128], lhsT=Nnx[:], rhs=Pm[r][:], start=True, stop=True))
                    S.op("dve", [pp, Pm[r]], [Pm[r]], lambda e, pp=pp: e.tensor_tensor(out=Pm[r][:], in0=pp[:, 0:128], in1=Pm[r][:], op=ALU.add))
                    Ncur, Acur = Nnx, Anx
                pu = self.ps()
                S.op("pe", [Pm[r], vb[r]], [pu], lambda e, pu=pu: e.matmul(pu[:, 0:128], lhsT=Pm[r][:], rhs=vb[r][:], start=True, stop=True))
                evac_copy(pu, U_, U_[:, c, :], pu[:, 0:128])
                pw = self.ps()
                S.op("pe", [Pm[r], kbg[r]], [pw], lambda e, pw=pw: e.matmul(pw[:, 0:128], lhsT=kbg[r][:], rhs=Pm[r][:], start=True, stop=True))
                evac_copy(pw, WT, WT[:, c, :], pw[:, 0:128])
            S.op("pool", [], [Sbf], lambda e: e.memset(Sbf[:], 0.0))
            S.op("pool", [], [Sst], lambda e: e.memset(Sst[:], 0.0))
            for c in range(nchunks):
                cs = slice(c * C, (c + 1) * C)
                col = slice(h, h + 1)
                r2 = c % 2
                p1 = self.ps()
                S.op("pe", [WT, Sbf], [p1], lambda e, p1=p1: e.matmul(p1[:, 0:128], lhsT=WT[:, c, :], rhs=Sbf[:], start=True, stop=True))
                p2a = self.ps()
                S.op("pe", [qTb, Sbf], [p2a], lambda e, p2a=p2a: e.matmul(p2a[:, 0:128], lhsT=qTb[:, cs], rhs=Sbf[:], start=True, stop=True))
                S.op("dve", [U_, p1], [vnew[r2]], lambda e, p1=p1: e.tensor_tensor(out=vnew[r2][:], in0=U_[:, c, :], in1=p1[:, 0:128], op=ALU.subtract))
                p2b = self.ps()
                S.op("pe", [AT, vnew[r2]], [p2b], lambda e, p2b=p2b: e.matmul(p2b[:, 0:128], lhsT=AT[:, c, :], rhs=vnew[r2][:], start=True, stop=True))
                p3 = self.ps()
                S.op("pe", [KD, vnew[r2]], [p3], lambda e, p3=p3: e.matmul(p3[:, 0:128], lhsT=KD[:, c, :], rhs=vnew[r2][:], start=True, stop=True))
                S.op("dve", [Sst, egl, p3], [Sst], lambda e, p3=p3: e.scalar_tensor_tensor(
                    out=Sst[:], in0=Sst[:], scalar=egl[:, c, col], in1=p3[:, 0:128], op0=ALU.mult, op1=ALU.add))
                S.op("act", [Sst], [Sbf], lambda e: e.activation(out=Sbf[:], in_=Sst[:], func=AF.Copy))
                S.op("act", [p2b], [o2b[r2]], lambda e, p2b=p2b: e.activation(out=o2b[r2][:], in_=p2b[:, 0:128], func=AF.Copy))
                S.op("dve", [p2a, egc, o2b[r2]], [osb[r2]], lambda e, p2a=p2a: e.scalar_tensor_tensor(
                    out=osb[r2][:], in0=p2a[:, 0:128], scalar=egc[:, c, col], in1=o2b[r2][:], op0=ALU.mult, op1=ALU.add))
                S.op("act", [osb[r2]], [junk, ssq], lambda e: e.activation(out=junk[:], in_=osb[r2][:], func=AF.Square, accum_out=ssq[:, r2:r2 + 1]))
                S.op("dve", [ssq], [ssq], lambda e: e.tensor_scalar(out=ssq[:, r2:r2 + 1], in0=ssq[:, r2:r2 + 1], scalar1=1.0 / 128, scalar2=EPS, op0=ALU.mult, op1=ALU.add))
                S.op("act", [ssq], [ssq], lambda e: e.activation(out=ssq[:, r2:r2 + 1], in_=ssq[:, r2:r2 + 1], func=AF.Sqrt))
                S.op("dve", [ssq], [ssq], lambda e: e.reciprocal(out=ssq[:, r2:r2 + 1], in_=ssq[:, r2:r2 + 1]))
                S.op("dve", [osb[r2], ssq, zg], [ysb[r2]], lambda e: e.scalar_tensor_tensor(
                    out=ysb[r2][:], in0=osb[r2][:], scalar=ssq[:, r2:r2 + 1], in1=zg[:, c, :], op0=ALU.mult, op1=ALU.mult))
                py = self.ps()
                S.op("pe", [ysb[r2], I32m], [py], lambda e, py=py: e.transpose(out=py[:, 0:128], in_=ysb[r2][:], identity=I32m[:, :]))
                evac_copy(py, ot, ot[:, cs], py[:, 0:128])
            S.dma([ot], [], OT[h * 128:(h + 1) * 128, :], ot[:])
        P.close()


def prep_shared(inp):
    perm = rope_partner_perm()
    w = np.asarray(inp["attn_w_in"], np.float32)
    w4 = np.concatenate([w, w[:, :, 0:512][:, :, perm], w[:, :, 512:1024][:, :, perm]], axis=2)
    return {
        "ada_w": np.ascontiguousarray(inp["ada_w"], np.float32),
        "attn_w_in": np.ascontiguousarray(w4),
        "attn_w_out": np.ascontiguousarray(inp["attn_w_out"], np.float32),
        "gdn_w_in": np.ascontiguousarray(inp["gdn_w_in"], np.float32),
        "gdn_w_out": np.ascontiguousarray(inp["gdn_w_out"], np.float32),
        "ffn_w_in": np.ascontiguousarray(inp["ffn_w_in"], np.float32),
        "ffn_w_out": np.ascontiguousarray(inp["ffn_w_out"], np.float32),
    }


def prep_core(inp, b, shared):
    m = dict(shared)
    m["xT"] = np.ascontiguousarray(np.asarray(inp["x"][b], np.float32).T)
    m["vecs"] = build_vecs(inp, b)
    m["pos"] = np.ascontiguousarray(np.asarray(inp["positions"][b], np.int32).reshape(1, T))
    return m


def build_full():
    pg = Prog()
    pg.phase0()
    for l in range(NL):
        pg.phase1(l)
        if l % 2 == 0:
            pg.phase_attn()
        else:
            pg.phase_gdn(l // 2)
        pg.phase34(l, final=(l == NL - 1))
    return pg


def kernel(**inputs):
    inp = {k: np.asarray(v) for k, v in inputs.items()}
    pg = build_full()
    shared = prep_shared(inp)
    in_maps = []
    for b in range(8):
        m = prep_core(inp, b, shared)
        in_maps.append({k: m[k] for k in pg.used_inputs})
    res = run_bass_kernel_spmd(pg.nc, in_maps, core_ids=list(range(8)))
    out = np.empty((8, T, D), np.float32)
    for b in range(8):
        out[b] = np.asarray(res.results[b]["outT"], np.float32).T
    return out
```

```python
import numpy as np
from contextlib import ExitStack
import concourse.bass as bass
import concourse.mybir as mybir
from concourse.bass_utils import run_bass_kernel_spmd

F32 = mybir.dt.float32
BF16 = mybir.dt.bfloat16
I32 = mybir.dt.int32
AF = mybir.ActivationFunctionType
ALU = mybir.AluOpType
AX = mybir.AxisListType

D = 1024
T = 4096
NL = 4
KC = 8
FH = 2816
NJ = 22
EPS = 1e-6
NDMA = 48
BIG = 30000.0
SAME_ENG_SYNC = False
GDN_NR = 2
PSQ_SEPARATE = False
GDN_DBG = 0


class Buf:
    __slots__ = ("t", "w", "r", "name")

    def __init__(self, t, name=""):
        self.t = t
        self.w = None
        self.r = {}
        self.name = name

    def __getitem__(self, idx):
        return self.t[idx]


class PQ:
    __slots__ = ("ap", "b")

    def __init__(self, bank, q):
        self.b = bank
        self.ap = bank.t[:, q * 128:(q + 1) * 128]


class Sched:
    def __init__(self, nc):
        self.nc = nc
        self.E = {"pe": nc.tensor, "act": nc.scalar, "dve": nc.vector, "pool": nc.gpsimd, "sp": nc.sync}
        self.sem = {k: nc.alloc_semaphore("s_" + k) for k in ("pe", "act", "dve", "pool")}
        self.cnt = {k: 0 for k in self.sem}
        self.seen = {k: {} for k in self.E}
        self.dsem = [nc.alloc_semaphore("d%d" % i) for i in range(NDMA)]
        self.dcnt = [0] * NDMA
        self.drr = 0
        self.bufs = []
        self.ps_rr = 0

    def sbuf(self, name, shape, dtype):
        b = Buf(self.nc.alloc_sbuf_tensor(name, list(shape), dtype), name)
        self.bufs.append(b)
        return b

    def psum(self, name, shape, dtype=F32):
        b = Buf(self.nc.alloc_psum_tensor(name, list(shape), dtype), name)
        self.bufs.append(b)
        return b

    def dram(self, t, name=""):
        b = Buf(t, name)
        self.bufs.append(b)
        return b

    def _wait(self, eng, ev):
        if ev is None:
            return
        sem, val, key = ev
        if self.seen[eng].get(key, 0) >= val:
            return
        self.E[eng].wait_ge(sem, val)
        self.seen[eng][key] = val

    def _deps(self, eng, reads, writes):
        reads = [getattr(b, "b", b) for b in reads]
        writes = [getattr(b, "b", b) for b in writes]
        for b in reads:
            self._wait(eng, b.w)
        for b in writes:
            if b.w is not None and (SAME_ENG_SYNC or b.w[2] != eng):
                self._wait(eng, b.w)
            for ev in b.r.values():
                if SAME_ENG_SYNC or ev[2] != eng:
                    self._wait(eng, ev)

    def _done(self, ev, reads, writes):
        reads = [getattr(b, "b", b) for b in reads]
        writes = [getattr(b, "b", b) for b in writes]
        for b in reads:
            b.r[ev[2]] = ev
        for b in writes:
            b.w = ev
            b.r = {}

    def op(self, eng, reads, writes, fn):
        self._deps(eng, reads, writes)
        ins = fn(self.E[eng])
        self.cnt[eng] += 1
        ins.then_inc(self.sem[eng], 1)
        ev = (self.sem[eng], self.cnt[eng], eng)
        self._done(ev, reads, writes)
        return ev

    def seq(self, eng, buf, fns, reads=()):
        ev = None
        for k, fn in enumerate(fns):
            ev = self.op(eng, ([buf] if k > 0 else []) + list(reads), [buf], fn)
        return ev

    def dma(self, reads, writes, out, in_, eng="sp"):
        k = self.drr
        self.drr = (k + 1) % NDMA
        key = ("d", k)
        if self.dcnt[k] > 0:
            self._wait(eng, (self.dsem[k], self.dcnt[k], key))
        self._deps(eng, reads, writes)
        ins = self.E[eng].dma_start(out=out, in_=in_)
        self.dcnt[k] += 16
        ins.then_inc(self.dsem[k], 16)
        ev = (self.dsem[k], self.dcnt[k], key)
        self._done(ev, reads, writes)
        return ev

    def barrier(self):
        evs = []
        for k in self.sem:
            if self.cnt[k] > 0:
                evs.append((self.sem[k], self.cnt[k], k))
        for k in range(NDMA):
            if self.dcnt[k] > 0:
                evs.append((self.dsem[k], self.dcnt[k], ("d", k)))
        for eng in self.E:
            for ev in evs:
                self._wait(eng, ev)
        for b in self.bufs:
            b.w = None
            b.r = {}


class Phase:
    _uid = [0]

    def __init__(self, S):
        Phase._uid[0] += 1
        self.uid = Phase._uid[0]
        self.S = S
        self.nc = S.nc
        self.st = ExitStack()
        self.nbufs0 = len(S.bufs)

    def sbuf(self, name, shape, dtype):
        name = "%s_p%d" % (name, self.uid)
        t = self.st.enter_context(self.nc.sbuf_tensor(name, list(shape), dtype))
        b = Buf(t, name)
        self.S.bufs.append(b)
        return b

    def psum(self, name, shape, dtype=F32):
        name = "%s_p%d" % (name, self.uid)
        t = self.st.enter_context(self.nc.psum_tensor(name, list(shape), dtype))
        b = Buf(t, name)
        self.S.bufs.append(b)
        return b

    def close(self):
        self.S.barrier()
        self.st.close()
        del self.S.bufs[self.nbufs0:]


VOFF = {}


def _vec_layout():
    off = 0
    for name, n in [("gm", 32), ("gf", 32), ("fin", 8), ("adab", 192), ("conv", 192), ("gngb", 256),
                    ("alog", 16), ("dtb", 16), ("invf", 1), ("sinsign", 1), ("cT", 8)]:
        VOFF[name] = off
        off += n
    return off


NV = _vec_layout()


def build_vecs(inp, b):
    v = np.zeros((128, NV), np.float32)
    def chunked(a):
        a = np.asarray(a, np.float32).reshape(-1, 128)
        return a.T
    v[:, VOFF["gm"]:VOFF["gm"] + 32] = chunked(inp["norm_mix_g"])
    v[:, VOFF["gf"]:VOFF["gf"] + 32] = chunked(inp["norm_ffn_g"])
    v[:, VOFF["fin"]:VOFF["fin"] + 8] = chunked(inp["final_norm_g"])
    v[:, VOFF["adab"]:VOFF["adab"] + 192] = chunked(inp["ada_b"])
    v[:, VOFF["conv"]:VOFF["conv"] + 192] = chunked(inp["gdn_conv_w"])
    v[:, VOFF["gngb"]:VOFF["gngb"] + 256] = np.broadcast_to(np.asarray(inp["gdn_norm_g"], np.float32).reshape(1, 256), (128, 256))
    v[:, VOFF["alog"]:VOFF["alog"] + 16] = np.broadcast_to(np.asarray(inp["gdn_a_log"], np.float32).reshape(1, 16), (128, 16))
    v[:, VOFF["dtb"]:VOFF["dtb"] + 16] = np.broadcast_to(np.asarray(inp["gdn_dt_bias"], np.float32).reshape(1, 16), (128, 16))
    p = np.arange(128) % 64
    inv = (500000.0 ** (-np.arange(8, dtype=np.float32) * 2.0 / 16)).astype(np.float32)
    v[:, VOFF["invf"]] = np.where(p < 16, inv[p % 8], 0.0)
    v[:, VOFF["sinsign"]] = np.where(p < 8, -1.0, np.where(p < 16, 1.0, 0.0))
    v[:, VOFF["cT"]:VOFF["cT"] + 8] = chunked(inp["c"][b])
    return v


def rope_partner_perm():
    idx = np.arange(512)
    d = idx % 64
    return np.where(d < 8, idx + 8, np.where(d < 16, idx - 8, idx))


class LazyDram(dict):
    def __init__(self, prog):
        super().__init__()
        self.prog = prog

    def __missing__(self, name):
        pg = self.prog
        if name in pg.in_shapes:
            shape, dt = pg.in_shapes[name]
            kind = "ExternalInput"
            pg.used_inputs.append(name)
        else:
            shape, dt = pg.scr_shapes[name]
            kind = "Internal"
            if name in pg.ext_out or name == "outT":
                kind = "ExternalOutput"
            if name in pg.ext_in:
                kind = "ExternalInput"
                pg.used_inputs.append(name)
        ap = pg.nc.dram_tensor(name, list(shape), dt, kind=kind).ap()
        self[name] = ap
        return ap


class Prog:
    def __init__(self, ext_out=(), ext_in=()):
        self.nc = bass.Bass("TRN2", target_bir_lowering=False)
        nc = self.nc
        self.ext_out = set(ext_out)
        self.ext_in = set(ext_in)
        self.S = Sched(nc)
        S = self.S
        self.in_shapes = {
            "xT": ([D, T], F32), "vecs": ([128, NV], F32), "pos": ([1, T], I32),
            "ada_w": ([NL, D, 6 * D], F32), "attn_w_in": ([2, D, 4096], F32), "attn_w_out": ([2, D, D], F32),
            "gdn_w_in": ([2, D, 4112], F32), "gdn_w_out": ([2, D, D], F32),
            "ffn_w_in": ([NL, D, 2 * FH], F32), "ffn_w_out": ([NL, FH, D], F32),
        }
        self.used_inputs = []
        self.dr = LazyDram(self)
        self.dbg_outs = []
        self.scr_shapes = {
            "XT": ([D, T], F32), "COS": ([128, T], F32), "SIN": ([128, T], F32),
            "QT": ([2048, T], BF16),
            "VTOK": ([T, 1024], BF16),
            "OT": ([D, T], BF16),
            "GT": ([3072, T], BF16),
            "AB": ([T, 16], F32),
            "outT": ([D, T], F32),
        }
        self.VEC = S.sbuf("VEC", [128, NV], F32)
        self.MOD = S.sbuf("MOD", [128, NL * 48], F32)
        self.AM = S.sbuf("AMc", [128, NL * 8], F32)
        self.AFc = S.sbuf("AFc", [128, NL * 8], F32)
        self.ones32 = S.sbuf("ones32", [128, 128], F32)
        self.ident32 = S.sbuf("ident32", [128, 128], F32)
        self.PS = [S.psum("ps%d" % i, [128, 512], F32) for i in range(8)]
        self.ps_i = 0
        S.op("pool", [], [self.ones32], lambda e: e.memset(self.ones32[:], 1.0))
        S.seq("pool", self.ident32, [
            lambda e: e.memset(self.ident32[:], 0.0),
            lambda e: e.affine_select(out=self.ident32[:], in_=self.ident32[:], compare_op=ALU.not_equal,
                                      fill=1.0, base=0, pattern=[[-1, 128]], channel_multiplier=1)])

    def dbg(self, name, buf, ap):
        shape = [int(x) for x in ap.shape]
        d = self.nc.dram_tensor(name, shape, buf.t.dtype, kind="ExternalOutput").ap()
        self.S.dma([buf], [], d, ap)
        self.dbg_outs.append(name)

    def ps(self):
        b = self.PS[self.ps_i]
        self.ps_i = (self.ps_i + 1) % 8
        return b

    def psqs(self, n):
        if PSQ_SEPARATE:
            return [PQ(self.ps(), 0) for q in range(n)]
        b = self.ps()
        return [PQ(b, q) for q in range(n)]

    def vcol(self, name, i=0, n=1):
        o = VOFF[name] + i
        return self.VEC[:, o:o + n]

    def phase0(self):
        S, nc = self.S, self.nc
        P = Phase(S)
        VEC, MOD = self.VEC, self.MOD
        S.dma([], [VEC], VEC[:], self.dr["vecs"])
        condT = P.sbuf("condT", [128, 8], F32)
        S.op("act", [VEC], [condT], lambda e: e.activation(out=condT[:], in_=self.vcol("cT", 0, 8), func=AF.Silu))
        CB = 1536
        wst = [P.sbuf("adaw%d" % i, [128, KC, CB], F32) for i in range(2)]
        k = 0
        for l in range(NL):
            pm = self.ps()
            for cb in range(4):
                w = wst[k % 2]
                k += 1
                src = self.dr["ada_w"][l, :, cb * CB:(cb + 1) * CB].rearrange("(c p) n -> p c n", p=128)
                S.dma([], [w], w[:], src)
                def mm(e, w=w, cb=cb, pm=pm):
                    last = None
                    for j in range(12):
                        col = cb * 12 + j
                        for kc in range(KC):
                            last = e.matmul(pm[:, col:col + 1], lhsT=w[:, kc, j * 128:(j + 1) * 128],
                                            rhs=condT[:, kc:kc + 1], start=(kc == 0), stop=(kc == KC - 1))
                    return last
                S.op("pe", [w, condT], [pm], mm)
            S.op("dve", [pm, VEC], [MOD], lambda e, l=l, pm=pm: e.tensor_tensor(
                out=MOD[:, l * 48:(l + 1) * 48], in0=pm[:, 0:48], in1=self.vcol("adab", l * 48, 48), op=ALU.add))
            S.op("dve", [MOD, VEC], [self.AM], lambda e, l=l: e.scalar_tensor_tensor(
                out=self.AM[:, l * 8:(l + 1) * 8], in0=MOD[:, l * 48 + 8:l * 48 + 16], scalar=1.0,
                in1=self.vcol("gm", l * 8, 8), op0=ALU.add, op1=ALU.mult))
            S.op("dve", [MOD, VEC], [self.AFc], lambda e, l=l: e.scalar_tensor_tensor(
                out=self.AFc[:, l * 8:(l + 1) * 8], in0=MOD[:, l * 48 + 32:l * 48 + 40], scalar=1.0,
                in1=self.vcol("gf", l * 8, 8), op0=ALU.add, op1=ALU.mult))
        posi = P.sbuf("posi", [1, T], I32)
        posf = P.sbuf("posf", [1, T], F32)
        S.dma([], [posi], posi[:], self.dr["pos"])
        S.op("dve", [posi], [posf], lambda e: e.tensor_copy(out=posf[:], in_=posi[:]))
        angs = [P.sbuf("ang%d" % k, [128, 512], F32) for k in range(2)]
        rbuf = {}
        for wh in ("SIN", "COS"):
            for k in range(2):
                rbuf[(wh, k)] = (P.sbuf("r%s%d" % (wh, k), [128, 512], F32), P.sbuf("xs%s%d" % (wh, k), [128, 512], F32),
                                 P.sbuf("ki%s%d" % (wh, k), [128, 512], I32))
        for tt in range(8):
            sl = slice(tt * 512, (tt + 1) * 512)
            pb = self.ps()
            S.op("pe", [posf, self.ones32], [pb], lambda e, pb=pb, sl=sl: e.matmul(
                pb[:, :], lhsT=self.ones32[0:1, :], rhs=posf[0:1, sl], start=True, stop=True))
            ang = angs[tt % 2]
            S.op("dve", [pb, VEC], [ang], lambda e, pb=pb, ang=ang: e.tensor_scalar(
                out=ang[:], in0=pb[:, :], scalar1=self.vcol("invf"), scalar2=None, op0=ALU.mult))
            for which, shift in (("SIN", 0.0), ("COS", 0.5 * np.pi)):
                r, xs, ki = rbuf[(which, tt % 2)]
                C1 = 6.28125
                C2 = float(2 * np.pi - 6.28125)
                S.op("dve", [ang], [xs], lambda e, xs=xs, ang=ang, shift=shift: e.tensor_scalar(
                    out=xs[:], in0=ang[:], scalar1=float(shift), scalar2=None, op0=ALU.add))
                S.op("dve", [xs], [r], lambda e, r=r, xs=xs: e.tensor_scalar(
                    out=r[:], in0=xs[:], scalar1=float(1.0 / (2 * np.pi)), scalar2=None, op0=ALU.mult))
                S.op("dve", [r], [ki], lambda e, r=r, ki=ki: e.tensor_copy(out=ki[:], in_=r[:]))
                S.op("dve", [ki], [r], lambda e, r=r, ki=ki: e.tensor_copy(out=r[:], in_=ki[:]))
                S.op("dve", [r, xs], [xs], lambda e, r=r, xs=xs: e.scalar_tensor_tensor(
                    out=xs[:], in0=r[:], scalar=-C1, in1=xs[:], op0=ALU.mult, op1=ALU.add))
                S.op("dve", [r, xs], [xs], lambda e, r=r, xs=xs: e.scalar_tensor_tensor(
                    out=xs[:], in0=r[:], scalar=-C2, in1=xs[:], op0=ALU.mult, op1=ALU.add))
                S.op("dve", [xs], [xs], lambda e, xs=xs: e.tensor_scalar(
                    out=xs[:], in0=xs[:], scalar1=-3.14159, scalar2=3.14159, op0=ALU.max, op1=ALU.min))
                S.op("act", [xs], [r], lambda e, r=r, xs=xs: e.activation(out=r[:], in_=xs[:], func=AF.Sin))
                if which == "SIN":
                    S.op("dve", [r, VEC], [r], lambda e, r=r: e.tensor_scalar(
                        out=r[:], in0=r[:], scalar1=self.vcol("sinsign"), scalar2=None, op0=ALU.mult))
                S.dma([r], [], self.dr[which][:, sl], r[:])
        P.close()

    def load_weight_bf16(self, P, Wb, src3, ncols, stages, col0=0):
        S = self.S
        nk = src3.shape[1]
        CB = stages[0].t.shape[2]
        c = 0
        i = 0
        while c < ncols:
            n = min(CB, ncols - c)
            st = stages[i % len(stages)]
            i += 1
            S.dma([], [st], st[:, 0:nk, 0:n], src3[:, :, c:c + n])
            S.op("pool", [st], [Wb], lambda e, st=st, c=c, n=n: e.tensor_copy(
                out=Wb[:, 0:nk, col0 + c:col0 + c + n], in_=st[:, 0:nk, 0:n]))
            c += n

    def norm_tile(self, P, X, A, B, h, sq, tmps, rstd, n=512):
        S = self.S
        S.op("act", [X], [sq], lambda e: e.activation(out=sq[:, :, 0:n], in_=X[:, :, 0:n], func=AF.Square))
        pss = self.ps()
        def mm(e):
            last = None
            for kc in range(KC):
                last = e.matmul(pss[:, 0:n], lhsT=self.ones32[:, :], rhs=sq[:, kc, 0:n], start=(kc == 0), stop=(kc == KC - 1))
            return last
        S.op("pe", [sq, self.ones32], [pss], mm)
        S.op("dve", [pss], [rstd], lambda e: e.tensor_scalar(
            out=rstd[:, 0:n], in0=pss[:, 0:n], scalar1=1.0 / D, scalar2=EPS, op0=ALU.mult, op1=ALU.add))
        S.op("act", [rstd], [rstd], lambda e: e.activation(out=rstd[:, 0:n], in_=rstd[:, 0:n], func=AF.Sqrt))
        S.op("dve", [rstd], [rstd], lambda e: e.reciprocal(out=rstd[:, 0:n], in_=rstd[:, 0:n]))
        for kc in range(KC):
            tmp = tmps[kc % len(tmps)]
            S.op("dve", [X, rstd, self.AM, self.AFc], [tmp], lambda e, kc=kc, tmp=tmp: e.scalar_tensor_tensor(
                out=tmp[:, 0:n], in0=X[:, kc, 0:n], scalar=A[:, kc:kc + 1], in1=rstd[:, 0:n], op0=ALU.mult, op1=ALU.mult))
            S.op("act", [tmp, self.MOD], [h], lambda e, kc=kc, tmp=tmp: e.activation(
                out=h[:, kc, 0:n], in_=tmp[:, 0:n], func=AF.Identity, bias=B[:, kc:kc + 1], scale=1.0))

    def phase1(self, l):
        S = self.S
        P = Phase(S)
        attn = (l % 2 == 0)
        i = l // 2
        NC_ = 4096 if attn else 4112
        Wb = P.sbuf("W1", [128, KC, NC_], BF16)
        stages = [P.sbuf("wst%d" % k, [128, KC, 512], F32) for k in range(2)]
        wsrc = self.dr["attn_w_in" if attn else "gdn_w_in"][i].rearrange("(c p) n -> p c n", p=128)
        self.load_weight_bf16(P, Wb, wsrc, NC_, stages)
        Xb = [P.sbuf("X%d" % k, [128, KC, 512], F32) for k in range(2)]
        sq = P.sbuf("sq", [128, KC, 512], F32)
        tmps = [P.sbuf("tmp%d" % k, [128, 512], F32) for k in range(2)]
        rstd = P.sbuf("rstd", [128, 512], F32)
        h = P.sbuf("h", [128, KC, 512], BF16)
        stg = [P.sbuf("stg%d" % k, [128, 512], BF16) for k in range(4)]
        stg32 = [P.sbuf("stgf%d" % k, [128, 16], F32) for k in range(2)]
        rt = [P.sbuf("rt%d" % k, [128, 512], F32) for k in range(4)]
        cs = [[P.sbuf("cs%d_%d" % (a, k), [128, 512], F32) for k in range(2)] for a in range(2)]
        A = self.AM[:, l * 8:(l + 1) * 8]
        B = self.MOD[:, l * 48:l * 48 + 8]
        XT3 = self.dr["xT" if l == 0 else "XT"].rearrange("(c p) t -> p c t", p=128)
        NT = T // 512
        S.dma([], [Xb[0]], Xb[0][:], XT3[:, :, 0:512])
        sk = 0
        ev = 0
        for tt in range(NT):
            sl = slice(tt * 512, (tt + 1) * 512)
            X = Xb[tt % 2]
            if tt + 1 < NT:
                S.dma([], [Xb[(tt + 1) % 2]], Xb[(tt + 1) % 2][:], XT3[:, :, (tt + 1) * 512:(tt + 2) * 512])
            if attn:
                cosb, sinb = cs[0][tt % 2], cs[1][tt % 2]
                S.dma([], [cosb], cosb[:], self.dr["COS"][:, sl])
                S.dma([], [sinb], sinb[:], self.dr["SIN"][:, sl])
            self.norm_tile(P, X, A, B, h, sq, tmps, rstd)

            def proj_fm(col0):
                pp = self.ps()
                def mm(e, pp=pp, col0=col0):
                    last = None
                    for kc in range(KC):
                        last = e.matmul(pp[:, :], lhsT=Wb[:, kc, col0:col0 + 128], rhs=h[:, kc, :],
                                        start=(kc == 0), stop=(kc == KC - 1))
                    return last
                S.op("pe", [Wb, h], [pp], mm)
                return pp

            def evac(pp, dst_ap, scale, width=512):
                nonlocal sk, ev
                st = stg[sk % 4]
                sk += 1
                if ev % 2 == 0:
                    S.op("act", [pp], [st], lambda e: e.activation(out=st[:, 0:width], in_=pp[:, 0:width], func=AF.Copy, scale=float(scale)))
                else:
                    S.op("dve", [pp], [st], lambda e: e.tensor_scalar(out=st[:, 0:width], in0=pp[:, 0:width], scalar1=float(scale), scalar2=None, op0=ALU.mult))
                ev += 1
                S.dma([st], [], dst_ap, st[:, 0:width])

            def proj_tm(col0, ncols, dst, dcol0, f32out=False):
                nonlocal sk
                for s in range(4):
                    pp = self.ps()
                    def mm(e, pp=pp, s=s):
                        last = None
                        for kc in range(KC):
                            last = e.matmul(pp[:, 0:ncols], lhsT=h[:, kc, s * 128:(s + 1) * 128], rhs=Wb[:, kc, col0:col0 + ncols],
                                            start=(kc == 0), stop=(kc == KC - 1))
                        return last
                    S.op("pe", [Wb, h], [pp], mm)
                    r0 = tt * 512 + s * 128
                    if f32out:
                        st = stg32[s % 2]
                        S.op("dve", [pp], [st], lambda e, st=st, pp=pp: e.tensor_copy(out=st[:, 0:ncols], in_=pp[:, 0:ncols]))
                        S.dma([st], [], dst[r0:r0 + 128, dcol0:dcol0 + ncols], st[:, 0:ncols])
                    else:
                        evac(pp, dst[r0:r0 + 128, dcol0:dcol0 + ncols], 1.0, ncols)

            if attn:
                QT = self.dr["QT"]
                for grp, (c0, pc0, d0, scale) in enumerate([(0, 3072, 0, 0.125), (512, 3584, 512, 1.0)]):
                    for n in range(4):
                        pq = proj_fm(c0 + n * 128)
                        pp = proj_fm(pc0 + n * 128)
                        t1, t2 = rt[(2 * n) % 4], rt[(2 * n + 1) % 4]
                        st = stg[sk % 4]
                        sk += 1
                        S.op("dve", [pq, cosb], [t1], lambda e, pq=pq, t1=t1, scale=scale: e.scalar_tensor_tensor(
                            out=t1[:], in0=pq[:, :], scalar=float(scale), in1=cosb[:], op0=ALU.mult, op1=ALU.mult))
                        S.op("dve", [pp, sinb], [t2], lambda e, pp=pp, t2=t2, scale=scale: e.scalar_tensor_tensor(
                            out=t2[:], in0=pp[:, :], scalar=float(scale), in1=sinb[:], op0=ALU.mult, op1=ALU.mult))
                        S.op("pool", [t1, t2], [st], lambda e, st=st, t1=t1, t2=t2: e.tensor_tensor(
                            out=st[:], in0=t1[:], in1=t2[:], op=ALU.add))
                        S.dma([st], [], QT[d0 + n * 128:d0 + (n + 1) * 128, sl], st[:])
                for n in range(4):
                    evac(proj_fm(1536 + n * 128), QT[1024 + n * 128:1024 + (n + 1) * 128, sl], 0.125)
                for n in range(4):
                    evac(proj_fm(2048 + n * 128), QT[1536 + n * 128:1536 + (n + 1) * 128, sl], 1.0)
                proj_tm(1024, 512, self.dr["VTOK"], 0)
                proj_tm(2560, 512, self.dr["VTOK"], 512)
            else:
                GT = self.dr["GT"]
                for n in range(24):
                    evac(proj_fm(n * 128), GT[n * 128:(n + 1) * 128, sl], 1.0)
                proj_tm(3072, 512, self.dr["VTOK"], 0)
                proj_tm(3584, 512, self.dr["VTOK"], 512)
                proj_tm(4096, 16, self.dr["AB"], 0, f32out=True)
        P.close()

    def phase34(self, l, final=False, tiles=None):
        S = self.S
        P = Phase(S)
        i = l // 2
        TT = 256
        Wo = P.sbuf("Wo", [128, KC, D], BF16)
        Wi = P.sbuf("Wi", [128, KC, 2 * FH], BF16)
        W2 = P.sbuf("W2", [128, NJ, D], BF16)
        PW = Phase(S)
        stA = [PW.sbuf("stA%d" % k, [128, KC, 512], F32) for k in range(2)]
        stB = [PW.sbuf("stB%d" % k, [128, NJ, 128], F32) for k in range(2)]
        wo_src = self.dr["attn_w_out" if l % 2 == 0 else "gdn_w_out"][i].rearrange("(c p) n -> p c n", p=128)
        self.load_weight_bf16(PW, Wo, wo_src, D, stA)
        self.load_weight_bf16(PW, Wi, self.dr["ffn_w_in"][l].rearrange("(c p) n -> p c n", p=128), 2 * FH, stA)
        self.load_weight_bf16(PW, W2, self.dr["ffn_w_out"][l].rearrange("(c p) n -> p c n", p=128), D, stB)
        PW.close()
        Xb = [P.sbuf("X%d" % k, [128, KC, TT], F32) for k in range(1)]
        ob = [P.sbuf("oT%d" % k, [128, KC, TT], BF16) for k in range(2)]
        sq = P.sbuf("sq", [128, KC, TT], F32)
        tmps = [P.sbuf("tmp%d" % k, [128, TT], F32) for k in range(2)]
        sgs = [P.sbuf("sg%d" % k, [128, TT], F32) for k in range(2)]
        rstd = P.sbuf("rstd", [128, TT], F32)
        h = P.sbuf("h", [128, KC, TT], BF16)
        hid = P.sbuf("hid", [128, NJ, TT], BF16)
        GM = self.MOD[:, l * 48 + 16:l * 48 + 24]
        GF = self.MOD[:, l * 48 + 40:l * 48 + 48]
        A = self.AFc[:, l * 8:(l + 1) * 8]
        B = self.MOD[:, l * 48 + 24:l * 48 + 32]
        XS3 = self.dr["xT" if l == 0 else "XT"].rearrange("(c p) t -> p c t", p=128)
        XD3 = self.dr["XT"].rearrange("(c p) t -> p c t", p=128)
        OT3 = self.dr["OT"].rearrange("(c p) t -> p c t", p=128)
        if final:
            OUT3 = self.dr["outT"].rearrange("(c p) t -> p c t", p=128)
            yout = P.sbuf("yout", [128, KC, TT], F32)
        NT = T // TT
        tl = list(range(NT)) if tiles is None else list(tiles)
        S.dma([], [ob[0]], ob[0][:], OT3[:, :, tl[0] * TT:(tl[0] + 1) * TT])
        for ti, tt in enumerate(tl):
            sl = slice(tt * TT, (tt + 1) * TT)
            X = Xb[0]
            oT = ob[ti % 2]
            S.dma([], [X], X[:], XS3[:, :, sl])
            if ti + 1 < len(tl):
                nt = tl[ti + 1]
                S.dma([], [ob[(ti + 1) % 2]], ob[(ti + 1) % 2][:], OT3[:, :, nt * TT:(nt + 1) * TT])
            for n in range(KC):
                pp = self.ps()
                def mm(e, pp=pp, n=n):
                    last = None
                    for kc in range(KC):
                        last = e.matmul(pp[:, 0:TT], lhsT=Wo[:, kc, n * 128:(n + 1) * 128], rhs=oT[:, kc, :],
                                        start=(kc == 0), stop=(kc == KC - 1))
                    return last
                S.op("pe", [Wo, oT], [pp], mm)
                S.op("dve", [pp, X, self.MOD], [X], lambda e, pp=pp, n=n: e.scalar_tensor_tensor(
                    out=X[:, n, :], in0=pp[:, 0:TT], scalar=GM[:, n:n + 1], in1=X[:, n, :], op0=ALU.mult, op1=ALU.add))
            self.norm_tile(P, X, A, B, h, sq, tmps, rstd, n=TT)
            for j in range(NJ):
                pg = self.ps()
                pu = self.ps()
                def mmg(e, pg=pg, j=j):
                    last = None
                    for kc in range(KC):
                        last = e.matmul(pg[:, 0:TT], lhsT=Wi[:, kc, j * 128:(j + 1) * 128], rhs=h[:, kc, :],
                                        start=(kc == 0), stop=(kc == KC - 1))
                    return last
                def mmu(e, pu=pu, j=j):
                    last = None
                    for kc in range(KC):
                        last = e.matmul(pu[:, 0:TT], lhsT=Wi[:, kc, FH + j * 128:FH + (j + 1) * 128], rhs=h[:, kc, :],
                                        start=(kc == 0), stop=(kc == KC - 1))
                    return last
                S.op("pe", [Wi, h], [pg], mmg)
                S.op("pe", [Wi, h], [pu], mmu)
                sg = sgs[j % 2]
                S.op("act", [pg], [sg], lambda e, pg=pg, sg=sg: e.activation(out=sg[:], in_=pg[:, 0:TT], func=AF.Silu))
                S.op("dve", [sg, pu], [hid], lambda e, pu=pu, sg=sg, j=j: e.tensor_tensor(
                    out=hid[:, j, :], in0=pu[:, 0:TT], in1=sg[:], op=ALU.mult))
            for n in range(KC):
                pp = self.ps()
                def mm2(e, pp=pp, n=n):
                    last = None
                    for j in range(NJ):
                        last = e.matmul(pp[:, 0:TT], lhsT=W2[:, j, n * 128:(n + 1) * 128], rhs=hid[:, j, :],
                                        start=(j == 0), stop=(j == NJ - 1))
                    return last
                S.op("pe", [W2, hid], [pp], mm2)
                S.op("dve", [pp, X, self.MOD], [X], lambda e, pp=pp, n=n: e.scalar_tensor_tensor(
                    out=X[:, n, :], in0=pp[:, 0:TT], scalar=GF[:, n:n + 1], in1=X[:, n, :], op0=ALU.mult, op1=ALU.add))
            if final:
                self.final_norm_tile(X, yout, sq, rstd, TT)
                S.dma([yout], [], OUT3[:, :, sl], yout[:])
            else:
                S.dma([X], [], XD3[:, :, sl], X[:])
        P.close()

    def final_norm_tile(self, X, yout, sq, rstd, n):
        S = self.S
        S.op("act", [X], [sq], lambda e: e.activation(out=sq[:, :, 0:n], in_=X[:, :, 0:n], func=AF.Square))
        pss = self.ps()
        def mm(e):
            last = None
            for kc in range(KC):
                last = e.matmul(pss[:, 0:n], lhsT=self.ones32[:, :], rhs=sq[:, kc, 0:n], start=(kc == 0), stop=(kc == KC - 1))
            return last
        S.op("pe", [sq, self.ones32], [pss], mm)
        S.op("dve", [pss], [rstd], lambda e: e.tensor_scalar(
            out=rstd[:, 0:n], in0=pss[:, 0:n], scalar1=1.0 / D, scalar2=EPS, op0=ALU.mult, op1=ALU.add))
        S.op("act", [rstd], [rstd], lambda e: e.activation(out=rstd[:, 0:n], in_=rstd[:, 0:n], func=AF.Sqrt))
        S.op("dve", [rstd], [rstd], lambda e: e.reciprocal(out=rstd[:, 0:n], in_=rstd[:, 0:n]))
        for kc in range(KC):
            S.op("dve", [X, rstd, self.VEC], [yout], lambda e, kc=kc: e.scalar_tensor_tensor(
                out=yout[:, kc, 0:n], in0=X[:, kc, 0:n], scalar=self.vcol("fin", kc), in1=rstd[:, 0:n], op0=ALU.mult, op1=ALU.mult))

    def phase_attn(self, heads_moba=range(8), heads_sb=range(8), qtiles=range(8)):
        S = self.S
        P = Phase(S)
        QT, VTOK, OT = self.dr["QT"], self.dr["VTOK"], self.dr["OT"]
        cmask = [P.sbuf("cmask%d" % d, [128, 512], BF16) for d in range(4)]
        smask = [P.sbuf("smask%d" % d, [128, 512], BF16) for d in range(4)]
        for d in range(4):
            for mk, op in ((cmask[d], ALU.is_ge), (smask[d], ALU.is_gt)):
                S.seq("pool", mk, [
                    lambda e, mk=mk: e.memset(mk[:], 1.0),
                    lambda e, mk=mk, op=op, d=d: e.affine_select(out=mk[:], in_=mk[:], compare_op=op, fill=0.0, base=-128 * d,
                                                                 pattern=[[1, 512]], channel_multiplier=-1)])
        NUI = P.sbuf("NUI", [128, 128], BF16)
        S.seq("pool", NUI, [
            lambda e: e.memset(NUI[:], -1.0),
            lambda e: e.affine_select(out=NUI[:], in_=NUI[:], compare_op=ALU.is_ge, fill=0.0, base=0,
                                      pattern=[[-1, 128]], channel_multiplier=1)])
        onesb = P.sbuf("onesb", [128, 128], BF16)
        S.op("pool", [], [onesb], lambda e: e.memset(onesb[:], 1.0))
        KA = P.sbuf("KA", [33, T], BF16)
        S.seq("pool", KA, [
            lambda e: e.memset(KA[:], 0.0),
            lambda e: e.memset(KA[0:16, :], BIG),
            lambda e: e.affine_select(out=KA[0:16, :], in_=KA[0:16, :], compare_op=ALU.is_ge, fill=0.0, base=0,
                                      pattern=[[1, T]], channel_multiplier=-256),
            lambda e: e.affine_select(out=KA[0:16, :], in_=KA[0:16, :], compare_op=ALU.is_ge, fill=0.0, base=255,
                                      pattern=[[-1, T]], channel_multiplier=256),
            lambda e: e.memset(KA[32:33, :], -1.0)])
        Esel = P.sbuf("Esel", [64, 33], BF16)
        S.seq("pool", Esel, [lambda e: e.memset(Esel[:], 0.0), lambda e: e.memset(Esel[:, 32:33], 1.0)])
        past01 = P.sbuf("past01", [128, 32, 16], F32)
        pastb = P.sbuf("pastb", [128, 32, 16], F32)
        ownm1 = P.sbuf("ownm1", [128, 32, 16], F32)
        S.seq("pool", past01, [
            lambda e: e.memset(past01[:], 1.0),
            lambda e: e.affine_select(out=past01[:], in_=past01[:], compare_op=ALU.is_ge, fill=0.0, base=-2,
                                      pattern=[[1, 32], [-2, 16]], channel_multiplier=0)])
        S.seq("pool", pastb, [
            lambda e: e.memset(pastb[:], 0.0),
            lambda e: e.affine_select(out=pastb[:], in_=pastb[:], compare_op=ALU.is_ge, fill=-1e30, base=-2,
                                      pattern=[[1, 32], [-2, 16]], channel_multiplier=0)])
        S.seq("pool", ownm1, [
            lambda e: e.memset(ownm1[:], 0.0),
            lambda e: e.affine_select(out=ownm1[:], in_=ownm1[:], compare_op=ALU.is_ge, fill=-1.0, base=0,
                                      pattern=[[1, 32], [-2, 16]], channel_multiplier=0),
            lambda e: e.affine_select(out=ownm1[:], in_=ownm1[:], compare_op=ALU.is_ge, fill=-1.0, base=1,
                                      pattern=[[-1, 32], [2, 16]], channel_multiplier=0)])
        Qb = [P.sbuf("Q%d" % k, [64, T], BF16) for k in range(2)]
        Kb = [P.sbuf("K%d" % k, [64, T], BF16) for k in range(2)]
        Vb = [P.sbuf("V%d" % k, [128, 32, 65], BF16) for k in range(2)]
        for k in range(2):
            S.op("pool", [], [Vb[k]], lambda e, k=k: e.memset(Vb[k][:], 1.0))
        QA = P.sbuf("QA", [33, T], BF16)
        S.op("pool", [], [QA], lambda e: e.memset(QA[:], 0.0))
        sqQ = P.sbuf("sqQ", [64, T], BF16)
        kmean = P.sbuf("kmean", [64, 16], F32)
        kmeanb = P.sbuf("kmeanb", [64, 16], BF16)
        gm = P.sbuf("gm", [128, 32, 16], F32)
        m8 = P.sbuf("m8", [128, 32, 8], F32)
        MB = P.sbuf("MB", [128, 32, 16], F32)
        kmx = P.sbuf("kmx", [33, 8], F32)
        kmax2 = P.sbuf("kmax2", [33, 1], F32)
        Pt = [P.sbuf("Pt%d" % k, [128, 512], BF16) for k in range(3)]
        Eb = [P.sbuf("E%d" % k, [128, 512], F32) for k in range(2)]
        Lp = [P.sbuf("Lp%d" % k, [128, 512], BF16) for k in range(3)]
        exb = [P.sbuf("ex%d" % k, [128, 512], F32) for k in range(2)]
        rs = P.sbuf("rs", [128, 512], F32)
        Osb = P.sbuf("Osb", [65, 512], F32)
        rden = P.sbuf("rden", [128, 512], F32)
        S.op("pool", [], [rden], lambda e: e.memset(rden[:], 0.0))
        sel64 = P.sbuf("sel64", [128, 128], F32)
        S.seq("pool", sel64, [lambda e: e.memset(sel64[:], 0.0), lambda e: e.memset(sel64[64:65, :], 1.0)])
        ost = [P.sbuf("ost%d" % k, [64, 512], BF16) for k in range(2)]
        PSr = self.PS[0:6]
        PSo = self.PS[6:8]
        rr = [0]
        def psr():
            b = PSr[rr[0] % 6]
            rr[0] += 1
            return b
        jobs = [("m", h) for h in heads_moba] + [("s", h) for h in heads_sb]
        def load_head(idx):
            kind, h = jobs[idx]
            k = idx % 2
            qrow = (0 if kind == "m" else 1024) + h * 64
            krow = (512 if kind == "m" else 1536) + h * 64
            vcol = (0 if kind == "m" else 512) + h * 64
            S.dma([], [Qb[k]], Qb[k][:], QT[qrow:qrow + 64, :])
            S.dma([], [Kb[k]], Kb[k][:], QT[krow:krow + 64, :])
            S.dma([], [Vb[k]], Vb[k][:, :, 0:64], VTOK[:, vcol:vcol + 64].rearrange("(j p) d -> p j d", p=128))
        load_head(0)
        oi = 0
        for idx, (kind, h) in enumerate(jobs):
            Q, Kt, V = Qb[idx % 2], Kb[idx % 2], Vb[idx % 2]
            if idx + 1 < len(jobs):
                load_head(idx + 1)
            if kind == "m":
                S.op("dve", [Kt], [kmean], lambda e: e.tensor_reduce(
                    out=kmean[:], in_=Kt[:].rearrange("p (n k) -> p n k", k=256), axis=AX.X, op=ALU.add))
                S.op("dve", [kmean], [kmeanb], lambda e: e.tensor_scalar(
                    out=kmeanb[:], in0=kmean[:], scalar1=1.0 / 256, scalar2=None, op0=ALU.mult))
                GP = psr()
                def mm(e):
                    last = None
                    for tt in range(32):
                        last = e.matmul(GP[:, tt * 16:(tt + 1) * 16], lhsT=Q[:, tt * 128:(tt + 1) * 128], rhs=kmeanb[:, :],
                                        start=True, stop=True)
                    return last
                S.op("pe", [Q, kmeanb], [GP], mm)
                S.op("dve", [GP, pastb], [gm], lambda e: e.tensor_tensor(
                    out=gm[:].rearrange("p a b -> p (a b)"), in0=GP[:, :], in1=pastb[:].rearrange("p a b -> p (a b)"), op=ALU.add))
                for tt in range(32):
                    S.op("dve", [gm], [m8], lambda e, tt=tt: e.max(out=m8[:, tt, :], in_=gm[:, tt, :]))
                for tt in range(32):
                    S.op("dve", [gm, m8, past01], [MB], lambda e, tt=tt: e.scalar_tensor_tensor(
                        out=MB[:, tt, :], in0=gm[:, tt, :], scalar=m8[:, tt, 2:3], in1=past01[:, tt, :], op0=ALU.is_ge, op1=ALU.mult))
                S.op("dve", [MB, ownm1], [MB], lambda e: e.tensor_tensor(
                    out=MB[:].rearrange("p a b -> p (a b)"), in0=MB[:].rearrange("p a b -> p (a b)"),
                    in1=ownm1[:].rearrange("p a b -> p (a b)"), op=ALU.add))
                for g in range(8):
                    TP = psr()
                    def tr(e, TP=TP, g=g):
                        last = None
                        for k in range(4):
                            last = e.transpose(out=TP[0:16, k * 128:(k + 1) * 128], in_=MB[:, 4 * g + k, :], identity=self.ident32[:, :])
                        return last
                    S.op("pe", [MB, self.ident32], [TP], tr)
                    S.op("dve", [TP], [QA], lambda e, TP=TP, g=g: e.tensor_copy(out=QA[0:16, g * 512:(g + 1) * 512], in_=TP[0:16, :]))
                S.op("pool", [Kt], [sqQ], lambda e: e.tensor_tensor(out=sqQ[:], in0=Kt[:], in1=Kt[:], op=ALU.mult))
                for g in range(8):
                    KP = psr()
                    S.op("pe", [sqQ, Esel], [KP], lambda e, KP=KP, g=g: e.matmul(
                        KP[0:33, :], lhsT=Esel[:, :], rhs=sqQ[:, g * 512:(g + 1) * 512], start=True, stop=True))
                    S.op("dve", [KP], [kmx], lambda e, KP=KP, g=g: e.tensor_reduce(
                        out=kmx[32:33, g:g + 1], in_=KP[32:33, :], axis=AX.X, op=ALU.max))
                S.op("dve", [kmx], [kmax2], lambda e: e.tensor_reduce(out=kmax2[32:33, :], in_=kmx[32:33, :], axis=AX.X, op=ALU.max))
                S.op("pool", [Q], [sqQ], lambda e: e.tensor_tensor(out=sqQ[:], in0=Q[:], in1=Q[:], op=ALU.mult))
                for g in range(8):
                    SP = psr()
                    S.op("pe", [sqQ, Esel], [SP], lambda e, SP=SP, g=g: e.matmul(
                        SP[0:33, :], lhsT=Esel[:, :], rhs=sqQ[:, g * 512:(g + 1) * 512], start=True, stop=True))
                    S.op("act", [SP, kmax2], [QA], lambda e, SP=SP, g=g: e.activation(
                        out=QA[32:33, g * 512:(g + 1) * 512], in_=SP[32:33, :], func=AF.Sqrt, scale=kmax2[32:33, 0:1]))
                if getattr(self, "debug", False):
                    self.dbg("dbgQA", QA, QA[:])
                    self.dbg("dbgkmax2", kmax2, kmax2[:])
                    self.dbg("dbgMB", MB, MB[:])
                    self.dbg("dbggm", gm, gm[:])
                pairs = [(i, j) for i in qtiles for j in range(4 * i + 4)]
                st = {}
                def stageA(p):
                    i, j = pairs[p]
                    sp = psr()
                    def mm(e):
                        e.matmul(sp[:, :], lhsT=Kt[:, j * 128:(j + 1) * 128], rhs=Q[:, i * 512:(i + 1) * 512], start=True, stop=False)
                        return e.matmul(sp[:, :], lhsT=KA[0:33, j * 128:(j + 1) * 128], rhs=QA[0:33, i * 512:(i + 1) * 512], start=False, stop=True)
                    S.op("pe", [Kt, Q, KA, QA], [sp], mm)
                    pt = Pt[p % 3]
                    S.op("act", [sp], [pt], lambda e: e.activation(out=pt[:], in_=sp[:, :], func=AF.Exp))
                    if j >= 4 * i:
                        S.op("pool", [pt, cmask[j - 4 * i]], [pt], lambda e: e.tensor_tensor(out=pt[:], in0=pt[:], in1=cmask[j - 4 * i][:], op=ALU.mult))
                    st[p] = pt
                def stageC(p):
                    nonlocal oi
                    i, j = pairs[p]
                    pt = st.pop(p)
                    last = 4 * i + 3
                    if j == 0:
                        oi += 1
                    OP = PSo[oi % 2]
                    S.op("pe", [V, pt], [OP], lambda e: e.matmul(OP[0:65, :], lhsT=V[:, j, :], rhs=pt[:], start=(j == 0), stop=(j == last)))
                    if j == last:
                        S.op("act", [OP], [Osb], lambda e: e.activation(out=Osb[:], in_=OP[0:65, :], func=AF.Copy))
                        if getattr(self, "debug", False) and i == 0:
                            self.dbg("dbgOsb", Osb, Osb[:])
                        S.op("dve", [Osb], [rden], lambda e: e.reciprocal(out=rden[64:65, :], in_=Osb[64:65, :]))
                        BP = psr()
                        S.op("pe", [rden, sel64], [BP], lambda e: e.matmul(
                            BP[:, :], lhsT=sel64[:, :], rhs=rden[:, :], start=True, stop=True))
                        o = ost[i % 2]
                        S.op("dve", [Osb, BP], [o], lambda e: e.tensor_tensor(out=o[:], in0=Osb[0:64, :], in1=BP[0:64, :], op=ALU.mult))
                        S.dma([o], [], OT[h * 64:(h + 1) * 64, i * 512:(i + 1) * 512], o[:])
                for p in range(len(pairs) + 1):
                    if p < len(pairs):
                        stageA(p)
                    if p >= 1:
                        stageC(p - 1)
            else:
                pairs = [(i, j) for i in qtiles for j in range(4 * i + 3, -1, -1)]
                st = {}
                def stageA(p):
                    i, j = pairs[p]
                    z = psr()
                    S.op("pe", [Kt, Q], [z], lambda e: e.matmul(
                        z[:, :], lhsT=Kt[:, j * 128:(j + 1) * 128], rhs=Q[:, i * 512:(i + 1) * 512], start=True, stop=False))
                    E = Eb[p % 2]
                    lp = Lp[p % 3]
                    S.op("act", [z], [E], lambda e: e.activation(out=E[:], in_=z[:, :], func=AF.Exp))
                    S.op("act", [E], [lp], lambda e: e.activation(out=lp[:], in_=E[:], func=AF.Ln, bias=1.0, scale=1.0))
                    if j >= 4 * i:
                        S.op("pool", [lp, smask[j - 4 * i]], [lp], lambda e: e.tensor_tensor(out=lp[:], in0=lp[:], in1=smask[j - 4 * i][:], op=ALU.mult))
                    st[p] = (z, lp)
                def stageB(p):
                    i, j = pairs[p]
                    z, lp = st[p]
                    first = (j == 4 * i + 3)
                    S.op("pe", [NUI, lp], [z], lambda e: e.matmul(z[:, :], lhsT=NUI[:, :], rhs=lp[:], start=False, stop=True))
                    cs = psr()
                    S.op("pe", [onesb, lp], [cs], lambda e: e.matmul(cs[:, :], lhsT=onesb[:, :], rhs=lp[:], start=True, stop=True))
                    ex = exb[p % 2]
                    if first:
                        S.op("dve", [z], [ex], lambda e: e.tensor_copy(out=ex[:], in_=z[:, :]))
                        S.op("dve", [cs], [rs], lambda e: e.tensor_copy(out=rs[:], in_=cs[:, :]))
                    else:
                        S.op("dve", [z, rs], [ex], lambda e: e.tensor_tensor(out=ex[:], in0=z[:, :], in1=rs[:], op=ALU.subtract))
                        S.op("dve", [cs, rs], [rs], lambda e: e.tensor_tensor(out=rs[:], in0=cs[:, :], in1=rs[:], op=ALU.add))
                    w = Pt[p % 3]
                    S.op("act", [ex], [w], lambda e: e.activation(out=w[:], in_=ex[:], func=AF.Exp))
                    if j >= 4 * i:
                        S.op("pool", [w, smask[j - 4 * i]], [w], lambda e: e.tensor_tensor(out=w[:], in0=w[:], in1=smask[j - 4 * i][:], op=ALU.mult))
                    st[p] = w
                def stageC(p):
                    nonlocal oi
                    i, j = pairs[p]
                    w = st.pop(p)
                    first = (j == 4 * i + 3)
                    if first:
                        oi += 1
                    OP = PSo[oi % 2]
                    S.op("pe", [V, w], [OP], lambda e: e.matmul(OP[0:64, :], lhsT=V[:, j, 0:64], rhs=w[:], start=first, stop=(j == 0)))
                    if j == 0:
                        o = ost[i % 2]
                        S.op("act", [OP], [o], lambda e: e.activation(out=o[:], in_=OP[0:64, :], func=AF.Copy))
                        S.dma([o], [], OT[512 + h * 64:512 + (h + 1) * 64, i * 512:(i + 1) * 512], o[:])
                n = len(pairs)
                for p in range(n + 2):
                    if p < n:
                        stageA(p)
                    if 1 <= p <= n:
                        stageB(p - 1)
                    if p >= 2:
                        stageC(p - 2)
        P.close()

    def phase_gdn(self, li, heads=range(8), nchunks=32):
        S = self.S
        P = Phase(S)
        GT, ZT, AB, OT = self.dr["GT"], self.dr["VTOK"], self.dr["AB"], self.dr["OT"]
        NCH = 32
        C = 128
        UT = P.sbuf("UT", [128, 128], F32)
        SU = P.sbuf("SU", [128, 128], F32)
        S.seq("pool", UT, [
            lambda e: e.memset(UT[:], 1.0),
            lambda e: e.affine_select(out=UT[:], in_=UT[:], compare_op=ALU.is_ge, fill=0.0, base=0,
                                      pattern=[[1, 128]], channel_multiplier=-1)])
        S.seq("pool", SU, [
            lambda e: e.memset(SU[:], 1.0),
            lambda e: e.affine_select(out=SU[:], in_=SU[:], compare_op=ALU.is_gt, fill=0.0, base=0,
                                      pattern=[[-1, 128]], channel_multiplier=1)])
        I32m = self.ident32
        ab = P.sbuf("ab", [128, NCH, 16], F32)
        S.dma([], [ab], ab[:], AB.rearrange("(c p) n -> p c n", p=128))
        graw = P.sbuf("graw", [128, NCH, 8], F32)
        beta = P.sbuf("beta", [128, NCH, 8], F32)
        nbeta = P.sbuf("nbeta", [128, NCH, 8], F32)
        negA = P.sbuf("negA", [128, 8], F32)
        tmpa = P.sbuf("tmpa", [128, NCH, 8], F32)
        S.op("act", [self.VEC], [negA], lambda e: e.activation(out=negA[:], in_=self.vcol("alog", li * 8, 8), func=AF.Exp))
        S.op("dve", [negA], [negA], lambda e: e.tensor_scalar(out=negA[:], in0=negA[:], scalar1=-1.0, scalar2=None, op0=ALU.mult))
        for c in range(NCH):
            S.op("dve", [ab, self.VEC], [tmpa], lambda e, c=c: e.tensor_tensor(
                out=tmpa[:, c, :], in0=ab[:, c, 0:8], in1=self.vcol("dtb", li * 8, 8), op=ALU.add))
        S.op("act", [tmpa], [tmpa], lambda e: e.activation(out=tmpa[:], in_=tmpa[:], func=AF.Exp))
        S.op("act", [tmpa], [tmpa], lambda e: e.activation(out=tmpa[:], in_=tmpa[:], func=AF.Ln, bias=1.0, scale=1.0))
        for c in range(NCH):
            S.op("dve", [tmpa, negA], [graw], lambda e, c=c: e.tensor_tensor(
                out=graw[:, c, :], in0=tmpa[:, c, :], in1=negA[:], op=ALU.mult))
        S.op("act", [ab], [beta], lambda e: e.activation(out=beta[:], in_=ab[:, :, 8:16], func=AF.Sigmoid))
        S.op("dve", [beta], [nbeta], lambda e: e.tensor_scalar(out=nbeta[:], in0=beta[:], scalar1=-1.0, scalar2=None, op0=ALU.mult))
        gc = P.sbuf("gc", [128, NCH, 8], F32)
        egc = P.sbuf("egc", [128, NCH, 8], F32)
        egl = P.sbuf("egl", [128, NCH, 8], F32)
        edec = P.sbuf("edec", [128, NCH, 8], F32)
        bgc = P.sbuf("bgc", [128, NCH, 8], F32)
        pgc = self.ps()
        S.op("pe", [UT, graw], [pgc], lambda e: e.matmul(pgc[:, 0:256], lhsT=UT[:, :], rhs=graw[:].rearrange("p a b -> p (a b)"), start=True, stop=True))
        pgl = self.ps()
        S.op("pe", [self.ones32, graw], [pgl], lambda e: e.matmul(pgl[:, 0:256], lhsT=self.ones32[:, :], rhs=graw[:].rearrange("p a b -> p (a b)"), start=True, stop=True))
        fl = lambda b: b[:].rearrange("p a b -> p (a b)")
        S.op("dve", [pgc], [gc], lambda e: e.tensor_copy(out=fl(gc), in_=pgc[:, 0:256]))
        S.op("act", [pgc], [egc], lambda e: e.activation(out=fl(egc), in_=pgc[:, 0:256], func=AF.Exp))
        S.op("act", [pgl], [egl], lambda e: e.activation(out=fl(egl), in_=pgl[:, 0:256], func=AF.Exp))
        S.op("dve", [pgl, gc], [edec], lambda e: e.tensor_tensor(out=fl(edec), in0=pgl[:, 0:256], in1=fl(gc), op=ALU.subtract))
        S.op("act", [edec], [edec], lambda e: e.activation(out=fl(edec), in_=fl(edec), func=AF.Exp))
        S.op("dve", [beta, egc], [bgc], lambda e: e.tensor_tensor(out=fl(bgc), in0=fl(beta), in1=fl(egc), op=ALU.mult))
        xin = [P.sbuf("xin%d" % k, [128, T], BF16) for k in range(1)]
        acc = P.sbuf("acc", [128, T], F32)
        ybuf = P.sbuf("ybuf", [128, T], F32)
        qTb = P.sbuf("qTb", [128, T], BF16)
        kTf = P.sbuf("kTf", [128, T], F32)
        kTb = P.sbuf("kTb", [128, T], BF16)
        vTf = P.sbuf("vTf", [128, T], F32)
        rn = P.sbuf("rn", [128, 512], F32)
        zt = P.sbuf("zt", [128, NCH, 128], BF16)
        zg = P.sbuf("zg", [128, NCH, 128], BF16)
        def chunked(name):
            big = P.sbuf(name, [128, NCH, 128], BF16)
            out = []
            for k in range(NCH):
                b = Buf(big.t[:, k, :], "%s%d" % (name, k))
                S.bufs.append(b)
                out.append(b)
            return out
        U_, WT, AT, KD = chunked("U_"), chunked("WT"), chunked("AT"), chunked("KD")
        ot = P.sbuf("ot", [128, T], BF16)
        Sst = P.sbuf("Sst", [128, 128], F32)
        Sbfs = [P.sbuf("Sbf%d" % k, [128, 128], BF16) for k in range(2)]
        NR = GDN_NR
        def rot(name, dt=F32, n=NR):
            return [P.sbuf("%s%d" % (name, k), [128, 128], dt) for k in range(n)]
        ktok, vb, kbg, G1, dcs, dsc, Nm, Am, Pm, Nn, An = (rot(x) for x in ("ktok", "vb", "kbg", "G1", "dcs", "dsc", "Nm", "Am", "Pm", "Nn", "An"))
        vnew = rot("vnew", BF16, 2)
        o2b = rot("o2b", F32, 2)
        osb = rot("osb", F32, 2)
        ysb = rot("ysb", F32, 2)
        ssq = P.sbuf("ssq", [128, 2], F32)
        junk = P.sbuf("junk", [128, 128], F32)
        ew = [0]
        def evac_copy(src_ps, dst, dst_ap, src_ap, extra_reads=()):
            ew[0] += 1
            if ew[0] % 2 == 0:
                S.op("act", [src_ps] + list(extra_reads), [dst], lambda e: e.activation(out=dst_ap, in_=src_ap, func=AF.Copy))
            else:
                S.op("dve", [src_ps] + list(extra_reads), [dst], lambda e: e.tensor_copy(out=dst_ap, in_=src_ap))
        for h in heads:
            for wi, which in enumerate(("q", "k", "v")):
                xi = xin[0]
                row = wi * 1024 + h * 128
                S.dma([], [xi], xi[:], GT[row:row + 128, :])
                def wcol(j, wi=wi):
                    return self.vcol("conv", (li * 4 + j) * 24 + wi * 8 + h)
                S.op("dve", [xi, self.VEC], [acc], lambda e, xi=xi: e.tensor_scalar(out=acc[:], in0=xi[:], scalar1=wcol(3), scalar2=None, op0=ALU.mult))
                for sh in (1, 2, 3):
                    S.op("dve", [xi, self.VEC, acc], [acc], lambda e, xi=xi, sh=sh: e.scalar_tensor_tensor(
                        out=acc[:, sh:], in0=xi[:, 0:T - sh], scalar=wcol(3 - sh), in1=acc[:, sh:], op0=ALU.mult, op1=ALU.add))
                dst = vTf if which == "v" else ybuf
                S.op("act", [acc], [dst], lambda e, dst=dst: e.activation(out=dst[:], in_=acc[:], func=AF.Silu))
                if which == "v":
                    continue
                S.op("act", [ybuf], [acc], lambda e: e.activation(out=acc[:], in_=ybuf[:], func=AF.Square))
                for g in range(8):
                    sl = slice(g * 512, (g + 1) * 512)
                    pss = self.ps()
                    S.op("pe", [acc, self.ones32], [pss], lambda e, pss=pss, sl=sl: e.matmul(pss[:, :], lhsT=self.ones32[:, :], rhs=acc[:, sl], start=True, stop=True))
                    S.op("dve", [pss], [rn], lambda e, pss=pss: e.tensor_scalar(out=rn[:], in0=pss[:, :], scalar1=EPS, scalar2=None, op0=ALU.add))
                    S.op("act", [rn], [rn], lambda e: e.activation(out=rn[:], in_=rn[:], func=AF.Sqrt))
                    S.op("dve", [rn], [rn], lambda e: e.reciprocal(out=rn[:], in_=rn[:]))
                    if which == "q":
                        S.op("dve", [ybuf, rn], [qTb], lambda e, sl=sl: e.scalar_tensor_tensor(
                            out=qTb[:, sl], in0=ybuf[:, sl], scalar=float(128 ** -0.5), in1=rn[:], op0=ALU.mult, op1=ALU.mult))
                    else:
                        S.op("dve", [ybuf, rn], [kTf], lambda e, sl=sl: e.tensor_tensor(out=kTf[:, sl], in0=ybuf[:, sl], in1=rn[:], op=ALU.mult))
                        S.op("pool", [kTf], [kTb], lambda e, sl=sl: e.tensor_copy(out=kTb[:, sl], in_=kTf[:, sl]))
            S.dma([], [zt], zt[:], ZT[:, h * 128:(h + 1) * 128].rearrange("(c p) d -> p c d", p=128))
            S.op("act", [zt], [zt], lambda e: e.activation(out=zt[:], in_=zt[:], func=AF.Silu))
            for c in range(NCH):
                S.op("pool", [zt, self.VEC], [zg], lambda e, c=c: e.tensor_tensor(
                    out=zg[:, c, :], in0=zt[:, c, :], in1=self.vcol("gngb", li * 128, 128), op=ALU.mult))
            done1 = [False] * nchunks

            class Pool3:
                def __init__(self, banks):
                    self.banks = banks
                    self.i = 0

                def get(self):
                    b = self.banks[self.i % len(self.banks)]
                    self.i += 1
                    return PQ(b, 0)
            pools = [Pool3(self.PS[0:3]), Pool3(self.PS[3:6])]
            pool2 = Pool3(self.PS[6:8])
            def evq(pq, dst, dst_ap, extra_reads=()):
                evac_copy(pq, dst, dst_ap, pq.ap, extra_reads)

            def stage1(c, r, h=h):
                cs = slice(c * C, (c + 1) * C)
                col = slice(h, h + 1)
                pl = pools[r]
                pk, pv = pl.get(), pl.get()
                S.op("pe", [kTf, I32m], [pk], lambda e: e.transpose(out=pk.ap, in_=kTf[:, cs], identity=I32m[:, :]))
                S.op("pe", [vTf, I32m], [pv], lambda e: e.transpose(out=pv.ap, in_=vTf[:, cs], identity=I32m[:, :]))
                S.op("dve", [SU, graw], [G1[r]], lambda e: e.tensor_scalar(out=G1[r][:], in0=SU[:], scalar1=graw[:, c, col], scalar2=None, op0=ALU.mult))
                yield
                evq(pk, ktok[r], ktok[r][:])
                S.op("dve", [pv, beta], [vb[r]], lambda e: e.tensor_scalar(out=vb[r][:], in0=pv.ap, scalar1=beta[:, c, col], scalar2=None, op0=ALU.mult))
                pd1, pd2 = pl.get(), pl.get()
                S.op("pe", [UT, G1[r]], [pd1], lambda e: e.matmul(pd1.ap, lhsT=UT[:, :], rhs=G1[r][:], start=True, stop=True))
                S.op("pe", [UT, G1[r]], [pd2], lambda e: e.matmul(pd2.ap, lhsT=G1[r][:], rhs=UT[:, :], start=True, stop=True))
                yield
                S.op("act", [pd1], [dcs[r]], lambda e: e.activation(out=dcs[r][:], in_=pd1.ap, func=AF.Exp))
                S.op("act", [pd2], [dsc[r]], lambda e: e.activation(out=dsc[r][:], in_=pd2.ap, func=AF.Exp))
                S.op("act", [ktok[r], bgc], [kbg[r]], lambda e: e.activation(out=kbg[r][:], in_=ktok[r][:], func=AF.Copy, scale=bgc[:, c, col]))
                S.op("act", [ktok[r], edec], [KD[c]], lambda e: e.activation(out=KD[c][:], in_=ktok[r][:], func=AF.Copy, scale=edec[:, c, col]))
                pkk, pqk = pl.get(), pl.get()
                S.op("pe", [kTf], [pkk], lambda e: e.matmul(pkk.ap, lhsT=kTf[:, cs], rhs=kTf[:, cs], start=True, stop=True))
                S.op("pe", [kTb, qTb], [pqk], lambda e: e.matmul(pqk.ap, lhsT=kTb[:, cs], rhs=qTb[:, cs], start=True, stop=True))
                yield
                S.op("pool", [dcs[r], SU], [dcs[r]], lambda e: e.tensor_tensor(out=dcs[r][:], in0=dcs[r][:], in1=SU[:], op=ALU.mult))
                S.op("pool", [dsc[r], UT], [dsc[r]], lambda e: e.tensor_tensor(out=dsc[r][:], in0=dsc[r][:], in1=UT[:], op=ALU.mult))
                yield
                S.op("dve", [pkk, nbeta, dcs[r]], [Nm[r]], lambda e: e.scalar_tensor_tensor(
                    out=Nm[r][:], in0=pkk.ap, scalar=nbeta[:, c, col], in1=dcs[r][:], op0=ALU.mult, op1=ALU.mult))
                S.op("dve", [pqk, dsc[r]], [AT[c]], lambda e: e.tensor_tensor(out=AT[c][:], in0=pqk.ap, in1=dsc[r][:], op=ALU.mult))
                yield
                pa = pl.get()
                S.op("pe", [Nm[r], I32m], [pa], lambda e: e.transpose(out=pa.ap, in_=Nm[r][:], identity=I32m[:, :]))
                yield
                evq(pa, Am[r], Am[r][:])
                yield
                S.op("pool", [Am[r], I32m], [Pm[r]], lambda e: e.tensor_tensor(out=Pm[r][:], in0=Am[r][:], in1=I32m[:], op=ALU.add))
                Ncur, Acur = Nm[r], Am[r]
                for lvl in range(1, 7):
                    Nnx = Nn[r] if lvl % 2 == 1 else Nm[r]
                    Anx = An[r] if lvl % 2 == 1 else Am[r]
                    pn = pl.get()
                    S.op("pe", [Acur, Ncur], [pn], lambda e: e.matmul(pn.ap, lhsT=Acur[:], rhs=Ncur[:], start=True, stop=True))
                    if lvl < 6:
                        pa2 = pl.get()
                        S.op("pe", [Acur, Ncur], [pa2], lambda e: e.matmul(pa2.ap, lhsT=Ncur[:], rhs=Acur[:], start=True, stop=True))
                    yield
                    evq(pn, Nnx, Nnx[:])
                    if lvl < 6:
                        evq(pa2, Anx, Anx[:])
                    yield
                    pp = pl.get()
                    S.op("pe", [Nnx, Pm[r]], [pp], lambda e: e.matmul(pp.ap, lhsT=Nnx[:], rhs=Pm[r][:], start=True, stop=True))
                    yield
                    S.op("dve", [pp, Pm[r]], [Pm[r]], lambda e: e.tensor_tensor(out=Pm[r][:], in0=pp.ap, in1=Pm[r][:], op=ALU.add))
                    yield
                    Ncur, Acur = Nnx, Anx
                pu, pw = pl.get(), pl.get()
                S.op("pe", [Pm[r], vb[r]], [pu], lambda e: e.matmul(pu.ap, lhsT=Pm[r][:], rhs=vb[r][:], start=True, stop=True))
                S.op("pe", [Pm[r], kbg[r]], [pw], lambda e: e.matmul(pw.ap, lhsT=kbg[r][:], rhs=Pm[r][:], start=True, stop=True))
                yield
                evq(pu, U_[c], U_[c][:])
                evq(pw, WT[c], WT[c][:])
                done1[c] = True

            def stage2(h=h):
                S.op("pool", [], [Sbfs[0]], lambda e: e.memset(Sbfs[0][:], 0.0))
                S.op("pool", [], [Sst], lambda e: e.memset(Sst[:], 0.0))
                col = slice(h, h + 1)
                for c in range(nchunks):
                    while not done1[c]:
                        yield
                    cs = slice(c * C, (c + 1) * C)
                    r2 = c % 2
                    Sold, Snew = Sbfs[c % 2], Sbfs[(c + 1) % 2]
                    p1, p2a = pool2.get(), pool2.get()
                    S.op("pe", [WT[c], Sold], [p1], lambda e: e.matmul(p1.ap, lhsT=WT[c][:], rhs=Sold[:], start=True, stop=True))
                    S.op("pe", [qTb, Sold], [p2a], lambda e: e.matmul(p2a.ap, lhsT=qTb[:, cs], rhs=Sold[:], start=True, stop=True))
                    yield
                    S.op("dve", [U_[c], p1], [vnew[r2]], lambda e: e.tensor_tensor(out=vnew[r2][:], in0=U_[c][:], in1=p1.ap, op=ALU.subtract))
                    S.op("act", [p2a, egc], [o2b[r2]], lambda e: e.activation(out=o2b[r2][:], in_=p2a.ap, func=AF.Copy, scale=egc[:, c, col]))
                    yield
                    p3, p2b = pool2.get(), pool2.get()
                    S.op("pe", [KD[c], vnew[r2]], [p3], lambda e: e.matmul(p3.ap, lhsT=KD[c][:], rhs=vnew[r2][:], start=True, stop=True))
                    S.op("pe", [AT[c], vnew[r2]], [p2b], lambda e: e.matmul(p2b.ap, lhsT=AT[c][:], rhs=vnew[r2][:], start=True, stop=True))
                    yield
                    S.op("dve", [Sst, egl, p3], [Sst], lambda e: e.scalar_tensor_tensor(
                        out=Sst[:], in0=Sst[:], scalar=egl[:, c, col], in1=p3.ap, op0=ALU.mult, op1=ALU.add))
                    yield
                    S.op("act", [Sst], [Snew], lambda e: e.activation(out=Snew[:], in_=Sst[:], func=AF.Copy))
                    S.op("dve", [p2b, o2b[r2]], [osb[r2]], lambda e: e.tensor_tensor(out=osb[r2][:], in0=p2b.ap, in1=o2b[r2][:], op=ALU.add))
                    yield
                    S.op("act", [osb[r2]], [junk, ssq], lambda e: e.activation(out=junk[:], in_=osb[r2][:], func=AF.Square, accum_out=ssq[:, r2:r2 + 1]))
                    S.op("dve", [ssq], [ssq], lambda e: e.tensor_scalar(out=ssq[:, r2:r2 + 1], in0=ssq[:, r2:r2 + 1], scalar1=1.0 / 128, scalar2=EPS, op0=ALU.mult, op1=ALU.add))
                    S.op("act", [ssq], [ssq], lambda e: e.activation(out=ssq[:, r2:r2 + 1], in_=ssq[:, r2:r2 + 1], func=AF.Sqrt))
                    S.op("dve", [ssq], [ssq], lambda e: e.reciprocal(out=ssq[:, r2:r2 + 1], in_=ssq[:, r2:r2 + 1]))
                    S.op("dve", [osb[r2], ssq, zg], [ysb[r2]], lambda e: e.scalar_tensor_tensor(
                        out=ysb[r2][:], in0=osb[r2][:], scalar=ssq[:, r2:r2 + 1], in1=zg[:, c, :], op0=ALU.mult, op1=ALU.mult))
                    yield
                    py = pool2.get()
                    S.op("pe", [ysb[r2], I32m], [py], lambda e: e.transpose(out=py.ap, in_=ysb[r2][:], identity=I32m[:, :]))
                    S.op("act", [py], [ot], lambda e: e.activation(out=ot[:, cs], in_=py.ap, func=AF.Copy))
                    yield

            pending = list(range(nchunks))
            active = {}
            s2 = stage2()
            s2_alive = True
            while pending or active or s2_alive:
                while pending and len(active) < NR:
                    c = pending.pop(0)
                    slot = [k for k in range(NR) if k not in active][0]
                    active[slot] = stage1(c, slot)
                for slot in list(active.keys()):
                    try:
                        next(active[slot])
                    except StopIteration:
                        del active[slot]
                if s2_alive:
                    try:
                        next(s2)
                    except StopIteration:
                        s2_alive = False
            S.dma([ot], [], OT[h * 128:(h + 1) * 128, :], ot[:])
        P.close()


def prep_shared(inp):
    perm = rope_partner_perm()
    w = np.asarray(inp["attn_w_in"], np.float32)
    w4 = np.concatenate([w, w[:, :, 0:512][:, :, perm], w[:, :, 512:1024][:, :, perm]], axis=2)
    return {
        "ada_w": np.ascontiguousarray(inp["ada_w"], np.float32),
        "attn_w_in": np.ascontiguousarray(w4),
        "attn_w_out": np.ascontiguousarray(inp["attn_w_out"], np.float32),
        "gdn_w_in": np.ascontiguousarray(inp["gdn_w_in"], np.float32),
        "gdn_w_out": np.ascontiguousarray(inp["gdn_w_out"], np.float32),
        "ffn_w_in": np.ascontiguousarray(inp["ffn_w_in"], np.float32),
        "ffn_w_out": np.ascontiguousarray(inp["ffn_w_out"], np.float32),
    }


def prep_core(inp, b, shared):
    m = dict(shared)
    m["xT"] = np.ascontiguousarray(np.asarray(inp["x"][b], np.float32).T)
    m["vecs"] = build_vecs(inp, b)
    m["pos"] = np.ascontiguousarray(np.asarray(inp["positions"][b], np.int32).reshape(1, T))
    return m


def build_full():
    pg = Prog()
    pg.phase0()
    for l in range(NL):
        pg.phase1(l)
        if l % 2 == 0:
            pg.phase_attn()
        else:
            pg.phase_gdn(l // 2)
        pg.phase34(l, final=(l == NL - 1))
    return pg


def kernel(**inputs):
    inp = {k: np.asarray(v) for k, v in inputs.items()}
    pg = build_full()
    shared = prep_shared(inp)
    in_maps = []
    for b in range(8):
        m = prep_core(inp, b, shared)
        in_maps.append({k: m[k] for k in pg.used_inputs})
    res = run_bass_kernel_spmd(pg.nc, in_maps, core_ids=list(range(8)))
    out = np.empty((8, T, D), np.float32)
    for b in range(8):
        out[b] = np.asarray(res.results[b]["outT"], np.float32).T
    return out
```

```python
import numpy as np
from contextlib import ExitStack
import concourse.bass as bass
import concourse.mybir as mybir
from concourse.bass_utils import run_bass_kernel_spmd

F32 = mybir.dt.float32
BF16 = mybir.dt.bfloat16
I32 = mybir.dt.int32
AF = mybir.ActivationFunctionType
ALU = mybir.AluOpType
AX = mybir.AxisListType

D = 1024
T = 4096
NL = 4
KC = 8
FH = 2816
NJ = 22
EPS = 1e-6
NDMA = 48
BIG = 30000.0
SAME_ENG_SYNC = False
GDN_NR = 2
PSQ_SEPARATE = False
GDN_DBG = 0


class Buf:
    __slots__ = ("t", "w", "r", "name")

    def __init__(self, t, name=""):
        self.t = t
        self.w = None
        self.r = {}
        self.name = name

    def __getitem__(self, idx):
        return self.t[idx]


class PQ:
    __slots__ = ("ap", "b")

    def __init__(self, bank, q):
        self.b = bank
        self.ap = bank.t[:, q * 128:(q + 1) * 128]


class Sched:
    def __init__(self, nc):
        self.nc = nc
        self.E = {"pe": nc.tensor, "act": nc.scalar, "dve": nc.vector, "pool": nc.gpsimd, "sp": nc.sync}
        self.sem = {k: nc.alloc_semaphore("s_" + k) for k in ("pe", "act", "dve", "pool")}
        self.cnt = {k: 0 for k in self.sem}
        self.seen = {k: {} for k in self.E}
        self.dsem = [nc.alloc_semaphore("d%d" % i) for i in range(NDMA)]
        self.dcnt = [0] * NDMA
        self.drr = 0
        self.bufs = []
        self.ps_rr = 0

    def sbuf(self, name, shape, dtype):
        b = Buf(self.nc.alloc_sbuf_tensor(name, list(shape), dtype), name)
        self.bufs.append(b)
        return b

    def psum(self, name, shape, dtype=F32):
        b = Buf(self.nc.alloc_psum_tensor(name, list(shape), dtype), name)
        self.bufs.append(b)
        return b

    def dram(self, t, name=""):
        b = Buf(t, name)
        self.bufs.append(b)
        return b

    def _wait(self, eng, ev):
        if ev is None:
            return
        sem, val, key = ev
        if self.seen[eng].get(key, 0) >= val:
            return
        self.E[eng].wait_ge(sem, val)
        self.seen[eng][key] = val

    def _deps(self, eng, reads, writes):
        reads = [getattr(b, "b", b) for b in reads]
        writes = [getattr(b, "b", b) for b in writes]
        for b in reads:
            self._wait(eng, b.w)
        for b in writes:
            if b.w is not None and (SAME_ENG_SYNC or b.w[2] != eng):
                self._wait(eng, b.w)
            for ev in b.r.values():
                if SAME_ENG_SYNC or ev[2] != eng:
                    self._wait(eng, ev)

    def _done(self, ev, reads, writes):
        reads = [getattr(b, "b", b) for b in reads]
        writes = [getattr(b, "b", b) for b in writes]
        for b in reads:
            b.r[ev[2]] = ev
        for b in writes:
            b.w = ev
            b.r = {}

    def op(self, eng, reads, writes, fn):
        self._deps(eng, reads, writes)
        ins = fn(self.E[eng])
        self.cnt[eng] += 1
        ins.then_inc(self.sem[eng], 1)
        ev = (self.sem[eng], self.cnt[eng], eng)
        self._done(ev, reads, writes)
        return ev

    def seq(self, eng, buf, fns, reads=()):
        ev = None
        for k, fn in enumerate(fns):
            ev = self.op(eng, ([buf] if k > 0 else []) + list(reads), [buf], fn)
        return ev

    def dma(self, reads, writes, out, in_, eng="sp"):
        k = self.drr
        self.drr = (k + 1) % NDMA
        key = ("d", k)
        if self.dcnt[k] > 0:
            self._wait(eng, (self.dsem[k], self.dcnt[k], key))
        self._deps(eng, reads, writes)
        ins = self.E[eng].dma_start(out=out, in_=in_)
        self.dcnt[k] += 16
        ins.then_inc(self.dsem[k], 16)
        ev = (self.dsem[k], self.dcnt[k], key)
        self._done(ev, reads, writes)
        return ev

    def barrier(self):
        evs = []
        for k in self.sem:
            if self.cnt[k] > 0:
                evs.append((self.sem[k], self.cnt[k], k))
        for k in range(NDMA):
            if self.dcnt[k] > 0:
                evs.append((self.dsem[k], self.dcnt[k], ("d", k)))
        for eng in self.E:
            for ev in evs:
                self._wait(eng, ev)
        for b in self.bufs:
            b.w = None
            b.r = {}


class Phase:
    _uid = [0]

    def __init__(self, S):
        Phase._uid[0] += 1
        self.uid = Phase._uid[0]
        self.S = S
        self.nc = S.nc
        self.st = ExitStack()
        self.nbufs0 = len(S.bufs)

    def sbuf(self, name, shape, dtype):
        name = "%s_p%d" % (name, self.uid)
        t = self.st.enter_context(self.nc.sbuf_tensor(name, list(shape), dtype))
        b = Buf(t, name)
        self.S.bufs.append(b)
        return b

    def psum(self, name, shape, dtype=F32):
        name = "%s_p%d" % (name, self.uid)
        t = self.st.enter_context(self.nc.psum_tensor(name, list(shape), dtype))
        b = Buf(t, name)
        self.S.bufs.append(b)
        return b

    def close(self):
        self.S.barrier()
        self.st.close()
        del self.S.bufs[self.nbufs0:]


VOFF = {}


def _vec_layout():
    off = 0
    for name, n in [("gm", 32), ("gf", 32), ("fin", 8), ("adab", 192), ("conv", 192), ("gngb", 256),
                    ("alog", 16), ("dtb", 16), ("invf", 1), ("sinsign", 1), ("cT", 8)]:
        VOFF[name] = off
        off += n
    return off


NV = _vec_layout()


def build_vecs(inp, b):
    v = np.zeros((128, NV), np.float32)
    def chunked(a):
        a = np.asarray(a, np.float32).reshape(-1, 128)
        return a.T
    v[:, VOFF["gm"]:VOFF["gm"] + 32] = chunked(inp["norm_mix_g"])
    v[:, VOFF["gf"]:VOFF["gf"] + 32] = chunked(inp["norm_ffn_g"])
    v[:, VOFF["fin"]:VOFF["fin"] + 8] = chunked(inp["final_norm_g"])
    v[:, VOFF["adab"]:VOFF["adab"] + 192] = chunked(inp["ada_b"])
    v[:, VOFF["conv"]:VOFF["conv"] + 192] = chunked(inp["gdn_conv_w"])
    v[:, VOFF["gngb"]:VOFF["gngb"] + 256] = np.broadcast_to(np.asarray(inp["gdn_norm_g"], np.float32).reshape(1, 256), (128, 256))
    v[:, VOFF["alog"]:VOFF["alog"] + 16] = np.broadcast_to(np.asarray(inp["gdn_a_log"], np.float32).reshape(1, 16), (128, 16))
    v[:, VOFF["dtb"]:VOFF["dtb"] + 16] = np.broadcast_to(np.asarray(inp["gdn_dt_bias"], np.float32).reshape(1, 16), (128, 16))
    p = np.arange(128) % 64
    inv = (500000.0 ** (-np.arange(8, dtype=np.float32) * 2.0 / 16)).astype(np.float32)
    v[:, VOFF["invf"]] = np.where(p < 16, inv[p % 8], 0.0)
    v[:, VOFF["sinsign"]] = np.where(p < 8, -1.0, np.where(p < 16, 1.0, 0.0))
    v[:, VOFF["cT"]:VOFF["cT"] + 8] = chunked(inp["c"][b])
    return v


def rope_partner_perm():
    idx = np.arange(512)
    d = idx % 64
    return np.where(d < 8, idx + 8, np.where(d < 16, idx - 8, idx))


class LazyDram(dict):
    def __init__(self, prog):
        super().__init__()
        self.prog = prog

    def __missing__(self, name):
        pg = self.prog
        if name in pg.in_shapes:
            shape, dt = pg.in_shapes[name]
            kind = "ExternalInput"
            pg.used_inputs.append(name)
        else:
            shape, dt = pg.scr_shapes[name]
            kind = "Internal"
            if name in pg.ext_out or name == "outT":
                kind = "ExternalOutput"
            if name in pg.ext_in:
                kind = "ExternalInput"
                pg.used_inputs.append(name)
        ap = pg.nc.dram_tensor(name, list(shape), dt, kind=kind).ap()
        self[name] = ap
        return ap


class Prog:
    def __init__(self, ext_out=(), ext_in=()):
        self.nc = bass.Bass("TRN2", target_bir_lowering=False)
        nc = self.nc
        self.ext_out = set(ext_out)
        self.ext_in = set(ext_in)
        self.S = Sched(nc)
        S = self.S
        self.in_shapes = {
            "xT": ([D, T], F32), "vecs": ([128, NV], F32), "pos": ([1, T], I32),
            "ada_w": ([NL, D, 6 * D], F32), "attn_w_in": ([2, D, 4096], F32), "attn_w_out": ([2, D, D], F32),
            "gdn_w_in": ([2, D, 4112], F32), "gdn_w_out": ([2, D, D], F32),
            "ffn_w_in": ([NL, D, 2 * FH], F32), "ffn_w_out": ([NL, FH, D], F32),
        }
        self.used_inputs = []
        self.dr = LazyDram(self)
        self.dbg_outs = []
        self.scr_shapes = {
            "XT": ([D, T], F32), "COS": ([128, T], F32), "SIN": ([128, T], F32),
            "QT": ([2048, T], BF16),
            "VTOK": ([T, 1024], BF16),
            "OT": ([D, T], BF16),
            "GT": ([3072, T], BF16),
            "AB": ([T, 16], F32),
            "outT": ([D, T], F32),
        }
        self.VEC = S.sbuf("VEC", [128, NV], F32)
        self.MOD = S.sbuf("MOD", [128, NL * 48], F32)
        self.AM = S.sbuf("AMc", [128, NL * 8], F32)
        self.AFc = S.sbuf("AFc", [128, NL * 8], F32)
        self.ones32 = S.sbuf("ones32", [128, 128], F32)
        self.ident32 = S.sbuf("ident32", [128, 128], F32)
        self.PS = [S.psum("ps%d" % i, [128, 512], F32) for i in range(8)]
        self.ps_i = 0
        S.op("pool", [], [self.ones32], lambda e: e.memset(self.ones32[:], 1.0))
        S.seq("pool", self.ident32, [
            lambda e: e.memset(self.ident32[:], 0.0),
            lambda e: e.affine_select(out=self.ident32[:], in_=self.ident32[:], compare_op=ALU.not_equal,
                                      fill=1.0, base=0, pattern=[[-1, 128]], channel_multiplier=1)])

    def dbg(self, name, buf, ap):
        shape = [int(x) for x in ap.shape]
        d = self.nc.dram_tensor(name, shape, buf.t.dtype, kind="ExternalOutput").ap()
        self.S.dma([buf], [], d, ap)
        self.dbg_outs.append(name)

    def ps(self):
        b = self.PS[self.ps_i]
        self.ps_i = (self.ps_i + 1) % 8
        return b

    def psqs(self, n):
        if PSQ_SEPARATE:
            return [PQ(self.ps(), 0) for q in range(n)]
        b = self.ps()
        return [PQ(b, q) for q in range(n)]

    def vcol(self, name, i=0, n=1):
        o = VOFF[name] + i
        return self.VEC[:, o:o + n]

    def phase0(self):
        S, nc = self.S, self.nc
        P = Phase(S)
        VEC, MOD = self.VEC, self.MOD
        S.dma([], [VEC], VEC[:], self.dr["vecs"])
        condT = P.sbuf("condT", [128, 8], F32)
        S.op("act", [VEC], [condT], lambda e: e.activation(out=condT[:], in_=self.vcol("cT", 0, 8), func=AF.Silu))
        CB = 1536
        wst = [P.sbuf("adaw%d" % i, [128, KC, CB], F32) for i in range(2)]
        k = 0
        for l in range(NL):
            pm = self.ps()
            for cb in range(4):
                w = wst[k % 2]
                k += 1
                src = self.dr["ada_w"][l, :, cb * CB:(cb + 1) * CB].rearrange("(c p) n -> p c n", p=128)
                S.dma([], [w], w[:], src)
                def mm(e, w=w, cb=cb, pm=pm):
                    last = None
                    for j in range(12):
                        col = cb * 12 + j
                        for kc in range(KC):
                            last = e.matmul(pm[:, col:col + 1], lhsT=w[:, kc, j * 128:(j + 1) * 128],
                                            rhs=condT[:, kc:kc + 1], start=(kc == 0), stop=(kc == KC - 1))
                    return last
                S.op("pe", [w, condT], [pm], mm)
            S.op("dve", [pm, VEC], [MOD], lambda e, l=l, pm=pm: e.tensor_tensor(
                out=MOD[:, l * 48:(l + 1) * 48], in0=pm[:, 0:48], in1=self.vcol("adab", l * 48, 48), op=ALU.add))
            S.op("dve", [MOD, VEC], [self.AM], lambda e, l=l: e.scalar_tensor_tensor(
                out=self.AM[:, l * 8:(l + 1) * 8], in0=MOD[:, l * 48 + 8:l * 48 + 16], scalar=1.0,
                in1=self.vcol("gm", l * 8, 8), op0=ALU.add, op1=ALU.mult))
            S.op("dve", [MOD, VEC], [self.AFc], lambda e, l=l: e.scalar_tensor_tensor(
                out=self.AFc[:, l * 8:(l + 1) * 8], in0=MOD[:, l * 48 + 32:l * 48 + 40], scalar=1.0,
                in1=self.vcol("gf", l * 8, 8), op0=ALU.add, op1=ALU.mult))
        posi = P.sbuf("posi", [1, T], I32)
        posf = P.sbuf("posf", [1, T], F32)
        S.dma([], [posi], posi[:], self.dr["pos"])
        S.op("dve", [posi], [posf], lambda e: e.tensor_copy(out=posf[:], in_=posi[:]))
        angs = [P.sbuf("ang%d" % k, [128, 512], F32) for k in range(2)]
        rbuf = {}
        for wh in ("SIN", "COS"):
            for k in range(2):
                rbuf[(wh, k)] = (P.sbuf("r%s%d" % (wh, k), [128, 512], F32), P.sbuf("xs%s%d" % (wh, k), [128, 512], F32),
                                 P.sbuf("ki%s%d" % (wh, k), [128, 512], I32))
        for tt in range(8):
            sl = slice(tt * 512, (tt + 1) * 512)
            pb = self.ps()
            S.op("pe", [posf, self.ones32], [pb], lambda e, pb=pb, sl=sl: e.matmul(
                pb[:, :], lhsT=self.ones32[0:1, :], rhs=posf[0:1, sl], start=True, stop=True))
            ang = angs[tt % 2]
            S.op("dve", [pb, VEC], [ang], lambda e, pb=pb, ang=ang: e.tensor_scalar(
                out=ang[:], in0=pb[:, :], scalar1=self.vcol("invf"), scalar2=None, op0=ALU.mult))
            for which, shift in (("SIN", 0.0), ("COS", 0.5 * np.pi)):
                r, xs, ki = rbuf[(which, tt % 2)]
                C1 = 6.28125
                C2 = float(2 * np.pi - 6.28125)
                S.op("dve", [ang], [xs], lambda e, xs=xs, ang=ang, shift=shift: e.tensor_scalar(
                    out=xs[:], in0=ang[:], scalar1=float(shift), scalar2=None, op0=ALU.add))
                S.op("dve", [xs], [r], lambda e, r=r, xs=xs: e.tensor_scalar(
                    out=r[:], in0=xs[:], scalar1=float(1.0 / (2 * np.pi)), scalar2=None, op0=ALU.mult))
                S.op("dve", [r], [ki], lambda e, r=r, ki=ki: e.tensor_copy(out=ki[:], in_=r[:]))
                S.op("dve", [ki], [r], lambda e, r=r, ki=ki: e.tensor_copy(out=r[:], in_=ki[:]))
                S.op("dve", [r, xs], [xs], lambda e, r=r, xs=xs: e.scalar_tensor_tensor(
                    out=xs[:], in0=r[:], scalar=-C1, in1=xs[:], op0=ALU.mult, op1=ALU.add))
                S.op("dve", [r, xs], [xs], lambda e, r=r, xs=xs: e.scalar_tensor_tensor(
                    out=xs[:], in0=r[:], scalar=-C2, in1=xs[:], op0=ALU.mult, op1=ALU.add))
                S.op("dve", [xs], [xs], lambda e, xs=xs: e.tensor_scalar(
                    out=xs[:], in0=xs[:], scalar1=-3.14159, scalar2=3.14159, op0=ALU.max, op1=ALU.min))
                S.op("act", [xs], [r], lambda e, r=r, xs=xs: e.activation(out=r[:], in_=xs[:], func=AF.Sin))
                if which == "SIN":
                    S.op("dve", [r, VEC], [r], lambda e, r=r: e.tensor_scalar(
                        out=r[:], in0=r[:], scalar1=self.vcol("sinsign"), scalar2=None, op0=ALU.mult))
                S.dma([r], [], self.dr[which][:, sl], r[:])
        P.close()

    def load_weight_bf16(self, P, Wb, src3, ncols, stages, col0=0):
        S = self.S
        nk = src3.shape[1]
        CB = stages[0].t.shape[2]
        c = 0
        i = 0
        while c < ncols:
            n = min(CB, ncols - c)
            st = stages[i % len(stages)]
            i += 1
            S.dma([], [st], st[:, 0:nk, 0:n], src3[:, :, c:c + n])
            S.op("pool", [st], [Wb], lambda e, st=st, c=c, n=n: e.tensor_copy(
                out=Wb[:, 0:nk, col0 + c:col0 + c + n], in_=st[:, 0:nk, 0:n]))
            c += n

    def norm_tile(self, P, X, A, B, h, sq, tmps, rstd, n=512):
        S = self.S
        S.op("act", [X], [sq], lambda e: e.activation(out=sq[:, :, 0:n], in_=X[:, :, 0:n], func=AF.Square))
        pss = self.ps()
        def mm(e):
            last = None
            for kc in range(KC):
                last = e.matmul(pss[:, 0:n], lhsT=self.ones32[:, :], rhs=sq[:, kc, 0:n], start=(kc == 0), stop=(kc == KC - 1))
            return last
        S.op("pe", [sq, self.ones32], [pss], mm)
        S.op("dve", [pss], [rstd], lambda e: e.tensor_scalar(
            out=rstd[:, 0:n], in0=pss[:, 0:n], scalar1=1.0 / D, scalar2=EPS, op0=ALU.mult, op1=ALU.add))
        S.op("act", [rstd], [rstd], lambda e: e.activation(out=rstd[:, 0:n], in_=rstd[:, 0:n], func=AF.Sqrt))
        S.op("dve", [rstd], [rstd], lambda e: e.reciprocal(out=rstd[:, 0:n], in_=rstd[:, 0:n]))
        for kc in range(KC):
            tmp = tmps[kc % len(tmps)]
            S.op("dve", [X, rstd, self.AM, self.AFc], [tmp], lambda e, kc=kc, tmp=tmp: e.scalar_tensor_tensor(
                out=tmp[:, 0:n], in0=X[:, kc, 0:n], scalar=A[:, kc:kc + 1], in1=rstd[:, 0:n], op0=ALU.mult, op1=ALU.mult))
            S.op("act", [tmp, self.MOD], [h], lambda e, kc=kc, tmp=tmp: e.activation(
                out=h[:, kc, 0:n], in_=tmp[:, 0:n], func=AF.Identity, bias=B[:, kc:kc + 1], scale=1.0))

    def phase1(self, l):
        S = self.S
        P = Phase(S)
        attn = (l % 2 == 0)
        i = l // 2
        NC_ = 4096 if attn else 4112
        Wb = P.sbuf("W1", [128, KC, NC_], BF16)
        stages = [P.sbuf("wst%d" % k, [128, KC, 512], F32) for k in range(2)]
        wsrc = self.dr["attn_w_in" if attn else "gdn_w_in"][i].rearrange("(c p) n -> p c n", p=128)
        self.load_weight_bf16(P, Wb, wsrc, NC_, stages)
        Xb = [P.sbuf("X%d" % k, [128, KC, 512], F32) for k in range(2)]
        sq = P.sbuf("sq", [128, KC, 512], F32)
        tmps = [P.sbuf("tmp%d" % k, [128, 512], F32) for k in range(2)]
        rstd = P.sbuf("rstd", [128, 512], F32)
        h = P.sbuf("h", [128, KC, 512], BF16)
        stg = [P.sbuf("stg%d" % k, [128, 512], BF16) for k in range(4)]
        stg32 = [P.sbuf("stgf%d" % k, [128, 16], F32) for k in range(2)]
        rt = [P.sbuf("rt%d" % k, [128, 512], F32) for k in range(4)]
        cs = [[P.sbuf("cs%d_%d" % (a, k), [128, 512], F32) for k in range(2)] for a in range(2)]
        A = self.AM[:, l * 8:(l + 1) * 8]
        B = self.MOD[:, l * 48:l * 48 + 8]
        XT3 = self.dr["xT" if l == 0 else "XT"].rearrange("(c p) t -> p c t", p=128)
        NT = T // 512
        S.dma([], [Xb[0]], Xb[0][:], XT3[:, :, 0:512])
        sk = 0
        ev = 0
        for tt in range(NT):
            sl = slice(tt * 512, (tt + 1) * 512)
            X = Xb[tt % 2]
            if tt + 1 < NT:
                S.dma([], [Xb[(tt + 1) % 2]], Xb[(tt + 1) % 2][:], XT3[:, :, (tt + 1) * 512:(tt + 2) * 512])
            if attn:
                cosb, sinb = cs[0][tt % 2], cs[1][tt % 2]
                S.dma([], [cosb], cosb[:], self.dr["COS"][:, sl])
                S.dma([], [sinb], sinb[:], self.dr["SIN"][:, sl])
            self.norm_tile(P, X, A, B, h, sq, tmps, rstd)

            def proj_fm(col0):
                pp = self.ps()
                def mm(e, pp=pp, col0=col0):
                    last = None
                    for kc in range(KC):
                        last = e.matmul(pp[:, :], lhsT=Wb[:, kc, col0:col0 + 128], rhs=h[:, kc, :],
                                        start=(kc == 0), stop=(kc == KC - 1))
                    return last
                S.op("pe", [Wb, h], [pp], mm)
                return pp

            def evac(pp, dst_ap, scale, width=512):
                nonlocal sk, ev
                st = stg[sk % 4]
                sk += 1
                if ev % 2 == 0:
                    S.op("act", [pp], [st], lambda e: e.activation(out=st[:, 0:width], in_=pp[:, 0:width], func=AF.Copy, scale=float(scale)))
                else:
                    S.op("dve", [pp], [st], lambda e: e.tensor_scalar(out=st[:, 0:width], in0=pp[:, 0:width], scalar1=float(scale), scalar2=None, op0=ALU.mult))
                ev += 1
                S.dma([st], [], dst_ap, st[:, 0:width])

            def proj_tm(col0, ncols, dst, dcol0, f32out=False):
                nonlocal sk
                for s in range(4):
                    pp = self.ps()
                    def mm(e, pp=pp, s=s):
                        last = None
                        for kc in range(KC):
                            last = e.matmul(pp[:, 0:ncols], lhsT=h[:, kc, s * 128:(s + 1) * 128], rhs=Wb[:, kc, col0:col0 + ncols],
                                            start=(kc == 0), stop=(kc == KC - 1))
                        return last
                    S.op("pe", [Wb, h], [pp], mm)
                    r0 = tt * 512 + s * 128
                    if f32out:
                        st = stg32[s % 2]
                        S.op("dve", [pp], [st], lambda e, st=st, pp=pp: e.tensor_copy(out=st[:, 0:ncols], in_=pp[:, 0:ncols]))
                        S.dma([st], [], dst[r0:r0 + 128, dcol0:dcol0 + ncols], st[:, 0:ncols])
                    else:
                        evac(pp, dst[r0:r0 + 128, dcol0:dcol0 + ncols], 1.0, ncols)

            if attn:
                QT = self.dr["QT"]
                for grp, (c0, pc0, d0, scale) in enumerate([(0, 3072, 0, 0.125), (512, 3584, 512, 1.0)]):
                    for n in range(4):
                        pq = proj_fm(c0 + n * 128)
                        pp = proj_fm(pc0 + n * 128)
                        t1, t2 = rt[(2 * n) % 4], rt[(2 * n + 1) % 4]
                        st = stg[sk % 4]
                        sk += 1
                        S.op("dve", [pq, cosb], [t1], lambda e, pq=pq, t1=t1, scale=scale: e.scalar_tensor_tensor(
                            out=t1[:], in0=pq[:, :], scalar=float(scale), in1=cosb[:], op0=ALU.mult, op1=ALU.mult))
                        S.op("dve", [pp, sinb], [t2], lambda e, pp=pp, t2=t2, scale=scale: e.scalar_tensor_tensor(
                            out=t2[:], in0=pp[:, :], scalar=float(scale), in1=sinb[:], op0=ALU.mult, op1=ALU.mult))
                        S.op("pool", [t1, t2], [st], lambda e, st=st, t1=t1, t2=t2: e.tensor_tensor(
                            out=st[:], in0=t1[:], in1=t2[:], op=ALU.add))
                        S.dma([st], [], QT[d0 + n * 128:d0 + (n + 1) * 128, sl], st[:])
                for n in range(4):
                    evac(proj_fm(1536 + n * 128), QT[1024 + n * 128:1024 + (n + 1) * 128, sl], 0.125)
                for n in range(4):
                    evac(proj_fm(2048 + n * 128), QT[1536 + n * 128:1536 + (n + 1) * 128, sl], 1.0)
                proj_tm(1024, 512, self.dr["VTOK"], 0)
                proj_tm(2560, 512, self.dr["VTOK"], 512)
            else:
                GT = self.dr["GT"]
                for n in range(24):
                    evac(proj_fm(n * 128), GT[n * 128:(n + 1) * 128, sl], 1.0)
                proj_tm(3072, 512, self.dr["VTOK"], 0)
                proj_tm(3584, 512, self.dr["VTOK"], 512)
                proj_tm(4096, 16, self.dr["AB"], 0, f32out=True)
        P.close()

    def phase34(self, l, final=False, tiles=None):
        S = self.S
        P = Phase(S)
        i = l // 2
        TT = 256
        Wo = P.sbuf("Wo", [128, KC, D], BF16)
        Wi = P.sbuf("Wi", [128, KC, 2 * FH], BF16)
        W2 = P.sbuf("W2", [128, NJ, D], BF16)
        PW = Phase(S)
        stA = [PW.sbuf("stA%d" % k, [128, KC, 512], F32) for k in range(2)]
        stB = [PW.sbuf("stB%d" % k, [128, NJ, 128], F32) for k in range(2)]
        wo_src = self.dr["attn_w_out" if l % 2 == 0 else "gdn_w_out"][i].rearrange("(c p) n -> p c n", p=128)
        self.load_weight_bf16(PW, Wo, wo_src, D, stA)
        self.load_weight_bf16(PW, Wi, self.dr["ffn_w_in"][l].rearrange("(c p) n -> p c n", p=128), 2 * FH, stA)
        self.load_weight_bf16(PW, W2, self.dr["ffn_w_out"][l].rearrange("(c p) n -> p c n", p=128), D, stB)
        PW.close()
        Xb = [P.sbuf("X%d" % k, [128, KC, TT], F32) for k in range(1)]
        ob = [P.sbuf("oT%d" % k, [128, KC, TT], BF16) for k in range(2)]
        sq = P.sbuf("sq", [128, KC, TT], F32)
        tmps = [P.sbuf("tmp%d" % k, [128, TT], F32) for k in range(2)]
        sgs = [P.sbuf("sg%d" % k, [128, TT], F32) for k in range(2)]
        rstd = P.sbuf("rstd", [128, TT], F32)
        h = P.sbuf("h", [128, KC, TT], BF16)
        hid = P.sbuf("hid", [128, NJ, TT], BF16)
        GM = self.MOD[:, l * 48 + 16:l * 48 + 24]
        GF = self.MOD[:, l * 48 + 40:l * 48 + 48]
        A = self.AFc[:, l * 8:(l + 1) * 8]
        B = self.MOD[:, l * 48 + 24:l * 48 + 32]
        XS3 = self.dr["xT" if l == 0 else "XT"].rearrange("(c p) t -> p c t", p=128)
        XD3 = self.dr["XT"].rearrange("(c p) t -> p c t", p=128)
        OT3 = self.dr["OT"].rearrange("(c p) t -> p c t", p=128)
        if final:
            OUT3 = self.dr["outT"].rearrange("(c p) t -> p c t", p=128)
            yout = P.sbuf("yout", [128, KC, TT], F32)
        NT = T // TT
        tl = list(range(NT)) if tiles is None else list(tiles)
        S.dma([], [ob[0]], ob[0][:], OT3[:, :, tl[0] * TT:(tl[0] + 1) * TT])
        for ti, tt in enumerate(tl):
            sl = slice(tt * TT, (tt + 1) * TT)
            X = Xb[0]
            oT = ob[ti % 2]
            S.dma([], [X], X[:], XS3[:, :, sl])
            if ti + 1 < len(tl):
                nt = tl[ti + 1]
                S.dma([], [ob[(ti + 1) % 2]], ob[(ti + 1) % 2][:], OT3[:, :, nt * TT:(nt + 1) * TT])
            for n in range(KC):
                pp = self.ps()
                def mm(e, pp=pp, n=n):
                    last = None
                    for kc in range(KC):
                        last = e.matmul(pp[:, 0:TT], lhsT=Wo[:, kc, n * 128:(n + 1) * 128], rhs=oT[:, kc, :],
                                        start=(kc == 0), stop=(kc == KC - 1))
                    return last
                S.op("pe", [Wo, oT], [pp], mm)
                S.op("dve", [pp, X, self.MOD], [X], lambda e, pp=pp, n=n: e.scalar_tensor_tensor(
                    out=X[:, n, :], in0=pp[:, 0:TT], scalar=GM[:, n:n + 1], in1=X[:, n, :], op0=ALU.mult, op1=ALU.add))
            self.norm_tile(P, X, A, B, h, sq, tmps, rstd, n=TT)
            for j in range(NJ):
                pg = self.ps()
                pu = self.ps()
                def mmg(e, pg=pg, j=j):
                    last = None
                    for kc in range(KC):
                        last = e.matmul(pg[:, 0:TT], lhsT=Wi[:, kc, j * 128:(j + 1) * 128], rhs=h[:, kc, :],
                                        start=(kc == 0), stop=(kc == KC - 1))
                    return last
                def mmu(e, pu=pu, j=j):
                    last = None
                    for kc in range(KC):
                        last = e.matmul(pu[:, 0:TT], lhsT=Wi[:, kc, FH + j * 128:FH + (j + 1) * 128], rhs=h[:, kc, :],
                                        start=(kc == 0), stop=(kc == KC - 1))
                    return last
                S.op("pe", [Wi, h], [pg], mmg)
                S.op("pe", [Wi, h], [pu], mmu)
                sg = sgs[j % 2]
                S.op("act", [pg], [sg], lambda e, pg=pg, sg=sg: e.activation(out=sg[:], in_=pg[:, 0:TT], func=AF.Silu))
                S.op("dve", [sg, pu], [hid], lambda e, pu=pu, sg=sg, j=j: e.tensor_tensor(
                    out=hid[:, j, :], in0=pu[:, 0:TT], in1=sg[:], op=ALU.mult))
            for n in range(KC):
                pp = self.ps()
                def mm2(e, pp=pp, n=n):
                    last = None
                    for j in range(NJ):
                        last = e.matmul(pp[:, 0:TT], lhsT=W2[:, j, n * 128:(n + 1) * 128], rhs=hid[:, j, :],
                                        start=(j == 0), stop=(j == NJ - 1))
                    return last
                S.op("pe", [W2, hid], [pp], mm2)
                S.op("dve", [pp, X, self.MOD], [X], lambda e, pp=pp, n=n: e.scalar_tensor_tensor(
                    out=X[:, n, :], in0=pp[:, 0:TT], scalar=GF[:, n:n + 1], in1=X[:, n, :], op0=ALU.mult, op1=ALU.add))
            if final:
                self.final_norm_tile(X, yout, sq, rstd, TT)
                S.dma([yout], [], OUT3[:, :, sl], yout[:])
            else:
                S.dma([X], [], XD3[:, :, sl], X[:])
        P.close()

    def final_norm_tile(self, X, yout, sq, rstd, n):
        S = self.S
        S.op("act", [X], [sq], lambda e: e.activation(out=sq[:, :, 0:n], in_=X[:, :, 0:n], func=AF.Square))
        pss = self.ps()
        def mm(e):
            last = None
            for kc in range(KC):
                last = e.matmul(pss[:, 0:n], lhsT=self.ones32[:, :], rhs=sq[:, kc, 0:n], start=(kc == 0), stop=(kc == KC - 1))
            return last
        S.op("pe", [sq, self.ones32], [pss], mm)
        S.op("dve", [pss], [rstd], lambda e: e.tensor_scalar(
            out=rstd[:, 0:n], in0=pss[:, 0:n], scalar1=1.0 / D, scalar2=EPS, op0=ALU.mult, op1=ALU.add))
        S.op("act", [rstd], [rstd], lambda e: e.activation(out=rstd[:, 0:n], in_=rstd[:, 0:n], func=AF.Sqrt))
        S.op("dve", [rstd], [rstd], lambda e: e.reciprocal(out=rstd[:, 0:n], in_=rstd[:, 0:n]))
        for kc in range(KC):
            S.op("dve", [X, rstd, self.VEC], [yout], lambda e, kc=kc: e.scalar_tensor_tensor(
                out=yout[:, kc, 0:n], in0=X[:, kc, 0:n], scalar=self.vcol("fin", kc), in1=rstd[:, 0:n], op0=ALU.mult, op1=ALU.mult))

    def phase_attn(self, heads_moba=range(8), heads_sb=range(8), qtiles=range(8)):
        S = self.S
        P = Phase(S)
        QT, VTOK, OT = self.dr["QT"], self.dr["VTOK"], self.dr["OT"]
        cmask = [P.sbuf("cmask%d" % d, [128, 512], BF16) for d in range(4)]
        smask = [P.sbuf("smask%d" % d, [128, 512], BF16) for d in range(4)]
        for d in range(4):
            for mk, op in ((cmask[d], ALU.is_ge), (smask[d], ALU.is_gt)):
                S.seq("pool", mk, [
                    lambda e, mk=mk: e.memset(mk[:], 1.0),
                    lambda e, mk=mk, op=op, d=d: e.affine_select(out=mk[:], in_=mk[:], compare_op=op, fill=0.0, base=-128 * d,
                                                                 pattern=[[1, 512]], channel_multiplier=-1)])
        NUI = P.sbuf("NUI", [128, 128], BF16)
        S.seq("pool", NUI, [
            lambda e: e.memset(NUI[:], -1.0),
            lambda e: e.affine_select(out=NUI[:], in_=NUI[:], compare_op=ALU.is_ge, fill=0.0, base=0,
                                      pattern=[[-1, 128]], channel_multiplier=1)])
        onesb = P.sbuf("onesb", [128, 128], BF16)
        S.op("pool", [], [onesb], lambda e: e.memset(onesb[:], 1.0))
        past01 = P.sbuf("past01", [128, 32, 16], F32)
        pastb = P.sbuf("pastb", [128, 32, 16], F32)
        ownm1 = P.sbuf("ownm1", [128, 32, 16], F32)
        S.seq("pool", past01, [
            lambda e: e.memset(past01[:], 1.0),
            lambda e: e.affine_select(out=past01[:], in_=past01[:], compare_op=ALU.is_ge, fill=0.0, base=-2,
                                      pattern=[[1, 32], [-2, 16]], channel_multiplier=0)])
        S.seq("pool", pastb, [
            lambda e: e.memset(pastb[:], 0.0),
            lambda e: e.affine_select(out=pastb[:], in_=pastb[:], compare_op=ALU.is_ge, fill=-1e30, base=-2,
                                      pattern=[[1, 32], [-2, 16]], channel_multiplier=0)])
        S.seq("pool", ownm1, [
            lambda e: e.memset(ownm1[:], 0.0),
            lambda e: e.affine_select(out=ownm1[:], in_=ownm1[:], compare_op=ALU.is_ge, fill=-1.0, base=0,
                                      pattern=[[1, 32], [-2, 16]], channel_multiplier=0),
            lambda e: e.affine_select(out=ownm1[:], in_=ownm1[:], compare_op=ALU.is_ge, fill=-1.0, base=1,
                                      pattern=[[-1, 32], [2, 16]], channel_multiplier=0)])
        QAUG = [P.sbuf("QAUG%d" % k, [128, T], BF16) for k in range(2)]
        KAUG = [P.sbuf("KAUG%d" % k, [128, T], BF16) for k in range(2)]
        KS = [P.sbuf("KS%d" % k, [128, T], BF16) for k in range(2)]
        KAc = P.sbuf("KAc", [16, T], BF16)
        S.seq("pool", KAc, [
            lambda e: e.memset(KAc[:], BIG),
            lambda e: e.affine_select(out=KAc[:], in_=KAc[:], compare_op=ALU.is_ge, fill=0.0, base=0,
                                      pattern=[[1, T]], channel_multiplier=-256),
            lambda e: e.affine_select(out=KAc[:], in_=KAc[:], compare_op=ALU.is_ge, fill=0.0, base=255,
                                      pattern=[[-1, T]], channel_multiplier=256)])
        for k in range(2):
            S.op("pool", [], [QAUG[k]], lambda e, k=k: e.memset(QAUG[k][:], 0.0))
            S.op("pool", [], [KS[k]], lambda e, k=k: e.memset(KS[k][:], 0.0))
            S.seq("pool", KAUG[k], [
                lambda e, k=k: e.memset(KAUG[k][:], 0.0),
                lambda e, k=k: e.memset(KAUG[k][96:97, :], -1.0)])
            S.dma([KAc], [KAUG[k]], KAUG[k][64:80, :], KAc[:])
        Vb = [P.sbuf("V%d" % k, [128, 32, 65], BF16) for k in range(2)]
        for k in range(2):
            S.op("pool", [], [Vb[k]], lambda e, k=k: e.memset(Vb[k][:], 1.0))
        sqQ = P.sbuf("sqQ", [64, T], BF16)
        kmean = P.sbuf("kmean", [64, 16], F32)
        kmeanb = P.sbuf("kmeanb", [64, 16], BF16)
        gm = P.sbuf("gm", [128, 32, 16], F32)
        m8 = P.sbuf("m8", [128, 32, 8], F32)
        MBw = P.sbuf("MBw", [128, 32, 97], F32)
        S.op("pool", [], [MBw], lambda e: e.memset(MBw[:], 0.0))
        kmx = P.sbuf("kmx", [128, 8], F32)
        kmax2 = P.sbuf("kmax2", [128, 1], F32)
        Pt = [P.sbuf("Pt%d" % k, [128, 512], BF16) for k in range(3)]
        Eb = [P.sbuf("E%d" % k, [128, 512], F32) for k in range(2)]
        Lp = [P.sbuf("Lp%d" % k, [128, 512], BF16) for k in range(3)]
        exb = [P.sbuf("ex%d" % k, [128, 512], F32) for k in range(2)]
        rs = P.sbuf("rs", [128, 512], F32)
        Osb = P.sbuf("Osb", [65, 512], F32)
        rden = P.sbuf("rden", [128, 512], F32)
        S.op("pool", [], [rden], lambda e: e.memset(rden[:], 0.0))
        sel64 = P.sbuf("sel64", [128, 128], F32)
        S.seq("pool", sel64, [lambda e: e.memset(sel64[:], 0.0), lambda e: e.memset(sel64[64:65, :], 1.0)])
        ost = [P.sbuf("ost%d" % k, [64, 512], BF16) for k in range(2)]
        PSr = self.PS[0:6]
        PSo = self.PS[6:8]
        rr = [0]
        def psr():
            b = PSr[rr[0] % 6]
            rr[0] += 1
            return b
        jobs = [("m", h) for h in heads_moba] + [("s", h) for h in heads_sb]
        def load_head(idx):
            kind, h = jobs[idx]
            k = idx % 2
            qrow = (0 if kind == "m" else 1024) + h * 64
            krow = (512 if kind == "m" else 1536) + h * 64
            vcol = (0 if kind == "m" else 512) + h * 64
            kdst = KAUG[k] if kind == "m" else KS[k]
            S.dma([], [QAUG[k]], QAUG[k][0:64, :], QT[qrow:qrow + 64, :])
            S.dma([], [kdst], kdst[0:64, :], QT[krow:krow + 64, :])
            S.dma([], [Vb[k]], Vb[k][:, :, 0:64], VTOK[:, vcol:vcol + 64].rearrange("(j p) d -> p j d", p=128))
        load_head(0)
        oi = 0
        pre_done = set()
        for idx, (kind, h) in enumerate(jobs):
            Q, V = QAUG[idx % 2], Vb[idx % 2]
            Kt = KAUG[idx % 2] if kind == "m" else KS[idx % 2]
            if idx + 1 < len(jobs):
                load_head(idx + 1)
            if kind == "m":
                def prelude(Q, Kt):
                    S.op("dve", [Kt], [kmean], lambda e: e.tensor_reduce(
                        out=kmean[:], in_=Kt[0:64, :].rearrange("p (n k) -> p n k", k=256), axis=AX.X, op=ALU.add))
                    S.op("dve", [kmean], [kmeanb], lambda e: e.tensor_scalar(
                        out=kmeanb[:], in0=kmean[:], scalar1=1.0 / 256, scalar2=None, op0=ALU.mult))
                    yield
                    GP = psr()
                    def mm(e):
                        last = None
                        for tt in range(32):
                            last = e.matmul(GP[:, tt * 16:(tt + 1) * 16], lhsT=Q[0:64, tt * 128:(tt + 1) * 128], rhs=kmeanb[:, :],
                                            start=True, stop=True)
                        return last
                    S.op("pe", [Q, kmeanb], [GP], mm)
                    S.op("dve", [GP, pastb], [gm], lambda e: e.tensor_tensor(
                        out=gm[:].rearrange("p a b -> p (a b)"), in0=GP[:, :], in1=pastb[:].rearrange("p a b -> p (a b)"), op=ALU.add))
                    yield
                    for tt in range(32):
                        S.op("dve", [gm], [m8], lambda e, tt=tt: e.max(out=m8[:, tt, :], in_=gm[:, tt, :]))
                    yield
                    for tt in range(32):
                        S.op("dve", [gm, m8, past01], [MBw], lambda e, tt=tt: e.scalar_tensor_tensor(
                            out=MBw[:, tt, 64:80], in0=gm[:, tt, :], scalar=m8[:, tt, 2:3], in1=past01[:, tt, :], op0=ALU.is_ge, op1=ALU.mult))
                    S.op("dve", [MBw, ownm1], [MBw], lambda e: e.tensor_tensor(
                        out=MBw[:, :, 64:80], in0=MBw[:, :, 64:80], in1=ownm1[:], op=ALU.add))
                    yield
                    S.op("pool", [Kt], [sqQ], lambda e: e.tensor_tensor(out=sqQ[:], in0=Kt[0:64, :], in1=Kt[0:64, :], op=ALU.mult))
                    for g in range(8):
                        KP = psr()
                        S.op("pe", [sqQ, onesb], [KP], lambda e, KP=KP, g=g: e.matmul(
                            KP[:, :], lhsT=onesb[0:64, :], rhs=sqQ[:, g * 512:(g + 1) * 512], start=True, stop=True))
                        S.op("dve", [KP], [kmx], lambda e, KP=KP, g=g: e.tensor_reduce(
                            out=kmx[:, g:g + 1], in_=KP[:, :], axis=AX.X, op=ALU.max))
                    yield
                    S.op("dve", [kmx], [kmax2], lambda e: e.tensor_reduce(out=kmax2[:], in_=kmx[:], axis=AX.X, op=ALU.max))
                    S.op("pool", [Q], [sqQ], lambda e: e.tensor_tensor(out=sqQ[:], in0=Q[0:64, :], in1=Q[0:64, :], op=ALU.mult))
                    yield
                    QQ = psr()
                    def mmq(e):
                        last = None
                        for tt in range(32):
                            last = e.matmul(QQ[:, tt:tt + 1], lhsT=sqQ[:, tt * 128:(tt + 1) * 128], rhs=onesb[0:64, 0:1], start=True, stop=True)
                        return last
                    S.op("pe", [sqQ, onesb], [QQ], mmq)
                    S.op("act", [QQ, kmax2], [MBw], lambda e: e.activation(
                        out=MBw[:, :, 96], in_=QQ[:, 0:32], func=AF.Sqrt, scale=kmax2[:, 0:1]))
                    yield
                    for g in range(8):
                        TP = psr()
                        def tr(e, TP=TP, g=g):
                            last = None
                            for k in range(4):
                                last = e.transpose(out=TP[0:97, k * 128:(k + 1) * 128], in_=MBw[:, 4 * g + k, :], identity=self.ident32[:, :])
                            return last
                        S.op("pe", [MBw, self.ident32], [TP], tr)
                        S.op("dve", [TP], [Q], lambda e, TP=TP, g=g: e.tensor_copy(out=Q[64:97, g * 512:(g + 1) * 512], in_=TP[64:97, :]))

                if idx not in pre_done:
                    for _ in prelude(Q, Kt):
                        pass
                nxt = None
                if idx + 1 < len(jobs) and jobs[idx + 1][0] == "m":
                    nxt = prelude(QAUG[(idx + 1) % 2], KAUG[(idx + 1) % 2])
                    pre_done.add(idx + 1)
                pairs = [(i, j) for i in qtiles for j in range(4 * i + 4)]
                st = {}
                def stageA(p):
                    i, j = pairs[p]
                    sp = psr()
                    S.op("pe", [Kt, Q], [sp], lambda e: e.matmul(
                        sp[:, :], lhsT=Kt[:, j * 128:(j + 1) * 128], rhs=Q[:, i * 512:(i + 1) * 512], start=True, stop=True))
                    pt = Pt[p % 3]
                    S.op("act", [sp], [pt], lambda e: e.activation(out=pt[:], in_=sp[:, :], func=AF.Exp))
                    if j >= 4 * i:
                        S.op("pool", [pt, cmask[j - 4 * i]], [pt], lambda e: e.tensor_tensor(out=pt[:], in0=pt[:], in1=cmask[j - 4 * i][:], op=ALU.mult))
                    st[p] = pt
                def stageC(p):
                    nonlocal oi
                    i, j = pairs[p]
                    pt = st.pop(p)
                    last = 4 * i + 3
                    if j == 0:
                        oi += 1
                    OP = PSo[oi % 2]
                    S.op("pe", [V, pt], [OP], lambda e: e.matmul(OP[0:65, :], lhsT=V[:, j, :], rhs=pt[:], start=(j == 0), stop=(j == last)))
                    if j == last:
                        S.op("act", [OP], [Osb], lambda e: e.activation(out=Osb[:], in_=OP[0:65, :], func=AF.Copy))
                        if getattr(self, "debug", False) and i == 0:
                            self.dbg("dbgOsb", Osb, Osb[:])
                        S.op("dve", [Osb], [rden], lambda e: e.reciprocal(out=rden[64:65, :], in_=Osb[64:65, :]))
                        BP = psr()
                        S.op("pe", [rden, sel64], [BP], lambda e: e.matmul(
                            BP[:, :], lhsT=sel64[:, :], rhs=rden[:, :], start=True, stop=True))
                        o = ost[i % 2]
                        S.op("dve", [Osb, BP], [o], lambda e: e.tensor_tensor(out=o[:], in0=Osb[0:64, :], in1=BP[0:64, :], op=ALU.mult))
                        S.dma([o], [], OT[h * 64:(h + 1) * 64, i * 512:(i + 1) * 512], o[:])
                for p in range(len(pairs) + 2):
                    if p < len(pairs):
                        stageA(p)
                    if p >= 2:
                        stageC(p - 2)
                    if nxt is not None and p % 8 == 4:
                        try:
                            next(nxt)
                        except StopIteration:
                            nxt = None
                if nxt is not None:
                    for _ in nxt:
                        pass
            else:
                pairs = [(i, j) for i in qtiles for j in range(4 * i + 3, -1, -1)]
                st = {}
                def stageA(p):
                    i, j = pairs[p]
                    z = psr()
                    S.op("pe", [Kt, Q], [z], lambda e: e.matmul(
                        z[:, :], lhsT=Kt[:, j * 128:(j + 1) * 128], rhs=Q[:, i * 512:(i + 1) * 512], start=True, stop=False))
                    E = Eb[p % 2]
                    lp = Lp[p % 3]
                    S.op("act", [z], [E], lambda e: e.activation(out=E[:], in_=z[:, :], func=AF.Exp))
                    S.op("act", [E], [lp], lambda e: e.activation(out=lp[:], in_=E[:], func=AF.Ln, bias=1.0, scale=1.0))
                    if j >= 4 * i:
                        S.op("pool", [lp, smask[j - 4 * i]], [lp], lambda e: e.tensor_tensor(out=lp[:], in0=lp[:], in1=smask[j - 4 * i][:], op=ALU.mult))
                    st[p] = (z, lp)
                def stageB(p):
                    i, j = pairs[p]
                    z, lp = st[p]
                    first = (j == 4 * i + 3)
                    S.op("pe", [NUI, lp], [z], lambda e: e.matmul(z[:, :], lhsT=NUI[:, :], rhs=lp[:], start=False, stop=True))
                    cs = psr()
                    S.op("pe", [onesb, lp], [cs], lambda e: e.matmul(cs[:, :], lhsT=onesb[:, :], rhs=lp[:], start=True, stop=True))
                    ex = exb[p % 2]
                    if first:
                        S.op("dve", [z], [ex], lambda e: e.tensor_copy(out=ex[:], in_=z[:, :]))
                        S.op("dve", [cs], [rs], lambda e: e.tensor_copy(out=rs[:], in_=cs[:, :]))
                    else:
                        S.op("dve", [z, rs], [ex], lambda e: e.tensor_tensor(out=ex[:], in0=z[:, :], in1=rs[:], op=ALU.subtract))
                        S.op("dve", [cs, rs], [rs], lambda e: e.tensor_tensor(out=rs[:], in0=cs[:, :], in1=rs[:], op=ALU.add))
                    w = Pt[p % 3]
                    S.op("act", [ex], [w], lambda e: e.activation(out=w[:], in_=ex[:], func=AF.Exp))
                    if j >= 4 * i:
                        S.op("pool", [w, smask[j - 4 * i]], [w], lambda e: e.tensor_tensor(out=w[:], in0=w[:], in1=smask[j - 4 * i][:], op=ALU.mult))
                    st[p] = w
                def stageC(p):
                    nonlocal oi
                    i, j = pairs[p]
                    w = st.pop(p)
                    first = (j == 4 * i + 3)
                    if first:
                        oi += 1
                    OP = PSo[oi % 2]
                    S.op("pe", [V, w], [OP], lambda e: e.matmul(OP[0:64, :], lhsT=V[:, j, 0:64], rhs=w[:], start=first, stop=(j == 0)))
                    if j == 0:
                        o = ost[i % 2]
                        S.op("act", [OP], [o], lambda e: e.activation(out=o[:], in_=OP[0:64, :], func=AF.Copy))
                        S.dma([o], [], OT[512 + h * 64:512 + (h + 1) * 64, i * 512:(i + 1) * 512], o[:])
                n = len(pairs)
                for p in range(n + 2):
                    if p < n:
                        stageA(p)
                    if 1 <= p <= n:
                        stageB(p - 1)
                    if p >= 2:
                        stageC(p - 2)
        P.close()

    def phase_gdn(self, li, heads=range(8), nchunks=32):
        S = self.S
        P = Phase(S)
        GT, ZT, AB, OT = self.dr["GT"], self.dr["VTOK"], self.dr["AB"], self.dr["OT"]
        NCH = 32
        C = 128
        UT = P.sbuf("UT", [128, 128], F32)
        SU = P.sbuf("SU", [128, 128], F32)
        S.seq("pool", UT, [
            lambda e: e.memset(UT[:], 1.0),
            lambda e: e.affine_select(out=UT[:], in_=UT[:], compare_op=ALU.is_ge, fill=0.0, base=0,
                                      pattern=[[1, 128]], channel_multiplier=-1)])
        S.seq("pool", SU, [
            lambda e: e.memset(SU[:], 1.0),
            lambda e: e.affine_select(out=SU[:], in_=SU[:], compare_op=ALU.is_gt, fill=0.0, base=0,
                                      pattern=[[-1, 128]], channel_multiplier=1)])
        I32m = self.ident32
        ab = P.sbuf("ab", [128, NCH, 16], F32)
        S.dma([], [ab], ab[:], AB.rearrange("(c p) n -> p c n", p=128))
        graw = P.sbuf("graw", [128, NCH, 8], F32)
        beta = P.sbuf("beta", [128, NCH, 8], F32)
        nbeta = P.sbuf("nbeta", [128, NCH, 8], F32)
        negA = P.sbuf("negA", [128, 8], F32)
        tmpa = P.sbuf("tmpa", [128, NCH, 8], F32)
        S.op("act", [self.VEC], [negA], lambda e: e.activation(out=negA[:], in_=self.vcol("alog", li * 8, 8), func=AF.Exp))
        S.op("dve", [negA], [negA], lambda e: e.tensor_scalar(out=negA[:], in0=negA[:], scalar1=-1.0, scalar2=None, op0=ALU.mult))
        for c in range(NCH):
            S.op("dve", [ab, self.VEC], [tmpa], lambda e, c=c: e.tensor_tensor(
                out=tmpa[:, c, :], in0=ab[:, c, 0:8], in1=self.vcol("dtb", li * 8, 8), op=ALU.add))
        S.op("act", [tmpa], [tmpa], lambda e: e.activation(out=tmpa[:], in_=tmpa[:], func=AF.Exp))
        S.op("act", [tmpa], [tmpa], lambda e: e.activation(out=tmpa[:], in_=tmpa[:], func=AF.Ln, bias=1.0, scale=1.0))
        for c in range(NCH):
            S.op("dve", [tmpa, negA], [graw], lambda e, c=c: e.tensor_tensor(
                out=graw[:, c, :], in0=tmpa[:, c, :], in1=negA[:], op=ALU.mult))
        S.op("act", [ab], [beta], lambda e: e.activation(out=beta[:], in_=ab[:, :, 8:16], func=AF.Sigmoid))
        S.op("dve", [beta], [nbeta], lambda e: e.tensor_scalar(out=nbeta[:], in0=beta[:], scalar1=-1.0, scalar2=None, op0=ALU.mult))
        gc = P.sbuf("gc", [128, NCH, 8], F32)
        egc = P.sbuf("egc", [128, NCH, 8], F32)
        egl = P.sbuf("egl", [128, NCH, 8], F32)
        edec = P.sbuf("edec", [128, NCH, 8], F32)
        bgc = P.sbuf("bgc", [128, NCH, 8], F32)
        pgc = self.ps()
        S.op("pe", [UT, graw], [pgc], lambda e: e.matmul(pgc[:, 0:256], lhsT=UT[:, :], rhs=graw[:].rearrange("p a b -> p (a b)"), start=True, stop=True))
        pgl = self.ps()
        S.op("pe", [self.ones32, graw], [pgl], lambda e: e.matmul(pgl[:, 0:256], lhsT=self.ones32[:, :], rhs=graw[:].rearrange("p a b -> p (a b)"), start=True, stop=True))
        fl = lambda b: b[:].rearrange("p a b -> p (a b)")
        S.op("dve", [pgc], [gc], lambda e: e.tensor_copy(out=fl(gc), in_=pgc[:, 0:256]))
        S.op("act", [pgc], [egc], lambda e: e.activation(out=fl(egc), in_=pgc[:, 0:256], func=AF.Exp))
        S.op("act", [pgl], [egl], lambda e: e.activation(out=fl(egl), in_=pgl[:, 0:256], func=AF.Exp))
        S.op("dve", [pgl, gc], [edec], lambda e: e.tensor_tensor(out=fl(edec), in0=pgl[:, 0:256], in1=fl(gc), op=ALU.subtract))
        S.op("act", [edec], [edec], lambda e: e.activation(out=fl(edec), in_=fl(edec), func=AF.Exp))
        S.op("dve", [beta, egc], [bgc], lambda e: e.tensor_tensor(out=fl(bgc), in0=fl(beta), in1=fl(egc), op=ALU.mult))
        xin = [P.sbuf("xin%d" % k, [128, T], BF16) for k in range(1)]
        acc = P.sbuf("acc", [128, T], F32)
        ybuf = P.sbuf("ybuf", [128, T], F32)
        qTb = P.sbuf("qTb", [128, T], BF16)
        kTf = P.sbuf("kTf", [128, T], F32)
        kTb = P.sbuf("kTb", [128, T], BF16)
        vTf = P.sbuf("vTf", [128, T], F32)
        rn = P.sbuf("rn", [128, 512], F32)
        zt = P.sbuf("zt", [128, NCH, 128], BF16)
        zg = P.sbuf("zg", [128, NCH, 128], BF16)
        def chunked(name):
            big = P.sbuf(name, [128, NCH, 128], BF16)
            out = []
            for k in range(NCH):
                b = Buf(big.t[:, k, :], "%s%d" % (name, k))
                S.bufs.append(b)
                out.append(b)
            return out
        U_, WT, AT, KD = chunked("U_"), chunked("WT"), chunked("AT"), chunked("KD")
        ot = P.sbuf("ot", [128, T], BF16)
        Sst = P.sbuf("Sst", [128, 128], F32)
        Sbfs = [P.sbuf("Sbf%d" % k, [128, 128], BF16) for k in range(2)]
        NR = GDN_NR
        def rot(name, dt=F32, n=NR):
            return [P.sbuf("%s%d" % (name, k), [128, 128], dt) for k in range(n)]
        ktok, vb, kbg, G1, dcs, dsc, Nm, Am, Pm, Nn, An = (rot(x) for x in ("ktok", "vb", "kbg", "G1", "dcs", "dsc", "Nm", "Am", "Pm", "Nn", "An"))
        vnew = rot("vnew", BF16, 2)
        o2b = rot("o2b", F32, 2)
        osb = rot("osb", F32, 2)
        ysb = rot("ysb", F32, 2)
        ssq = P.sbuf("ssq", [128, 2], F32)
        junk = P.sbuf("junk", [128, 128], F32)
        ew = [0]
        def evac_copy(src_ps, dst, dst_ap, src_ap, extra_reads=()):
            ew[0] += 1
            if ew[0] % 2 == 0:
                S.op("act", [src_ps] + list(extra_reads), [dst], lambda e: e.activation(out=dst_ap, in_=src_ap, func=AF.Copy))
            else:
                S.op("dve", [src_ps] + list(extra_reads), [dst], lambda e: e.tensor_copy(out=dst_ap, in_=src_ap))
        for h in heads:
            for wi, which in enumerate(("q", "k", "v")):
                xi = xin[0]
                row = wi * 1024 + h * 128
                S.dma([], [xi], xi[:], GT[row:row + 128, :])
                def wcol(j, wi=wi):
                    return self.vcol("conv", (li * 4 + j) * 24 + wi * 8 + h)
                S.op("dve", [xi, self.VEC], [acc], lambda e, xi=xi: e.tensor_scalar(out=acc[:], in0=xi[:], scalar1=wcol(3), scalar2=None, op0=ALU.mult))
                for sh in (1, 2, 3):
                    S.op("dve", [xi, self.VEC, acc], [acc], lambda e, xi=xi, sh=sh: e.scalar_tensor_tensor(
                        out=acc[:, sh:], in0=xi[:, 0:T - sh], scalar=wcol(3 - sh), in1=acc[:, sh:], op0=ALU.mult, op1=ALU.add))
                dst = vTf if which == "v" else ybuf
                S.op("act", [acc], [dst], lambda e, dst=dst: e.activation(out=dst[:], in_=acc[:], func=AF.Silu))
                if which == "v":
                    continue
                S.op("act", [ybuf], [acc], lambda e: e.activation(out=acc[:], in_=ybuf[:], func=AF.Square))
                for g in range(8):
                    sl = slice(g * 512, (g + 1) * 512)
                    pss = self.ps()
                    S.op("pe", [acc, self.ones32], [pss], lambda e, pss=pss, sl=sl: e.matmul(pss[:, :], lhsT=self.ones32[:, :], rhs=acc[:, sl], start=True, stop=True))
                    S.op("dve", [pss], [rn], lambda e, pss=pss: e.tensor_scalar(out=rn[:], in0=pss[:, :], scalar1=EPS, scalar2=None, op0=ALU.add))
                    S.op("act", [rn], [rn], lambda e: e.activation(out=rn[:], in_=rn[:], func=AF.Sqrt))
                    S.op("dve", [rn], [rn], lambda e: e.reciprocal(out=rn[:], in_=rn[:]))
                    if which == "q":
                        S.op("dve", [ybuf, rn], [qTb], lambda e, sl=sl: e.scalar_tensor_tensor(
                            out=qTb[:, sl], in0=ybuf[:, sl], scalar=float(128 ** -0.5), in1=rn[:], op0=ALU.mult, op1=ALU.mult))
                    else:
                        S.op("dve", [ybuf, rn], [kTf], lambda e, sl=sl: e.tensor_tensor(out=kTf[:, sl], in0=ybuf[:, sl], in1=rn[:], op=ALU.mult))
                        S.op("pool", [kTf], [kTb], lambda e, sl=sl: e.tensor_copy(out=kTb[:, sl], in_=kTf[:, sl]))
            S.dma([], [zt], zt[:], ZT[:, h * 128:(h + 1) * 128].rearrange("(c p) d -> p c d", p=128))
            S.op("act", [zt], [zt], lambda e: e.activation(out=zt[:], in_=zt[:], func=AF.Silu))
            for c in range(NCH):
                S.op("pool", [zt, self.VEC], [zg], lambda e, c=c: e.tensor_tensor(
                    out=zg[:, c, :], in0=zt[:, c, :], in1=self.vcol("gngb", li * 128, 128), op=ALU.mult))
            done1 = [False] * nchunks

            class Pool3:
                def __init__(self, banks):
                    self.banks = banks
                    self.i = 0

                def get(self):
                    b = self.banks[self.i % len(self.banks)]
                    self.i += 1
                    return PQ(b, 0)
            pools = [Pool3(self.PS[0:3]), Pool3(self.PS[3:6])]
            pool2 = Pool3(self.PS[6:8])
            def evq(pq, dst, dst_ap, extra_reads=()):
                evac_copy(pq, dst, dst_ap, pq.ap, extra_reads)

            def stage1(c, r, h=h):
                cs = slice(c * C, (c + 1) * C)
                col = slice(h, h + 1)
                pl = pools[r]
                pk, pv = pl.get(), pl.get()
                S.op("pe", [kTf, I32m], [pk], lambda e: e.transpose(out=pk.ap, in_=kTf[:, cs], identity=I32m[:, :]))
                S.op("pe", [vTf, I32m], [pv], lambda e: e.transpose(out=pv.ap, in_=vTf[:, cs], identity=I32m[:, :]))
                S.op("dve", [SU, graw], [G1[r]], lambda e: e.tensor_scalar(out=G1[r][:], in0=SU[:], scalar1=graw[:, c, col], scalar2=None, op0=ALU.mult))
                yield
                evq(pk, ktok[r], ktok[r][:])
                S.op("dve", [pv, beta], [vb[r]], lambda e: e.tensor_scalar(out=vb[r][:], in0=pv.ap, scalar1=beta[:, c, col], scalar2=None, op0=ALU.mult))
                pd1, pd2 = pl.get(), pl.get()
                S.op("pe", [UT, G1[r]], [pd1], lambda e: e.matmul(pd1.ap, lhsT=UT[:, :], rhs=G1[r][:], start=True, stop=True))
                S.op("pe", [UT, G1[r]], [pd2], lambda e: e.matmul(pd2.ap, lhsT=G1[r][:], rhs=UT[:, :], start=True, stop=True))
                yield
                S.op("act", [pd1], [dcs[r]], lambda e: e.activation(out=dcs[r][:], in_=pd1.ap, func=AF.Exp))
                S.op("act", [pd2], [dsc[r]], lambda e: e.activation(out=dsc[r][:], in_=pd2.ap, func=AF.Exp))
                S.op("act", [ktok[r], bgc], [kbg[r]], lambda e: e.activation(out=kbg[r][:], in_=ktok[r][:], func=AF.Copy, scale=bgc[:, c, col]))
                S.op("act", [ktok[r], edec], [KD[c]], lambda e: e.activation(out=KD[c][:], in_=ktok[r][:], func=AF.Copy, scale=edec[:, c, col]))
                pkk, pqk = pl.get(), pl.get()
                S.op("pe", [kTf], [pkk], lambda e: e.matmul(pkk.ap, lhsT=kTf[:, cs], rhs=kTf[:, cs], start=True, stop=True))
                S.op("pe", [kTb, qTb], [pqk], lambda e: e.matmul(pqk.ap, lhsT=kTb[:, cs], rhs=qTb[:, cs], start=True, stop=True))
                yield
                S.op("pool", [dcs[r], SU], [dcs[r]], lambda e: e.tensor_tensor(out=dcs[r][:], in0=dcs[r][:], in1=SU[:], op=ALU.mult))
                S.op("pool", [dsc[r], UT], [dsc[r]], lambda e: e.tensor_tensor(out=dsc[r][:], in0=dsc[r][:], in1=UT[:], op=ALU.mult))
                yield
                S.op("dve", [pkk, nbeta, dcs[r]], [Nm[r]], lambda e: e.scalar_tensor_tensor(
                    out=Nm[r][:], in0=pkk.ap, scalar=nbeta[:, c, col], in1=dcs[r][:], op0=ALU.mult, op1=ALU.mult))
                S.op("dve", [pqk, dsc[r]], [AT[c]], lambda e: e.tensor_tensor(out=AT[c][:], in0=pqk.ap, in1=dsc[r][:], op=ALU.mult))
                yield
                pa = pl.get()
                S.op("pe", [Nm[r], I32m], [pa], lambda e: e.transpose(out=pa.ap, in_=Nm[r][:], identity=I32m[:, :]))
                yield
                evq(pa, Am[r], Am[r][:])
                yield
                S.op("pool", [Am[r], I32m], [Pm[r]], lambda e: e.tensor_tensor(out=Pm[r][:], in0=Am[r][:], in1=I32m[:], op=ALU.add))
                Ncur, Acur = Nm[r], Am[r]
                for lvl in range(1, 7):
                    Nnx = Nn[r] if lvl % 2 == 1 else Nm[r]
                    Anx = An[r] if lvl % 2 == 1 else Am[r]
                    pn = pl.get()
                    S.op("pe", [Acur, Ncur], [pn], lambda e: e.matmul(pn.ap, lhsT=Acur[:], rhs=Ncur[:], start=True, stop=True))
                    if lvl < 6:
                        pa2 = pl.get()
                        S.op("pe", [Acur, Ncur], [pa2], lambda e: e.matmul(pa2.ap, lhsT=Ncur[:], rhs=Acur[:], start=True, stop=True))
                    yield
                    evq(pn, Nnx, Nnx[:])
                    if lvl < 6:
                        evq(pa2, Anx, Anx[:])
                    yield
                    pp = pl.get()
                    S.op("pe", [Nnx, Pm[r]], [pp], lambda e: e.matmul(pp.ap, lhsT=Nnx[:], rhs=Pm[r][:], start=True, stop=True))
                    yield
                    S.op("dve", [pp, Pm[r]], [Pm[r]], lambda e: e.tensor_tensor(out=Pm[r][:], in0=pp.ap, in1=Pm[r][:], op=ALU.add))
                    yield
                    Ncur, Acur = Nnx, Anx
                pu, pw = pl.get(), pl.get()
                S.op("pe", [Pm[r], vb[r]], [pu], lambda e: e.matmul(pu.ap, lhsT=Pm[r][:], rhs=vb[r][:], start=True, stop=True))
                S.op("pe", [Pm[r], kbg[r]], [pw], lambda e: e.matmul(pw.ap, lhsT=kbg[r][:], rhs=Pm[r][:], start=True, stop=True))
                yield
                evq(pu, U_[c], U_[c][:])
                evq(pw, WT[c], WT[c][:])
                done1[c] = True

            def stage2(h=h):
                S.op("pool", [], [Sbfs[0]], lambda e: e.memset(Sbfs[0][:], 0.0))
                S.op("pool", [], [Sst], lambda e: e.memset(Sst[:], 0.0))
                col = slice(h, h + 1)
                for c in range(nchunks):
                    while not done1[c]:
                        yield
                    cs = slice(c * C, (c + 1) * C)
                    r2 = c % 2
                    Sold, Snew = Sbfs[c % 2], Sbfs[(c + 1) % 2]
                    p1, p2a = pool2.get(), pool2.get()
                    S.op("pe", [WT[c], Sold], [p1], lambda e: e.matmul(p1.ap, lhsT=WT[c][:], rhs=Sold[:], start=True, stop=True))
                    S.op("pe", [qTb, Sold], [p2a], lambda e: e.matmul(p2a.ap, lhsT=qTb[:, cs], rhs=Sold[:], start=True, stop=True))
                    yield
                    S.op("dve", [U_[c], p1], [vnew[r2]], lambda e: e.tensor_tensor(out=vnew[r2][:], in0=U_[c][:], in1=p1.ap, op=ALU.subtract))
                    S.op("act", [p2a, egc], [o2b[r2]], lambda e: e.activation(out=o2b[r2][:], in_=p2a.ap, func=AF.Copy, scale=egc[:, c, col]))
                    yield
                    p3, p2b = pool2.get(), pool2.get()
                    S.op("pe", [KD[c], vnew[r2]], [p3], lambda e: e.matmul(p3.ap, lhsT=KD[c][:], rhs=vnew[r2][:], start=True, stop=True))
                    S.op("pe", [AT[c], vnew[r2]], [p2b], lambda e: e.matmul(p2b.ap, lhsT=AT[c][:], rhs=vnew[r2][:], start=True, stop=True))
                    yield
                    S.op("dve", [Sst, egl, p3], [Sst], lambda e: e.scalar_tensor_tensor(
                        out=Sst[:], in0=Sst[:], scalar=egl[:, c, col], in1=p3.ap, op0=ALU.mult, op1=ALU.add))
                    yield
                    S.op("act", [Sst], [Snew], lambda e: e.activation(out=Snew[:], in_=Sst[:], func=AF.Copy))
                    S.op("dve", [p2b, o2b[r2]], [osb[r2]], lambda e: e.tensor_tensor(out=osb[r2][:], in0=p2b.ap, in1=o2b[r2][:], op=ALU.add))
                    yield
                    S.op("act", [osb[r2]], [junk, ssq], lambda e: e.activation(out=junk[:], in_=osb[r2][:], func=AF.Square, accum_out=ssq[:, r2:r2 + 1]))
                    S.op("dve", [ssq], [ssq], lambda e: e.tensor_scalar(out=ssq[:, r2:r2 + 1], in0=ssq[:, r2:r2 + 1], scalar1=1.0 / 128, scalar2=EPS, op0=ALU.mult, op1=ALU.add))
                    S.op("act", [ssq], [ssq], lambda e: e.activation(out=ssq[:, r2:r2 + 1], in_=ssq[:, r2:r2 + 1], func=AF.Sqrt))
                    S.op("dve", [ssq], [ssq], lambda e: e.reciprocal(out=ssq[:, r2:r2 + 1], in_=ssq[:, r2:r2 + 1]))
                    S.op("dve", [osb[r2], ssq, zg], [ysb[r2]], lambda e: e.scalar_tensor_tensor(
                        out=ysb[r2][:], in0=osb[r2][:], scalar=ssq[:, r2:r2 + 1], in1=zg[:, c, :], op0=ALU.mult, op1=ALU.mult))
                    yield
                    py = pool2.get()
                    S.op("pe", [ysb[r2], I32m], [py], lambda e: e.transpose(out=py.ap, in_=ysb[r2][:], identity=I32m[:, :]))
                    S.op("act", [py], [ot], lambda e: e.activation(out=ot[:, cs], in_=py.ap, func=AF.Copy))
                    yield

            pending = list(range(nchunks))
            active = {}
            s2 = stage2()
            s2_alive = True
            while pending or active or s2_alive:
                while pending and len(active) < NR:
                    c = pending.pop(0)
                    slot = [k for k in range(NR) if k not in active][0]
                    active[slot] = stage1(c, slot)
                for slot in list(active.keys()):
                    try:
                        next(active[slot])
                    except StopIteration:
                        del active[slot]
                if s2_alive:
                    try:
                        next(s2)
                    except StopIteration:
                        s2_alive = False
            S.dma([ot], [], OT[h * 128:(h + 1) * 128, :], ot[:])
        P.close()


def prep_shared(inp):
    perm = rope_partner_perm()
    w = np.asarray(inp["attn_w_in"], np.float32)
    w4 = np.concatenate([w, w[:, :, 0:512][:, :, perm], w[:, :, 512:1024][:, :, perm]], axis=2)
    return {
        "ada_w": np.ascontiguousarray(inp["ada_w"], np.float32),
        "attn_w_in": np.ascontiguousarray(w4),
        "attn_w_out": np.ascontiguousarray(inp["attn_w_out"], np.float32),
        "gdn_w_in": np.ascontiguousarray(inp["gdn_w_in"], np.float32),
        "gdn_w_out": np.ascontiguousarray(inp["gdn_w_out"], np.float32),
        "ffn_w_in": np.ascontiguousarray(inp["ffn_w_in"], np.float32),
        "ffn_w_out": np.ascontiguousarray(inp["ffn_w_out"], np.float32),
    }


def prep_core(inp, b, shared):
    m = dict(shared)
    m["xT"] = np.ascontiguousarray(np.asarray(inp["x"][b], np.float32).T)
    m["vecs"] = build_vecs(inp, b)
    m["pos"] = np.ascontiguousarray(np.asarray(inp["positions"][b], np.int32).reshape(1, T))
    return m


def build_full():
    pg = Prog()
    pg.phase0()
    for l in range(NL):
        pg.phase1(l)
        if l % 2 == 0:
            pg.phase_attn()
        else:
            pg.phase_gdn(l // 2)
        pg.phase34(l, final=(l == NL - 1))
    return pg


def kernel(**inputs):
    inp = {k: np.asarray(v) for k, v in inputs.items()}
    pg = build_full()
    shared = prep_shared(inp)
    in_maps = []
    for b in range(8):
        m = prep_core(inp, b, shared)
        in_maps.append({k: m[k] for k in pg.used_inputs})
    res = run_bass_kernel_spmd(pg.nc, in_maps, core_ids=list(range(8)))
    out = np.empty((8, T, D), np.float32)
    for b in range(8):
        out[b] = np.asarray(res.results[b]["outT"], np.float32).T
    return out
```

```python
import numpy as np
from contextlib import ExitStack
import concourse.bass as bass
import concourse.mybir as mybir
from concourse.bass_utils import run_bass_kernel_spmd

F32 = mybir.dt.float32
BF16 = mybir.dt.bfloat16
I32 = mybir.dt.int32
AF = mybir.ActivationFunctionType
ALU = mybir.AluOpType
AX = mybir.AxisListType

D = 1024
T = 4096
NL = 4
KC = 8
FH = 2816
NJ = 22
EPS = 1e-6
NDMA = 48
BIG = 30000.0
SAME_ENG_SYNC = False
GDN_NR = 3
PSQ_SEPARATE = False
GDN_DBG = 0


class Buf:
    __slots__ = ("t", "w", "r", "name")

    def __init__(self, t, name=""):
        self.t = t
        self.w = None
        self.r = {}
        self.name = name

    def __getitem__(self, idx):
        return self.t[idx]


class PQ:
    __slots__ = ("ap", "b")

    def __init__(self, bank, q):
        self.b = bank
        self.ap = bank.t[:, q * 128:(q + 1) * 128]


class Sched:
    def __init__(self, nc):
        self.nc = nc
        self.E = {"pe": nc.tensor, "act": nc.scalar, "dve": nc.vector, "pool": nc.gpsimd, "sp": nc.sync}
        self.sem = {k: nc.alloc_semaphore("s_" + k) for k in ("pe", "act", "dve", "pool")}
        self.cnt = {k: 0 for k in self.sem}
        self.seen = {k: {} for k in self.E}
        self.dsem = [nc.alloc_semaphore("d%d" % i) for i in range(NDMA)]
        self.dcnt = [0] * NDMA
        self.drr = 0
        self.bufs = []
        self.ps_rr = 0

    def sbuf(self, name, shape, dtype):
        b = Buf(self.nc.alloc_sbuf_tensor(name, list(shape), dtype), name)
        self.bufs.append(b)
        return b

    def psum(self, name, shape, dtype=F32):
        b = Buf(self.nc.alloc_psum_tensor(name, list(shape), dtype), name)
        self.bufs.append(b)
        return b

    def dram(self, t, name=""):
        b = Buf(t, name)
        self.bufs.append(b)
        return b

    def _wait(self, eng, ev):
        if ev is None:
            return
        sem, val, key = ev
        if self.seen[eng].get(key, 0) >= val:
            return
        self.E[eng].wait_ge(sem, val)
        self.seen[eng][key] = val

    def _deps(self, eng, reads, writes):
        reads = [getattr(b, "b", b) for b in reads]
        writes = [getattr(b, "b", b) for b in writes]
        for b in reads:
            self._wait(eng, b.w)
        for b in writes:
            if b.w is not None and (SAME_ENG_SYNC or b.w[2] != eng):
                self._wait(eng, b.w)
            for ev in b.r.values():
                if SAME_ENG_SYNC or ev[2] != eng:
                    self._wait(eng, ev)

    def _done(self, ev, reads, writes):
        reads = [getattr(b, "b", b) for b in reads]
        writes = [getattr(b, "b", b) for b in writes]
        for b in reads:
            b.r[ev[2]] = ev
        for b in writes:
            b.w = ev
            b.r = {}

    def op(self, eng, reads, writes, fn):
        self._deps(eng, reads, writes)
        ins = fn(self.E[eng])
        self.cnt[eng] += 1
        ins.then_inc(self.sem[eng], 1)
        ev = (self.sem[eng], self.cnt[eng], eng)
        self._done(ev, reads, writes)
        return ev

    def seq(self, eng, buf, fns, reads=()):
        ev = None
        for k, fn in enumerate(fns):
            ev = self.op(eng, ([buf] if k > 0 else []) + list(reads), [buf], fn)
        return ev

    def dma(self, reads, writes, out, in_, eng="sp"):
        k = self.drr
        self.drr = (k + 1) % NDMA
        key = ("d", k)
        if self.dcnt[k] > 0:
            self._wait(eng, (self.dsem[k], self.dcnt[k], key))
        self._deps(eng, reads, writes)
        ins = self.E[eng].dma_start(out=out, in_=in_)
        self.dcnt[k] += 16
        ins.then_inc(self.dsem[k], 16)
        ev = (self.dsem[k], self.dcnt[k], key)
        self._done(ev, reads, writes)
        return ev

    def barrier(self):
        evs = []
        for k in self.sem:
            if self.cnt[k] > 0:
                evs.append((self.sem[k], self.cnt[k], k))
        for k in range(NDMA):
            if self.dcnt[k] > 0:
                evs.append((self.dsem[k], self.dcnt[k], ("d", k)))
        for eng in self.E:
            for ev in evs:
                self._wait(eng, ev)
        for b in self.bufs:
            b.w = None
            b.r = {}


class Phase:
    _uid = [0]

    def __init__(self, S):
        Phase._uid[0] += 1
        self.uid = Phase._uid[0]
        self.S = S
        self.nc = S.nc
        self.st = ExitStack()
        self.nbufs0 = len(S.bufs)

    def sbuf(self, name, shape, dtype):
        name = "%s_p%d" % (name, self.uid)
        t = self.st.enter_context(self.nc.sbuf_tensor(name, list(shape), dtype))
        b = Buf(t, name)
        self.S.bufs.append(b)
        return b

    def psum(self, name, shape, dtype=F32):
        name = "%s_p%d" % (name, self.uid)
        t = self.st.enter_context(self.nc.psum_tensor(name, list(shape), dtype))
        b = Buf(t, name)
        self.S.bufs.append(b)
        return b

    def close(self):
        self.S.barrier()
        self.st.close()
        del self.S.bufs[self.nbufs0:]


VOFF = {}


def _vec_layout():
    off = 0
    for name, n in [("gm", 32), ("gf", 32), ("fin", 8), ("adab", 192), ("conv", 192), ("gngb", 256),
                    ("alog", 16), ("dtb", 16), ("invf", 1), ("sinsign", 1), ("cT", 8)]:
        VOFF[name] = off
        off += n
    return off


NV = _vec_layout()


def build_vecs(inp, b):
    v = np.zeros((128, NV), np.float32)
    def chunked(a):
        a = np.asarray(a, np.float32).reshape(-1, 128)
        return a.T
    v[:, VOFF["gm"]:VOFF["gm"] + 32] = chunked(inp["norm_mix_g"])
    v[:, VOFF["gf"]:VOFF["gf"] + 32] = chunked(inp["norm_ffn_g"])
    v[:, VOFF["fin"]:VOFF["fin"] + 8] = chunked(inp["final_norm_g"])
    v[:, VOFF["adab"]:VOFF["adab"] + 192] = chunked(inp["ada_b"])
    v[:, VOFF["conv"]:VOFF["conv"] + 192] = chunked(inp["gdn_conv_w"])
    v[:, VOFF["gngb"]:VOFF["gngb"] + 256] = np.broadcast_to(np.asarray(inp["gdn_norm_g"], np.float32).reshape(1, 256), (128, 256))
    v[:, VOFF["alog"]:VOFF["alog"] + 16] = np.broadcast_to(np.asarray(inp["gdn_a_log"], np.float32).reshape(1, 16), (128, 16))
    v[:, VOFF["dtb"]:VOFF["dtb"] + 16] = np.broadcast_to(np.asarray(inp["gdn_dt_bias"], np.float32).reshape(1, 16), (128, 16))
    p = np.arange(128) % 64
    inv = (500000.0 ** (-np.arange(8, dtype=np.float32) * 2.0 / 16)).astype(np.float32)
    v[:, VOFF["invf"]] = np.where(p < 16, inv[p % 8], 0.0)
    v[:, VOFF["sinsign"]] = np.where(p < 8, -1.0, np.where(p < 16, 1.0, 0.0))
    v[:, VOFF["cT"]:VOFF["cT"] + 8] = chunked(inp["c"][b])
    return v


def rope_partner_perm():
    idx = np.arange(512)
    d = idx % 64
    return np.where(d < 8, idx + 8, np.where(d < 16, idx - 8, idx))


class LazyDram(dict):
    def __init__(self, prog):
        super().__init__()
        self.prog = prog

    def __missing__(self, name):
        pg = self.prog
        if name in pg.in_shapes:
            shape, dt = pg.in_shapes[name]
            kind = "ExternalInput"
            pg.used_inputs.append(name)
        else:
            shape, dt = pg.scr_shapes[name]
            kind = "Internal"
            if name in pg.ext_out or name == "outT":
                kind = "ExternalOutput"
            if name in pg.ext_in:
                kind = "ExternalInput"
                pg.used_inputs.append(name)
        ap = pg.nc.dram_tensor(name, list(shape), dt, kind=kind).ap()
        self[name] = ap
        return ap


class Prog:
    def __init__(self, ext_out=(), ext_in=()):
        self.nc = bass.Bass("TRN2", target_bir_lowering=False)
        nc = self.nc
        self.ext_out = set(ext_out)
        self.ext_in = set(ext_in)
        self.S = Sched(nc)
        S = self.S
        self.in_shapes = {
            "xT": ([D, T], F32), "vecs": ([128, NV], F32), "pos": ([1, T], I32),
            "ada_w": ([NL, D, 6 * D], F32), "attn_w_in": ([2, D, 4096], F32), "attn_w_out": ([2, D, D], F32),
            "gdn_w_in": ([2, D, 4112], F32), "gdn_w_out": ([2, D, D], F32),
            "ffn_w_in": ([NL, D, 2 * FH], F32), "ffn_w_out": ([NL, FH, D], F32),
        }
        self.used_inputs = []
        self.defer_mod = False
        self.dr = LazyDram(self)
        self.dbg_outs = []
        self.scr_shapes = {
            "XT": ([D, T], F32), "COS": ([128, T], F32), "SIN": ([128, T], F32),
            "QT": ([2048, T], BF16),
            "VTOK": ([T, 1024], BF16),
            "OT": ([D, T], BF16),
            "GT": ([3072, T], BF16),
            "AB": ([T, 16], F32),
            "outT": ([D, T], F32),
        }
        self.VEC = S.sbuf("VEC", [128, NV], F32)
        self.MOD = S.sbuf("MOD", [128, NL * 48], F32)
        self.AM = S.sbuf("AMc", [128, NL * 8], F32)
        self.AFc = S.sbuf("AFc", [128, NL * 8], F32)
        self.ones32 = S.sbuf("ones32", [128, 128], F32)
        self.ident32 = S.sbuf("ident32", [128, 128], F32)
        self.onesbf = S.sbuf("onesbf", [128, 128], BF16)
        self.condT = S.sbuf("condT", [128, 8], F32)
        self.PS = [S.psum("ps%d" % i, [128, 512], F32) for i in range(8)]
        self.ps_i = 0
        S.op("pool", [], [self.ones32], lambda e: e.memset(self.ones32[:], 1.0))
        S.op("pool", [], [self.onesbf], lambda e: e.memset(self.onesbf[:], 1.0))
        S.seq("pool", self.ident32, [
            lambda e: e.memset(self.ident32[:], 0.0),
            lambda e: e.affine_select(out=self.ident32[:], in_=self.ident32[:], compare_op=ALU.not_equal,
                                      fill=1.0, base=0, pattern=[[-1, 128]], channel_multiplier=1)])

    def dbg(self, name, buf, ap):
        shape = [int(x) for x in ap.shape]
        d = self.nc.dram_tensor(name, shape, buf.t.dtype, kind="ExternalOutput").ap()
        self.S.dma([buf], [], d, ap)
        self.dbg_outs.append(name)

    def ps(self):
        b = self.PS[self.ps_i]
        self.ps_i = (self.ps_i + 1) % 8
        return b

    def psqs(self, n):
        if PSQ_SEPARATE:
            return [PQ(self.ps(), 0) for q in range(n)]
        b = self.ps()
        return [PQ(b, q) for q in range(n)]

    def vcol(self, name, i=0, n=1):
        o = VOFF[name] + i
        return self.VEC[:, o:o + n]

    def phase0(self):
        S, nc = self.S, self.nc
        P = Phase(S)
        VEC, MOD = self.VEC, self.MOD
        S.dma([], [VEC], VEC[:], self.dr["vecs"])
        condT = self.condT
        S.op("act", [VEC], [condT], lambda e: e.activation(out=condT[:], in_=self.vcol("cT", 0, 8), func=AF.Silu))
        CB = 1536
        wst = [P.sbuf("adaw%d" % i, [128, KC, CB], F32) for i in range(2)]
        k = 0
        for l in range(1 if self.defer_mod else NL):
            pm = self.ps()
            for cb in range(4):
                w = wst[k % 2]
                k += 1
                src = self.dr["ada_w"][l, :, cb * CB:(cb + 1) * CB].rearrange("(c p) n -> p c n", p=128)
                S.dma([], [w], w[:], src)
                def mm(e, w=w, cb=cb, pm=pm):
                    last = None
                    for j in range(12):
                        col = cb * 12 + j
                        for kc in range(KC):
                            last = e.matmul(pm[:, col:col + 1], lhsT=w[:, kc, j * 128:(j + 1) * 128],
                                            rhs=condT[:, kc:kc + 1], start=(kc == 0), stop=(kc == KC - 1))
                    return last
                S.op("pe", [w, condT], [pm], mm)
            S.op("dve", [pm, VEC], [MOD], lambda e, l=l, pm=pm: e.tensor_tensor(
                out=MOD[:, l * 48:(l + 1) * 48], in0=pm[:, 0:48], in1=self.vcol("adab", l * 48, 48), op=ALU.add))
            S.op("dve", [MOD, VEC], [self.AM], lambda e, l=l: e.scalar_tensor_tensor(
                out=self.AM[:, l * 8:(l + 1) * 8], in0=MOD[:, l * 48 + 8:l * 48 + 16], scalar=1.0,
                in1=self.vcol("gm", l * 8, 8), op0=ALU.add, op1=ALU.mult))
            S.op("dve", [MOD, VEC], [self.AFc], lambda e, l=l: e.scalar_tensor_tensor(
                out=self.AFc[:, l * 8:(l + 1) * 8], in0=MOD[:, l * 48 + 32:l * 48 + 40], scalar=1.0,
                in1=self.vcol("gf", l * 8, 8), op0=ALU.add, op1=ALU.mult))
        posi = P.sbuf("posi", [1, T], I32)
        posf = P.sbuf("posf", [1, T], F32)
        S.dma([], [posi], posi[:], self.dr["pos"])
        S.op("dve", [posi], [posf], lambda e: e.tensor_copy(out=posf[:], in_=posi[:]))
        angs = [P.sbuf("ang%d" % k, [128, 512], F32) for k in range(2)]
        rbuf = {}
        for wh in ("SIN", "COS"):
            for k in range(2):
                rbuf[(wh, k)] = (P.sbuf("r%s%d" % (wh, k), [128, 512], F32), P.sbuf("xs%s%d" % (wh, k), [128, 512], F32),
                                 P.sbuf("ki%s%d" % (wh, k), [128, 512], I32))
        for tt in range(8):
            sl = slice(tt * 512, (tt + 1) * 512)
            pb = self.ps()
            S.op("pe", [posf, self.ones32], [pb], lambda e, pb=pb, sl=sl: e.matmul(
                pb[:, :], lhsT=self.ones32[0:1, :], rhs=posf[0:1, sl], start=True, stop=True))
            ang = angs[tt % 2]
            S.op("dve", [pb, VEC], [ang], lambda e, pb=pb, ang=ang: e.tensor_scalar(
                out=ang[:], in0=pb[:, :], scalar1=self.vcol("invf"), scalar2=None, op0=ALU.mult))
            for which, shift in (("SIN", 0.0), ("COS", 0.5 * np.pi)):
                r, xs, ki = rbuf[(which, tt % 2)]
                C1 = 6.28125
                C2 = float(2 * np.pi - 6.28125)
                S.op("dve", [ang], [xs], lambda e, xs=xs, ang=ang, shift=shift: e.tensor_scalar(
                    out=xs[:], in0=ang[:], scalar1=float(shift), scalar2=None, op0=ALU.add))
                S.op("dve", [xs], [r], lambda e, r=r, xs=xs: e.tensor_scalar(
                    out=r[:], in0=xs[:], scalar1=float(1.0 / (2 * np.pi)), scalar2=None, op0=ALU.mult))
                S.op("dve", [r], [ki], lambda e, r=r, ki=ki: e.tensor_copy(out=ki[:], in_=r[:]))
                S.op("dve", [ki], [r], lambda e, r=r, ki=ki: e.tensor_copy(out=r[:], in_=ki[:]))
                S.op("dve", [r, xs], [xs], lambda e, r=r, xs=xs: e.scalar_tensor_tensor(
                    out=xs[:], in0=r[:], scalar=-C1, in1=xs[:], op0=ALU.mult, op1=ALU.add))
                S.op("dve", [r, xs], [xs], lambda e, r=r, xs=xs: e.scalar_tensor_tensor(
                    out=xs[:], in0=r[:], scalar=-C2, in1=xs[:], op0=ALU.mult, op1=ALU.add))
                S.op("dve", [xs], [xs], lambda e, xs=xs: e.tensor_scalar(
                    out=xs[:], in0=xs[:], scalar1=-3.14159, scalar2=3.14159, op0=ALU.max, op1=ALU.min))
                S.op("act", [xs], [r], lambda e, r=r, xs=xs: e.activation(out=r[:], in_=xs[:], func=AF.Sin))
                if which == "SIN":
                    S.op("dve", [r, VEC], [r], lambda e, r=r: e.tensor_scalar(
                        out=r[:], in0=r[:], scalar1=self.vcol("sinsign"), scalar2=None, op0=ALU.mult))
                S.dma([r], [], self.dr[which][:, sl], r[:])
        P.close()

    def load_weight_bf16(self, P, Wb, src3, ncols, stages, col0=0):
        S = self.S
        nk = src3.shape[1]
        CB = stages[0].t.shape[2]
        c = 0
        i = 0
        while c < ncols:
            n = min(CB, ncols - c)
            st = stages[i % len(stages)]
            i += 1
            S.dma([], [st], st[:, 0:nk, 0:n], src3[:, :, c:c + n])
            S.op("pool", [st], [Wb], lambda e, st=st, c=c, n=n: e.tensor_copy(
                out=Wb[:, 0:nk, col0 + c:col0 + c + n], in_=st[:, 0:nk, 0:n]))
            c += n

    def norm_tile(self, P, X, A, B, h, sq, tmps, rstd, n=512):
        S = self.S
        S.op("act", [X], [sq], lambda e: e.activation(out=sq[:, :, 0:n], in_=X[:, :, 0:n], func=AF.Square))
        pss = self.ps()
        def mm(e):
            last = None
            for kc in range(KC):
                last = e.matmul(pss[:, 0:n], lhsT=(self.ones32 if sq.t.dtype == F32 else self.onesbf)[:, :], rhs=sq[:, kc, 0:n],
                                start=(kc == 0), stop=(kc == KC - 1))
            return last
        S.op("pe", [sq, self.ones32, self.onesbf], [pss], mm)
        S.op("dve", [pss], [rstd], lambda e: e.tensor_scalar(
            out=rstd[:, 0:n], in0=pss[:, 0:n], scalar1=1.0 / D, scalar2=EPS, op0=ALU.mult, op1=ALU.add))
        S.op("act", [rstd], [rstd], lambda e: e.activation(out=rstd[:, 0:n], in_=rstd[:, 0:n], func=AF.Sqrt))
        S.op("dve", [rstd], [rstd], lambda e: e.reciprocal(out=rstd[:, 0:n], in_=rstd[:, 0:n]))
        for kc in range(KC):
            tmp = tmps[kc % len(tmps)]
            S.op("dve", [X, rstd, self.AM, self.AFc], [tmp], lambda e, kc=kc, tmp=tmp: e.scalar_tensor_tensor(
                out=tmp[:, 0:n], in0=X[:, kc, 0:n], scalar=A[:, kc:kc + 1], in1=rstd[:, 0:n], op0=ALU.mult, op1=ALU.mult))
            S.op("act", [tmp, self.MOD], [h], lambda e, kc=kc, tmp=tmp: e.activation(
                out=h[:, kc, 0:n], in_=tmp[:, 0:n], func=AF.Identity, bias=B[:, kc:kc + 1], scale=1.0))

    def phase1(self, l):
        S = self.S
        P = Phase(S)
        attn = (l % 2 == 0)
        i = l // 2
        NC_ = 4096 if attn else 4112
        Wb = P.sbuf("W1", [128, KC, NC_], BF16)
        stages = [P.sbuf("wst%d" % k, [128, KC, 512], F32) for k in range(2)]
        wsrc = self.dr["attn_w_in" if attn else "gdn_w_in"][i].rearrange("(c p) n -> p c n", p=128)
        self.load_weight_bf16(P, Wb, wsrc, NC_, stages)
        Xb = [P.sbuf("X%d" % k, [128, KC, 512], F32) for k in range(2)]
        sq = P.sbuf("sq", [128, KC, 512], F32)
        tmps = [P.sbuf("tmp%d" % k, [128, 512], F32) for k in range(2)]
        rstd = P.sbuf("rstd", [128, 512], F32)
        h = P.sbuf("h", [128, KC, 512], BF16)
        stg = [P.sbuf("stg%d" % k, [128, 512], BF16) for k in range(4)]
        stg32 = [P.sbuf("stgf%d" % k, [128, 16], F32) for k in range(2)]
        rt = [P.sbuf("rt%d" % k, [128, 512], F32) for k in range(4)]
        cs = [[P.sbuf("cs%d_%d" % (a, k), [128, 512], F32) for k in range(2)] for a in range(2)]
        A = self.AM[:, l * 8:(l + 1) * 8]
        B = self.MOD[:, l * 48:l * 48 + 8]
        XT3 = self.dr["xT" if l == 0 else "XT"].rearrange("(c p) t -> p c t", p=128)
        NT = T // 512
        S.dma([], [Xb[0]], Xb[0][:], XT3[:, :, 0:512])
        sk = 0
        ev = 0
        for tt in range(NT):
            sl = slice(tt * 512, (tt + 1) * 512)
            X = Xb[tt % 2]
            if tt + 1 < NT:
                S.dma([], [Xb[(tt + 1) % 2]], Xb[(tt + 1) % 2][:], XT3[:, :, (tt + 1) * 512:(tt + 2) * 512])
            if attn:
                cosb, sinb = cs[0][tt % 2], cs[1][tt % 2]
                S.dma([], [cosb], cosb[:], self.dr["COS"][:, sl])
                S.dma([], [sinb], sinb[:], self.dr["SIN"][:, sl])
            self.norm_tile(P, X, A, B, h, sq, tmps, rstd)

            def proj_fm(col0):
                pp = self.ps()
                def mm(e, pp=pp, col0=col0):
                    last = None
                    for kc in range(KC):
                        last = e.matmul(pp[:, :], lhsT=Wb[:, kc, col0:col0 + 128], rhs=h[:, kc, :],
                                        start=(kc == 0), stop=(kc == KC - 1))
                    return last
                S.op("pe", [Wb, h], [pp], mm)
                return pp

            def evac(pp, dst_ap, scale, width=512):
                nonlocal sk, ev
                st = stg[sk % 4]
                sk += 1
                if ev % 2 == 0:
                    S.op("act", [pp], [st], lambda e: e.activation(out=st[:, 0:width], in_=pp[:, 0:width], func=AF.Copy, scale=float(scale)))
                else:
                    S.op("dve", [pp], [st], lambda e: e.tensor_scalar(out=st[:, 0:width], in0=pp[:, 0:width], scalar1=float(scale), scalar2=None, op0=ALU.mult))
                ev += 1
                S.dma([st], [], dst_ap, st[:, 0:width])

            def proj_tm(col0, ncols, dst, dcol0, f32out=False):
                nonlocal sk
                for s in range(4):
                    pp = self.ps()
                    def mm(e, pp=pp, s=s):
                        last = None
                        for kc in range(KC):
                            last = e.matmul(pp[:, 0:ncols], lhsT=h[:, kc, s * 128:(s + 1) * 128], rhs=Wb[:, kc, col0:col0 + ncols],
                                            start=(kc == 0), stop=(kc == KC - 1))
                        return last
                    S.op("pe", [Wb, h], [pp], mm)
                    r0 = tt * 512 + s * 128
                    if f32out:
                        st = stg32[s % 2]
                        S.op("dve", [pp], [st], lambda e, st=st, pp=pp: e.tensor_copy(out=st[:, 0:ncols], in_=pp[:, 0:ncols]))
                        S.dma([st], [], dst[r0:r0 + 128, dcol0:dcol0 + ncols], st[:, 0:ncols])
                    else:
                        evac(pp, dst[r0:r0 + 128, dcol0:dcol0 + ncols], 1.0, ncols)

            if attn:
                QT = self.dr["QT"]
                for grp, (c0, pc0, d0, scale) in enumerate([(0, 3072, 0, 0.125), (512, 3584, 512, 1.0)]):
                    for n in range(4):
                        pq = proj_fm(c0 + n * 128)
                        pp = proj_fm(pc0 + n * 128)
                        t1, t2 = rt[(2 * n) % 4], rt[(2 * n + 1) % 4]
                        st = stg[sk % 4]
                        sk += 1
                        S.op("dve", [pq, cosb], [t1], lambda e, pq=pq, t1=t1, scale=scale: e.scalar_tensor_tensor(
                            out=t1[:], in0=pq[:, :], scalar=float(scale), in1=cosb[:], op0=ALU.mult, op1=ALU.mult))
                        S.op("dve", [pp, sinb], [t2], lambda e, pp=pp, t2=t2, scale=scale: e.scalar_tensor_tensor(
                            out=t2[:], in0=pp[:, :], scalar=float(scale), in1=sinb[:], op0=ALU.mult, op1=ALU.mult))
                        S.op("pool", [t1, t2], [st], lambda e, st=st, t1=t1, t2=t2: e.tensor_tensor(
                            out=st[:], in0=t1[:], in1=t2[:], op=ALU.add))
                        S.dma([st], [], QT[d0 + n * 128:d0 + (n + 1) * 128, sl], st[:])
                for n in range(4):
                    evac(proj_fm(1536 + n * 128), QT[1024 + n * 128:1024 + (n + 1) * 128, sl], 0.125)
                for n in range(4):
                    evac(proj_fm(2048 + n * 128), QT[1536 + n * 128:1536 + (n + 1) * 128, sl], 1.0)
                proj_tm(1024, 512, self.dr["VTOK"], 0)
                proj_tm(2560, 512, self.dr["VTOK"], 512)
            else:
                GT = self.dr["GT"]
                for n in range(24):
                    evac(proj_fm(n * 128), GT[n * 128:(n + 1) * 128, sl], 1.0)
                proj_tm(3072, 512, self.dr["VTOK"], 0)
                proj_tm(3584, 512, self.dr["VTOK"], 512)
                proj_tm(4096, 16, self.dr["AB"], 0, f32out=True)
        P.close()

    def phase34(self, l, final=False, tiles=None):
        S = self.S
        P = Phase(S)
        i = l // 2
        TT = 256
        Wo = P.sbuf("Wo", [128, KC, D], BF16)
        Wi = P.sbuf("Wi", [128, KC, 2 * FH], BF16)
        W2 = P.sbuf("W2", [128, NJ, D], BF16)
        PW = Phase(S)
        stA = [PW.sbuf("stA%d" % k, [128, KC, 512], F32) for k in range(2)]
        stB = [PW.sbuf("stB%d" % k, [128, NJ, 128], F32) for k in range(2)]
        wo_src = self.dr["attn_w_out" if l % 2 == 0 else "gdn_w_out"][i].rearrange("(c p) n -> p c n", p=128)
        self.load_weight_bf16(PW, Wo, wo_src, D, stA)
        self.load_weight_bf16(PW, Wi, self.dr["ffn_w_in"][l].rearrange("(c p) n -> p c n", p=128), 2 * FH, stA)
        self.load_weight_bf16(PW, W2, self.dr["ffn_w_out"][l].rearrange("(c p) n -> p c n", p=128), D, stB)
        PW.close()
        NXB = 1 if final else 2
        Xb = [P.sbuf("X%d" % k, [128, KC, TT], F32) for k in range(NXB)]
        ob = [P.sbuf("oT%d" % k, [128, KC, TT], BF16) for k in range(2)]
        sq = P.sbuf("sq", [128, KC, TT], BF16)
        tmps = [P.sbuf("tmp%d" % k, [128, TT], F32) for k in range(2)]
        sgs = [P.sbuf("sg%d" % k, [128, TT], F32) for k in range(2)]
        rstd = P.sbuf("rstd", [128, TT], F32)
        h = P.sbuf("h", [128, KC, TT], BF16)
        hid = P.sbuf("hid", [128, NJ, TT], BF16)
        GM = self.MOD[:, l * 48 + 16:l * 48 + 24]
        GF = self.MOD[:, l * 48 + 40:l * 48 + 48]
        A = self.AFc[:, l * 8:(l + 1) * 8]
        B = self.MOD[:, l * 48 + 24:l * 48 + 32]
        XS3 = self.dr["xT" if l == 0 else "XT"].rearrange("(c p) t -> p c t", p=128)
        XD3 = self.dr["XT"].rearrange("(c p) t -> p c t", p=128)
        OT3 = self.dr["OT"].rearrange("(c p) t -> p c t", p=128)
        if final:
            OUT3 = self.dr["outT"].rearrange("(c p) t -> p c t", p=128)
            yout = P.sbuf("yout", [128, KC, TT], F32)
        NT = T // TT
        tl = list(range(NT)) if tiles is None else list(tiles)
        S.dma([], [ob[0]], ob[0][:], OT3[:, :, tl[0] * TT:(tl[0] + 1) * TT])
        for ti, tt in enumerate(tl):
            sl = slice(tt * TT, (tt + 1) * TT)
            X = Xb[ti % NXB]
            oT = ob[ti % 2]
            if NXB == 1 or ti == 0:
                S.dma([], [X], X[:], XS3[:, :, sl])
            if ti + 1 < len(tl):
                nt = tl[ti + 1]
                S.dma([], [ob[(ti + 1) % 2]], ob[(ti + 1) % 2][:], OT3[:, :, nt * TT:(nt + 1) * TT])
                if NXB == 2:
                    S.dma([], [Xb[(ti + 1) % 2]], Xb[(ti + 1) % 2][:], XS3[:, :, nt * TT:(nt + 1) * TT])
            for n in range(KC):
                pp = self.ps()
                def mm(e, pp=pp, n=n):
                    last = None
                    for kc in range(KC):
                        last = e.matmul(pp[:, 0:TT], lhsT=Wo[:, kc, n * 128:(n + 1) * 128], rhs=oT[:, kc, :],
                                        start=(kc == 0), stop=(kc == KC - 1))
                    return last
                S.op("pe", [Wo, oT], [pp], mm)
                S.op("dve", [pp, X, self.MOD], [X], lambda e, pp=pp, n=n: e.scalar_tensor_tensor(
                    out=X[:, n, :], in0=pp[:, 0:TT], scalar=GM[:, n:n + 1], in1=X[:, n, :], op0=ALU.mult, op1=ALU.add))
            self.norm_tile(P, X, A, B, h, sq, tmps, rstd, n=TT)
            for j in range(NJ):
                pg = self.ps()
                pu = self.ps()
                def mmg(e, pg=pg, j=j):
                    last = None
                    for kc in range(KC):
                        last = e.matmul(pg[:, 0:TT], lhsT=Wi[:, kc, j * 128:(j + 1) * 128], rhs=h[:, kc, :],
                                        start=(kc == 0), stop=(kc == KC - 1))
                    return last
                def mmu(e, pu=pu, j=j):
                    last = None
                    for kc in range(KC):
                        last = e.matmul(pu[:, 0:TT], lhsT=Wi[:, kc, FH + j * 128:FH + (j + 1) * 128], rhs=h[:, kc, :],
                                        start=(kc == 0), stop=(kc == KC - 1))
                    return last
                S.op("pe", [Wi, h], [pg], mmg)
                S.op("pe", [Wi, h], [pu], mmu)
                sg = sgs[j % 2]
                S.op("act", [pg], [sg], lambda e, pg=pg, sg=sg: e.activation(out=sg[:], in_=pg[:, 0:TT], func=AF.Silu))
                S.op("dve", [sg, pu], [hid], lambda e, pu=pu, sg=sg, j=j: e.tensor_tensor(
                    out=hid[:, j, :], in0=pu[:, 0:TT], in1=sg[:], op=ALU.mult))
            for n in range(KC):
                pp = self.ps()
                def mm2(e, pp=pp, n=n):
                    last = None
                    for j in range(NJ):
                        last = e.matmul(pp[:, 0:TT], lhsT=W2[:, j, n * 128:(n + 1) * 128], rhs=hid[:, j, :],
                                        start=(j == 0), stop=(j == NJ - 1))
                    return last
                S.op("pe", [W2, hid], [pp], mm2)
                S.op("dve", [pp, X, self.MOD], [X], lambda e, pp=pp, n=n: e.scalar_tensor_tensor(
                    out=X[:, n, :], in0=pp[:, 0:TT], scalar=GF[:, n:n + 1], in1=X[:, n, :], op0=ALU.mult, op1=ALU.add))
            if final:
                self.final_norm_tile(X, yout, sq, rstd, TT)
                S.dma([yout], [], OUT3[:, :, sl], yout[:])
            else:
                S.dma([X], [], XD3[:, :, sl], X[:])
        P.close()

    def final_norm_tile(self, X, yout, sq, rstd, n):
        S = self.S
        S.op("act", [X], [sq], lambda e: e.activation(out=sq[:, :, 0:n], in_=X[:, :, 0:n], func=AF.Square))
        pss = self.ps()
        def mm(e):
            last = None
            for kc in range(KC):
                last = e.matmul(pss[:, 0:n], lhsT=(self.ones32 if sq.t.dtype == F32 else self.onesbf)[:, :], rhs=sq[:, kc, 0:n],
                                start=(kc == 0), stop=(kc == KC - 1))
            return last
        S.op("pe", [sq, self.ones32, self.onesbf], [pss], mm)
        S.op("dve", [pss], [rstd], lambda e: e.tensor_scalar(
            out=rstd[:, 0:n], in0=pss[:, 0:n], scalar1=1.0 / D, scalar2=EPS, op0=ALU.mult, op1=ALU.add))
        S.op("act", [rstd], [rstd], lambda e: e.activation(out=rstd[:, 0:n], in_=rstd[:, 0:n], func=AF.Sqrt))
        S.op("dve", [rstd], [rstd], lambda e: e.reciprocal(out=rstd[:, 0:n], in_=rstd[:, 0:n]))
        for kc in range(KC):
            S.op("dve", [X, rstd, self.VEC], [yout], lambda e, kc=kc: e.scalar_tensor_tensor(
                out=yout[:, kc, 0:n], in0=X[:, kc, 0:n], scalar=self.vcol("fin", kc), in1=rstd[:, 0:n], op0=ALU.mult, op1=ALU.mult))

    def mod_gen(self, P, layers, psfn):
        S = self.S
        stg = [P.sbuf("mst%d" % k, [128, KC, 512], F32) for k in range(2)]
        condT, MOD = self.condT, self.MOD
        k = 0
        for l in layers:
            for cb in range(12):
                w = stg[k % 2]
                k += 1
                S.dma([], [w], w[:], self.dr["ada_w"][l, :, cb * 512:(cb + 1) * 512].rearrange("(c p) n -> p c n", p=128))
                yield
                pm = psfn()
                def mm(e, w=w, pm=pm):
                    last = None
                    for j in range(4):
                        for kc in range(KC):
                            last = e.matmul(pm[:, j:j + 1], lhsT=w[:, kc, j * 128:(j + 1) * 128], rhs=condT[:, kc:kc + 1],
                                            start=(kc == 0), stop=(kc == KC - 1))
                    return last
                S.op("pe", [w, condT], [pm], mm)
                c0 = l * 48 + cb * 4
                S.op("dve", [pm, self.VEC], [MOD], lambda e, pm=pm, c0=c0: e.tensor_tensor(
                    out=MOD[:, c0:c0 + 4], in0=pm[:, 0:4], in1=self.VEC[:, VOFF["adab"] + c0:VOFF["adab"] + c0 + 4], op=ALU.add))
                yield
            S.op("dve", [MOD, self.VEC], [self.AM], lambda e, l=l: e.scalar_tensor_tensor(
                out=self.AM[:, l * 8:(l + 1) * 8], in0=MOD[:, l * 48 + 8:l * 48 + 16], scalar=1.0,
                in1=self.vcol("gm", l * 8, 8), op0=ALU.add, op1=ALU.mult))
            S.op("dve", [MOD, self.VEC], [self.AFc], lambda e, l=l: e.scalar_tensor_tensor(
                out=self.AFc[:, l * 8:(l + 1) * 8], in0=MOD[:, l * 48 + 32:l * 48 + 40], scalar=1.0,
                in1=self.vcol("gf", l * 8, 8), op0=ALU.add, op1=ALU.mult))

    def phase_attn(self, heads_moba=range(8), heads_sb=range(8), qtiles=range(8), mod_layers=()):
        S = self.S
        P = Phase(S)
        QT, VTOK, OT = self.dr["QT"], self.dr["VTOK"], self.dr["OT"]
        cmask = [P.sbuf("cmask%d" % d, [128, 512], BF16) for d in range(4)]
        smask = [P.sbuf("smask%d" % d, [128, 512], BF16) for d in range(4)]
        for d in range(4):
            for mk, op in ((cmask[d], ALU.is_ge), (smask[d], ALU.is_gt)):
                S.seq("pool", mk, [
                    lambda e, mk=mk: e.memset(mk[:], 1.0),
                    lambda e, mk=mk, op=op, d=d: e.affine_select(out=mk[:], in_=mk[:], compare_op=op, fill=0.0, base=-128 * d,
                                                                 pattern=[[1, 512]], channel_multiplier=-1)])
        NUI = P.sbuf("NUI", [128, 128], BF16)
        S.seq("pool", NUI, [
            lambda e: e.memset(NUI[:], -1.0),
            lambda e: e.affine_select(out=NUI[:], in_=NUI[:], compare_op=ALU.is_ge, fill=0.0, base=0,
                                      pattern=[[-1, 128]], channel_multiplier=1)])
        onesb = P.sbuf("onesb", [128, 128], BF16)
        S.op("pool", [], [onesb], lambda e: e.memset(onesb[:], 1.0))
        past01 = P.sbuf("past01", [128, 32, 16], F32)
        pastb = P.sbuf("pastb", [128, 32, 16], F32)
        ownm1 = P.sbuf("ownm1", [128, 32, 16], F32)
        S.seq("pool", past01, [
            lambda e: e.memset(past01[:], 1.0),
            lambda e: e.affine_select(out=past01[:], in_=past01[:], compare_op=ALU.is_ge, fill=0.0, base=-2,
                                      pattern=[[1, 32], [-2, 16]], channel_multiplier=0)])
        S.seq("pool", pastb, [
            lambda e: e.memset(pastb[:], 0.0),
            lambda e: e.affine_select(out=pastb[:], in_=pastb[:], compare_op=ALU.is_ge, fill=-1e30, base=-2,
                                      pattern=[[1, 32], [-2, 16]], channel_multiplier=0)])
        S.seq("pool", ownm1, [
            lambda e: e.memset(ownm1[:], 0.0),
            lambda e: e.affine_select(out=ownm1[:], in_=ownm1[:], compare_op=ALU.is_ge, fill=-1.0, base=0,
                                      pattern=[[1, 32], [-2, 16]], channel_multiplier=0),
            lambda e: e.affine_select(out=ownm1[:], in_=ownm1[:], compare_op=ALU.is_ge, fill=-1.0, base=1,
                                      pattern=[[-1, 32], [2, 16]], channel_multiplier=0)])
        QAUG = [P.sbuf("QAUG%d" % k, [128, T], BF16) for k in range(2)]
        KAUG = [P.sbuf("KAUG%d" % k, [128, T], BF16) for k in range(2)]
        KS = [P.sbuf("KS%d" % k, [128, T], BF16) for k in range(2)]
        KAc = P.sbuf("KAc", [16, T], BF16)
        S.seq("pool", KAc, [
            lambda e: e.memset(KAc[:], BIG),
            lambda e: e.affine_select(out=KAc[:], in_=KAc[:], compare_op=ALU.is_ge, fill=0.0, base=0,
                                      pattern=[[1, T]], channel_multiplier=-256),
            lambda e: e.affine_select(out=KAc[:], in_=KAc[:], compare_op=ALU.is_ge, fill=0.0, base=255,
                                      pattern=[[-1, T]], channel_multiplier=256)])
        for k in range(2):
            S.op("pool", [], [QAUG[k]], lambda e, k=k: e.memset(QAUG[k][:], 0.0))
            S.op("pool", [], [KS[k]], lambda e, k=k: e.memset(KS[k][:], 0.0))
            S.seq("pool", KAUG[k], [
                lambda e, k=k: e.memset(KAUG[k][:], 0.0),
                lambda e, k=k: e.memset(KAUG[k][96:97, :], -1.0)])
            S.dma([KAc], [KAUG[k]], KAUG[k][64:80, :], KAc[:])
        Vb = [P.sbuf("V%d" % k, [128, 32, 65], BF16) for k in range(2)]
        for k in range(2):
            S.op("pool", [], [Vb[k]], lambda e, k=k: e.memset(Vb[k][:], 1.0))
        sqQ = P.sbuf("sqQ", [64, T], BF16)
        kmean = P.sbuf("kmean", [64, 16], F32)
        kmeanb = P.sbuf("kmeanb", [64, 16], BF16)
        gm = P.sbuf("gm", [128, 32, 16], F32)
        m8 = P.sbuf("m8", [128, 32, 8], F32)
        MBw = P.sbuf("MBw", [128, 32, 97], F32)
        S.op("pool", [], [MBw], lambda e: e.memset(MBw[:], 0.0))
        kmx = P.sbuf("kmx", [128, 8], F32)
        kmax2 = P.sbuf("kmax2", [128, 1], F32)
        Pt = [P.sbuf("Pt%d" % k, [128, 512], BF16) for k in range(3)]
        Eb = [P.sbuf("E%d" % k, [128, 512], F32) for k in range(2)]
        Lp = [P.sbuf("Lp%d" % k, [128, 512], BF16) for k in range(3)]
        exb = [P.sbuf("ex%d" % k, [128, 512], F32) for k in range(2)]
        rs = P.sbuf("rs", [128, 512], F32)
        Osb = P.sbuf("Osb", [65, 512], F32)
        rden = P.sbuf("rden", [128, 512], F32)
        S.op("pool", [], [rden], lambda e: e.memset(rden[:], 0.0))
        sel64 = P.sbuf("sel64", [128, 128], F32)
        S.seq("pool", sel64, [lambda e: e.memset(sel64[:], 0.0), lambda e: e.memset(sel64[64:65, :], 1.0)])
        ost = [P.sbuf("ost%d" % k, [64, 512], BF16) for k in range(2)]
        PSr = self.PS[0:6]
        PSo = self.PS[6:8]
        rr = [0]
        def psr():
            b = PSr[rr[0] % 6]
            rr[0] += 1
            return b
        jobs = [("m", h) for h in heads_moba] + [("s", h) for h in heads_sb]
        mg = self.mod_gen(P, list(mod_layers), psr) if mod_layers else None
        def mg_step():
            nonlocal mg
            if mg is not None:
                try:
                    next(mg)
                except StopIteration:
                    mg = None
        def load_head(idx):
            kind, h = jobs[idx]
            k = idx % 2
            qrow = (0 if kind == "m" else 1024) + h * 64
            krow = (512 if kind == "m" else 1536) + h * 64
            vcol = (0 if kind == "m" else 512) + h * 64
            kdst = KAUG[k] if kind == "m" else KS[k]
            S.dma([], [QAUG[k]], QAUG[k][0:64, :], QT[qrow:qrow + 64, :])
            S.dma([], [kdst], kdst[0:64, :], QT[krow:krow + 64, :])
            S.dma([], [Vb[k]], Vb[k][:, :, 0:64], VTOK[:, vcol:vcol + 64].rearrange("(j p) d -> p j d", p=128))
        load_head(0)
        oi = 0
        pre_done = set()
        for idx, (kind, h) in enumerate(jobs):
            Q, V = QAUG[idx % 2], Vb[idx % 2]
            Kt = KAUG[idx % 2] if kind == "m" else KS[idx % 2]
            if idx + 1 < len(jobs):
                load_head(idx + 1)
            if kind == "m":
                def prelude(Q, Kt):
                    S.op("dve", [Kt], [kmean], lambda e: e.tensor_reduce(
                        out=kmean[:], in_=Kt[0:64, :].rearrange("p (n k) -> p n k", k=256), axis=AX.X, op=ALU.add))
                    S.op("dve", [kmean], [kmeanb], lambda e: e.tensor_scalar(
                        out=kmeanb[:], in0=kmean[:], scalar1=1.0 / 256, scalar2=None, op0=ALU.mult))
                    yield
                    GP = psr()
                    def mm(e):
                        last = None
                        for tt in range(32):
                            last = e.matmul(GP[:, tt * 16:(tt + 1) * 16], lhsT=Q[0:64, tt * 128:(tt + 1) * 128], rhs=kmeanb[:, :],
                                            start=True, stop=True)
                        return last
                    S.op("pe", [Q, kmeanb], [GP], mm)
                    S.op("dve", [GP, pastb], [gm], lambda e: e.tensor_tensor(
                        out=gm[:].rearrange("p a b -> p (a b)"), in0=GP[:, :], in1=pastb[:].rearrange("p a b -> p (a b)"), op=ALU.add))
                    yield
                    for tt in range(32):
                        S.op("dve", [gm], [m8], lambda e, tt=tt: e.max(out=m8[:, tt, :], in_=gm[:, tt, :]))
                    yield
                    for tt in range(32):
                        S.op("dve", [gm, m8, past01], [MBw], lambda e, tt=tt: e.scalar_tensor_tensor(
                            out=MBw[:, tt, 64:80], in0=gm[:, tt, :], scalar=m8[:, tt, 2:3], in1=past01[:, tt, :], op0=ALU.is_ge, op1=ALU.mult))
                    S.op("dve", [MBw, ownm1], [MBw], lambda e: e.tensor_tensor(
                        out=MBw[:, :, 64:80], in0=MBw[:, :, 64:80], in1=ownm1[:], op=ALU.add))
                    yield
                    S.op("pool", [Kt], [sqQ], lambda e: e.tensor_tensor(out=sqQ[:], in0=Kt[0:64, :], in1=Kt[0:64, :], op=ALU.mult))
                    for g in range(8):
                        KP = psr()
                        S.op("pe", [sqQ, onesb], [KP], lambda e, KP=KP, g=g: e.matmul(
                            KP[:, :], lhsT=onesb[0:64, :], rhs=sqQ[:, g * 512:(g + 1) * 512], start=True, stop=True))
                        S.op("dve", [KP], [kmx], lambda e, KP=KP, g=g: e.tensor_reduce(
                            out=kmx[:, g:g + 1], in_=KP[:, :], axis=AX.X, op=ALU.max))
                    yield
                    S.op("dve", [kmx], [kmax2], lambda e: e.tensor_reduce(out=kmax2[:], in_=kmx[:], axis=AX.X, op=ALU.max))
                    S.op("pool", [Q], [sqQ], lambda e: e.tensor_tensor(out=sqQ[:], in0=Q[0:64, :], in1=Q[0:64, :], op=ALU.mult))
                    yield
                    QQ = psr()
                    def mmq(e):
                        last = None
                        for tt in range(32):
                            last = e.matmul(QQ[:, tt:tt + 1], lhsT=sqQ[:, tt * 128:(tt + 1) * 128], rhs=onesb[0:64, 0:1], start=True, stop=True)
                        return last
                    S.op("pe", [sqQ, onesb], [QQ], mmq)
                    S.op("act", [QQ, kmax2], [MBw], lambda e: e.activation(
                        out=MBw[:, :, 96], in_=QQ[:, 0:32], func=AF.Sqrt, scale=kmax2[:, 0:1]))
                    yield
                    for g in range(8):
                        TP = psr()
                        def tr(e, TP=TP, g=g):
                            last = None
                            for k in range(4):
                                last = e.transpose(out=TP[0:97, k * 128:(k + 1) * 128], in_=MBw[:, 4 * g + k, :], identity=self.ident32[:, :])
                            return last
                        S.op("pe", [MBw, self.ident32], [TP], tr)
                        S.op("dve", [TP], [Q], lambda e, TP=TP, g=g: e.tensor_copy(out=Q[64:97, g * 512:(g + 1) * 512], in_=TP[64:97, :]))

                if idx not in pre_done:
                    for _ in prelude(Q, Kt):
                        pass
                nxt = None
                if idx + 1 < len(jobs) and jobs[idx + 1][0] == "m":
                    nxt = prelude(QAUG[(idx + 1) % 2], KAUG[(idx + 1) % 2])
                    pre_done.add(idx + 1)
                pairs = [(i, j) for i in qtiles for j in range(4 * i + 4)]
                st = {}
                def stageA(p):
                    i, j = pairs[p]
                    sp = psr()
                    S.op("pe", [Kt, Q], [sp], lambda e: e.matmul(
                        sp[:, :], lhsT=Kt[:, j * 128:(j + 1) * 128], rhs=Q[:, i * 512:(i + 1) * 512], start=True, stop=True))
                    pt = Pt[p % 3]
                    S.op("act", [sp], [pt], lambda e: e.activation(out=pt[:], in_=sp[:, :], func=AF.Exp))
                    if j >= 4 * i:
                        S.op("pool", [pt, cmask[j - 4 * i]], [pt], lambda e: e.tensor_tensor(out=pt[:], in0=pt[:], in1=cmask[j - 4 * i][:], op=ALU.mult))
                    st[p] = pt
                def stageC(p):
                    nonlocal oi
                    i, j = pairs[p]
                    pt = st.pop(p)
                    last = 4 * i + 3
                    if j == 0:
                        oi += 1
                    OP = PSo[oi % 2]
                    S.op("pe", [V, pt], [OP], lambda e: e.matmul(OP[0:65, :], lhsT=V[:, j, :], rhs=pt[:], start=(j == 0), stop=(j == last)))
                    if j == last:
                        S.op("act", [OP], [Osb], lambda e: e.activation(out=Osb[:], in_=OP[0:65, :], func=AF.Copy))
                        if getattr(self, "debug", False) and i == 0:
                            self.dbg("dbgOsb", Osb, Osb[:])
                        S.op("dve", [Osb], [rden], lambda e: e.reciprocal(out=rden[64:65, :], in_=Osb[64:65, :]))
                        BP = psr()
                        S.op("pe", [rden, sel64], [BP], lambda e: e.matmul(
                            BP[:, :], lhsT=sel64[:, :], rhs=rden[:, :], start=True, stop=True))
                        o = ost[i % 2]
                        S.op("dve", [Osb, BP], [o], lambda e: e.tensor_tensor(out=o[:], in0=Osb[0:64, :], in1=BP[0:64, :], op=ALU.mult))
                        S.dma([o], [], OT[h * 64:(h + 1) * 64, i * 512:(i + 1) * 512], o[:])
                for p in range(len(pairs) + 2):
                    if p < len(pairs):
                        stageA(p)
                    if p >= 2:
                        stageC(p - 2)
                    if nxt is not None and p % 8 == 4:
                        try:
                            next(nxt)
                        except StopIteration:
                            nxt = None
                    if p % 16 == 9:
                        mg_step()
                if nxt is not None:
                    for _ in nxt:
                        pass
            else:
                pairs = [(i, j) for i in qtiles for j in range(4 * i + 3, -1, -1)]
                st = {}
                def stageA(p):
                    i, j = pairs[p]
                    z = psr()
                    S.op("pe", [Kt, Q], [z], lambda e: e.matmul(
                        z[:, :], lhsT=Kt[:, j * 128:(j + 1) * 128], rhs=Q[:, i * 512:(i + 1) * 512], start=True, stop=False))
                    E = Eb[p % 2]
                    lp = Lp[p % 3]
                    S.op("act", [z], [E], lambda e: e.activation(out=E[:], in_=z[:, :], func=AF.Exp))
                    S.op("act", [E], [lp], lambda e: e.activation(out=lp[:], in_=E[:], func=AF.Ln, bias=1.0, scale=1.0))
                    if j >= 4 * i:
                        S.op("pool", [lp, smask[j - 4 * i]], [lp], lambda e: e.tensor_tensor(out=lp[:], in0=lp[:], in1=smask[j - 4 * i][:], op=ALU.mult))
                    st[p] = (z, lp)
                def stageB(p):
                    i, j = pairs[p]
                    z, lp = st[p]
                    first = (j == 4 * i + 3)
                    S.op("pe", [NUI, lp], [z], lambda e: e.matmul(z[:, :], lhsT=NUI[:, :], rhs=lp[:], start=False, stop=True))
                    cs = psr()
                    S.op("pe", [onesb, lp], [cs], lambda e: e.matmul(cs[:, :], lhsT=onesb[:, :], rhs=lp[:], start=True, stop=True))
                    ex = exb[p % 2]
                    if first:
                        S.op("dve", [z], [ex], lambda e: e.tensor_copy(out=ex[:], in_=z[:, :]))
                        S.op("dve", [cs], [rs], lambda e: e.tensor_copy(out=rs[:], in_=cs[:, :]))
                    else:
                        S.op("dve", [z, rs], [ex], lambda e: e.tensor_tensor(out=ex[:], in0=z[:, :], in1=rs[:], op=ALU.subtract))
                        S.op("dve", [cs, rs], [rs], lambda e: e.tensor_tensor(out=rs[:], in0=cs[:, :], in1=rs[:], op=ALU.add))
                    w = Pt[p % 3]
                    S.op("act", [ex], [w], lambda e: e.activation(out=w[:], in_=ex[:], func=AF.Exp))
                    if j >= 4 * i:
                        S.op("pool", [w, smask[j - 4 * i]], [w], lambda e: e.tensor_tensor(out=w[:], in0=w[:], in1=smask[j - 4 * i][:], op=ALU.mult))
                    st[p] = w
                def stageC(p):
                    nonlocal oi
                    i, j = pairs[p]
                    w = st.pop(p)
                    first = (j == 4 * i + 3)
                    if first:
                        oi += 1
                    OP = PSo[oi % 2]
                    S.op("pe", [V, w], [OP], lambda e: e.matmul(OP[0:64, :], lhsT=V[:, j, 0:64], rhs=w[:], start=first, stop=(j == 0)))
                    if j == 0:
                        o = ost[i % 2]
                        S.op("act", [OP], [o], lambda e: e.activation(out=o[:], in_=OP[0:64, :], func=AF.Copy))
                        S.dma([o], [], OT[512 + h * 64:512 + (h + 1) * 64, i * 512:(i + 1) * 512], o[:])
                n = len(pairs)
                for p in range(n + 2):
                    if p % 16 == 9:
                        mg_step()
                    if p < n:
                        stageA(p)
                    if 1 <= p <= n:
                        stageB(p - 1)
                    if p >= 2:
                        stageC(p - 2)
        while mg is not None:
            mg_step()
        P.close()

    def phase_gdn(self, li, heads=range(8), nchunks=32):
        S = self.S
        P = Phase(S)
        GT, ZT, AB, OT = self.dr["GT"], self.dr["VTOK"], self.dr["AB"], self.dr["OT"]
        NCH = 32
        C = 128
        UT = P.sbuf("UT", [128, 128], F32)
        SU = P.sbuf("SU", [128, 128], F32)
        S.seq("pool", UT, [
            lambda e: e.memset(UT[:], 1.0),
            lambda e: e.affine_select(out=UT[:], in_=UT[:], compare_op=ALU.is_ge, fill=0.0, base=0,
                                      pattern=[[1, 128]], channel_multiplier=-1)])
        S.seq("pool", SU, [
            lambda e: e.memset(SU[:], 1.0),
            lambda e: e.affine_select(out=SU[:], in_=SU[:], compare_op=ALU.is_gt, fill=0.0, base=0,
                                      pattern=[[-1, 128]], channel_multiplier=1)])
        I32m = self.ident32
        ab = P.sbuf("ab", [128, NCH, 16], F32)
        S.dma([], [ab], ab[:], AB.rearrange("(c p) n -> p c n", p=128))
        graw = P.sbuf("graw", [128, NCH, 8], F32)
        beta = P.sbuf("beta", [128, NCH, 8], F32)
        nbeta = P.sbuf("nbeta", [128, NCH, 8], F32)
        negA = P.sbuf("negA", [128, 8], F32)
        tmpa = P.sbuf("tmpa", [128, NCH, 8], F32)
        S.op("act", [self.VEC], [negA], lambda e: e.activation(out=negA[:], in_=self.vcol("alog", li * 8, 8), func=AF.Exp))
        S.op("dve", [negA], [negA], lambda e: e.tensor_scalar(out=negA[:], in0=negA[:], scalar1=-1.0, scalar2=None, op0=ALU.mult))
        for c in range(NCH):
            S.op("dve", [ab, self.VEC], [tmpa], lambda e, c=c: e.tensor_tensor(
                out=tmpa[:, c, :], in0=ab[:, c, 0:8], in1=self.vcol("dtb", li * 8, 8), op=ALU.add))
        S.op("act", [tmpa], [tmpa], lambda e: e.activation(out=tmpa[:], in_=tmpa[:], func=AF.Exp))
        S.op("act", [tmpa], [tmpa], lambda e: e.activation(out=tmpa[:], in_=tmpa[:], func=AF.Ln, bias=1.0, scale=1.0))
        for c in range(NCH):
            S.op("dve", [tmpa, negA], [graw], lambda e, c=c: e.tensor_tensor(
                out=graw[:, c, :], in0=tmpa[:, c, :], in1=negA[:], op=ALU.mult))
        S.op("act", [ab], [beta], lambda e: e.activation(out=beta[:], in_=ab[:, :, 8:16], func=AF.Sigmoid))
        S.op("dve", [beta], [nbeta], lambda e: e.tensor_scalar(out=nbeta[:], in0=beta[:], scalar1=-1.0, scalar2=None, op0=ALU.mult))
        gc = P.sbuf("gc", [128, NCH, 8], F32)
        egc = P.sbuf("egc", [128, NCH, 8], F32)
        egl = P.sbuf("egl", [128, NCH, 8], F32)
        edec = P.sbuf("edec", [128, NCH, 8], F32)
        bgc = P.sbuf("bgc", [128, NCH, 8], F32)
        pgc = self.ps()
        S.op("pe", [UT, graw], [pgc], lambda e: e.matmul(pgc[:, 0:256], lhsT=UT[:, :], rhs=graw[:].rearrange("p a b -> p (a b)"), start=True, stop=True))
        pgl = self.ps()
        S.op("pe", [self.ones32, graw], [pgl], lambda e: e.matmul(pgl[:, 0:256], lhsT=self.ones32[:, :], rhs=graw[:].rearrange("p a b -> p (a b)"), start=True, stop=True))
        fl = lambda b: b[:].rearrange("p a b -> p (a b)")
        S.op("dve", [pgc], [gc], lambda e: e.tensor_copy(out=fl(gc), in_=pgc[:, 0:256]))
        S.op("act", [pgc], [egc], lambda e: e.activation(out=fl(egc), in_=pgc[:, 0:256], func=AF.Exp))
        S.op("act", [pgl], [egl], lambda e: e.activation(out=fl(egl), in_=pgl[:, 0:256], func=AF.Exp))
        S.op("dve", [pgl, gc], [edec], lambda e: e.tensor_tensor(out=fl(edec), in0=pgl[:, 0:256], in1=fl(gc), op=ALU.subtract))
        S.op("act", [edec], [edec], lambda e: e.activation(out=fl(edec), in_=fl(edec), func=AF.Exp))
        S.op("dve", [beta, egc], [bgc], lambda e: e.tensor_tensor(out=fl(bgc), in0=fl(beta), in1=fl(egc), op=ALU.mult))
        xin = [P.sbuf("xin%d" % k, [128, T], BF16) for k in range(1)]
        acc = P.sbuf("acc", [128, T], F32)
        ybuf = P.sbuf("ybuf", [128, T], F32)
        qTb = P.sbuf("qTb", [128, T], BF16)
        kTf = P.sbuf("kTf", [128, T], F32)
        kTb = P.sbuf("kTb", [128, T], BF16)
        vTf = P.sbuf("vTf", [128, T], F32)
        rn = P.sbuf("rn", [128, 512], F32)
        zt = P.sbuf("zt", [128, NCH, 128], BF16)
        zg = P.sbuf("zg", [128, NCH, 128], BF16)
        def chunked(name):
            big = P.sbuf(name, [128, NCH, 128], BF16)
            out = []
            for k in range(NCH):
                b = Buf(big.t[:, k, :], "%s%d" % (name, k))
                S.bufs.append(b)
                out.append(b)
            return out
        U_, WT, AT, KD = chunked("U_"), chunked("WT"), chunked("AT"), chunked("KD")
        ot = P.sbuf("ot", [128, T], BF16)
        Sst = P.sbuf("Sst", [128, 128], F32)
        Sbfs = [P.sbuf("Sbf%d" % k, [128, 128], BF16) for k in range(2)]
        NR = GDN_NR
        def rot(name, dt=F32, n=NR):
            return [P.sbuf("%s%d" % (name, k), [128, 128], dt) for k in range(n)]
        ktok, vb, kbg, G1, dcs, dsc, Nm, Am, Pm, Nn, An = (rot(x) for x in ("ktok", "vb", "kbg", "G1", "dcs", "dsc", "Nm", "Am", "Pm", "Nn", "An"))
        vnew = rot("vnew", BF16, 2)
        o2b = rot("o2b", F32, 2)
        osb = rot("osb", F32, 2)
        ysb = rot("ysb", F32, 2)
        ssq = P.sbuf("ssq", [128, 2], F32)
        junk = P.sbuf("junk", [128, 128], F32)
        ew = [0]
        def evac_copy(src_ps, dst, dst_ap, src_ap, extra_reads=()):
            ew[0] += 1
            if ew[0] % 2 == 0:
                S.op("act", [src_ps] + list(extra_reads), [dst], lambda e: e.activation(out=dst_ap, in_=src_ap, func=AF.Copy))
            else:
                S.op("dve", [src_ps] + list(extra_reads), [dst], lambda e: e.tensor_copy(out=dst_ap, in_=src_ap))
        for h in heads:
            for wi, which in enumerate(("q", "k", "v")):
                xi = xin[0]
                row = wi * 1024 + h * 128
                S.dma([], [xi], xi[:], GT[row:row + 128, :])
                def wcol(j, wi=wi):
                    return self.vcol("conv", (li * 4 + j) * 24 + wi * 8 + h)
                S.op("dve", [xi, self.VEC], [acc], lambda e, xi=xi: e.tensor_scalar(out=acc[:], in0=xi[:], scalar1=wcol(3), scalar2=None, op0=ALU.mult))
                for sh in (1, 2, 3):
                    S.op("dve", [xi, self.VEC, acc], [acc], lambda e, xi=xi, sh=sh: e.scalar_tensor_tensor(
                        out=acc[:, sh:], in0=xi[:, 0:T - sh], scalar=wcol(3 - sh), in1=acc[:, sh:], op0=ALU.mult, op1=ALU.add))
                dst = vTf if which == "v" else ybuf
                S.op("act", [acc], [dst], lambda e, dst=dst: e.activation(out=dst[:], in_=acc[:], func=AF.Silu))
                if which == "v":
                    continue
                S.op("act", [ybuf], [acc], lambda e: e.activation(out=acc[:], in_=ybuf[:], func=AF.Square))
                for g in range(8):
                    sl = slice(g * 512, (g + 1) * 512)
                    pss = self.ps()
                    S.op("pe", [acc, self.ones32], [pss], lambda e, pss=pss, sl=sl: e.matmul(pss[:, :], lhsT=self.ones32[:, :], rhs=acc[:, sl], start=True, stop=True))
                    S.op("dve", [pss], [rn], lambda e, pss=pss: e.tensor_scalar(out=rn[:], in0=pss[:, :], scalar1=EPS, scalar2=None, op0=ALU.add))
                    S.op("act", [rn], [rn], lambda e: e.activation(out=rn[:], in_=rn[:], func=AF.Sqrt))
                    S.op("dve", [rn], [rn], lambda e: e.reciprocal(out=rn[:], in_=rn[:]))
                    if which == "q":
                        S.op("dve", [ybuf, rn], [qTb], lambda e, sl=sl: e.scalar_tensor_tensor(
                            out=qTb[:, sl], in0=ybuf[:, sl], scalar=float(128 ** -0.5), in1=rn[:], op0=ALU.mult, op1=ALU.mult))
                    else:
                        S.op("dve", [ybuf, rn], [kTf], lambda e, sl=sl: e.tensor_tensor(out=kTf[:, sl], in0=ybuf[:, sl], in1=rn[:], op=ALU.mult))
                        S.op("pool", [kTf], [kTb], lambda e, sl=sl: e.tensor_copy(out=kTb[:, sl], in_=kTf[:, sl]))
            S.dma([], [zt], zt[:], ZT[:, h * 128:(h + 1) * 128].rearrange("(c p) d -> p c d", p=128))
            S.op("act", [zt], [zt], lambda e: e.activation(out=zt[:], in_=zt[:], func=AF.Silu))
            for c in range(NCH):
                S.op("pool", [zt, self.VEC], [zg], lambda e, c=c: e.tensor_tensor(
                    out=zg[:, c, :], in0=zt[:, c, :], in1=self.vcol("gngb", li * 128, 128), op=ALU.mult))
            done1 = [False] * nchunks

            class Pool3:
                def __init__(self, banks):
                    self.banks = banks
                    self.i = 0

                def get(self):
                    b = self.banks[self.i % len(self.banks)]
                    self.i += 1
                    return PQ(b, 0)

                def pair(self):
                    b = self.banks[self.i % len(self.banks)]
                    self.i += 1
                    return PQ(b, 0), PQ(b, 1)
            pools = [Pool3(self.PS[0:2]), Pool3(self.PS[2:4]), Pool3(self.PS[4:6])]
            pool2 = Pool3(self.PS[6:8])
            def evq(pq, dst, dst_ap, extra_reads=()):
                evac_copy(pq, dst, dst_ap, pq.ap, extra_reads)

            def stage1(c, r, h=h):
                cs = slice(c * C, (c + 1) * C)
                col = slice(h, h + 1)
                pl = pools[r]
                pk, pv = pl.pair()
                def f(e):
                    e.transpose(out=pk.ap, in_=kTf[:, cs], identity=I32m[:, :])
                    return e.transpose(out=pv.ap, in_=vTf[:, cs], identity=I32m[:, :])
                S.op("pe", [kTf, vTf, I32m], [pk], f)
                S.op("dve", [SU, graw], [G1[r]], lambda e: e.tensor_scalar(out=G1[r][:], in0=SU[:], scalar1=graw[:, c, col], scalar2=None, op0=ALU.mult))
                yield
                S.op("dve", [pk], [ktok[r]], lambda e: e.tensor_copy(out=ktok[r][:], in_=pk.ap))
                S.op("dve", [pv, beta], [vb[r]], lambda e: e.tensor_scalar(out=vb[r][:], in0=pv.ap, scalar1=beta[:, c, col], scalar2=None, op0=ALU.mult))
                pd1, pd2 = pl.pair()
                def f(e):
                    e.matmul(pd1.ap, lhsT=UT[:, :], rhs=G1[r][:], start=True, stop=True)
                    return e.matmul(pd2.ap, lhsT=G1[r][:], rhs=UT[:, :], start=True, stop=True)
                S.op("pe", [UT, G1[r]], [pd1], f)
                yield
                S.op("act", [pd1], [dcs[r]], lambda e: e.activation(out=dcs[r][:], in_=pd1.ap, func=AF.Exp))
                S.op("act", [pd2], [dsc[r]], lambda e: e.activation(out=dsc[r][:], in_=pd2.ap, func=AF.Exp))
                S.op("act", [ktok[r], bgc], [kbg[r]], lambda e: e.activation(out=kbg[r][:], in_=ktok[r][:], func=AF.Copy, scale=bgc[:, c, col]))
                S.op("act", [ktok[r], edec], [KD[c]], lambda e: e.activation(out=KD[c][:], in_=ktok[r][:], func=AF.Copy, scale=edec[:, c, col]))
                pkk, pqk = pl.pair()
                def f(e):
                    e.matmul(pkk.ap, lhsT=kTf[:, cs], rhs=kTf[:, cs], start=True, stop=True)
                    return e.matmul(pqk.ap, lhsT=kTb[:, cs], rhs=qTb[:, cs], start=True, stop=True)
                S.op("pe", [kTf, kTb, qTb], [pkk], f)
                yield
                S.op("pool", [dcs[r], SU], [dcs[r]], lambda e: e.tensor_tensor(out=dcs[r][:], in0=dcs[r][:], in1=SU[:], op=ALU.mult))
                S.op("pool", [dsc[r], UT], [dsc[r]], lambda e: e.tensor_tensor(out=dsc[r][:], in0=dsc[r][:], in1=UT[:], op=ALU.mult))
                yield
                S.op("dve", [pkk, nbeta, dcs[r]], [Nm[r]], lambda e: e.scalar_tensor_tensor(
                    out=Nm[r][:], in0=pkk.ap, scalar=nbeta[:, c, col], in1=dcs[r][:], op0=ALU.mult, op1=ALU.mult))
                S.op("dve", [pqk, dsc[r]], [AT[c]], lambda e: e.tensor_tensor(out=AT[c][:], in0=pqk.ap, in1=dsc[r][:], op=ALU.mult))
                yield
                pa = pl.get()
                S.op("pe", [Nm[r], I32m], [pa], lambda e: e.transpose(out=pa.ap, in_=Nm[r][:], identity=I32m[:, :]))
                yield
                S.op("act", [pa], [Am[r]], lambda e: e.activation(out=Am[r][:], in_=pa.ap, func=AF.Copy))
                yield
                S.op("pool", [Am[r], I32m], [Pm[r]], lambda e: e.tensor_tensor(out=Pm[r][:], in0=Am[r][:], in1=I32m[:], op=ALU.add))
                Ncur, Acur = Nm[r], Am[r]
                for lvl in range(1, 7):
                    Nnx = Nn[r] if lvl % 2 == 1 else Nm[r]
                    Anx = An[r] if lvl % 2 == 1 else Am[r]
                    pn, pa2 = pl.pair()
                    def f(e):
                        last = e.matmul(pn.ap, lhsT=Acur[:], rhs=Ncur[:], start=True, stop=True)
                        if lvl < 6:
                            last = e.matmul(pa2.ap, lhsT=Ncur[:], rhs=Acur[:], start=True, stop=True)
                        return last
                    S.op("pe", [Acur, Ncur], [pn], f)
                    yield
                    if (lvl + r) % 2 == 0:
                        S.op("act", [pn], [Nnx], lambda e: e.activation(out=Nnx[:], in_=pn.ap, func=AF.Copy))
                        if lvl < 6:
                            S.op("act", [pa2], [Anx], lambda e: e.activation(out=Anx[:], in_=pa2.ap, func=AF.Copy))
                    else:
                        S.op("dve", [pn], [Nnx], lambda e: e.tensor_copy(out=Nnx[:], in_=pn.ap))
                        if lvl < 6:
                            S.op("dve", [pa2], [Anx], lambda e: e.tensor_copy(out=Anx[:], in_=pa2.ap))
                    yield
                    pp = pl.get()
                    S.op("pe", [Nnx, Pm[r]], [pp], lambda e: e.matmul(pp.ap, lhsT=Nnx[:], rhs=Pm[r][:], start=True, stop=True))
                    yield
                    S.op("dve", [pp, Pm[r]], [Pm[r]], lambda e: e.tensor_tensor(out=Pm[r][:], in0=pp.ap, in1=Pm[r][:], op=ALU.add))
                    yield
                    Ncur, Acur = Nnx, Anx
                pu, pw = pl.pair()
                def f(e):
                    e.matmul(pu.ap, lhsT=Pm[r][:], rhs=vb[r][:], start=True, stop=True)
                    return e.matmul(pw.ap, lhsT=kbg[r][:], rhs=Pm[r][:], start=True, stop=True)
                S.op("pe", [Pm[r], vb[r], kbg[r]], [pu], f)
                yield
                S.op("act", [pu], [U_[c]], lambda e: e.activation(out=U_[c][:], in_=pu.ap, func=AF.Copy))
                S.op("act", [pw], [WT[c]], lambda e: e.activation(out=WT[c][:], in_=pw.ap, func=AF.Copy))
                done1[c] = True

            def stage2(h=h):
                S.op("pool", [], [Sbfs[0]], lambda e: e.memset(Sbfs[0][:], 0.0))
                S.op("pool", [], [Sst], lambda e: e.memset(Sst[:], 0.0))
                col = slice(h, h + 1)
                for c in range(nchunks):
                    while not done1[c]:
                        yield
                    cs = slice(c * C, (c + 1) * C)
                    r2 = c % 2
                    Sold, Snew = Sbfs[c % 2], Sbfs[(c + 1) % 2]
                    p1, p2a = pool2.get(), pool2.get()
                    S.op("pe", [WT[c], Sold], [p1], lambda e: e.matmul(p1.ap, lhsT=WT[c][:], rhs=Sold[:], start=True, stop=True))
                    S.op("pe", [qTb, Sold], [p2a], lambda e: e.matmul(p2a.ap, lhsT=qTb[:, cs], rhs=Sold[:], start=True, stop=True))
                    yield
                    S.op("dve", [U_[c], p1], [vnew[r2]], lambda e: e.tensor_tensor(out=vnew[r2][:], in0=U_[c][:], in1=p1.ap, op=ALU.subtract))
                    S.op("act", [p2a, egc], [o2b[r2]], lambda e: e.activation(out=o2b[r2][:], in_=p2a.ap, func=AF.Copy, scale=egc[:, c, col]))
                    yield
                    p3, p2b = pool2.get(), pool2.get()
                    S.op("pe", [KD[c], vnew[r2]], [p3], lambda e: e.matmul(p3.ap, lhsT=KD[c][:], rhs=vnew[r2][:], start=True, stop=True))
                    S.op("pe", [AT[c], vnew[r2]], [p2b], lambda e: e.matmul(p2b.ap, lhsT=AT[c][:], rhs=vnew[r2][:], start=True, stop=True))
                    yield
                    S.op("dve", [Sst, egl, p3], [Sst], lambda e: e.scalar_tensor_tensor(
                        out=Sst[:], in0=Sst[:], scalar=egl[:, c, col], in1=p3.ap, op0=ALU.mult, op1=ALU.add))
                    yield
                    S.op("act", [Sst], [Snew], lambda e: e.activation(out=Snew[:], in_=Sst[:], func=AF.Copy))
                    S.op("dve", [p2b, o2b[r2]], [osb[r2]], lambda e: e.tensor_tensor(out=osb[r2][:], in0=p2b.ap, in1=o2b[r2][:], op=ALU.add))
                    yield
                    S.op("act", [osb[r2]], [junk, ssq], lambda e: e.activation(out=junk[:], in_=osb[r2][:], func=AF.Square, accum_out=ssq[:, r2:r2 + 1]))
                    S.op("dve", [ssq], [ssq], lambda e: e.tensor_scalar(out=ssq[:, r2:r2 + 1], in0=ssq[:, r2:r2 + 1], scalar1=1.0 / 128, scalar2=EPS, op0=ALU.mult, op1=ALU.add))
                    S.op("act", [ssq], [ssq], lambda e: e.activation(out=ssq[:, r2:r2 + 1], in_=ssq[:, r2:r2 + 1], func=AF.Sqrt))
                    S.op("dve", [ssq], [ssq], lambda e: e.reciprocal(out=ssq[:, r2:r2 + 1], in_=ssq[:, r2:r2 + 1]))
                    S.op("dve", [osb[r2], ssq, zg], [ysb[r2]], lambda e: e.scalar_tensor_tensor(
                        out=ysb[r2][:], in0=osb[r2][:], scalar=ssq[:, r2:r2 + 1], in1=zg[:, c, :], op0=ALU.mult, op1=ALU.mult))
                    yield
                    py = pool2.get()
                    S.op("pe", [ysb[r2], I32m], [py], lambda e: e.transpose(out=py.ap, in_=ysb[r2][:], identity=I32m[:, :]))
                    S.op("act", [py], [ot], lambda e: e.activation(out=ot[:, cs], in_=py.ap, func=AF.Copy))
                    yield

            pending = list(range(nchunks))
            active = {}
            s2 = stage2()
            s2_alive = True
            while pending or active or s2_alive:
                while pending and len(active) < NR:
                    c = pending.pop(0)
                    slot = [k for k in range(NR) if k not in active][0]
                    active[slot] = stage1(c, slot)
                for slot in list(active.keys()):
                    try:
                        next(active[slot])
                    except StopIteration:
                        del active[slot]
                if s2_alive:
                    try:
                        next(s2)
                    except StopIteration:
                        s2_alive = False
            S.dma([ot], [], OT[h * 128:(h + 1) * 128, :], ot[:])
        P.close()


def prep_shared(inp):
    perm = rope_partner_perm()
    w = np.asarray(inp["attn_w_in"], np.float32)
    w4 = np.concatenate([w, w[:, :, 0:512][:, :, perm], w[:, :, 512:1024][:, :, perm]], axis=2)
    return {
        "ada_w": np.ascontiguousarray(inp["ada_w"], np.float32),
        "attn_w_in": np.ascontiguousarray(w4),
        "attn_w_out": np.ascontiguousarray(inp["attn_w_out"], np.float32),
        "gdn_w_in": np.ascontiguousarray(inp["gdn_w_in"], np.float32),
        "gdn_w_out": np.ascontiguousarray(inp["gdn_w_out"], np.float32),
        "ffn_w_in": np.ascontiguousarray(inp["ffn_w_in"], np.float32),
        "ffn_w_out": np.ascontiguousarray(inp["ffn_w_out"], np.float32),
    }


def prep_core(inp, b, shared):
    m = dict(shared)
    m["xT"] = np.ascontiguousarray(np.asarray(inp["x"][b], np.float32).T)
    m["vecs"] = build_vecs(inp, b)
    m["pos"] = np.ascontiguousarray(np.asarray(inp["positions"][b], np.int32).reshape(1, T))
    return m


def build_full():
    pg = Prog()
    pg.defer_mod = True
    pg.phase0()
    for l in range(NL):
        pg.phase1(l)
        if l % 2 == 0:
            pg.phase_attn(mod_layers=((1, 2) if l == 0 else (3,)))
        else:
            pg.phase_gdn(l // 2)
        pg.phase34(l, final=(l == NL - 1))
    return pg


def kernel(**inputs):
    inp = {k: np.asarray(v) for k, v in inputs.items()}
    pg = build_full()
    shared = prep_shared(inp)
    in_maps = []
    for b in range(8):
        m = prep_core(inp, b, shared)
        in_maps.append({k: m[k] for k in pg.used_inputs})
    res = run_bass_kernel_spmd(pg.nc, in_maps, core_ids=list(range(8)))
    out = np.empty((8, T, D), np.float32)
    for b in range(8):
        out[b] = np.asarray(res.results[b]["outT"], np.float32).T
    return out
```

```python
import numpy as np
from contextlib import ExitStack
import concourse.bass as bass
import concourse.mybir as mybir
from concourse.bass_utils import run_bass_kernel_spmd

F32 = mybir.dt.float32
BF16 = mybir.dt.bfloat16
I32 = mybir.dt.int32
AF = mybir.ActivationFunctionType
ALU = mybir.AluOpType
AX = mybir.AxisListType

D = 1024
T = 4096
NL = 4
KC = 8
FH = 2816
NJ = 22
EPS = 1e-6
NDMA = 48
BIG = 30000.0
SAME_ENG_SYNC = False
GDN_NR = 3
PSQ_SEPARATE = False
GDN_DBG = 0


class Buf:
    __slots__ = ("t", "w", "r", "name")

    def __init__(self, t, name=""):
        self.t = t
        self.w = None
        self.r = {}
        self.name = name

    def __getitem__(self, idx):
        return self.t[idx]


class PQ:
    __slots__ = ("ap", "b")

    def __init__(self, bank, q):
        self.b = bank
        self.ap = bank.t[:, q * 128:(q + 1) * 128]


class Sched:
    def __init__(self, nc):
        self.nc = nc
        self.E = {"pe": nc.tensor, "act": nc.scalar, "dve": nc.vector, "pool": nc.gpsimd, "sp": nc.sync}
        self.sem = {k: nc.alloc_semaphore("s_" + k) for k in ("pe", "act", "dve", "pool")}
        self.cnt = {k: 0 for k in self.sem}
        self.seen = {k: {} for k in self.E}
        self.dsem = [nc.alloc_semaphore("d%d" % i) for i in range(NDMA)]
        self.dcnt = [0] * NDMA
        self.drr = 0
        self.bufs = []
        self.ps_rr = 0

    def sbuf(self, name, shape, dtype):
        b = Buf(self.nc.alloc_sbuf_tensor(name, list(shape), dtype), name)
        self.bufs.append(b)
        return b

    def psum(self, name, shape, dtype=F32):
        b = Buf(self.nc.alloc_psum_tensor(name, list(shape), dtype), name)
        self.bufs.append(b)
        return b

    def dram(self, t, name=""):
        b = Buf(t, name)
        self.bufs.append(b)
        return b

    def _wait(self, eng, ev):
        if ev is None:
            return
        sem, val, key = ev
        if self.seen[eng].get(key, 0) >= val:
            return
        self.E[eng].wait_ge(sem, val)
        self.seen[eng][key] = val

    def _deps(self, eng, reads, writes):
        reads = [getattr(b, "b", b) for b in reads]
        writes = [getattr(b, "b", b) for b in writes]
        for b in reads:
            self._wait(eng, b.w)
        for b in writes:
            if b.w is not None and (SAME_ENG_SYNC or b.w[2] != eng):
                self._wait(eng, b.w)
            for ev in b.r.values():
                if SAME_ENG_SYNC or ev[2] != eng:
                    self._wait(eng, ev)

    def _done(self, ev, reads, writes):
        reads = [getattr(b, "b", b) for b in reads]
        writes = [getattr(b, "b", b) for b in writes]
        for b in reads:
            b.r[ev[2]] = ev
        for b in writes:
            b.w = ev
            b.r = {}

    def op(self, eng, reads, writes, fn):
        self._deps(eng, reads, writes)
        ins = fn(self.E[eng])
        self.cnt[eng] += 1
        ins.then_inc(self.sem[eng], 1)
        ev = (self.sem[eng], self.cnt[eng], eng)
        self._done(ev, reads, writes)
        return ev

    def seq(self, eng, buf, fns, reads=()):
        ev = None
        for k, fn in enumerate(fns):
            ev = self.op(eng, ([buf] if k > 0 else []) + list(reads), [buf], fn)
        return ev

    def dma(self, reads, writes, out, in_, eng="sp"):
        k = self.drr
        self.drr = (k + 1) % NDMA
        key = ("d", k)
        if self.dcnt[k] > 0:
            self._wait(eng, (self.dsem[k], self.dcnt[k], key))
        self._deps(eng, reads, writes)
        ins = self.E[eng].dma_start(out=out, in_=in_)
        self.dcnt[k] += 16
        ins.then_inc(self.dsem[k], 16)
        ev = (self.dsem[k], self.dcnt[k], key)
        self._done(ev, reads, writes)
        return ev

    def barrier(self):
        evs = []
        for k in self.sem:
            if self.cnt[k] > 0:
                evs.append((self.sem[k], self.cnt[k], k))
        for k in range(NDMA):
            if self.dcnt[k] > 0:
                evs.append((self.dsem[k], self.dcnt[k], ("d", k)))
        for eng in self.E:
            for ev in evs:
                self._wait(eng, ev)
        for b in self.bufs:
            b.w = None
            b.r = {}


class Phase:
    _uid = [0]

    def __init__(self, S):
        Phase._uid[0] += 1
        self.uid = Phase._uid[0]
        self.S = S
        self.nc = S.nc
        self.st = ExitStack()
        self.nbufs0 = len(S.bufs)

    def sbuf(self, name, shape, dtype):
        name = "%s_p%d" % (name, self.uid)
        t = self.st.enter_context(self.nc.sbuf_tensor(name, list(shape), dtype))
        b = Buf(t, name)
        self.S.bufs.append(b)
        return b

    def psum(self, name, shape, dtype=F32):
        name = "%s_p%d" % (name, self.uid)
        t = self.st.enter_context(self.nc.psum_tensor(name, list(shape), dtype))
        b = Buf(t, name)
        self.S.bufs.append(b)
        return b

    def close(self):
        self.S.barrier()
        self.st.close()
        del self.S.bufs[self.nbufs0:]


VOFF = {}


def _vec_layout():
    off = 0
    for name, n in [("gm", 32), ("gf", 32), ("fin", 8), ("adab", 192), ("conv", 192), ("gngb", 256),
                    ("alog", 16), ("dtb", 16), ("invf", 1), ("sinsign", 1), ("cT", 8)]:
        VOFF[name] = off
        off += n
    return off


NV = _vec_layout()


def build_vecs(inp, b):
    v = np.zeros((128, NV), np.float32)
    def chunked(a):
        a = np.asarray(a, np.float32).reshape(-1, 128)
        return a.T
    v[:, VOFF["gm"]:VOFF["gm"] + 32] = chunked(inp["norm_mix_g"])
    v[:, VOFF["gf"]:VOFF["gf"] + 32] = chunked(inp["norm_ffn_g"])
    v[:, VOFF["fin"]:VOFF["fin"] + 8] = chunked(inp["final_norm_g"])
    v[:, VOFF["adab"]:VOFF["adab"] + 192] = chunked(inp["ada_b"])
    v[:, VOFF["conv"]:VOFF["conv"] + 192] = chunked(inp["gdn_conv_w"])
    v[:, VOFF["gngb"]:VOFF["gngb"] + 256] = np.broadcast_to(np.asarray(inp["gdn_norm_g"], np.float32).reshape(1, 256), (128, 256))
    v[:, VOFF["alog"]:VOFF["alog"] + 16] = np.broadcast_to(np.asarray(inp["gdn_a_log"], np.float32).reshape(1, 16), (128, 16))
    v[:, VOFF["dtb"]:VOFF["dtb"] + 16] = np.broadcast_to(np.asarray(inp["gdn_dt_bias"], np.float32).reshape(1, 16), (128, 16))
    p = np.arange(128) % 64
    inv = (500000.0 ** (-np.arange(8, dtype=np.float32) * 2.0 / 16)).astype(np.float32)
    v[:, VOFF["invf"]] = np.where(p < 16, inv[p % 8], 0.0)
    v[:, VOFF["sinsign"]] = np.where(p < 8, -1.0, np.where(p < 16, 1.0, 0.0))
    v[:, VOFF["cT"]:VOFF["cT"] + 8] = chunked(inp["c"][b])
    return v


def rope_partner_perm():
    idx = np.arange(512)
    d = idx % 64
    return np.where(d < 8, idx + 8, np.where(d < 16, idx - 8, idx))


class LazyDram(dict):
    def __init__(self, prog):
        super().__init__()
        self.prog = prog

    def __missing__(self, name):
        pg = self.prog
        if name in pg.in_shapes:
            shape, dt = pg.in_shapes[name]
            kind = "ExternalInput"
            pg.used_inputs.append(name)
        else:
            shape, dt = pg.scr_shapes[name]
            kind = "Internal"
            if name in pg.ext_out or name == "outT":
                kind = "ExternalOutput"
            if name in pg.ext_in:
                kind = "ExternalInput"
                pg.used_inputs.append(name)
        ap = pg.nc.dram_tensor(name, list(shape), dt, kind=kind).ap()
        self[name] = ap
        return ap


class Prog:
    def __init__(self, ext_out=(), ext_in=()):
        self.nc = bass.Bass("TRN2", target_bir_lowering=False)
        nc = self.nc
        self.ext_out = set(ext_out)
        self.ext_in = set(ext_in)
        self.S = Sched(nc)
        S = self.S
        self.in_shapes = {
            "xT": ([D, T], F32), "vecs": ([128, NV], F32), "pos": ([1, T], I32),
            "ada_w": ([NL, D, 6 * D], F32), "attn_w_in": ([2, D, 4096], F32), "attn_w_out": ([2, D, D], F32),
            "gdn_w_in": ([2, D, 4112], F32), "gdn_w_out": ([2, D, D], F32),
            "ffn_w_in": ([NL, D, 2 * FH], F32), "ffn_w_out": ([NL, FH, D], F32),
        }
        self.used_inputs = []
        self.defer_mod = False
        self.dr = LazyDram(self)
        self.dbg_outs = []
        self.scr_shapes = {
            "XT": ([D, T], F32), "COS": ([128, T], F32), "SIN": ([128, T], F32),
            "QT": ([2048, T], BF16),
            "VTOK": ([T, 1024], BF16),
            "OT": ([D, T], BF16),
            "GT": ([3072, T], BF16),
            "AB": ([T, 16], F32),
            "outT": ([D, T], F32),
        }
        self.VEC = S.sbuf("VEC", [128, NV], F32)
        self.MOD = S.sbuf("MOD", [128, NL * 48], F32)
        self.AM = S.sbuf("AMc", [128, NL * 8], F32)
        self.AFc = S.sbuf("AFc", [128, NL * 8], F32)
        self.ones32 = S.sbuf("ones32", [128, 128], F32)
        self.ident32 = S.sbuf("ident32", [128, 128], F32)
        self.onesbf = S.sbuf("onesbf", [128, 128], BF16)
        self.condT = S.sbuf("condT", [128, 8], F32)
        self.PS = [S.psum("ps%d" % i, [128, 512], F32) for i in range(8)]
        self.ps_i = 0
        S.op("pool", [], [self.ones32], lambda e: e.memset(self.ones32[:], 1.0))
        S.op("pool", [], [self.onesbf], lambda e: e.memset(self.onesbf[:], 1.0))
        S.seq("pool", self.ident32, [
            lambda e: e.memset(self.ident32[:], 0.0),
            lambda e: e.affine_select(out=self.ident32[:], in_=self.ident32[:], compare_op=ALU.not_equal,
                                      fill=1.0, base=0, pattern=[[-1, 128]], channel_multiplier=1)])

    def dbg(self, name, buf, ap):
        shape = [int(x) for x in ap.shape]
        d = self.nc.dram_tensor(name, shape, buf.t.dtype, kind="ExternalOutput").ap()
        self.S.dma([buf], [], d, ap)
        self.dbg_outs.append(name)

    def ps(self):
        b = self.PS[self.ps_i]
        self.ps_i = (self.ps_i + 1) % 8
        return b

    def psqs(self, n):
        if PSQ_SEPARATE:
            return [PQ(self.ps(), 0) for q in range(n)]
        b = self.ps()
        return [PQ(b, q) for q in range(n)]

    def vcol(self, name, i=0, n=1):
        o = VOFF[name] + i
        return self.VEC[:, o:o + n]

    def phase0(self):
        S, nc = self.S, self.nc
        P = Phase(S)
        VEC, MOD = self.VEC, self.MOD
        S.dma([], [VEC], VEC[:], self.dr["vecs"])
        condT = self.condT
        S.op("act", [VEC], [condT], lambda e: e.activation(out=condT[:], in_=self.vcol("cT", 0, 8), func=AF.Silu))
        CB = 1536
        wst = [P.sbuf("adaw%d" % i, [128, KC, CB], F32) for i in range(2)]
        k = 0
        for l in range(1 if self.defer_mod else NL):
            pm = self.ps()
            for cb in range(4):
                w = wst[k % 2]
                k += 1
                src = self.dr["ada_w"][l, :, cb * CB:(cb + 1) * CB].rearrange("(c p) n -> p c n", p=128)
                S.dma([], [w], w[:], src)
                def mm(e, w=w, cb=cb, pm=pm):
                    last = None
                    for j in range(12):
                        col = cb * 12 + j
                        for kc in range(KC):
                            last = e.matmul(pm[:, col:col + 1], lhsT=w[:, kc, j * 128:(j + 1) * 128],
                                            rhs=condT[:, kc:kc + 1], start=(kc == 0), stop=(kc == KC - 1))
                    return last
                S.op("pe", [w, condT], [pm], mm)
            S.op("dve", [pm, VEC], [MOD], lambda e, l=l, pm=pm: e.tensor_tensor(
                out=MOD[:, l * 48:(l + 1) * 48], in0=pm[:, 0:48], in1=self.vcol("adab", l * 48, 48), op=ALU.add))
            S.op("dve", [MOD, VEC], [self.AM], lambda e, l=l: e.scalar_tensor_tensor(
                out=self.AM[:, l * 8:(l + 1) * 8], in0=MOD[:, l * 48 + 8:l * 48 + 16], scalar=1.0,
                in1=self.vcol("gm", l * 8, 8), op0=ALU.add, op1=ALU.mult))
            S.op("dve", [MOD, VEC], [self.AFc], lambda e, l=l: e.scalar_tensor_tensor(
                out=self.AFc[:, l * 8:(l + 1) * 8], in0=MOD[:, l * 48 + 32:l * 48 + 40], scalar=1.0,
                in1=self.vcol("gf", l * 8, 8), op0=ALU.add, op1=ALU.mult))
        posi = P.sbuf("posi", [1, T], I32)
        posf = P.sbuf("posf", [1, T], F32)
        S.dma([], [posi], posi[:], self.dr["pos"])
        S.op("dve", [posi], [posf], lambda e: e.tensor_copy(out=posf[:], in_=posi[:]))
        angs = [P.sbuf("ang%d" % k, [128, 512], F32) for k in range(2)]
        rbuf = {}
        for wh in ("SIN", "COS"):
            for k in range(2):
                rbuf[(wh, k)] = (P.sbuf("r%s%d" % (wh, k), [128, 512], F32), P.sbuf("xs%s%d" % (wh, k), [128, 512], F32),
                                 P.sbuf("ki%s%d" % (wh, k), [128, 512], I32))
        for tt in range(8):
            sl = slice(tt * 512, (tt + 1) * 512)
            pb = self.ps()
            S.op("pe", [posf, self.ones32], [pb], lambda e, pb=pb, sl=sl: e.matmul(
                pb[:, :], lhsT=self.ones32[0:1, :], rhs=posf[0:1, sl], start=True, stop=True))
            ang = angs[tt % 2]
            S.op("dve", [pb, VEC], [ang], lambda e, pb=pb, ang=ang: e.tensor_scalar(
                out=ang[:], in0=pb[:, :], scalar1=self.vcol("invf"), scalar2=None, op0=ALU.mult))
            for which, shift in (("SIN", 0.0), ("COS", 0.5 * np.pi)):
                r, xs, ki = rbuf[(which, tt % 2)]
                C1 = 6.28125
                C2 = float(2 * np.pi - 6.28125)
                S.op("dve", [ang], [xs], lambda e, xs=xs, ang=ang, shift=shift: e.tensor_scalar(
                    out=xs[:], in0=ang[:], scalar1=float(shift), scalar2=None, op0=ALU.add))
                S.op("dve", [xs], [r], lambda e, r=r, xs=xs: e.tensor_scalar(
                    out=r[:], in0=xs[:], scalar1=float(1.0 / (2 * np.pi)), scalar2=None, op0=ALU.mult))
                S.op("dve", [r], [ki], lambda e, r=r, ki=ki: e.tensor_copy(out=ki[:], in_=r[:]))
                S.op("dve", [ki], [r], lambda e, r=r, ki=ki: e.tensor_copy(out=r[:], in_=ki[:]))
                S.op("dve", [r, xs], [xs], lambda e, r=r, xs=xs: e.scalar_tensor_tensor(
                    out=xs[:], in0=r[:], scalar=-C1, in1=xs[:], op0=ALU.mult, op1=ALU.add))
                S.op("dve", [r, xs], [xs], lambda e, r=r, xs=xs: e.scalar_tensor_tensor(
                    out=xs[:], in0=r[:], scalar=-C2, in1=xs[:], op0=ALU.mult, op1=ALU.add))
                S.op("dve", [xs], [xs], lambda e, xs=xs: e.tensor_scalar(
                    out=xs[:], in0=xs[:], scalar1=-3.14159, scalar2=3.14159, op0=ALU.max, op1=ALU.min))
                S.op("act", [xs], [r], lambda e, r=r, xs=xs: e.activation(out=r[:], in_=xs[:], func=AF.Sin))
                if which == "SIN":
                    S.op("dve", [r, VEC], [r], lambda e, r=r: e.tensor_scalar(
                        out=r[:], in0=r[:], scalar1=self.vcol("sinsign"), scalar2=None, op0=ALU.mult))
                S.dma([r], [], self.dr[which][:, sl], r[:])
        P.close()

    def load_weight_bf16(self, P, Wb, src3, ncols, stages, col0=0):
        S = self.S
        nk = src3.shape[1]
        CB = stages[0].t.shape[2]
        c = 0
        i = 0
        while c < ncols:
            n = min(CB, ncols - c)
            st = stages[i % len(stages)]
            i += 1
            S.dma([], [st], st[:, 0:nk, 0:n], src3[:, :, c:c + n])
            S.op("pool", [st], [Wb], lambda e, st=st, c=c, n=n: e.tensor_copy(
                out=Wb[:, 0:nk, col0 + c:col0 + c + n], in_=st[:, 0:nk, 0:n]))
            c += n

    def norm_tile(self, P, X, A, B, h, sq, tmps, rstd, n=512):
        S = self.S
        S.op("act", [X], [sq], lambda e: e.activation(out=sq[:, :, 0:n], in_=X[:, :, 0:n], func=AF.Square))
        pss = self.ps()
        def mm(e):
            last = None
            for kc in range(KC):
                last = e.matmul(pss[:, 0:n], lhsT=(self.ones32 if sq.t.dtype == F32 else self.onesbf)[:, :], rhs=sq[:, kc, 0:n],
                                start=(kc == 0), stop=(kc == KC - 1))
            return last
        S.op("pe", [sq, self.ones32, self.onesbf], [pss], mm)
        S.op("dve", [pss], [rstd], lambda e: e.tensor_scalar(
            out=rstd[:, 0:n], in0=pss[:, 0:n], scalar1=1.0 / D, scalar2=EPS, op0=ALU.mult, op1=ALU.add))
        S.op("act", [rstd], [rstd], lambda e: e.activation(out=rstd[:, 0:n], in_=rstd[:, 0:n], func=AF.Sqrt))
        S.op("dve", [rstd], [rstd], lambda e: e.reciprocal(out=rstd[:, 0:n], in_=rstd[:, 0:n]))
        for kc in range(KC):
            tmp = tmps[kc % len(tmps)]
            S.op("dve", [X, rstd, self.AM, self.AFc], [tmp], lambda e, kc=kc, tmp=tmp: e.scalar_tensor_tensor(
                out=tmp[:, 0:n], in0=X[:, kc, 0:n], scalar=A[:, kc:kc + 1], in1=rstd[:, 0:n], op0=ALU.mult, op1=ALU.mult))
            S.op("act", [tmp, self.MOD], [h], lambda e, kc=kc, tmp=tmp: e.activation(
                out=h[:, kc, 0:n], in_=tmp[:, 0:n], func=AF.Identity, bias=B[:, kc:kc + 1], scale=1.0))

    def phase1(self, l):
        S = self.S
        P = Phase(S)
        attn = (l % 2 == 0)
        i = l // 2
        NC_ = 4096 if attn else 4112
        Wb = P.sbuf("W1", [128, KC, NC_], BF16)
        stages = [P.sbuf("wst%d" % k, [128, KC, 512], F32) for k in range(2)]
        wsrc = self.dr["attn_w_in" if attn else "gdn_w_in"][i].rearrange("(c p) n -> p c n", p=128)
        self.load_weight_bf16(P, Wb, wsrc, NC_, stages)
        Xb = [P.sbuf("X%d" % k, [128, KC, 512], F32) for k in range(2)]
        sq = P.sbuf("sq", [128, KC, 512], F32)
        tmps = [P.sbuf("tmp%d" % k, [128, 512], F32) for k in range(2)]
        rstd = P.sbuf("rstd", [128, 512], F32)
        h = P.sbuf("h", [128, KC, 512], BF16)
        stg = [P.sbuf("stg%d" % k, [128, 512], BF16) for k in range(4)]
        stg32 = [P.sbuf("stgf%d" % k, [128, 16], F32) for k in range(2)]
        rt = [P.sbuf("rt%d" % k, [128, 512], F32) for k in range(4)]
        cs = [[P.sbuf("cs%d_%d" % (a, k), [128, 512], F32) for k in range(2)] for a in range(2)]
        A = self.AM[:, l * 8:(l + 1) * 8]
        B = self.MOD[:, l * 48:l * 48 + 8]
        XT3 = self.dr["xT" if l == 0 else "XT"].rearrange("(c p) t -> p c t", p=128)
        NT = T // 512
        S.dma([], [Xb[0]], Xb[0][:], XT3[:, :, 0:512])
        sk = 0
        ev = 0
        for tt in range(NT):
            sl = slice(tt * 512, (tt + 1) * 512)
            X = Xb[tt % 2]
            if tt + 1 < NT:
                S.dma([], [Xb[(tt + 1) % 2]], Xb[(tt + 1) % 2][:], XT3[:, :, (tt + 1) * 512:(tt + 2) * 512])
            if attn:
                cosb, sinb = cs[0][tt % 2], cs[1][tt % 2]
                S.dma([], [cosb], cosb[:], self.dr["COS"][:, sl])
                S.dma([], [sinb], sinb[:], self.dr["SIN"][:, sl])
            self.norm_tile(P, X, A, B, h, sq, tmps, rstd)

            def proj_fm(col0):
                pp = self.ps()
                def mm(e, pp=pp, col0=col0):
                    last = None
                    for kc in range(KC):
                        last = e.matmul(pp[:, :], lhsT=Wb[:, kc, col0:col0 + 128], rhs=h[:, kc, :],
                                        start=(kc == 0), stop=(kc == KC - 1))
                    return last
                S.op("pe", [Wb, h], [pp], mm)
                return pp

            def evac(pp, dst_ap, scale, width=512):
                nonlocal sk, ev
                st = stg[sk % 4]
                sk += 1
                if ev % 2 == 0:
                    S.op("act", [pp], [st], lambda e: e.activation(out=st[:, 0:width], in_=pp[:, 0:width], func=AF.Copy, scale=float(scale)))
                else:
                    S.op("dve", [pp], [st], lambda e: e.tensor_scalar(out=st[:, 0:width], in0=pp[:, 0:width], scalar1=float(scale), scalar2=None, op0=ALU.mult))
                ev += 1
                S.dma([st], [], dst_ap, st[:, 0:width])

            def proj_tm(col0, ncols, dst, dcol0, f32out=False):
                nonlocal sk
                for s in range(4):
                    pp = self.ps()
                    def mm(e, pp=pp, s=s):
                        last = None
                        for kc in range(KC):
                            last = e.matmul(pp[:, 0:ncols], lhsT=h[:, kc, s * 128:(s + 1) * 128], rhs=Wb[:, kc, col0:col0 + ncols],
                                            start=(kc == 0), stop=(kc == KC - 1))
                        return last
                    S.op("pe", [Wb, h], [pp], mm)
                    r0 = tt * 512 + s * 128
                    if f32out:
                        st = stg32[s % 2]
                        S.op("dve", [pp], [st], lambda e, st=st, pp=pp: e.tensor_copy(out=st[:, 0:ncols], in_=pp[:, 0:ncols]))
                        S.dma([st], [], dst[r0:r0 + 128, dcol0:dcol0 + ncols], st[:, 0:ncols])
                    else:
                        evac(pp, dst[r0:r0 + 128, dcol0:dcol0 + ncols], 1.0, ncols)

            if attn:
                QT = self.dr["QT"]
                for grp, (c0, pc0, d0, scale) in enumerate([(0, 3072, 0, 0.125), (512, 3584, 512, 1.0)]):
                    for n in range(4):
                        pq = proj_fm(c0 + n * 128)
                        pp = proj_fm(pc0 + n * 128)
                        t1, t2 = rt[(2 * n) % 4], rt[(2 * n + 1) % 4]
                        st = stg[sk % 4]
                        sk += 1
                        S.op("dve", [pq, cosb], [t1], lambda e, pq=pq, t1=t1, scale=scale: e.scalar_tensor_tensor(
                            out=t1[:], in0=pq[:, :], scalar=float(scale), in1=cosb[:], op0=ALU.mult, op1=ALU.mult))
                        S.op("dve", [pp, sinb], [t2], lambda e, pp=pp, t2=t2, scale=scale: e.scalar_tensor_tensor(
                            out=t2[:], in0=pp[:, :], scalar=float(scale), in1=sinb[:], op0=ALU.mult, op1=ALU.mult))
                        S.op("pool", [t1, t2], [st], lambda e, st=st, t1=t1, t2=t2: e.tensor_tensor(
                            out=st[:], in0=t1[:], in1=t2[:], op=ALU.add))
                        S.dma([st], [], QT[d0 + n * 128:d0 + (n + 1) * 128, sl], st[:])
                for n in range(4):
                    evac(proj_fm(1536 + n * 128), QT[1024 + n * 128:1024 + (n + 1) * 128, sl], 0.125)
                for n in range(4):
                    evac(proj_fm(2048 + n * 128), QT[1536 + n * 128:1536 + (n + 1) * 128, sl], 1.0)
                proj_tm(1024, 512, self.dr["VTOK"], 0)
                proj_tm(2560, 512, self.dr["VTOK"], 512)
            else:
                GT = self.dr["GT"]
                for n in range(24):
                    evac(proj_fm(n * 128), GT[n * 128:(n + 1) * 128, sl], 1.0)
                proj_tm(3072, 512, self.dr["VTOK"], 0)
                proj_tm(3584, 512, self.dr["VTOK"], 512)
                proj_tm(4096, 16, self.dr["AB"], 0, f32out=True)
        P.close()

    def phase34(self, l, final=False, tiles=None):
        S = self.S
        P = Phase(S)
        i = l // 2
        TT = 256
        Wo = P.sbuf("Wo", [128, KC, D], BF16)
        Wi = P.sbuf("Wi", [128, KC, 2 * FH], BF16)
        W2 = P.sbuf("W2", [128, NJ, D], BF16)
        PW = Phase(S)
        stA = [PW.sbuf("stA%d" % k, [128, KC, 512], F32) for k in range(2)]
        stB = [PW.sbuf("stB%d" % k, [128, NJ, 128], F32) for k in range(2)]
        wo_src = self.dr["attn_w_out" if l % 2 == 0 else "gdn_w_out"][i].rearrange("(c p) n -> p c n", p=128)
        self.load_weight_bf16(PW, Wo, wo_src, D, stA)
        self.load_weight_bf16(PW, Wi, self.dr["ffn_w_in"][l].rearrange("(c p) n -> p c n", p=128), 2 * FH, stA)
        self.load_weight_bf16(PW, W2, self.dr["ffn_w_out"][l].rearrange("(c p) n -> p c n", p=128), D, stB)
        PW.close()
        Xb = [P.sbuf("X%d" % k, [128, KC, TT], F32) for k in range(2)]
        ob = [P.sbuf("oT%d" % k, [128, KC, TT], BF16) for k in range(2)]
        sq = P.sbuf("sq", [128, KC, TT], BF16)
        tmps = [P.sbuf("tmp%d" % k, [128, TT], F32) for k in range(2)]
        sgs = [P.sbuf("sg%d" % k, [128, TT], F32) for k in range(2)]
        rstds = [P.sbuf("rstd%d" % k, [128, TT], F32) for k in range(2)]
        hb = [P.sbuf("h%d" % k, [128, KC, TT], BF16) for k in range(2)]
        hid = P.sbuf("hid", [128, NJ, TT], BF16)
        GM = self.MOD[:, l * 48 + 16:l * 48 + 24]
        GF = self.MOD[:, l * 48 + 40:l * 48 + 48]
        A = self.AFc[:, l * 8:(l + 1) * 8]
        B = self.MOD[:, l * 48 + 24:l * 48 + 32]
        XS3 = self.dr["xT" if l == 0 else "XT"].rearrange("(c p) t -> p c t", p=128)
        XD3 = self.dr["XT"].rearrange("(c p) t -> p c t", p=128)
        OT3 = self.dr["OT"].rearrange("(c p) t -> p c t", p=128)
        if final:
            OUT3 = self.dr["outT"].rearrange("(c p) t -> p c t", p=128)
        NT = T // TT
        tl = list(range(NT)) if tiles is None else list(tiles)

        def front(ti):
            tt = tl[ti]
            sl = slice(tt * TT, (tt + 1) * TT)
            X, oT, h = Xb[ti % 2], ob[ti % 2], hb[ti % 2]
            S.dma([], [oT], oT[:], OT3[:, :, sl])
            S.dma([], [X], X[:], XS3[:, :, sl])
            for n in range(KC):
                pp = self.ps()
                def mm(e, pp=pp, n=n):
                    last = None
                    for kc in range(KC):
                        last = e.matmul(pp[:, 0:TT], lhsT=Wo[:, kc, n * 128:(n + 1) * 128], rhs=oT[:, kc, :],
                                        start=(kc == 0), stop=(kc == KC - 1))
                    return last
                S.op("pe", [Wo, oT], [pp], mm)
                S.op("dve", [pp, X, self.MOD], [X], lambda e, pp=pp, n=n: e.scalar_tensor_tensor(
                    out=X[:, n, :], in0=pp[:, 0:TT], scalar=GM[:, n:n + 1], in1=X[:, n, :], op0=ALU.mult, op1=ALU.add))
            self.norm_tile(P, X, A, B, h, sq, tmps, rstds[ti % 2], n=TT)

        def back(ti):
            tt = tl[ti]
            sl = slice(tt * TT, (tt + 1) * TT)
            X, h = Xb[ti % 2], hb[ti % 2]
            for j in range(NJ):
                pg = self.ps()
                pu = self.ps()
                def mmg(e, pg=pg, j=j):
                    last = None
                    for kc in range(KC):
                        last = e.matmul(pg[:, 0:TT], lhsT=Wi[:, kc, j * 128:(j + 1) * 128], rhs=h[:, kc, :],
                                        start=(kc == 0), stop=(kc == KC - 1))
                    return last
                def mmu(e, pu=pu, j=j):
                    last = None
                    for kc in range(KC):
                        last = e.matmul(pu[:, 0:TT], lhsT=Wi[:, kc, FH + j * 128:FH + (j + 1) * 128], rhs=h[:, kc, :],
                                        start=(kc == 0), stop=(kc == KC - 1))
                    return last
                S.op("pe", [Wi, h], [pg], mmg)
                S.op("pe", [Wi, h], [pu], mmu)
                sg = sgs[j % 2]
                S.op("act", [pg], [sg], lambda e, pg=pg, sg=sg: e.activation(out=sg[:], in_=pg[:, 0:TT], func=AF.Silu))
                S.op("dve", [sg, pu], [hid], lambda e, pu=pu, sg=sg, j=j: e.tensor_tensor(
                    out=hid[:, j, :], in0=pu[:, 0:TT], in1=sg[:], op=ALU.mult))
            for n in range(KC):
                pp = self.ps()
                def mm2(e, pp=pp, n=n):
                    last = None
                    for j in range(NJ):
                        last = e.matmul(pp[:, 0:TT], lhsT=W2[:, j, n * 128:(n + 1) * 128], rhs=hid[:, j, :],
                                        start=(j == 0), stop=(j == NJ - 1))
                    return last
                S.op("pe", [W2, hid], [pp], mm2)
                S.op("dve", [pp, X, self.MOD], [X], lambda e, pp=pp, n=n: e.scalar_tensor_tensor(
                    out=X[:, n, :], in0=pp[:, 0:TT], scalar=GF[:, n:n + 1], in1=X[:, n, :], op0=ALU.mult, op1=ALU.add))
            if final:
                self.final_norm_tile(X, X, sq, rstds[ti % 2], TT)
                S.dma([X], [], OUT3[:, :, sl], X[:])
            else:
                S.dma([X], [], XD3[:, :, sl], X[:])

        front(0)
        for ti in range(len(tl)):
            if ti + 1 < len(tl):
                front(ti + 1)
            back(ti)
        P.close()

    def final_norm_tile(self, X, yout, sq, rstd, n):
        S = self.S
        S.op("act", [X], [sq], lambda e: e.activation(out=sq[:, :, 0:n], in_=X[:, :, 0:n], func=AF.Square))
        pss = self.ps()
        def mm(e):
            last = None
            for kc in range(KC):
                last = e.matmul(pss[:, 0:n], lhsT=(self.ones32 if sq.t.dtype == F32 else self.onesbf)[:, :], rhs=sq[:, kc, 0:n],
                                start=(kc == 0), stop=(kc == KC - 1))
            return last
        S.op("pe", [sq, self.ones32, self.onesbf], [pss], mm)
        S.op("dve", [pss], [rstd], lambda e: e.tensor_scalar(
            out=rstd[:, 0:n], in0=pss[:, 0:n], scalar1=1.0 / D, scalar2=EPS, op0=ALU.mult, op1=ALU.add))
        S.op("act", [rstd], [rstd], lambda e: e.activation(out=rstd[:, 0:n], in_=rstd[:, 0:n], func=AF.Sqrt))
        S.op("dve", [rstd], [rstd], lambda e: e.reciprocal(out=rstd[:, 0:n], in_=rstd[:, 0:n]))
        for kc in range(KC):
            S.op("dve", [X, rstd, self.VEC], [yout], lambda e, kc=kc: e.scalar_tensor_tensor(
                out=yout[:, kc, 0:n], in0=X[:, kc, 0:n], scalar=self.vcol("fin", kc), in1=rstd[:, 0:n], op0=ALU.mult, op1=ALU.mult))

    def mod_gen(self, P, layers, psfn):
        S = self.S
        stg = [P.sbuf("mst%d" % k, [128, KC, 512], F32) for k in range(2)]
        condT, MOD = self.condT, self.MOD
        k = 0
        for l in layers:
            for cb in range(12):
                w = stg[k % 2]
                k += 1
                S.dma([], [w], w[:], self.dr["ada_w"][l, :, cb * 512:(cb + 1) * 512].rearrange("(c p) n -> p c n", p=128))
                yield
                pm = psfn()
                def mm(e, w=w, pm=pm):
                    last = None
                    for j in range(4):
                        for kc in range(KC):
                            last = e.matmul(pm[:, j:j + 1], lhsT=w[:, kc, j * 128:(j + 1) * 128], rhs=condT[:, kc:kc + 1],
                                            start=(kc == 0), stop=(kc == KC - 1))
                    return last
                S.op("pe", [w, condT], [pm], mm)
                c0 = l * 48 + cb * 4
                S.op("dve", [pm, self.VEC], [MOD], lambda e, pm=pm, c0=c0: e.tensor_tensor(
                    out=MOD[:, c0:c0 + 4], in0=pm[:, 0:4], in1=self.VEC[:, VOFF["adab"] + c0:VOFF["adab"] + c0 + 4], op=ALU.add))
                yield
            S.op("dve", [MOD, self.VEC], [self.AM], lambda e, l=l: e.scalar_tensor_tensor(
                out=self.AM[:, l * 8:(l + 1) * 8], in0=MOD[:, l * 48 + 8:l * 48 + 16], scalar=1.0,
                in1=self.vcol("gm", l * 8, 8), op0=ALU.add, op1=ALU.mult))
            S.op("dve", [MOD, self.VEC], [self.AFc], lambda e, l=l: e.scalar_tensor_tensor(
                out=self.AFc[:, l * 8:(l + 1) * 8], in0=MOD[:, l * 48 + 32:l * 48 + 40], scalar=1.0,
                in1=self.vcol("gf", l * 8, 8), op0=ALU.add, op1=ALU.mult))

    def phase_attn(self, heads_moba=range(8), heads_sb=range(8), qtiles=range(8), mod_layers=()):
        S = self.S
        P = Phase(S)
        QT, VTOK, OT = self.dr["QT"], self.dr["VTOK"], self.dr["OT"]
        cmask = [P.sbuf("cmask%d" % d, [128, 512], BF16) for d in range(4)]
        smask = [P.sbuf("smask%d" % d, [128, 512], BF16) for d in range(4)]
        for d in range(4):
            for mk, op in ((cmask[d], ALU.is_ge), (smask[d], ALU.is_gt)):
                S.seq("pool", mk, [
                    lambda e, mk=mk: e.memset(mk[:], 1.0),
                    lambda e, mk=mk, op=op, d=d: e.affine_select(out=mk[:], in_=mk[:], compare_op=op, fill=0.0, base=-128 * d,
                                                                 pattern=[[1, 512]], channel_multiplier=-1)])
        NUI = P.sbuf("NUI", [128, 128], BF16)
        S.seq("pool", NUI, [
            lambda e: e.memset(NUI[:], -1.0),
            lambda e: e.affine_select(out=NUI[:], in_=NUI[:], compare_op=ALU.is_ge, fill=0.0, base=0,
                                      pattern=[[-1, 128]], channel_multiplier=1)])
        onesb = P.sbuf("onesb", [128, 128], BF16)
        S.op("pool", [], [onesb], lambda e: e.memset(onesb[:], 1.0))
        past01 = P.sbuf("past01", [128, 32, 16], F32)
        pastb = P.sbuf("pastb", [128, 32, 16], F32)
        ownm1 = P.sbuf("ownm1", [128, 32, 16], F32)
        S.seq("pool", past01, [
            lambda e: e.memset(past01[:], 1.0),
            lambda e: e.affine_select(out=past01[:], in_=past01[:], compare_op=ALU.is_ge, fill=0.0, base=-2,
                                      pattern=[[1, 32], [-2, 16]], channel_multiplier=0)])
        S.seq("pool", pastb, [
            lambda e: e.memset(pastb[:], 0.0),
            lambda e: e.affine_select(out=pastb[:], in_=pastb[:], compare_op=ALU.is_ge, fill=-1e30, base=-2,
                                      pattern=[[1, 32], [-2, 16]], channel_multiplier=0)])
        S.seq("pool", ownm1, [
            lambda e: e.memset(ownm1[:], 0.0),
            lambda e: e.affine_select(out=ownm1[:], in_=ownm1[:], compare_op=ALU.is_ge, fill=-1.0, base=0,
                                      pattern=[[1, 32], [-2, 16]], channel_multiplier=0),
            lambda e: e.affine_select(out=ownm1[:], in_=ownm1[:], compare_op=ALU.is_ge, fill=-1.0, base=1,
                                      pattern=[[-1, 32], [2, 16]], channel_multiplier=0)])
        QAUG = [P.sbuf("QAUG%d" % k, [128, T], BF16) for k in range(2)]
        KAUG = [P.sbuf("KAUG%d" % k, [128, T], BF16) for k in range(2)]
        KS = [P.sbuf("KS%d" % k, [128, T], BF16) for k in range(2)]
        KAc = P.sbuf("KAc", [16, T], BF16)
        S.seq("pool", KAc, [
            lambda e: e.memset(KAc[:], BIG),
            lambda e: e.affine_select(out=KAc[:], in_=KAc[:], compare_op=ALU.is_ge, fill=0.0, base=0,
                                      pattern=[[1, T]], channel_multiplier=-256),
            lambda e: e.affine_select(out=KAc[:], in_=KAc[:], compare_op=ALU.is_ge, fill=0.0, base=255,
                                      pattern=[[-1, T]], channel_multiplier=256)])
        for k in range(2):
            S.op("pool", [], [QAUG[k]], lambda e, k=k: e.memset(QAUG[k][:], 0.0))
            S.op("pool", [], [KS[k]], lambda e, k=k: e.memset(KS[k][:], 0.0))
            S.seq("pool", KAUG[k], [
                lambda e, k=k: e.memset(KAUG[k][:], 0.0),
                lambda e, k=k: e.memset(KAUG[k][96:97, :], -1.0)])
            S.dma([KAc], [KAUG[k]], KAUG[k][64:80, :], KAc[:])
        Vb = [P.sbuf("V%d" % k, [128, 32, 65], BF16) for k in range(2)]
        for k in range(2):
            S.op("pool", [], [Vb[k]], lambda e, k=k: e.memset(Vb[k][:], 1.0))
        sqQ = P.sbuf("sqQ", [64, T], BF16)
        kmean = P.sbuf("kmean", [64, 16], F32)
        kmeanb = P.sbuf("kmeanb", [64, 16], BF16)
        gm = P.sbuf("gm", [128, 32, 16], F32)
        m8 = P.sbuf("m8", [128, 32, 8], F32)
        MBw = P.sbuf("MBw", [128, 32, 97], F32)
        S.op("pool", [], [MBw], lambda e: e.memset(MBw[:], 0.0))
        kmx = P.sbuf("kmx", [128, 8], F32)
        kmax2 = P.sbuf("kmax2", [128, 1], F32)
        Pt = [P.sbuf("Pt%d" % k, [128, 512], BF16) for k in range(3)]
        Eb = [P.sbuf("E%d" % k, [128, 512], F32) for k in range(2)]
        Lp = [P.sbuf("Lp%d" % k, [128, 512], BF16) for k in range(3)]
        exb = [P.sbuf("ex%d" % k, [128, 512], F32) for k in range(2)]
        rs = P.sbuf("rs", [128, 512], F32)
        Osb = P.sbuf("Osb", [65, 512], F32)
        rden = P.sbuf("rden", [128, 512], F32)
        S.op("pool", [], [rden], lambda e: e.memset(rden[:], 0.0))
        sel64 = P.sbuf("sel64", [128, 128], F32)
        S.seq("pool", sel64, [lambda e: e.memset(sel64[:], 0.0), lambda e: e.memset(sel64[64:65, :], 1.0)])
        ost = [P.sbuf("ost%d" % k, [64, 512], BF16) for k in range(2)]
        PSr = self.PS[0:6]
        PSo = self.PS[6:8]
        rr = [0]
        def psr():
            b = PSr[rr[0] % 6]
            rr[0] += 1
            return b
        jobs = [("m", h) for h in heads_moba] + [("s", h) for h in heads_sb]
        mg = self.mod_gen(P, list(mod_layers), psr) if mod_layers else None
        def mg_step():
            nonlocal mg
            if mg is not None:
                try:
                    next(mg)
                except StopIteration:
                    mg = None
        def load_head(idx):
            kind, h = jobs[idx]
            k = idx % 2
            qrow = (0 if kind == "m" else 1024) + h * 64
            krow = (512 if kind == "m" else 1536) + h * 64
            vcol = (0 if kind == "m" else 512) + h * 64
            kdst = KAUG[k] if kind == "m" else KS[k]
            S.dma([], [QAUG[k]], QAUG[k][0:64, :], QT[qrow:qrow + 64, :])
            S.dma([], [kdst], kdst[0:64, :], QT[krow:krow + 64, :])
            S.dma([], [Vb[k]], Vb[k][:, :, 0:64], VTOK[:, vcol:vcol + 64].rearrange("(j p) d -> p j d", p=128))
        load_head(0)
        oi = 0
        pre_done = set()
        for idx, (kind, h) in enumerate(jobs):
            Q, V = QAUG[idx % 2], Vb[idx % 2]
            Kt = KAUG[idx % 2] if kind == "m" else KS[idx % 2]
            if idx + 1 < len(jobs):
                load_head(idx + 1)
            if kind == "m":
                def prelude(Q, Kt):
                    S.op("dve", [Kt], [kmean], lambda e: e.tensor_reduce(
                        out=kmean[:], in_=Kt[0:64, :].rearrange("p (n k) -> p n k", k=256), axis=AX.X, op=ALU.add))
                    S.op("dve", [kmean], [kmeanb], lambda e: e.tensor_scalar(
                        out=kmeanb[:], in0=kmean[:], scalar1=1.0 / 256, scalar2=None, op0=ALU.mult))
                    yield
                    GP = psr()
                    def mm(e):
                        last = None
                        for tt in range(32):
                            last = e.matmul(GP[:, tt * 16:(tt + 1) * 16], lhsT=Q[0:64, tt * 128:(tt + 1) * 128], rhs=kmeanb[:, :],
                                            start=True, stop=True)
                        return last
                    S.op("pe", [Q, kmeanb], [GP], mm)
                    S.op("dve", [GP, pastb], [gm], lambda e: e.tensor_tensor(
                        out=gm[:].rearrange("p a b -> p (a b)"), in0=GP[:, :], in1=pastb[:].rearrange("p a b -> p (a b)"), op=ALU.add))
                    yield
                    for tt in range(32):
                        S.op("dve", [gm], [m8], lambda e, tt=tt: e.max(out=m8[:, tt, :], in_=gm[:, tt, :]))
                    yield
                    for tt in range(32):
                        S.op("dve", [gm, m8, past01], [MBw], lambda e, tt=tt: e.scalar_tensor_tensor(
                            out=MBw[:, tt, 64:80], in0=gm[:, tt, :], scalar=m8[:, tt, 2:3], in1=past01[:, tt, :], op0=ALU.is_ge, op1=ALU.mult))
                    S.op("dve", [MBw, ownm1], [MBw], lambda e: e.tensor_tensor(
                        out=MBw[:, :, 64:80], in0=MBw[:, :, 64:80], in1=ownm1[:], op=ALU.add))
                    yield
                    S.op("pool", [Kt], [sqQ], lambda e: e.tensor_tensor(out=sqQ[:], in0=Kt[0:64, :], in1=Kt[0:64, :], op=ALU.mult))
                    for g in range(8):
                        KP = psr()
                        S.op("pe", [sqQ, onesb], [KP], lambda e, KP=KP, g=g: e.matmul(
                            KP[:, :], lhsT=onesb[0:64, :], rhs=sqQ[:, g * 512:(g + 1) * 512], start=True, stop=True))
                        S.op("dve", [KP], [kmx], lambda e, KP=KP, g=g: e.tensor_reduce(
                            out=kmx[:, g:g + 1], in_=KP[:, :], axis=AX.X, op=ALU.max))
                    yield
                    S.op("dve", [kmx], [kmax2], lambda e: e.tensor_reduce(out=kmax2[:], in_=kmx[:], axis=AX.X, op=ALU.max))
                    S.op("pool", [Q], [sqQ], lambda e: e.tensor_tensor(out=sqQ[:], in0=Q[0:64, :], in1=Q[0:64, :], op=ALU.mult))
                    yield
                    QQ = psr()
                    def mmq(e):
                        last = None
                        for tt in range(32):
                            last = e.matmul(QQ[:, tt:tt + 1], lhsT=sqQ[:, tt * 128:(tt + 1) * 128], rhs=onesb[0:64, 0:1], start=True, stop=True)
                        return last
                    S.op("pe", [sqQ, onesb], [QQ], mmq)
                    S.op("act", [QQ, kmax2], [MBw], lambda e: e.activation(
                        out=MBw[:, :, 96], in_=QQ[:, 0:32], func=AF.Sqrt, scale=kmax2[:, 0:1]))
                    yield
                    for g in range(8):
                        TP = psr()
                        def tr(e, TP=TP, g=g):
                            last = None
                            for k in range(4):
                                last = e.transpose(out=TP[0:97, k * 128:(k + 1) * 128], in_=MBw[:, 4 * g + k, :], identity=self.ident32[:, :])
                            return last
                        S.op("pe", [MBw, self.ident32], [TP], tr)
                        S.op("dve", [TP], [Q], lambda e, TP=TP, g=g: e.tensor_copy(out=Q[64:97, g * 512:(g + 1) * 512], in_=TP[64:97, :]))

                if idx not in pre_done:
                    for _ in prelude(Q, Kt):
                        pass
                nxt = None
                if idx + 1 < len(jobs) and jobs[idx + 1][0] == "m":
                    nxt = prelude(QAUG[(idx + 1) % 2], KAUG[(idx + 1) % 2])
                    pre_done.add(idx + 1)
                pairs = [(i, j) for i in qtiles for j in range(4 * i + 4)]
                st = {}
                def stageA(p):
                    i, j = pairs[p]
                    sp = psr()
                    S.op("pe", [Kt, Q], [sp], lambda e: e.matmul(
                        sp[:, :], lhsT=Kt[:, j * 128:(j + 1) * 128], rhs=Q[:, i * 512:(i + 1) * 512], start=True, stop=True))
                    pt = Pt[p % 3]
                    S.op("act", [sp], [pt], lambda e: e.activation(out=pt[:], in_=sp[:, :], func=AF.Exp))
                    if j >= 4 * i:
                        S.op("dve", [pt, cmask[j - 4 * i]], [pt], lambda e: e.tensor_tensor(out=pt[:], in0=pt[:], in1=cmask[j - 4 * i][:], op=ALU.mult))
                    st[p] = pt
                def stageC(p):
                    nonlocal oi
                    i, j = pairs[p]
                    pt = st.pop(p)
                    last = 4 * i + 3
                    if j == 0:
                        oi += 1
                    OP = PSo[oi % 2]
                    S.op("pe", [V, pt], [OP], lambda e: e.matmul(OP[0:65, :], lhsT=V[:, j, :], rhs=pt[:], start=(j == 0), stop=(j == last)))
                    if j == last:
                        S.op("act", [OP], [Osb], lambda e: e.activation(out=Osb[:], in_=OP[0:65, :], func=AF.Copy))
                        if getattr(self, "debug", False) and i == 0:
                            self.dbg("dbgOsb", Osb, Osb[:])
                        S.op("dve", [Osb], [rden], lambda e: e.reciprocal(out=rden[64:65, :], in_=Osb[64:65, :]))
                        BP = psr()
                        S.op("pe", [rden, sel64], [BP], lambda e: e.matmul(
                            BP[:, :], lhsT=sel64[:, :], rhs=rden[:, :], start=True, stop=True))
                        o = ost[i % 2]
                        S.op("dve", [Osb, BP], [o], lambda e: e.tensor_tensor(out=o[:], in0=Osb[0:64, :], in1=BP[0:64, :], op=ALU.mult))
                        S.dma([o], [], OT[h * 64:(h + 1) * 64, i * 512:(i + 1) * 512], o[:])
                for p in range(len(pairs) + 2):
                    if p < len(pairs):
                        stageA(p)
                    if p >= 2:
                        stageC(p - 2)
                    if nxt is not None and p % 8 == 4:
                        try:
                            next(nxt)
                        except StopIteration:
                            nxt = None
                    if p % 16 == 9:
                        mg_step()
                if nxt is not None:
                    for _ in nxt:
                        pass
            else:
                pairs = [(i, j) for i in qtiles for j in range(4 * i + 3, -1, -1)]
                st = {}
                def stageA(p):
                    i, j = pairs[p]
                    z = psr()
                    S.op("pe", [Kt, Q], [z], lambda e: e.matmul(
                        z[:, :], lhsT=Kt[:, j * 128:(j + 1) * 128], rhs=Q[:, i * 512:(i + 1) * 512], start=True, stop=False))
                    E = Eb[p % 2]
                    lp = Lp[p % 3]
                    S.op("act", [z], [E], lambda e: e.activation(out=E[:], in_=z[:, :], func=AF.Exp))
                    S.op("act", [E], [lp], lambda e: e.activation(out=lp[:], in_=E[:], func=AF.Ln, bias=1.0, scale=1.0))
                    if j >= 4 * i:
                        S.op("pool", [lp, smask[j - 4 * i]], [lp], lambda e: e.tensor_tensor(out=lp[:], in0=lp[:], in1=smask[j - 4 * i][:], op=ALU.mult))
                    st[p] = (z, lp)
                def stageB(p):
                    i, j = pairs[p]
                    z, lp = st[p]
                    first = (j == 4 * i + 3)
                    S.op("pe", [NUI, lp], [z], lambda e: e.matmul(z[:, :], lhsT=NUI[:, :], rhs=lp[:], start=False, stop=True))
                    cs = psr()
                    S.op("pe", [onesb, lp], [cs], lambda e: e.matmul(cs[:, :], lhsT=onesb[:, :], rhs=lp[:], start=True, stop=True))
                    ex = exb[p % 2]
                    if first:
                        S.op("dve", [z], [ex], lambda e: e.tensor_copy(out=ex[:], in_=z[:, :]))
                        S.op("dve", [cs], [rs], lambda e: e.tensor_copy(out=rs[:], in_=cs[:, :]))
                    else:
                        S.op("dve", [z, rs], [ex], lambda e: e.tensor_tensor(out=ex[:], in0=z[:, :], in1=rs[:], op=ALU.subtract))
                        S.op("dve", [cs, rs], [rs], lambda e: e.tensor_tensor(out=rs[:], in0=cs[:, :], in1=rs[:], op=ALU.add))
                    w = Pt[p % 3]
                    S.op("act", [ex], [w], lambda e: e.activation(out=w[:], in_=ex[:], func=AF.Exp))
                    if j >= 4 * i:
                        S.op("pool", [w, smask[j - 4 * i]], [w], lambda e: e.tensor_tensor(out=w[:], in0=w[:], in1=smask[j - 4 * i][:], op=ALU.mult))
                    st[p] = w
                def stageC(p):
                    nonlocal oi
                    i, j = pairs[p]
                    w = st.pop(p)
                    first = (j == 4 * i + 3)
                    if first:
                        oi += 1
                    OP = PSo[oi % 2]
                    S.op("pe", [V, w], [OP], lambda e: e.matmul(OP[0:64, :], lhsT=V[:, j, 0:64], rhs=w[:], start=first, stop=(j == 0)))
                    if j == 0:
                        o = ost[i % 2]
                        S.op("act", [OP], [o], lambda e: e.activation(out=o[:], in_=OP[0:64, :], func=AF.Copy))
                        S.dma([o], [], OT[512 + h * 64:512 + (h + 1) * 64, i * 512:(i + 1) * 512], o[:])
                n = len(pairs)
                for p in range(n + 2):
                    if p % 16 == 9:
                        mg_step()
                    if p < n:
                        stageA(p)
                    if 1 <= p <= n:
                        stageB(p - 1)
                    if p >= 2:
                        stageC(p - 2)
        while mg is not None:
            mg_step()
        P.close()

    def phase_gdn(self, li, heads=range(8), nchunks=32):
        S = self.S
        P = Phase(S)
        GT, ZT, AB, OT = self.dr["GT"], self.dr["VTOK"], self.dr["AB"], self.dr["OT"]
        NCH = 32
        C = 128
        UT = P.sbuf("UT", [128, 128], F32)
        SU = P.sbuf("SU", [128, 128], F32)
        S.seq("pool", UT, [
            lambda e: e.memset(UT[:], 1.0),
            lambda e: e.affine_select(out=UT[:], in_=UT[:], compare_op=ALU.is_ge, fill=0.0, base=0,
                                      pattern=[[1, 128]], channel_multiplier=-1)])
        S.seq("pool", SU, [
            lambda e: e.memset(SU[:], 1.0),
            lambda e: e.affine_select(out=SU[:], in_=SU[:], compare_op=ALU.is_gt, fill=0.0, base=0,
                                      pattern=[[-1, 128]], channel_multiplier=1)])
        I32m = self.ident32
        ab = P.sbuf("ab", [128, NCH, 16], F32)
        S.dma([], [ab], ab[:], AB.rearrange("(c p) n -> p c n", p=128))
        graw = P.sbuf("graw", [128, NCH, 8], F32)
        beta = P.sbuf("beta", [128, NCH, 8], F32)
        nbeta = P.sbuf("nbeta", [128, NCH, 8], F32)
        negA = P.sbuf("negA", [128, 8], F32)
        tmpa = P.sbuf("tmpa", [128, NCH, 8], F32)
        S.op("act", [self.VEC], [negA], lambda e: e.activation(out=negA[:], in_=self.vcol("alog", li * 8, 8), func=AF.Exp))
        S.op("dve", [negA], [negA], lambda e: e.tensor_scalar(out=negA[:], in0=negA[:], scalar1=-1.0, scalar2=None, op0=ALU.mult))
        for c in range(NCH):
            S.op("dve", [ab, self.VEC], [tmpa], lambda e, c=c: e.tensor_tensor(
                out=tmpa[:, c, :], in0=ab[:, c, 0:8], in1=self.vcol("dtb", li * 8, 8), op=ALU.add))
        S.op("act", [tmpa], [tmpa], lambda e: e.activation(out=tmpa[:], in_=tmpa[:], func=AF.Exp))
        S.op("act", [tmpa], [tmpa], lambda e: e.activation(out=tmpa[:], in_=tmpa[:], func=AF.Ln, bias=1.0, scale=1.0))
        for c in range(NCH):
            S.op("dve", [tmpa, negA], [graw], lambda e, c=c: e.tensor_tensor(
                out=graw[:, c, :], in0=tmpa[:, c, :], in1=negA[:], op=ALU.mult))
        S.op("act", [ab], [beta], lambda e: e.activation(out=beta[:], in_=ab[:, :, 8:16], func=AF.Sigmoid))
        S.op("dve", [beta], [nbeta], lambda e: e.tensor_scalar(out=nbeta[:], in0=beta[:], scalar1=-1.0, scalar2=None, op0=ALU.mult))
        gc = P.sbuf("gc", [128, NCH, 8], F32)
        egc = P.sbuf("egc", [128, NCH, 8], F32)
        egl = P.sbuf("egl", [128, NCH, 8], F32)
        edec = P.sbuf("edec", [128, NCH, 8], F32)
        bgc = P.sbuf("bgc", [128, NCH, 8], F32)
        pgc = self.ps()
        S.op("pe", [UT, graw], [pgc], lambda e: e.matmul(pgc[:, 0:256], lhsT=UT[:, :], rhs=graw[:].rearrange("p a b -> p (a b)"), start=True, stop=True))
        pgl = self.ps()
        S.op("pe", [self.ones32, graw], [pgl], lambda e: e.matmul(pgl[:, 0:256], lhsT=self.ones32[:, :], rhs=graw[:].rearrange("p a b -> p (a b)"), start=True, stop=True))
        fl = lambda b: b[:].rearrange("p a b -> p (a b)")
        S.op("dve", [pgc], [gc], lambda e: e.tensor_copy(out=fl(gc), in_=pgc[:, 0:256]))
        S.op("act", [pgc], [egc], lambda e: e.activation(out=fl(egc), in_=pgc[:, 0:256], func=AF.Exp))
        S.op("act", [pgl], [egl], lambda e: e.activation(out=fl(egl), in_=pgl[:, 0:256], func=AF.Exp))
        S.op("dve", [pgl, gc], [edec], lambda e: e.tensor_tensor(out=fl(edec), in0=pgl[:, 0:256], in1=fl(gc), op=ALU.subtract))
        S.op("act", [edec], [edec], lambda e: e.activation(out=fl(edec), in_=fl(edec), func=AF.Exp))
        S.op("dve", [beta, egc], [bgc], lambda e: e.tensor_tensor(out=fl(bgc), in0=fl(beta), in1=fl(egc), op=ALU.mult))
        xin = [P.sbuf("xin%d" % k, [128, T], BF16) for k in range(1)]
        acc = P.sbuf("acc", [128, T], F32)
        ybuf = P.sbuf("ybuf", [128, T], F32)
        qTb = P.sbuf("qTb", [128, T], BF16)
        kTf = P.sbuf("kTf", [128, T], F32)
        kTb = P.sbuf("kTb", [128, T], BF16)
        vTf = P.sbuf("vTf", [128, T], F32)
        rn = P.sbuf("rn", [128, 512], F32)
        zt = P.sbuf("zt", [128, NCH, 128], BF16)
        zg = P.sbuf("zg", [128, NCH, 128], BF16)
        def chunked(name):
            big = P.sbuf(name, [128, NCH, 128], BF16)
            out = []
            for k in range(NCH):
                b = Buf(big.t[:, k, :], "%s%d" % (name, k))
                S.bufs.append(b)
                out.append(b)
            return out
        U_, WT, AT, KD = chunked("U_"), chunked("WT"), chunked("AT"), chunked("KD")
        ot = P.sbuf("ot", [128, T], BF16)
        Sst = P.sbuf("Sst", [128, 128], F32)
        Sbfs = [P.sbuf("Sbf%d" % k, [128, 128], BF16) for k in range(2)]
        NR = GDN_NR
        def rot(name, dt=F32, n=NR):
            return [P.sbuf("%s%d" % (name, k), [128, 128], dt) for k in range(n)]
        ktok, vb, kbg, G1, dcs, dsc, Nm, Am, Pm, Nn, An = (rot(x) for x in ("ktok", "vb", "kbg", "G1", "dcs", "dsc", "Nm", "Am", "Pm", "Nn", "An"))
        vnew = rot("vnew", BF16, 2)
        o2b = rot("o2b", F32, 2)
        osb = rot("osb", F32, 2)
        ysb = rot("ysb", F32, 2)
        ssq = P.sbuf("ssq", [128, 2], F32)
        junk = P.sbuf("junk", [128, 128], F32)
        ew = [0]
        def evac_copy(src_ps, dst, dst_ap, src_ap, extra_reads=()):
            ew[0] += 1
            if ew[0] % 2 == 0:
                S.op("act", [src_ps] + list(extra_reads), [dst], lambda e: e.activation(out=dst_ap, in_=src_ap, func=AF.Copy))
            else:
                S.op("dve", [src_ps] + list(extra_reads), [dst], lambda e: e.tensor_copy(out=dst_ap, in_=src_ap))
        for h in heads:
            for wi, which in enumerate(("q", "k", "v")):
                xi = xin[0]
                row = wi * 1024 + h * 128
                S.dma([], [xi], xi[:], GT[row:row + 128, :])
                def wcol(j, wi=wi):
                    return self.vcol("conv", (li * 4 + j) * 24 + wi * 8 + h)
                S.op("dve", [xi, self.VEC], [acc], lambda e, xi=xi: e.tensor_scalar(out=acc[:], in0=xi[:], scalar1=wcol(3), scalar2=None, op0=ALU.mult))
                for sh in (1, 2, 3):
                    S.op("dve", [xi, self.VEC, acc], [acc], lambda e, xi=xi, sh=sh: e.scalar_tensor_tensor(
                        out=acc[:, sh:], in0=xi[:, 0:T - sh], scalar=wcol(3 - sh), in1=acc[:, sh:], op0=ALU.mult, op1=ALU.add))
                dst = vTf if which == "v" else ybuf
                S.op("act", [acc], [dst], lambda e, dst=dst: e.activation(out=dst[:], in_=acc[:], func=AF.Silu))
                if which == "v":
                    continue
                S.op("act", [ybuf], [acc], lambda e: e.activation(out=acc[:], in_=ybuf[:], func=AF.Square))
                for g in range(8):
                    sl = slice(g * 512, (g + 1) * 512)
                    pss = self.ps()
                    S.op("pe", [acc, self.ones32], [pss], lambda e, pss=pss, sl=sl: e.matmul(pss[:, :], lhsT=self.ones32[:, :], rhs=acc[:, sl], start=True, stop=True))
                    S.op("dve", [pss], [rn], lambda e, pss=pss: e.tensor_scalar(out=rn[:], in0=pss[:, :], scalar1=EPS, scalar2=None, op0=ALU.add))
                    S.op("act", [rn], [rn], lambda e: e.activation(out=rn[:], in_=rn[:], func=AF.Sqrt))
                    S.op("dve", [rn], [rn], lambda e: e.reciprocal(out=rn[:], in_=rn[:]))
                    if which == "q":
                        S.op("dve", [ybuf, rn], [qTb], lambda e, sl=sl: e.scalar_tensor_tensor(
                            out=qTb[:, sl], in0=ybuf[:, sl], scalar=float(128 ** -0.5), in1=rn[:], op0=ALU.mult, op1=ALU.mult))
                    else:
                        S.op("dve", [ybuf, rn], [kTf], lambda e, sl=sl: e.tensor_tensor(out=kTf[:, sl], in0=ybuf[:, sl], in1=rn[:], op=ALU.mult))
                        S.op("pool", [kTf], [kTb], lambda e, sl=sl: e.tensor_copy(out=kTb[:, sl], in_=kTf[:, sl]))
            S.dma([], [zt], zt[:], ZT[:, h * 128:(h + 1) * 128].rearrange("(c p) d -> p c d", p=128))
            S.op("act", [zt], [zt], lambda e: e.activation(out=zt[:], in_=zt[:], func=AF.Silu))
            for c in range(NCH):
                S.op("pool", [zt, self.VEC], [zg], lambda e, c=c: e.tensor_tensor(
                    out=zg[:, c, :], in0=zt[:, c, :], in1=self.vcol("gngb", li * 128, 128), op=ALU.mult))
            done1 = [False] * nchunks

            class Pool3:
                def __init__(self, banks):
                    self.banks = banks
                    self.i = 0

                def get(self):
                    b = self.banks[self.i % len(self.banks)]
                    self.i += 1
                    return PQ(b, 0)

                def pair(self):
                    b = self.banks[self.i % len(self.banks)]
                    self.i += 1
                    return PQ(b, 0), PQ(b, 1)
            pools = [Pool3(self.PS[0:2]), Pool3(self.PS[2:4]), Pool3(self.PS[4:6])]
            pool2 = Pool3(self.PS[6:8])
            def evq(pq, dst, dst_ap, extra_reads=()):
                evac_copy(pq, dst, dst_ap, pq.ap, extra_reads)

            def stage1(c, r, h=h):
                cs = slice(c * C, (c + 1) * C)
                col = slice(h, h + 1)
                pl = pools[r]
                pk, pv = pl.pair()
                def f(e):
                    e.transpose(out=pk.ap, in_=kTf[:, cs], identity=I32m[:, :])
                    return e.transpose(out=pv.ap, in_=vTf[:, cs], identity=I32m[:, :])
                S.op("pe", [kTf, vTf, I32m], [pk], f)
                S.op("dve", [SU, graw], [G1[r]], lambda e: e.tensor_scalar(out=G1[r][:], in0=SU[:], scalar1=graw[:, c, col], scalar2=None, op0=ALU.mult))
                yield
                S.op("dve", [pk], [ktok[r]], lambda e: e.tensor_copy(out=ktok[r][:], in_=pk.ap))
                S.op("dve", [pv, beta], [vb[r]], lambda e: e.tensor_scalar(out=vb[r][:], in0=pv.ap, scalar1=beta[:, c, col], scalar2=None, op0=ALU.mult))
                pd1, pd2 = pl.pair()
                def f(e):
                    e.matmul(pd1.ap, lhsT=UT[:, :], rhs=G1[r][:], start=True, stop=True)
                    return e.matmul(pd2.ap, lhsT=G1[r][:], rhs=UT[:, :], start=True, stop=True)
                S.op("pe", [UT, G1[r]], [pd1], f)
                yield
                S.op("act", [pd1], [dcs[r]], lambda e: e.activation(out=dcs[r][:], in_=pd1.ap, func=AF.Exp))
                S.op("act", [pd2], [dsc[r]], lambda e: e.activation(out=dsc[r][:], in_=pd2.ap, func=AF.Exp))
                S.op("act", [ktok[r], bgc], [kbg[r]], lambda e: e.activation(out=kbg[r][:], in_=ktok[r][:], func=AF.Copy, scale=bgc[:, c, col]))
                S.op("act", [ktok[r], edec], [KD[c]], lambda e: e.activation(out=KD[c][:], in_=ktok[r][:], func=AF.Copy, scale=edec[:, c, col]))
                pkk, pqk = pl.pair()
                def f(e):
                    e.matmul(pkk.ap, lhsT=kTf[:, cs], rhs=kTf[:, cs], start=True, stop=True)
                    return e.matmul(pqk.ap, lhsT=kTb[:, cs], rhs=qTb[:, cs], start=True, stop=True)
                S.op("pe", [kTf, kTb, qTb], [pkk], f)
                yield
                S.op("pool", [dcs[r], SU], [dcs[r]], lambda e: e.tensor_tensor(out=dcs[r][:], in0=dcs[r][:], in1=SU[:], op=ALU.mult))
                S.op("pool", [dsc[r], UT], [dsc[r]], lambda e: e.tensor_tensor(out=dsc[r][:], in0=dsc[r][:], in1=UT[:], op=ALU.mult))
                yield
                S.op("dve", [pkk, nbeta, dcs[r]], [Nm[r]], lambda e: e.scalar_tensor_tensor(
                    out=Nm[r][:], in0=pkk.ap, scalar=nbeta[:, c, col], in1=dcs[r][:], op0=ALU.mult, op1=ALU.mult))
                S.op("dve", [pqk, dsc[r]], [AT[c]], lambda e: e.tensor_tensor(out=AT[c][:], in0=pqk.ap, in1=dsc[r][:], op=ALU.mult))
                yield
                pa = pl.get()
                S.op("pe", [Nm[r], I32m], [pa], lambda e: e.transpose(out=pa.ap, in_=Nm[r][:], identity=I32m[:, :]))
                yield
                S.op("act", [pa], [Am[r]], lambda e: e.activation(out=Am[r][:], in_=pa.ap, func=AF.Copy))
                yield
                S.op("pool", [Am[r], I32m], [Pm[r]], lambda e: e.tensor_tensor(out=Pm[r][:], in0=Am[r][:], in1=I32m[:], op=ALU.add))
                Ncur, Acur = Nm[r], Am[r]
                for lvl in range(1, 7):
                    Nnx = Nn[r] if lvl % 2 == 1 else Nm[r]
                    Anx = An[r] if lvl % 2 == 1 else Am[r]
                    pn, pa2 = pl.pair()
                    def f(e):
                        last = e.matmul(pn.ap, lhsT=Acur[:], rhs=Ncur[:], start=True, stop=True)
                        if lvl < 6:
                            last = e.matmul(pa2.ap, lhsT=Ncur[:], rhs=Acur[:], start=True, stop=True)
                        return last
                    S.op("pe", [Acur, Ncur], [pn], f)
                    yield
                    if (lvl + r) % 2 == 0:
                        S.op("act", [pn], [Nnx], lambda e: e.activation(out=Nnx[:], in_=pn.ap, func=AF.Copy))
                        if lvl < 6:
                            S.op("act", [pa2], [Anx], lambda e: e.activation(out=Anx[:], in_=pa2.ap, func=AF.Copy))
                    else:
                        S.op("dve", [pn], [Nnx], lambda e: e.tensor_copy(out=Nnx[:], in_=pn.ap))
                        if lvl < 6:
                            S.op("dve", [pa2], [Anx], lambda e: e.tensor_copy(out=Anx[:], in_=pa2.ap))
                    yield
                    pp = pl.get()
                    S.op("pe", [Nnx, Pm[r]], [pp], lambda e: e.matmul(pp.ap, lhsT=Nnx[:], rhs=Pm[r][:], start=True, stop=True))
                    yield
                    S.op("dve", [pp, Pm[r]], [Pm[r]], lambda e: e.tensor_tensor(out=Pm[r][:], in0=pp.ap, in1=Pm[r][:], op=ALU.add))
                    yield
                    Ncur, Acur = Nnx, Anx
                pu, pw = pl.pair()
                def f(e):
                    e.matmul(pu.ap, lhsT=Pm[r][:], rhs=vb[r][:], start=True, stop=True)
                    return e.matmul(pw.ap, lhsT=kbg[r][:], rhs=Pm[r][:], start=True, stop=True)
                S.op("pe", [Pm[r], vb[r], kbg[r]], [pu], f)
                yield
                S.op("act", [pu], [U_[c]], lambda e: e.activation(out=U_[c][:], in_=pu.ap, func=AF.Copy))
                S.op("act", [pw], [WT[c]], lambda e: e.activation(out=WT[c][:], in_=pw.ap, func=AF.Copy))
                done1[c] = True

            def stage2(h=h):
                S.op("pool", [], [Sbfs[0]], lambda e: e.memset(Sbfs[0][:], 0.0))
                S.op("pool", [], [Sst], lambda e: e.memset(Sst[:], 0.0))
                col = slice(h, h + 1)
                for c in range(nchunks):
                    while not done1[c]:
                        yield
                    cs = slice(c * C, (c + 1) * C)
                    r2 = c % 2
                    Sold, Snew = Sbfs[c % 2], Sbfs[(c + 1) % 2]
                    p1, p2a = pool2.get(), pool2.get()
                    S.op("pe", [WT[c], Sold], [p1], lambda e: e.matmul(p1.ap, lhsT=WT[c][:], rhs=Sold[:], start=True, stop=True))
                    S.op("pe", [qTb, Sold], [p2a], lambda e: e.matmul(p2a.ap, lhsT=qTb[:, cs], rhs=Sold[:], start=True, stop=True))
                    yield
                    S.op("dve", [U_[c], p1], [vnew[r2]], lambda e: e.tensor_tensor(out=vnew[r2][:], in0=U_[c][:], in1=p1.ap, op=ALU.subtract))
                    S.op("act", [p2a, egc], [o2b[r2]], lambda e: e.activation(out=o2b[r2][:], in_=p2a.ap, func=AF.Copy, scale=egc[:, c, col]))
                    yield
                    p3, p2b = pool2.get(), pool2.get()
                    S.op("pe", [KD[c], vnew[r2]], [p3], lambda e: e.matmul(p3.ap, lhsT=KD[c][:], rhs=vnew[r2][:], start=True, stop=True))
                    S.op("pe", [AT[c], vnew[r2]], [p2b], lambda e: e.matmul(p2b.ap, lhsT=AT[c][:], rhs=vnew[r2][:], start=True, stop=True))
                    yield
                    S.op("dve", [Sst, egl, p3], [Sst], lambda e: e.scalar_tensor_tensor(
                        out=Sst[:], in0=Sst[:], scalar=egl[:, c, col], in1=p3.ap, op0=ALU.mult, op1=ALU.add))
                    yield
                    S.op("act", [Sst], [Snew], lambda e: e.activation(out=Snew[:], in_=Sst[:], func=AF.Copy))
                    S.op("dve", [p2b, o2b[r2]], [osb[r2]], lambda e: e.tensor_tensor(out=osb[r2][:], in0=p2b.ap, in1=o2b[r2][:], op=ALU.add))
                    yield
                    S.op("act", [osb[r2]], [junk, ssq], lambda e: e.activation(out=junk[:], in_=osb[r2][:], func=AF.Square, accum_out=ssq[:, r2:r2 + 1]))
                    S.op("dve", [ssq], [ssq], lambda e: e.tensor_scalar(out=ssq[:, r2:r2 + 1], in0=ssq[:, r2:r2 + 1], scalar1=1.0 / 128, scalar2=EPS, op0=ALU.mult, op1=ALU.add))
                    S.op("act", [ssq], [ssq], lambda e: e.activation(out=ssq[:, r2:r2 + 1], in_=ssq[:, r2:r2 + 1], func=AF.Sqrt))
                    S.op("dve", [ssq], [ssq], lambda e: e.reciprocal(out=ssq[:, r2:r2 + 1], in_=ssq[:, r2:r2 + 1]))
                    S.op("dve", [osb[r2], ssq, zg], [ysb[r2]], lambda e: e.scalar_tensor_tensor(
                        out=ysb[r2][:], in0=osb[r2][:], scalar=ssq[:, r2:r2 + 1], in1=zg[:, c, :], op0=ALU.mult, op1=ALU.mult))
                    yield
                    py = pool2.get()
                    S.op("pe", [ysb[r2], I32m], [py], lambda e: e.transpose(out=py.ap, in_=ysb[r2][:], identity=I32m[:, :]))
                    S.op("act", [py], [ot], lambda e: e.activation(out=ot[:, cs], in_=py.ap, func=AF.Copy))
                    yield

            pending = list(range(nchunks))
            active = {}
            s2 = stage2()
            s2_alive = True
            while pending or active or s2_alive:
                while pending and len(active) < NR:
                    c = pending.pop(0)
                    slot = [k for k in range(NR) if k not in active][0]
                    active[slot] = stage1(c, slot)
                for slot in list(active.keys()):
                    try:
                        next(active[slot])
                    except StopIteration:
                        del active[slot]
                if s2_alive:
                    try:
                        next(s2)
                    except StopIteration:
                        s2_alive = False
            S.dma([ot], [], OT[h * 128:(h + 1) * 128, :], ot[:])
        P.close()


def prep_shared(inp):
    perm = rope_partner_perm()
    w = np.asarray(inp["attn_w_in"], np.float32)
    w4 = np.concatenate([w, w[:, :, 0:512][:, :, perm], w[:, :, 512:1024][:, :, perm]], axis=2)
    return {
        "ada_w": np.ascontiguousarray(inp["ada_w"], np.float32),
        "attn_w_in": np.ascontiguousarray(w4),
        "attn_w_out": np.ascontiguousarray(inp["attn_w_out"], np.float32),
        "gdn_w_in": np.ascontiguousarray(inp["gdn_w_in"], np.float32),
        "gdn_w_out": np.ascontiguousarray(inp["gdn_w_out"], np.float32),
        "ffn_w_in": np.ascontiguousarray(inp["ffn_w_in"], np.float32),
        "ffn_w_out": np.ascontiguousarray(inp["ffn_w_out"], np.float32),
    }


def prep_core(inp, b, shared):
    m = dict(shared)
    m["xT"] = np.ascontiguousarray(np.asarray(inp["x"][b], np.float32).T)
    m["vecs"] = build_vecs(inp, b)
    m["pos"] = np.ascontiguousarray(np.asarray(inp["positions"][b], np.int32).reshape(1, T))
    return m


def build_full():
    pg = Prog()
    pg.defer_mod = True
    pg.phase0()
    for l in range(NL):
        pg.phase1(l)
        if l % 2 == 0:
            pg.phase_attn(mod_layers=((1, 2) if l == 0 else (3,)))
        else:
            pg.phase_gdn(l // 2)
        pg.phase34(l, final=(l == NL - 1))
    return pg


def kernel(**inputs):
    inp = {k: np.asarray(v) for k, v in inputs.items()}
    pg = build_full()
    shared = prep_shared(inp)
    in_maps = []
    for b in range(8):
        m = prep_core(inp, b, shared)
        in_maps.append({k: m[k] for k in pg.used_inputs})
    res = run_bass_kernel_spmd(pg.nc, in_maps, core_ids=list(range(8)))
    out = np.empty((8, T, D), np.float32)
    for b in range(8):
        out[b] = np.asarray(res.results[b]["outT"], np.float32).T
    return out
```

```python
import numpy as np
from contextlib import ExitStack
import concourse.bass as bass
import concourse.mybir as mybir
from concourse.bass_utils import run_bass_kernel_spmd

F32 = mybir.dt.float32
BF16 = mybir.dt.bfloat16
I32 = mybir.dt.int32
AF = mybir.ActivationFunctionType
ALU = mybir.AluOpType
AX = mybir.AxisListType

D = 1024
T = 4096
NL = 4
KC = 8
FH = 2816
NJ = 22
EPS = 1e-6
NDMA = 48
BIG = 30000.0
SAME_ENG_SYNC = False
GDN_NR = 4
S2_EVERY = 6
PSQ_SEPARATE = False
GDN_DBG = 0


class Buf:
    __slots__ = ("t", "w", "r", "name")

    def __init__(self, t, name=""):
        self.t = t
        self.w = None
        self.r = {}
        self.name = name

    def __getitem__(self, idx):
        return self.t[idx]


class PQ:
    __slots__ = ("ap", "b")

    def __init__(self, bank, q):
        self.b = bank
        self.ap = bank.t[:, q * 128:(q + 1) * 128]


class Sched:
    def __init__(self, nc):
        self.nc = nc
        self.E = {"pe": nc.tensor, "act": nc.scalar, "dve": nc.vector, "pool": nc.gpsimd, "sp": nc.sync}
        self.sem = {k: nc.alloc_semaphore("s_" + k) for k in ("pe", "act", "dve", "pool")}
        self.cnt = {k: 0 for k in self.sem}
        self.seen = {k: {} for k in self.E}
        self.dsem = [nc.alloc_semaphore("d%d" % i) for i in range(NDMA)]
        self.dcnt = [0] * NDMA
        self.drr = 0
        self.bufs = []
        self.ps_rr = 0

    def sbuf(self, name, shape, dtype):
        b = Buf(self.nc.alloc_sbuf_tensor(name, list(shape), dtype), name)
        self.bufs.append(b)
        return b

    def psum(self, name, shape, dtype=F32):
        b = Buf(self.nc.alloc_psum_tensor(name, list(shape), dtype), name)
        self.bufs.append(b)
        return b

    def dram(self, t, name=""):
        b = Buf(t, name)
        self.bufs.append(b)
        return b

    def _wait(self, eng, ev):
        if ev is None:
            return
        sem, val, key = ev
        if self.seen[eng].get(key, 0) >= val:
            return
        self.E[eng].wait_ge(sem, val)
        self.seen[eng][key] = val

    def _deps(self, eng, reads, writes):
        reads = [getattr(b, "b", b) for b in reads]
        writes = [getattr(b, "b", b) for b in writes]
        for b in reads:
            self._wait(eng, b.w)
        for b in writes:
            if b.w is not None and (SAME_ENG_SYNC or b.w[2] != eng):
                self._wait(eng, b.w)
            for ev in b.r.values():
                if SAME_ENG_SYNC or ev[2] != eng:
                    self._wait(eng, ev)

    def _done(self, ev, reads, writes):
        reads = [getattr(b, "b", b) for b in reads]
        writes = [getattr(b, "b", b) for b in writes]
        for b in reads:
            b.r[ev[2]] = ev
        for b in writes:
            b.w = ev
            b.r = {}

    def op(self, eng, reads, writes, fn):
        self._deps(eng, reads, writes)
        ins = fn(self.E[eng])
        self.cnt[eng] += 1
        ins.then_inc(self.sem[eng], 1)
        ev = (self.sem[eng], self.cnt[eng], eng)
        self._done(ev, reads, writes)
        return ev

    def seq(self, eng, buf, fns, reads=()):
        ev = None
        for k, fn in enumerate(fns):
            ev = self.op(eng, ([buf] if k > 0 else []) + list(reads), [buf], fn)
        return ev

    def dma(self, reads, writes, out, in_, eng="sp"):
        k = self.drr
        self.drr = (k + 1) % NDMA
        key = ("d", k)
        if self.dcnt[k] > 0:
            self._wait(eng, (self.dsem[k], self.dcnt[k], key))
        self._deps(eng, reads, writes)
        ins = self.E[eng].dma_start(out=out, in_=in_)
        self.dcnt[k] += 16
        ins.then_inc(self.dsem[k], 16)
        ev = (self.dsem[k], self.dcnt[k], key)
        self._done(ev, reads, writes)
        return ev

    def barrier(self):
        evs = []
        for k in self.sem:
            if self.cnt[k] > 0:
                evs.append((self.sem[k], self.cnt[k], k))
        for k in range(NDMA):
            if self.dcnt[k] > 0:
                evs.append((self.dsem[k], self.dcnt[k], ("d", k)))
        for eng in self.E:
            for ev in evs:
                self._wait(eng, ev)
        for b in self.bufs:
            b.w = None
            b.r = {}


class Phase:
    _uid = [0]

    def __init__(self, S):
        Phase._uid[0] += 1
        self.uid = Phase._uid[0]
        self.S = S
        self.nc = S.nc
        self.st = ExitStack()
        self.nbufs0 = len(S.bufs)

    def sbuf(self, name, shape, dtype):
        name = "%s_p%d" % (name, self.uid)
        t = self.st.enter_context(self.nc.sbuf_tensor(name, list(shape), dtype))
        b = Buf(t, name)
        self.S.bufs.append(b)
        return b

    def psum(self, name, shape, dtype=F32):
        name = "%s_p%d" % (name, self.uid)
        t = self.st.enter_context(self.nc.psum_tensor(name, list(shape), dtype))
        b = Buf(t, name)
        self.S.bufs.append(b)
        return b

    def close(self):
        self.S.barrier()
        self.st.close()
        del self.S.bufs[self.nbufs0:]


VOFF = {}


def _vec_layout():
    off = 0
    for name, n in [("gm", 32), ("gf", 32), ("fin", 8), ("adab", 192), ("conv", 192), ("gngb", 256),
                    ("alog", 16), ("dtb", 16), ("invf", 1), ("sinsign", 1), ("cT", 8)]:
        VOFF[name] = off
        off += n
    return off


NV = _vec_layout()


def build_vecs(inp, b):
    v = np.zeros((128, NV), np.float32)
    def chunked(a):
        a = np.asarray(a, np.float32).reshape(-1, 128)
        return a.T
    v[:, VOFF["gm"]:VOFF["gm"] + 32] = chunked(inp["norm_mix_g"])
    v[:, VOFF["gf"]:VOFF["gf"] + 32] = chunked(inp["norm_ffn_g"])
    v[:, VOFF["fin"]:VOFF["fin"] + 8] = chunked(inp["final_norm_g"])
    v[:, VOFF["adab"]:VOFF["adab"] + 192] = chunked(inp["ada_b"])
    v[:, VOFF["conv"]:VOFF["conv"] + 192] = chunked(inp["gdn_conv_w"])
    v[:, VOFF["gngb"]:VOFF["gngb"] + 256] = np.broadcast_to(np.asarray(inp["gdn_norm_g"], np.float32).reshape(1, 256), (128, 256))
    v[:, VOFF["alog"]:VOFF["alog"] + 16] = np.broadcast_to(np.asarray(inp["gdn_a_log"], np.float32).reshape(1, 16), (128, 16))
    v[:, VOFF["dtb"]:VOFF["dtb"] + 16] = np.broadcast_to(np.asarray(inp["gdn_dt_bias"], np.float32).reshape(1, 16), (128, 16))
    p = np.arange(128) % 64
    inv = (500000.0 ** (-np.arange(8, dtype=np.float32) * 2.0 / 16)).astype(np.float32)
    v[:, VOFF["invf"]] = np.where(p < 16, inv[p % 8], 0.0)
    v[:, VOFF["sinsign"]] = np.where(p < 8, -1.0, np.where(p < 16, 1.0, 0.0))
    v[:, VOFF["cT"]:VOFF["cT"] + 8] = chunked(inp["c"][b])
    return v


def rope_partner_perm():
    idx = np.arange(512)
    d = idx % 64
    return np.where(d < 8, idx + 8, np.where(d < 16, idx - 8, idx))


class LazyDram(dict):
    def __init__(self, prog):
        super().__init__()
        self.prog = prog

    def __missing__(self, name):
        pg = self.prog
        if name in pg.in_shapes:
            shape, dt = pg.in_shapes[name]
            kind = "ExternalInput"
            pg.used_inputs.append(name)
        else:
            shape, dt = pg.scr_shapes[name]
            kind = "Internal"
            if name in pg.ext_out or name == "outT":
                kind = "ExternalOutput"
            if name in pg.ext_in:
                kind = "ExternalInput"
                pg.used_inputs.append(name)
        ap = pg.nc.dram_tensor(name, list(shape), dt, kind=kind).ap()
        self[name] = ap
        return ap


class Prog:
    def __init__(self, ext_out=(), ext_in=()):
        self.nc = bass.Bass("TRN2", target_bir_lowering=False)
        nc = self.nc
        self.ext_out = set(ext_out)
        self.ext_in = set(ext_in)
        self.S = Sched(nc)
        S = self.S
        self.in_shapes = {
            "xT": ([D, T], F32), "vecs": ([128, NV], F32), "pos": ([1, T], I32),
            "ada_w": ([NL, D, 6 * D], F32), "attn_w_in": ([2, D, 4096], F32), "attn_w_out": ([2, D, D], F32),
            "gdn_w_in": ([2, D, 4112], F32), "gdn_w_out": ([2, D, D], F32),
            "ffn_w_in": ([NL, D, 2 * FH], F32), "ffn_w_out": ([NL, FH, D], F32),
        }
        self.used_inputs = []
        self.defer_mod = False
        self.dr = LazyDram(self)
        self.dbg_outs = []
        self.scr_shapes = {
            "XT": ([D, T], F32), "COS": ([128, T], F32), "SIN": ([128, T], F32),
            "QT": ([2048, T], BF16),
            "VTOK": ([T, 1024], BF16),
            "OT": ([D, T], BF16),
            "GT": ([3072, T], BF16),
            "AB": ([T, 16], F32),
            "outT": ([D, T], F32),
        }
        self.VEC = S.sbuf("VEC", [128, NV], F32)
        self.MOD = S.sbuf("MOD", [128, NL * 48], F32)
        self.AM = S.sbuf("AMc", [128, NL * 8], F32)
        self.AFc = S.sbuf("AFc", [128, NL * 8], F32)
        self.ones32 = S.sbuf("ones32", [128, 128], F32)
        self.ident32 = S.sbuf("ident32", [128, 128], F32)
        self.onesbf = S.sbuf("onesbf", [128, 128], BF16)
        self.condT = S.sbuf("condT", [128, 8], F32)
        self.PS = [S.psum("ps%d" % i, [128, 512], F32) for i in range(8)]
        self.ps_i = 0
        S.op("pool", [], [self.ones32], lambda e: e.memset(self.ones32[:], 1.0))
        S.op("pool", [], [self.onesbf], lambda e: e.memset(self.onesbf[:], 1.0))
        S.seq("pool", self.ident32, [
            lambda e: e.memset(self.ident32[:], 0.0),
            lambda e: e.affine_select(out=self.ident32[:], in_=self.ident32[:], compare_op=ALU.not_equal,
                                      fill=1.0, base=0, pattern=[[-1, 128]], channel_multiplier=1)])

    def dbg(self, name, buf, ap):
        shape = [int(x) for x in ap.shape]
        d = self.nc.dram_tensor(name, shape, buf.t.dtype, kind="ExternalOutput").ap()
        self.S.dma([buf], [], d, ap)
        self.dbg_outs.append(name)

    def ps(self):
        b = self.PS[self.ps_i]
        self.ps_i = (self.ps_i + 1) % 8
        return b

    def psqs(self, n):
        if PSQ_SEPARATE:
            return [PQ(self.ps(), 0) for q in range(n)]
        b = self.ps()
        return [PQ(b, q) for q in range(n)]

    def vcol(self, name, i=0, n=1):
        o = VOFF[name] + i
        return self.VEC[:, o:o + n]

    def phase0(self):
        S, nc = self.S, self.nc
        P = Phase(S)
        VEC, MOD = self.VEC, self.MOD
        S.dma([], [VEC], VEC[:], self.dr["vecs"])
        condT = self.condT
        S.op("act", [VEC], [condT], lambda e: e.activation(out=condT[:], in_=self.vcol("cT", 0, 8), func=AF.Silu))
        CB = 1536
        wst = [P.sbuf("adaw%d" % i, [128, KC, CB], F32) for i in range(2)]
        k = 0
        for l in range(1 if self.defer_mod else NL):
            pm = self.ps()
            for cb in range(4):
                w = wst[k % 2]
                k += 1
                src = self.dr["ada_w"][l, :, cb * CB:(cb + 1) * CB].rearrange("(c p) n -> p c n", p=128)
                S.dma([], [w], w[:], src)
                def mm(e, w=w, cb=cb, pm=pm):
                    last = None
                    for j in range(12):
                        col = cb * 12 + j
                        for kc in range(KC):
                            last = e.matmul(pm[:, col:col + 1], lhsT=w[:, kc, j * 128:(j + 1) * 128],
                                            rhs=condT[:, kc:kc + 1], start=(kc == 0), stop=(kc == KC - 1))
                    return last
                S.op("pe", [w, condT], [pm], mm)
            S.op("dve", [pm, VEC], [MOD], lambda e, l=l, pm=pm: e.tensor_tensor(
                out=MOD[:, l * 48:(l + 1) * 48], in0=pm[:, 0:48], in1=self.vcol("adab", l * 48, 48), op=ALU.add))
            S.op("dve", [MOD, VEC], [self.AM], lambda e, l=l: e.scalar_tensor_tensor(
                out=self.AM[:, l * 8:(l + 1) * 8], in0=MOD[:, l * 48 + 8:l * 48 + 16], scalar=1.0,
                in1=self.vcol("gm", l * 8, 8), op0=ALU.add, op1=ALU.mult))
            S.op("dve", [MOD, VEC], [self.AFc], lambda e, l=l: e.scalar_tensor_tensor(
                out=self.AFc[:, l * 8:(l + 1) * 8], in0=MOD[:, l * 48 + 32:l * 48 + 40], scalar=1.0,
                in1=self.vcol("gf", l * 8, 8), op0=ALU.add, op1=ALU.mult))
        posi = P.sbuf("posi", [1, T], I32)
        posf = P.sbuf("posf", [1, T], F32)
        S.dma([], [posi], posi[:], self.dr["pos"])
        S.op("dve", [posi], [posf], lambda e: e.tensor_copy(out=posf[:], in_=posi[:]))
        angs = [P.sbuf("ang%d" % k, [128, 512], F32) for k in range(2)]
        rbuf = {}
        for wh in ("SIN", "COS"):
            for k in range(2):
                rbuf[(wh, k)] = (P.sbuf("r%s%d" % (wh, k), [128, 512], F32), P.sbuf("xs%s%d" % (wh, k), [128, 512], F32),
                                 P.sbuf("ki%s%d" % (wh, k), [128, 512], I32))
        for tt in range(8):
            sl = slice(tt * 512, (tt + 1) * 512)
            pb = self.ps()
            S.op("pe", [posf, self.ones32], [pb], lambda e, pb=pb, sl=sl: e.matmul(
                pb[:, :], lhsT=self.ones32[0:1, :], rhs=posf[0:1, sl], start=True, stop=True))
            ang = angs[tt % 2]
            S.op("dve", [pb, VEC], [ang], lambda e, pb=pb, ang=ang: e.tensor_scalar(
                out=ang[:], in0=pb[:, :], scalar1=self.vcol("invf"), scalar2=None, op0=ALU.mult))
            for which, shift in (("SIN", 0.0), ("COS", 0.5 * np.pi)):
                r, xs, ki = rbuf[(which, tt % 2)]
                C1 = 6.28125
                C2 = float(2 * np.pi - 6.28125)
                S.op("dve", [ang], [xs], lambda e, xs=xs, ang=ang, shift=shift: e.tensor_scalar(
                    out=xs[:], in0=ang[:], scalar1=float(shift), scalar2=None, op0=ALU.add))
                S.op("dve", [xs], [r], lambda e, r=r, xs=xs: e.tensor_scalar(
                    out=r[:], in0=xs[:], scalar1=float(1.0 / (2 * np.pi)), scalar2=None, op0=ALU.mult))
                S.op("dve", [r], [ki], lambda e, r=r, ki=ki: e.tensor_copy(out=ki[:], in_=r[:]))
                S.op("dve", [ki], [r], lambda e, r=r, ki=ki: e.tensor_copy(out=r[:], in_=ki[:]))
                S.op("dve", [r, xs], [xs], lambda e, r=r, xs=xs: e.scalar_tensor_tensor(
                    out=xs[:], in0=r[:], scalar=-C1, in1=xs[:], op0=ALU.mult, op1=ALU.add))
                S.op("dve", [r, xs], [xs], lambda e, r=r, xs=xs: e.scalar_tensor_tensor(
                    out=xs[:], in0=r[:], scalar=-C2, in1=xs[:], op0=ALU.mult, op1=ALU.add))
                S.op("dve", [xs], [xs], lambda e, xs=xs: e.tensor_scalar(
                    out=xs[:], in0=xs[:], scalar1=-3.14159, scalar2=3.14159, op0=ALU.max, op1=ALU.min))
                S.op("act", [xs], [r], lambda e, r=r, xs=xs: e.activation(out=r[:], in_=xs[:], func=AF.Sin))
                if which == "SIN":
                    S.op("dve", [r, VEC], [r], lambda e, r=r: e.tensor_scalar(
                        out=r[:], in0=r[:], scalar1=self.vcol("sinsign"), scalar2=None, op0=ALU.mult))
                S.dma([r], [], self.dr[which][:, sl], r[:])
        P.close()

    def load_weight_bf16(self, P, Wb, src3, ncols, stages, col0=0):
        S = self.S
        nk = src3.shape[1]
        CB = stages[0].t.shape[2]
        c = 0
        i = 0
        while c < ncols:
            n = min(CB, ncols - c)
            st = stages[i % len(stages)]
            i += 1
            S.dma([], [st], st[:, 0:nk, 0:n], src3[:, :, c:c + n])
            S.op("pool", [st], [Wb], lambda e, st=st, c=c, n=n: e.tensor_copy(
                out=Wb[:, 0:nk, col0 + c:col0 + c + n], in_=st[:, 0:nk, 0:n]))
            c += n

    def norm_tile(self, P, X, A, B, h, sq, tmps, rstd, n=512):
        S = self.S
        S.op("act", [X], [sq], lambda e: e.activation(out=sq[:, :, 0:n], in_=X[:, :, 0:n], func=AF.Square))
        pss = self.ps()
        def mm(e):
            last = None
            for kc in range(KC):
                last = e.matmul(pss[:, 0:n], lhsT=(self.ones32 if sq.t.dtype == F32 else self.onesbf)[:, :], rhs=sq[:, kc, 0:n],
                                start=(kc == 0), stop=(kc == KC - 1))
            return last
        S.op("pe", [sq, self.ones32, self.onesbf], [pss], mm)
        S.op("dve", [pss], [rstd], lambda e: e.tensor_scalar(
            out=rstd[:, 0:n], in0=pss[:, 0:n], scalar1=1.0 / D, scalar2=EPS, op0=ALU.mult, op1=ALU.add))
        S.op("act", [rstd], [rstd], lambda e: e.activation(out=rstd[:, 0:n], in_=rstd[:, 0:n], func=AF.Sqrt))
        S.op("dve", [rstd], [rstd], lambda e: e.reciprocal(out=rstd[:, 0:n], in_=rstd[:, 0:n]))
        for kc in range(KC):
            tmp = tmps[kc % len(tmps)]
            S.op("dve", [X, rstd, self.AM, self.AFc], [tmp], lambda e, kc=kc, tmp=tmp: e.scalar_tensor_tensor(
                out=tmp[:, 0:n], in0=X[:, kc, 0:n], scalar=A[:, kc:kc + 1], in1=rstd[:, 0:n], op0=ALU.mult, op1=ALU.mult))
            S.op("act", [tmp, self.MOD], [h], lambda e, kc=kc, tmp=tmp: e.activation(
                out=h[:, kc, 0:n], in_=tmp[:, 0:n], func=AF.Identity, bias=B[:, kc:kc + 1], scale=1.0))

    def phase1(self, l):
        S = self.S
        P = Phase(S)
        attn = (l % 2 == 0)
        i = l // 2
        NC_ = 4096 if attn else 4112
        Wb = P.sbuf("W1", [128, KC, NC_], BF16)
        stages = [P.sbuf("wst%d" % k, [128, KC, 512], F32) for k in range(2)]
        wsrc = self.dr["attn_w_in" if attn else "gdn_w_in"][i].rearrange("(c p) n -> p c n", p=128)
        self.load_weight_bf16(P, Wb, wsrc, NC_, stages)
        Xb = [P.sbuf("X%d" % k, [128, KC, 512], F32) for k in range(2)]
        sq = P.sbuf("sq", [128, KC, 512], F32)
        tmps = [P.sbuf("tmp%d" % k, [128, 512], F32) for k in range(2)]
        rstd = P.sbuf("rstd", [128, 512], F32)
        h = P.sbuf("h", [128, KC, 512], BF16)
        stg = [P.sbuf("stg%d" % k, [128, 512], BF16) for k in range(4)]
        stg32 = [P.sbuf("stgf%d" % k, [128, 16], F32) for k in range(2)]
        rt = [P.sbuf("rt%d" % k, [128, 512], F32) for k in range(4)]
        cs = [[P.sbuf("cs%d_%d" % (a, k), [128, 512], F32) for k in range(2)] for a in range(2)]
        A = self.AM[:, l * 8:(l + 1) * 8]
        B = self.MOD[:, l * 48:l * 48 + 8]
        XT3 = self.dr["xT" if l == 0 else "XT"].rearrange("(c p) t -> p c t", p=128)
        NT = T // 512
        S.dma([], [Xb[0]], Xb[0][:], XT3[:, :, 0:512])
        sk = 0
        ev = 0
        for tt in range(NT):
            sl = slice(tt * 512, (tt + 1) * 512)
            X = Xb[tt % 2]
            if tt + 1 < NT:
                S.dma([], [Xb[(tt + 1) % 2]], Xb[(tt + 1) % 2][:], XT3[:, :, (tt + 1) * 512:(tt + 2) * 512])
            if attn:
                cosb, sinb = cs[0][tt % 2], cs[1][tt % 2]
                S.dma([], [cosb], cosb[:], self.dr["COS"][:, sl])
                S.dma([], [sinb], sinb[:], self.dr["SIN"][:, sl])
            self.norm_tile(P, X, A, B, h, sq, tmps, rstd)

            def proj_fm(col0):
                pp = self.ps()
                def mm(e, pp=pp, col0=col0):
                    last = None
                    for kc in range(KC):
                        last = e.matmul(pp[:, :], lhsT=Wb[:, kc, col0:col0 + 128], rhs=h[:, kc, :],
                                        start=(kc == 0), stop=(kc == KC - 1))
                    return last
                S.op("pe", [Wb, h], [pp], mm)
                return pp

            def evac(pp, dst_ap, scale, width=512):
                nonlocal sk, ev
                st = stg[sk % 4]
                sk += 1
                if ev % 2 == 0:
                    S.op("act", [pp], [st], lambda e: e.activation(out=st[:, 0:width], in_=pp[:, 0:width], func=AF.Copy, scale=float(scale)))
                else:
                    S.op("dve", [pp], [st], lambda e: e.tensor_scalar(out=st[:, 0:width], in0=pp[:, 0:width], scalar1=float(scale), scalar2=None, op0=ALU.mult))
                ev += 1
                S.dma([st], [], dst_ap, st[:, 0:width])

            def proj_tm(col0, ncols, dst, dcol0, f32out=False):
                nonlocal sk
                for s in range(4):
                    pp = self.ps()
                    def mm(e, pp=pp, s=s):
                        last = None
                        for kc in range(KC):
                            last = e.matmul(pp[:, 0:ncols], lhsT=h[:, kc, s * 128:(s + 1) * 128], rhs=Wb[:, kc, col0:col0 + ncols],
                                            start=(kc == 0), stop=(kc == KC - 1))
                        return last
                    S.op("pe", [Wb, h], [pp], mm)
                    r0 = tt * 512 + s * 128
                    if f32out:
                        st = stg32[s % 2]
                        S.op("dve", [pp], [st], lambda e, st=st, pp=pp: e.tensor_copy(out=st[:, 0:ncols], in_=pp[:, 0:ncols]))
                        S.dma([st], [], dst[r0:r0 + 128, dcol0:dcol0 + ncols], st[:, 0:ncols])
                    else:
                        evac(pp, dst[r0:r0 + 128, dcol0:dcol0 + ncols], 1.0, ncols)

            if attn:
                QT = self.dr["QT"]
                for grp, (c0, pc0, d0, scale) in enumerate([(0, 3072, 0, 0.125), (512, 3584, 512, 1.0)]):
                    for n in range(4):
                        pq = proj_fm(c0 + n * 128)
                        pp = proj_fm(pc0 + n * 128)
                        t1, t2 = rt[(2 * n) % 4], rt[(2 * n + 1) % 4]
                        st = stg[sk % 4]
                        sk += 1
                        S.op("dve", [pq, cosb], [t1], lambda e, pq=pq, t1=t1, scale=scale: e.scalar_tensor_tensor(
                            out=t1[:], in0=pq[:, :], scalar=float(scale), in1=cosb[:], op0=ALU.mult, op1=ALU.mult))
                        S.op("dve", [pp, sinb], [t2], lambda e, pp=pp, t2=t2, scale=scale: e.scalar_tensor_tensor(
                            out=t2[:], in0=pp[:, :], scalar=float(scale), in1=sinb[:], op0=ALU.mult, op1=ALU.mult))
                        S.op("pool", [t1, t2], [st], lambda e, st=st, t1=t1, t2=t2: e.tensor_tensor(
                            out=st[:], in0=t1[:], in1=t2[:], op=ALU.add))
                        S.dma([st], [], QT[d0 + n * 128:d0 + (n + 1) * 128, sl], st[:])
                for n in range(4):
                    evac(proj_fm(1536 + n * 128), QT[1024 + n * 128:1024 + (n + 1) * 128, sl], 0.125)
                for n in range(4):
                    evac(proj_fm(2048 + n * 128), QT[1536 + n * 128:1536 + (n + 1) * 128, sl], 1.0)
                proj_tm(1024, 512, self.dr["VTOK"], 0)
                proj_tm(2560, 512, self.dr["VTOK"], 512)
            else:
                GT = self.dr["GT"]
                for n in range(24):
                    evac(proj_fm(n * 128), GT[n * 128:(n + 1) * 128, sl], 1.0)
                proj_tm(3072, 512, self.dr["VTOK"], 0)
                proj_tm(3584, 512, self.dr["VTOK"], 512)
                proj_tm(4096, 16, self.dr["AB"], 0, f32out=True)
        P.close()

    def phase34(self, l, final=False, tiles=None):
        S = self.S
        P = Phase(S)
        i = l // 2
        TT = 256
        Wo = P.sbuf("Wo", [128, KC, D], BF16)
        Wi = P.sbuf("Wi", [128, KC, 2 * FH], BF16)
        W2 = P.sbuf("W2", [128, NJ, D], BF16)
        PW = Phase(S)
        stA = [PW.sbuf("stA%d" % k, [128, KC, 512], F32) for k in range(2)]
        stB = [PW.sbuf("stB%d" % k, [128, NJ, 128], F32) for k in range(2)]
        wo_src = self.dr["attn_w_out" if l % 2 == 0 else "gdn_w_out"][i].rearrange("(c p) n -> p c n", p=128)
        self.load_weight_bf16(PW, Wo, wo_src, D, stA)
        self.load_weight_bf16(PW, Wi, self.dr["ffn_w_in"][l].rearrange("(c p) n -> p c n", p=128), 2 * FH, stA)
        self.load_weight_bf16(PW, W2, self.dr["ffn_w_out"][l].rearrange("(c p) n -> p c n", p=128), D, stB)
        PW.close()
        Xb = [P.sbuf("X%d" % k, [128, KC, TT], F32) for k in range(2)]
        ob = [P.sbuf("oT%d" % k, [128, KC, TT], BF16) for k in range(2)]
        sq = P.sbuf("sq", [128, KC, TT], BF16)
        tmps = [P.sbuf("tmp%d" % k, [128, TT], F32) for k in range(2)]
        sgs = [P.sbuf("sg%d" % k, [128, TT], F32) for k in range(2)]
        rstds = [P.sbuf("rstd%d" % k, [128, TT], F32) for k in range(2)]
        hb = [P.sbuf("h%d" % k, [128, KC, TT], BF16) for k in range(2)]
        hid = P.sbuf("hid", [128, NJ, TT], BF16)
        GM = self.MOD[:, l * 48 + 16:l * 48 + 24]
        GF = self.MOD[:, l * 48 + 40:l * 48 + 48]
        A = self.AFc[:, l * 8:(l + 1) * 8]
        B = self.MOD[:, l * 48 + 24:l * 48 + 32]
        XS3 = self.dr["xT" if l == 0 else "XT"].rearrange("(c p) t -> p c t", p=128)
        XD3 = self.dr["XT"].rearrange("(c p) t -> p c t", p=128)
        OT3 = self.dr["OT"].rearrange("(c p) t -> p c t", p=128)
        if final:
            OUT3 = self.dr["outT"].rearrange("(c p) t -> p c t", p=128)
        NT = T // TT
        tl = list(range(NT)) if tiles is None else list(tiles)

        def front(ti):
            tt = tl[ti]
            sl = slice(tt * TT, (tt + 1) * TT)
            X, oT, h = Xb[ti % 2], ob[ti % 2], hb[ti % 2]
            S.dma([], [oT], oT[:], OT3[:, :, sl])
            S.dma([], [X], X[:], XS3[:, :, sl])
            for n in range(KC):
                pp = self.ps()
                def mm(e, pp=pp, n=n):
                    last = None
                    for kc in range(KC):
                        last = e.matmul(pp[:, 0:TT], lhsT=Wo[:, kc, n * 128:(n + 1) * 128], rhs=oT[:, kc, :],
                                        start=(kc == 0), stop=(kc == KC - 1))
                    return last
                S.op("pe", [Wo, oT], [pp], mm)
                S.op("dve", [pp, X, self.MOD], [X], lambda e, pp=pp, n=n: e.scalar_tensor_tensor(
                    out=X[:, n, :], in0=pp[:, 0:TT], scalar=GM[:, n:n + 1], in1=X[:, n, :], op0=ALU.mult, op1=ALU.add))
            self.norm_tile(P, X, A, B, h, sq, tmps, rstds[ti % 2], n=TT)

        def back(ti):
            tt = tl[ti]
            sl = slice(tt * TT, (tt + 1) * TT)
            X, h = Xb[ti % 2], hb[ti % 2]
            for j in range(NJ):
                pg = self.ps()
                pu = self.ps()
                def mmg(e, pg=pg, j=j):
                    last = None
                    for kc in range(KC):
                        last = e.matmul(pg[:, 0:TT], lhsT=Wi[:, kc, j * 128:(j + 1) * 128], rhs=h[:, kc, :],
                                        start=(kc == 0), stop=(kc == KC - 1))
                    return last
                def mmu(e, pu=pu, j=j):
                    last = None
                    for kc in range(KC):
                        last = e.matmul(pu[:, 0:TT], lhsT=Wi[:, kc, FH + j * 128:FH + (j + 1) * 128], rhs=h[:, kc, :],
                                        start=(kc == 0), stop=(kc == KC - 1))
                    return last
                S.op("pe", [Wi, h], [pg], mmg)
                S.op("pe", [Wi, h], [pu], mmu)
                sg = sgs[j % 2]
                S.op("act", [pg], [sg], lambda e, pg=pg, sg=sg: e.activation(out=sg[:], in_=pg[:, 0:TT], func=AF.Silu))
                S.op("dve", [sg, pu], [hid], lambda e, pu=pu, sg=sg, j=j: e.tensor_tensor(
                    out=hid[:, j, :], in0=pu[:, 0:TT], in1=sg[:], op=ALU.mult))
            for n in range(KC):
                pp = self.ps()
                def mm2(e, pp=pp, n=n):
                    last = None
                    for j in range(NJ):
                        last = e.matmul(pp[:, 0:TT], lhsT=W2[:, j, n * 128:(n + 1) * 128], rhs=hid[:, j, :],
                                        start=(j == 0), stop=(j == NJ - 1))
                    return last
                S.op("pe", [W2, hid], [pp], mm2)
                S.op("dve", [pp, X, self.MOD], [X], lambda e, pp=pp, n=n: e.scalar_tensor_tensor(
                    out=X[:, n, :], in0=pp[:, 0:TT], scalar=GF[:, n:n + 1], in1=X[:, n, :], op0=ALU.mult, op1=ALU.add))
            if final:
                self.final_norm_tile(X, X, sq, rstds[ti % 2], TT)
                S.dma([X], [], OUT3[:, :, sl], X[:])
            else:
                S.dma([X], [], XD3[:, :, sl], X[:])

        front(0)
        for ti in range(len(tl)):
            if ti + 1 < len(tl):
                front(ti + 1)
            back(ti)
        P.close()

    def final_norm_tile(self, X, yout, sq, rstd, n):
        S = self.S
        S.op("act", [X], [sq], lambda e: e.activation(out=sq[:, :, 0:n], in_=X[:, :, 0:n], func=AF.Square))
        pss = self.ps()
        def mm(e):
            last = None
            for kc in range(KC):
                last = e.matmul(pss[:, 0:n], lhsT=(self.ones32 if sq.t.dtype == F32 else self.onesbf)[:, :], rhs=sq[:, kc, 0:n],
                                start=(kc == 0), stop=(kc == KC - 1))
            return last
        S.op("pe", [sq, self.ones32, self.onesbf], [pss], mm)
        S.op("dve", [pss], [rstd], lambda e: e.tensor_scalar(
            out=rstd[:, 0:n], in0=pss[:, 0:n], scalar1=1.0 / D, scalar2=EPS, op0=ALU.mult, op1=ALU.add))
        S.op("act", [rstd], [rstd], lambda e: e.activation(out=rstd[:, 0:n], in_=rstd[:, 0:n], func=AF.Sqrt))
        S.op("dve", [rstd], [rstd], lambda e: e.reciprocal(out=rstd[:, 0:n], in_=rstd[:, 0:n]))
        for kc in range(KC):
            S.op("dve", [X, rstd, self.VEC], [yout], lambda e, kc=kc: e.scalar_tensor_tensor(
                out=yout[:, kc, 0:n], in0=X[:, kc, 0:n], scalar=self.vcol("fin", kc), in1=rstd[:, 0:n], op0=ALU.mult, op1=ALU.mult))

    def mod_gen(self, P, layers, psfn):
        S = self.S
        stg = [P.sbuf("mst%d" % k, [128, KC, 512], F32) for k in range(2)]
        condT, MOD = self.condT, self.MOD
        k = 0
        for l in layers:
            for cb in range(12):
                w = stg[k % 2]
                k += 1
                S.dma([], [w], w[:], self.dr["ada_w"][l, :, cb * 512:(cb + 1) * 512].rearrange("(c p) n -> p c n", p=128))
                yield
                pm = psfn()
                def mm(e, w=w, pm=pm):
                    last = None
                    for j in range(4):
                        for kc in range(KC):
                            last = e.matmul(pm[:, j:j + 1], lhsT=w[:, kc, j * 128:(j + 1) * 128], rhs=condT[:, kc:kc + 1],
                                            start=(kc == 0), stop=(kc == KC - 1))
                    return last
                S.op("pe", [w, condT], [pm], mm)
                c0 = l * 48 + cb * 4
                S.op("dve", [pm, self.VEC], [MOD], lambda e, pm=pm, c0=c0: e.tensor_tensor(
                    out=MOD[:, c0:c0 + 4], in0=pm[:, 0:4], in1=self.VEC[:, VOFF["adab"] + c0:VOFF["adab"] + c0 + 4], op=ALU.add))
                yield
            S.op("dve", [MOD, self.VEC], [self.AM], lambda e, l=l: e.scalar_tensor_tensor(
                out=self.AM[:, l * 8:(l + 1) * 8], in0=MOD[:, l * 48 + 8:l * 48 + 16], scalar=1.0,
                in1=self.vcol("gm", l * 8, 8), op0=ALU.add, op1=ALU.mult))
            S.op("dve", [MOD, self.VEC], [self.AFc], lambda e, l=l: e.scalar_tensor_tensor(
                out=self.AFc[:, l * 8:(l + 1) * 8], in0=MOD[:, l * 48 + 32:l * 48 + 40], scalar=1.0,
                in1=self.vcol("gf", l * 8, 8), op0=ALU.add, op1=ALU.mult))

    def phase_attn(self, heads_moba=range(8), heads_sb=range(8), qtiles=range(8), mod_layers=()):
        S = self.S
        P = Phase(S)
        QT, VTOK, OT = self.dr["QT"], self.dr["VTOK"], self.dr["OT"]
        cmask = [P.sbuf("cmask%d" % d, [128, 512], BF16) for d in range(4)]
        smask = [P.sbuf("smask%d" % d, [128, 512], BF16) for d in range(4)]
        for d in range(4):
            for mk, op in ((cmask[d], ALU.is_ge), (smask[d], ALU.is_gt)):
                S.seq("pool", mk, [
                    lambda e, mk=mk: e.memset(mk[:], 1.0),
                    lambda e, mk=mk, op=op, d=d: e.affine_select(out=mk[:], in_=mk[:], compare_op=op, fill=0.0, base=-128 * d,
                                                                 pattern=[[1, 512]], channel_multiplier=-1)])
        NUI = P.sbuf("NUI", [128, 128], BF16)
        S.seq("pool", NUI, [
            lambda e: e.memset(NUI[:], -1.0),
            lambda e: e.affine_select(out=NUI[:], in_=NUI[:], compare_op=ALU.is_ge, fill=0.0, base=0,
                                      pattern=[[-1, 128]], channel_multiplier=1)])
        onesb = P.sbuf("onesb", [128, 128], BF16)
        S.op("pool", [], [onesb], lambda e: e.memset(onesb[:], 1.0))
        past01 = P.sbuf("past01", [128, 32, 16], F32)
        pastb = P.sbuf("pastb", [128, 32, 16], F32)
        ownm1 = P.sbuf("ownm1", [128, 32, 16], F32)
        S.seq("pool", past01, [
            lambda e: e.memset(past01[:], 1.0),
            lambda e: e.affine_select(out=past01[:], in_=past01[:], compare_op=ALU.is_ge, fill=0.0, base=-2,
                                      pattern=[[1, 32], [-2, 16]], channel_multiplier=0)])
        S.seq("pool", pastb, [
            lambda e: e.memset(pastb[:], 0.0),
            lambda e: e.affine_select(out=pastb[:], in_=pastb[:], compare_op=ALU.is_ge, fill=-1e30, base=-2,
                                      pattern=[[1, 32], [-2, 16]], channel_multiplier=0)])
        S.seq("pool", ownm1, [
            lambda e: e.memset(ownm1[:], 0.0),
            lambda e: e.affine_select(out=ownm1[:], in_=ownm1[:], compare_op=ALU.is_ge, fill=-1.0, base=0,
                                      pattern=[[1, 32], [-2, 16]], channel_multiplier=0),
            lambda e: e.affine_select(out=ownm1[:], in_=ownm1[:], compare_op=ALU.is_ge, fill=-1.0, base=1,
                                      pattern=[[-1, 32], [2, 16]], channel_multiplier=0)])
        QAUG = [P.sbuf("QAUG%d" % k, [128, T], BF16) for k in range(2)]
        KAUG = [P.sbuf("KAUG%d" % k, [128, T], BF16) for k in range(2)]
        KS = [P.sbuf("KS%d" % k, [128, T], BF16) for k in range(2)]
        KAc = P.sbuf("KAc", [16, T], BF16)
        S.seq("pool", KAc, [
            lambda e: e.memset(KAc[:], BIG),
            lambda e: e.affine_select(out=KAc[:], in_=KAc[:], compare_op=ALU.is_ge, fill=0.0, base=0,
                                      pattern=[[1, T]], channel_multiplier=-256),
            lambda e: e.affine_select(out=KAc[:], in_=KAc[:], compare_op=ALU.is_ge, fill=0.0, base=255,
                                      pattern=[[-1, T]], channel_multiplier=256)])
        for k in range(2):
            S.op("pool", [], [QAUG[k]], lambda e, k=k: e.memset(QAUG[k][:], 0.0))
            S.op("pool", [], [KS[k]], lambda e, k=k: e.memset(KS[k][:], 0.0))
            S.seq("pool", KAUG[k], [
                lambda e, k=k: e.memset(KAUG[k][:], 0.0),
                lambda e, k=k: e.memset(KAUG[k][96:97, :], -1.0)])
            S.dma([KAc], [KAUG[k]], KAUG[k][64:80, :], KAc[:])
        Vb = [P.sbuf("V%d" % k, [128, 32, 65], BF16) for k in range(2)]
        for k in range(2):
            S.op("pool", [], [Vb[k]], lambda e, k=k: e.memset(Vb[k][:], 1.0))
        sqQ = P.sbuf("sqQ", [64, T], BF16)
        kmean = P.sbuf("kmean", [64, 16], F32)
        kmeanb = P.sbuf("kmeanb", [64, 16], BF16)
        gm = P.sbuf("gm", [128, 32, 16], F32)
        m8 = P.sbuf("m8", [128, 32, 8], F32)
        MBw = P.sbuf("MBw", [128, 32, 97], F32)
        S.op("pool", [], [MBw], lambda e: e.memset(MBw[:], 0.0))
        kmx = P.sbuf("kmx", [128, 8], F32)
        kmax2 = P.sbuf("kmax2", [128, 1], F32)
        Pt = [P.sbuf("Pt%d" % k, [128, 512], BF16) for k in range(3)]
        Eb = [P.sbuf("E%d" % k, [128, 512], F32) for k in range(2)]
        Lp = [P.sbuf("Lp%d" % k, [128, 512], BF16) for k in range(3)]
        exb = [P.sbuf("ex%d" % k, [128, 512], F32) for k in range(2)]
        rs = P.sbuf("rs", [128, 512], F32)
        Osb = P.sbuf("Osb", [65, 512], F32)
        rden = P.sbuf("rden", [128, 512], F32)
        S.op("pool", [], [rden], lambda e: e.memset(rden[:], 0.0))
        sel64 = P.sbuf("sel64", [128, 128], F32)
        S.seq("pool", sel64, [lambda e: e.memset(sel64[:], 0.0), lambda e: e.memset(sel64[64:65, :], 1.0)])
        ost = [P.sbuf("ost%d" % k, [64, 512], BF16) for k in range(2)]
        PSr = self.PS[0:6]
        PSo = self.PS[6:8]
        rr = [0]
        def psr():
            b = PSr[rr[0] % 6]
            rr[0] += 1
            return b
        jobs = [("m", h) for h in heads_moba] + [("s", h) for h in heads_sb]
        mg = self.mod_gen(P, list(mod_layers), psr) if mod_layers else None
        def mg_step():
            nonlocal mg
            if mg is not None:
                try:
                    next(mg)
                except StopIteration:
                    mg = None
        def load_head(idx):
            kind, h = jobs[idx]
            k = idx % 2
            qrow = (0 if kind == "m" else 1024) + h * 64
            krow = (512 if kind == "m" else 1536) + h * 64
            vcol = (0 if kind == "m" else 512) + h * 64
            kdst = KAUG[k] if kind == "m" else KS[k]
            S.dma([], [QAUG[k]], QAUG[k][0:64, :], QT[qrow:qrow + 64, :])
            S.dma([], [kdst], kdst[0:64, :], QT[krow:krow + 64, :])
            S.dma([], [Vb[k]], Vb[k][:, :, 0:64], VTOK[:, vcol:vcol + 64].rearrange("(j p) d -> p j d", p=128))
        load_head(0)
        oi = 0
        pre_done = set()
        for idx, (kind, h) in enumerate(jobs):
            Q, V = QAUG[idx % 2], Vb[idx % 2]
            Kt = KAUG[idx % 2] if kind == "m" else KS[idx % 2]
            if idx + 1 < len(jobs):
                load_head(idx + 1)
            if kind == "m":
                def prelude(Q, Kt):
                    S.op("dve", [Kt], [kmean], lambda e: e.tensor_reduce(
                        out=kmean[:], in_=Kt[0:64, :].rearrange("p (n k) -> p n k", k=256), axis=AX.X, op=ALU.add))
                    S.op("dve", [kmean], [kmeanb], lambda e: e.tensor_scalar(
                        out=kmeanb[:], in0=kmean[:], scalar1=1.0 / 256, scalar2=None, op0=ALU.mult))
                    yield
                    GP = psr()
                    def mm(e):
                        last = None
                        for tt in range(32):
                            last = e.matmul(GP[:, tt * 16:(tt + 1) * 16], lhsT=Q[0:64, tt * 128:(tt + 1) * 128], rhs=kmeanb[:, :],
                                            start=True, stop=True)
                        return last
                    S.op("pe", [Q, kmeanb], [GP], mm)
                    S.op("dve", [GP, pastb], [gm], lambda e: e.tensor_tensor(
                        out=gm[:].rearrange("p a b -> p (a b)"), in0=GP[:, :], in1=pastb[:].rearrange("p a b -> p (a b)"), op=ALU.add))
                    yield
                    for tt in range(32):
                        S.op("dve", [gm], [m8], lambda e, tt=tt: e.max(out=m8[:, tt, :], in_=gm[:, tt, :]))
                    yield
                    for tt in range(32):
                        S.op("dve", [gm, m8, past01], [MBw], lambda e, tt=tt: e.scalar_tensor_tensor(
                            out=MBw[:, tt, 64:80], in0=gm[:, tt, :], scalar=m8[:, tt, 2:3], in1=past01[:, tt, :], op0=ALU.is_ge, op1=ALU.mult))
                    S.op("dve", [MBw, ownm1], [MBw], lambda e: e.tensor_tensor(
                        out=MBw[:, :, 64:80], in0=MBw[:, :, 64:80], in1=ownm1[:], op=ALU.add))
                    yield
                    S.op("pool", [Kt], [sqQ], lambda e: e.tensor_tensor(out=sqQ[:], in0=Kt[0:64, :], in1=Kt[0:64, :], op=ALU.mult))
                    for g in range(8):
                        KP = psr()
                        S.op("pe", [sqQ, onesb], [KP], lambda e, KP=KP, g=g: e.matmul(
                            KP[:, :], lhsT=onesb[0:64, :], rhs=sqQ[:, g * 512:(g + 1) * 512], start=True, stop=True))
                        S.op("dve", [KP], [kmx], lambda e, KP=KP, g=g: e.tensor_reduce(
                            out=kmx[:, g:g + 1], in_=KP[:, :], axis=AX.X, op=ALU.max))
                    yield
                    S.op("dve", [kmx], [kmax2], lambda e: e.tensor_reduce(out=kmax2[:], in_=kmx[:], axis=AX.X, op=ALU.max))
                    S.op("pool", [Q], [sqQ], lambda e: e.tensor_tensor(out=sqQ[:], in0=Q[0:64, :], in1=Q[0:64, :], op=ALU.mult))
                    yield
                    QQ = psr()
                    def mmq(e):
                        last = None
                        for tt in range(32):
                            last = e.matmul(QQ[:, tt:tt + 1], lhsT=sqQ[:, tt * 128:(tt + 1) * 128], rhs=onesb[0:64, 0:1], start=True, stop=True)
                        return last
                    S.op("pe", [sqQ, onesb], [QQ], mmq)
                    S.op("act", [QQ, kmax2], [MBw], lambda e: e.activation(
                        out=MBw[:, :, 96], in_=QQ[:, 0:32], func=AF.Sqrt, scale=kmax2[:, 0:1]))
                    yield
                    for g in range(8):
                        TP = psr()
                        def tr(e, TP=TP, g=g):
                            last = None
                            for k in range(4):
                                last = e.transpose(out=TP[0:97, k * 128:(k + 1) * 128], in_=MBw[:, 4 * g + k, :], identity=self.ident32[:, :])
                            return last
                        S.op("pe", [MBw, self.ident32], [TP], tr)
                        S.op("dve", [TP], [Q], lambda e, TP=TP, g=g: e.tensor_copy(out=Q[64:97, g * 512:(g + 1) * 512], in_=TP[64:97, :]))

                if idx not in pre_done:
                    for _ in prelude(Q, Kt):
                        pass
                nxt = None
                if idx + 1 < len(jobs) and jobs[idx + 1][0] == "m":
                    nxt = prelude(QAUG[(idx + 1) % 2], KAUG[(idx + 1) % 2])
                    pre_done.add(idx + 1)
                pairs = [(i, j) for i in qtiles for j in range(4 * i + 4)]
                st = {}
                def stageA(p):
                    i, j = pairs[p]
                    sp = psr()
                    S.op("pe", [Kt, Q], [sp], lambda e: e.matmul(
                        sp[:, :], lhsT=Kt[:, j * 128:(j + 1) * 128], rhs=Q[:, i * 512:(i + 1) * 512], start=True, stop=True))
                    pt = Pt[p % 3]
                    S.op("act", [sp], [pt], lambda e: e.activation(out=pt[:], in_=sp[:, :], func=AF.Exp))
                    if j >= 4 * i:
                        S.op("dve", [pt, cmask[j - 4 * i]], [pt], lambda e: e.tensor_tensor(out=pt[:], in0=pt[:], in1=cmask[j - 4 * i][:], op=ALU.mult))
                    st[p] = pt
                def stageC(p):
                    nonlocal oi
                    i, j = pairs[p]
                    pt = st.pop(p)
                    last = 4 * i + 3
                    if j == 0:
                        oi += 1
                    OP = PSo[oi % 2]
                    S.op("pe", [V, pt], [OP], lambda e: e.matmul(OP[0:65, :], lhsT=V[:, j, :], rhs=pt[:], start=(j == 0), stop=(j == last)))
                    if j == last:
                        S.op("act", [OP], [Osb], lambda e: e.activation(out=Osb[:], in_=OP[0:65, :], func=AF.Copy))
                        if getattr(self, "debug", False) and i == 0:
                            self.dbg("dbgOsb", Osb, Osb[:])
                        S.op("dve", [Osb], [rden], lambda e: e.reciprocal(out=rden[64:65, :], in_=Osb[64:65, :]))
                        BP = psr()
                        S.op("pe", [rden, sel64], [BP], lambda e: e.matmul(
                            BP[:, :], lhsT=sel64[:, :], rhs=rden[:, :], start=True, stop=True))
                        o = ost[i % 2]
                        S.op("dve", [Osb, BP], [o], lambda e: e.tensor_tensor(out=o[:], in0=Osb[0:64, :], in1=BP[0:64, :], op=ALU.mult))
                        S.dma([o], [], OT[h * 64:(h + 1) * 64, i * 512:(i + 1) * 512], o[:])
                for p in range(len(pairs) + 2):
                    if p < len(pairs):
                        stageA(p)
                    if p >= 2:
                        stageC(p - 2)
                    if nxt is not None and p % 8 == 4:
                        try:
                            next(nxt)
                        except StopIteration:
                            nxt = None
                    if p % 16 == 9:
                        mg_step()
                if nxt is not None:
                    for _ in nxt:
                        pass
            else:
                pairs = [(i, j) for i in qtiles for j in range(4 * i + 3, -1, -1)]
                st = {}
                def stageA(p):
                    i, j = pairs[p]
                    z = psr()
                    S.op("pe", [Kt, Q], [z], lambda e: e.matmul(
                        z[:, :], lhsT=Kt[:, j * 128:(j + 1) * 128], rhs=Q[:, i * 512:(i + 1) * 512], start=True, stop=False))
                    E = Eb[p % 2]
                    lp = Lp[p % 3]
                    S.op("act", [z], [E], lambda e: e.activation(out=E[:], in_=z[:, :], func=AF.Exp))
                    S.op("act", [E], [lp], lambda e: e.activation(out=lp[:], in_=E[:], func=AF.Ln, bias=1.0, scale=1.0))
                    if j >= 4 * i:
                        S.op("pool", [lp, smask[j - 4 * i]], [lp], lambda e: e.tensor_tensor(out=lp[:], in0=lp[:], in1=smask[j - 4 * i][:], op=ALU.mult))
                    st[p] = (z, lp)
                def stageB(p):
                    i, j = pairs[p]
                    z, lp = st[p]
                    first = (j == 4 * i + 3)
                    S.op("pe", [NUI, lp], [z], lambda e: e.matmul(z[:, :], lhsT=NUI[:, :], rhs=lp[:], start=False, stop=True))
                    cs = psr()
                    S.op("pe", [onesb, lp], [cs], lambda e: e.matmul(cs[:, :], lhsT=onesb[:, :], rhs=lp[:], start=True, stop=True))
                    ex = exb[p % 2]
                    if first:
                        S.op("dve", [z], [ex], lambda e: e.tensor_copy(out=ex[:], in_=z[:, :]))
                        S.op("dve", [cs], [rs], lambda e: e.tensor_copy(out=rs[:], in_=cs[:, :]))
                    else:
                        S.op("dve", [z, rs], [ex], lambda e: e.tensor_tensor(out=ex[:], in0=z[:, :], in1=rs[:], op=ALU.subtract))
                        S.op("dve", [cs, rs], [rs], lambda e: e.tensor_tensor(out=rs[:], in0=cs[:, :], in1=rs[:], op=ALU.add))
                    w = Pt[p % 3]
                    S.op("act", [ex], [w], lambda e: e.activation(out=w[:], in_=ex[:], func=AF.Exp))
                    if j >= 4 * i:
                        S.op("pool", [w, smask[j - 4 * i]], [w], lambda e: e.tensor_tensor(out=w[:], in0=w[:], in1=smask[j - 4 * i][:], op=ALU.mult))
                    st[p] = w
                def stageC(p):
                    nonlocal oi
                    i, j = pairs[p]
                    w = st.pop(p)
                    first = (j == 4 * i + 3)
                    if first:
                        oi += 1
                    OP = PSo[oi % 2]
                    S.op("pe", [V, w], [OP], lambda e: e.matmul(OP[0:64, :], lhsT=V[:, j, 0:64], rhs=w[:], start=first, stop=(j == 0)))
                    if j == 0:
                        o = ost[i % 2]
                        S.op("act", [OP], [o], lambda e: e.activation(out=o[:], in_=OP[0:64, :], func=AF.Copy))
                        S.dma([o], [], OT[512 + h * 64:512 + (h + 1) * 64, i * 512:(i + 1) * 512], o[:])
                n = len(pairs)
                for p in range(n + 2):
                    if p % 16 == 9:
                        mg_step()
                    if p < n:
                        stageA(p)
                    if 1 <= p <= n:
                        stageB(p - 1)
                    if p >= 2:
                        stageC(p - 2)
        while mg is not None:
            mg_step()
        P.close()

    def phase_gdn(self, li, heads=range(8), nchunks=32):
        S = self.S
        P = Phase(S)
        GT, ZT, AB, OT = self.dr["GT"], self.dr["VTOK"], self.dr["AB"], self.dr["OT"]
        NCH = 32
        C = 128
        UT = P.sbuf("UT", [128, 128], F32)
        SU = P.sbuf("SU", [128, 128], F32)
        S.seq("pool", UT, [
            lambda e: e.memset(UT[:], 1.0),
            lambda e: e.affine_select(out=UT[:], in_=UT[:], compare_op=ALU.is_ge, fill=0.0, base=0,
                                      pattern=[[1, 128]], channel_multiplier=-1)])
        S.seq("pool", SU, [
            lambda e: e.memset(SU[:], 1.0),
            lambda e: e.affine_select(out=SU[:], in_=SU[:], compare_op=ALU.is_gt, fill=0.0, base=0,
                                      pattern=[[-1, 128]], channel_multiplier=1)])
        I32m = self.ident32
        ab = P.sbuf("ab", [128, NCH, 16], F32)
        S.dma([], [ab], ab[:], AB.rearrange("(c p) n -> p c n", p=128))
        graw = P.sbuf("graw", [128, NCH, 8], F32)
        beta = P.sbuf("beta", [128, NCH, 8], F32)
        nbeta = P.sbuf("nbeta", [128, NCH, 8], F32)
        negA = P.sbuf("negA", [128, 8], F32)
        tmpa = P.sbuf("tmpa", [128, NCH, 8], F32)
        S.op("act", [self.VEC], [negA], lambda e: e.activation(out=negA[:], in_=self.vcol("alog", li * 8, 8), func=AF.Exp))
        S.op("dve", [negA], [negA], lambda e: e.tensor_scalar(out=negA[:], in0=negA[:], scalar1=-1.0, scalar2=None, op0=ALU.mult))
        for c in range(NCH):
            S.op("dve", [ab, self.VEC], [tmpa], lambda e, c=c: e.tensor_tensor(
                out=tmpa[:, c, :], in0=ab[:, c, 0:8], in1=self.vcol("dtb", li * 8, 8), op=ALU.add))
        S.op("act", [tmpa], [tmpa], lambda e: e.activation(out=tmpa[:], in_=tmpa[:], func=AF.Exp))
        S.op("act", [tmpa], [tmpa], lambda e: e.activation(out=tmpa[:], in_=tmpa[:], func=AF.Ln, bias=1.0, scale=1.0))
        for c in range(NCH):
            S.op("dve", [tmpa, negA], [graw], lambda e, c=c: e.tensor_tensor(
                out=graw[:, c, :], in0=tmpa[:, c, :], in1=negA[:], op=ALU.mult))
        S.op("act", [ab], [beta], lambda e: e.activation(out=beta[:], in_=ab[:, :, 8:16], func=AF.Sigmoid))
        S.op("dve", [beta], [nbeta], lambda e: e.tensor_scalar(out=nbeta[:], in0=beta[:], scalar1=-1.0, scalar2=None, op0=ALU.mult))
        gc = P.sbuf("gc", [128, NCH, 8], F32)
        egc = P.sbuf("egc", [128, NCH, 8], F32)
        egl = P.sbuf("egl", [128, NCH, 8], F32)
        edec = P.sbuf("edec", [128, NCH, 8], F32)
        bgc = P.sbuf("bgc", [128, NCH, 8], F32)
        pgc = self.ps()
        S.op("pe", [UT, graw], [pgc], lambda e: e.matmul(pgc[:, 0:256], lhsT=UT[:, :], rhs=graw[:].rearrange("p a b -> p (a b)"), start=True, stop=True))
        pgl = self.ps()
        S.op("pe", [self.ones32, graw], [pgl], lambda e: e.matmul(pgl[:, 0:256], lhsT=self.ones32[:, :], rhs=graw[:].rearrange("p a b -> p (a b)"), start=True, stop=True))
        fl = lambda b: b[:].rearrange("p a b -> p (a b)")
        S.op("dve", [pgc], [gc], lambda e: e.tensor_copy(out=fl(gc), in_=pgc[:, 0:256]))
        S.op("act", [pgc], [egc], lambda e: e.activation(out=fl(egc), in_=pgc[:, 0:256], func=AF.Exp))
        S.op("act", [pgl], [egl], lambda e: e.activation(out=fl(egl), in_=pgl[:, 0:256], func=AF.Exp))
        S.op("dve", [pgl, gc], [edec], lambda e: e.tensor_tensor(out=fl(edec), in0=pgl[:, 0:256], in1=fl(gc), op=ALU.subtract))
        S.op("act", [edec], [edec], lambda e: e.activation(out=fl(edec), in_=fl(edec), func=AF.Exp))
        S.op("dve", [beta, egc], [bgc], lambda e: e.tensor_tensor(out=fl(bgc), in0=fl(beta), in1=fl(egc), op=ALU.mult))
        xin = [P.sbuf("xin%d" % k, [128, T + 3], BF16) for k in range(1)]
        for k in range(1):
            S.op("pool", [], [xin[k]], lambda e, k=k: e.memset(xin[k][:, 0:3], 0.0))
        xin_i = [0]
        identb = P.sbuf("identb", [128, 128], BF16)
        S.op("pool", [self.ident32], [identb], lambda e: e.tensor_copy(out=identb[:], in_=self.ident32[:]))
        dg = [[P.sbuf("dg%d_%d" % (a, j), [128, 128], BF16) for j in range(4)] for a in range(2)]
        sqb = [P.sbuf("sqb%d" % k, [128, 512], F32) for k in range(2)]
        ybuf = P.sbuf("ybuf", [128, T], F32)
        qTbs = [P.sbuf("qTb%d" % k, [128, T], BF16) for k in range(2)]
        kTf = P.sbuf("kTf", [128, T], F32)
        kTb = P.sbuf("kTb", [128, T], BF16)
        vTf = P.sbuf("vTf", [128, T], F32)
        rn = P.sbuf("rn", [128, 512], F32)
        zt = P.sbuf("zt", [128, NCH, 128], BF16)
        zgs = [P.sbuf("zg%d" % k, [128, NCH, 128], BF16) for k in range(2)]
        def chunked(name):
            big = P.sbuf(name, [128, NCH, 128], BF16)
            out = []
            for k in range(NCH):
                b = Buf(big.t[:, k, :], "%s%d" % (name, k))
                S.bufs.append(b)
                out.append(b)
            return out
        U_, WT, AT, KD = chunked("U_"), chunked("WT"), chunked("AT"), chunked("KD")
        ots = [P.sbuf("ot%d" % k, [128, T], BF16) for k in range(2)]
        Ssts = [P.sbuf("Sst%d" % k, [128, 128], F32) for k in range(2)]
        Sbfss = [[P.sbuf("Sbf%d_%d" % (a, k), [128, 128], BF16) for k in range(2)] for a in range(2)]
        tails = []
        s2prog = {}
        NR = GDN_NR
        def rot(name, dt=F32, n=NR):
            return [P.sbuf("%s%d" % (name, k), [128, 128], dt) for k in range(n)]
        ktok, vb, kbg, G1, dcs, dsc, Nm, Am, Pm, Nn, An = (rot(x) for x in ("ktok", "vb", "kbg", "G1", "dcs", "dsc", "Nm", "Am", "Pm", "Nn", "An"))
        vnew = rot("vnew", BF16, 2)
        o2b = rot("o2b", F32, 2)
        osb = rot("osb", F32, 2)
        ysb = rot("ysb", F32, 2)
        ssq = P.sbuf("ssq", [128, 2], F32)
        junk = P.sbuf("junk", [128, 128], F32)
        ew = [0]
        def evac_copy(src_ps, dst, dst_ap, src_ap, extra_reads=()):
            ew[0] += 1
            if ew[0] % 2 == 0:
                S.op("act", [src_ps] + list(extra_reads), [dst], lambda e: e.activation(out=dst_ap, in_=src_ap, func=AF.Copy))
            else:
                S.op("dve", [src_ps] + list(extra_reads), [dst], lambda e: e.tensor_copy(out=dst_ap, in_=src_ap))
        heads = list(heads)
        pre_rr = [0]
        def pre_ps():
            b = self.PS[pre_rr[0] % 4]
            pre_rr[0] += 1
            return b
        for hi, h in enumerate(heads):
            hp = hi % 2
            qTb, zg, ot, Sst, Sbfs = qTbs[hp], zgs[hp], ots[hp], Ssts[hp], Sbfss[hp]
            for wi, which in enumerate(("q", "k", "v")):
                xi = xin[0]
                xin_i[0] += 1
                row = wi * 1024 + h * 128
                S.dma([], [xi], xi[:, 3:], GT[row:row + 128, :])
                def wcol(j, wi=wi):
                    return self.vcol("conv", (li * 4 + j) * 24 + wi * 8 + h)
                dgs = dg[xin_i[0] % 2]
                for j in range(4):
                    S.op("dve", [identb, self.VEC], [dgs[j]], lambda e, j=j: e.tensor_scalar(
                        out=dgs[j][:], in0=identb[:], scalar1=wcol(j), scalar2=None, op0=ALU.mult))
                dst = vTf if which == "v" else ybuf
                for g in range(8):
                    pc = pre_ps()
                    def mmc(e, pc=pc, g=g):
                        last = None
                        for j in range(4):
                            last = e.matmul(pc[:, :], lhsT=dgs[j][:], rhs=xi[:, g * 512 + j:g * 512 + j + 512], start=(j == 0), stop=(j == 3))
                        return last
                    S.op("pe", [xi] + dgs, [pc], mmc)
                    S.op("act", [pc], [dst], lambda e, pc=pc, g=g: e.activation(out=dst[:, g * 512:(g + 1) * 512], in_=pc[:, :], func=AF.Silu))
                if which == "v":
                    continue
                for g in range(8):
                    sl = slice(g * 512, (g + 1) * 512)
                    sq_ = sqb[g % 2]
                    S.op("act", [ybuf], [sq_], lambda e, sq_=sq_, sl=sl: e.activation(out=sq_[:], in_=ybuf[:, sl], func=AF.Square))
                    pss = pre_ps()
                    S.op("pe", [sq_, self.ones32], [pss], lambda e, pss=pss, sq_=sq_: e.matmul(pss[:, :], lhsT=self.ones32[:, :], rhs=sq_[:], start=True, stop=True))
                    S.op("dve", [pss], [rn], lambda e, pss=pss: e.tensor_scalar(out=rn[:], in0=pss[:, :], scalar1=EPS, scalar2=None, op0=ALU.add))
                    S.op("act", [rn], [rn], lambda e: e.activation(out=rn[:], in_=rn[:], func=AF.Sqrt))
                    S.op("dve", [rn], [rn], lambda e: e.reciprocal(out=rn[:], in_=rn[:]))
                    if which == "q":
                        S.op("dve", [ybuf, rn], [qTb], lambda e, sl=sl: e.scalar_tensor_tensor(
                            out=qTb[:, sl], in0=ybuf[:, sl], scalar=float(128 ** -0.5), in1=rn[:], op0=ALU.mult, op1=ALU.mult))
                    else:
                        S.op("dve", [ybuf, rn], [kTf], lambda e, sl=sl: e.tensor_tensor(out=kTf[:, sl], in0=ybuf[:, sl], in1=rn[:], op=ALU.mult))
                        S.op("pool", [kTf], [kTb], lambda e, sl=sl: e.tensor_copy(out=kTb[:, sl], in_=kTf[:, sl]))
            S.dma([], [zt], zt[:], ZT[:, h * 128:(h + 1) * 128].rearrange("(c p) d -> p c d", p=128))
            S.op("act", [zt], [zt], lambda e: e.activation(out=zt[:], in_=zt[:], func=AF.Silu))
            for c in range(NCH):
                S.op("pool", [zt, self.VEC], [zg], lambda e, c=c: e.tensor_tensor(
                    out=zg[:, c, :], in0=zt[:, c, :], in1=self.vcol("gngb", li * 128, 128), op=ALU.mult))
            done1 = [False] * nchunks

            class Pool3:
                def __init__(self, banks):
                    self.banks = banks
                    self.i = 0

                def get(self):
                    b = self.banks[self.i % len(self.banks)]
                    self.i += 1
                    return PQ(b, 0)

                def pair(self):
                    b = self.banks[self.i % len(self.banks)]
                    self.i += 1
                    return PQ(b, 0), PQ(b, 1)

            class PoolH:
                def __init__(self, bank):
                    self.bank = bank
                    self.i = 0

                def pair(self):
                    hh = self.i % 2
                    self.i += 1
                    return PQ(self.bank, 2 * hh), PQ(self.bank, 2 * hh + 1)

                def get(self):
                    return self.pair()[0]
            pools = [PoolH(self.PS[k]) for k in range(4)]
            pool2 = Pool3(self.PS[4:6] if hp == 0 else self.PS[6:8])
            def evq(pq, dst, dst_ap, extra_reads=()):
                evac_copy(pq, dst, dst_ap, pq.ap, extra_reads)

            def stage1(c, r, h=h, hi=hi):
                while hi > 0 and s2prog.get(hi - 1, nchunks) <= c:
                    yield
                cs = slice(c * C, (c + 1) * C)
                col = slice(h, h + 1)
                pl = pools[r]
                pk, pv = pl.pair()
                def f(e):
                    e.transpose(out=pk.ap, in_=kTf[:, cs], identity=I32m[:, :])
                    return e.transpose(out=pv.ap, in_=vTf[:, cs], identity=I32m[:, :])
                S.op("pe", [kTf, vTf, I32m], [pk], f)
                S.op("dve", [SU, graw], [G1[r]], lambda e: e.tensor_scalar(out=G1[r][:], in0=SU[:], scalar1=graw[:, c, col], scalar2=None, op0=ALU.mult))
                yield
                S.op("dve", [pk], [ktok[r]], lambda e: e.tensor_copy(out=ktok[r][:], in_=pk.ap))
                S.op("dve", [pv, beta], [vb[r]], lambda e: e.tensor_scalar(out=vb[r][:], in0=pv.ap, scalar1=beta[:, c, col], scalar2=None, op0=ALU.mult))
                pd1, pd2 = pl.pair()
                def f(e):
                    e.matmul(pd1.ap, lhsT=UT[:, :], rhs=G1[r][:], start=True, stop=True)
                    return e.matmul(pd2.ap, lhsT=G1[r][:], rhs=UT[:, :], start=True, stop=True)
                S.op("pe", [UT, G1[r]], [pd1], f)
                yield
                S.op("act", [pd1], [dcs[r]], lambda e: e.activation(out=dcs[r][:], in_=pd1.ap, func=AF.Exp))
                S.op("act", [pd2], [dsc[r]], lambda e: e.activation(out=dsc[r][:], in_=pd2.ap, func=AF.Exp))
                S.op("act", [ktok[r], bgc], [kbg[r]], lambda e: e.activation(out=kbg[r][:], in_=ktok[r][:], func=AF.Copy, scale=bgc[:, c, col]))
                S.op("act", [ktok[r], edec], [KD[c]], lambda e: e.activation(out=KD[c][:], in_=ktok[r][:], func=AF.Copy, scale=edec[:, c, col]))
                pkk, pqk = pl.pair()
                def f(e):
                    e.matmul(pkk.ap, lhsT=kTf[:, cs], rhs=kTf[:, cs], start=True, stop=True)
                    return e.matmul(pqk.ap, lhsT=kTb[:, cs], rhs=qTb[:, cs], start=True, stop=True)
                S.op("pe", [kTf, kTb, qTb], [pkk], f)
                yield
                S.op("pool", [dcs[r], SU], [dcs[r]], lambda e: e.tensor_tensor(out=dcs[r][:], in0=dcs[r][:], in1=SU[:], op=ALU.mult))
                S.op("pool", [dsc[r], UT], [dsc[r]], lambda e: e.tensor_tensor(out=dsc[r][:], in0=dsc[r][:], in1=UT[:], op=ALU.mult))
                yield
                S.op("dve", [pkk, nbeta, dcs[r]], [Nm[r]], lambda e: e.scalar_tensor_tensor(
                    out=Nm[r][:], in0=pkk.ap, scalar=nbeta[:, c, col], in1=dcs[r][:], op0=ALU.mult, op1=ALU.mult))
                S.op("dve", [pqk, dsc[r]], [AT[c]], lambda e: e.tensor_tensor(out=AT[c][:], in0=pqk.ap, in1=dsc[r][:], op=ALU.mult))
                yield
                pa = pl.get()
                S.op("pe", [Nm[r], I32m], [pa], lambda e: e.transpose(out=pa.ap, in_=Nm[r][:], identity=I32m[:, :]))
                yield
                S.op("act", [pa], [Am[r]], lambda e: e.activation(out=Am[r][:], in_=pa.ap, func=AF.Copy))
                yield
                S.op("pool", [Am[r], I32m], [Pm[r]], lambda e: e.tensor_tensor(out=Pm[r][:], in0=Am[r][:], in1=I32m[:], op=ALU.add))
                Ncur, Acur = Nm[r], Am[r]
                for lvl in range(1, 7):
                    Nnx = Nn[r] if lvl % 2 == 1 else Nm[r]
                    Anx = An[r] if lvl % 2 == 1 else Am[r]
                    pn, pa2 = pl.pair()
                    def f(e):
                        last = e.matmul(pn.ap, lhsT=Acur[:], rhs=Ncur[:], start=True, stop=True)
                        if lvl < 6:
                            last = e.matmul(pa2.ap, lhsT=Ncur[:], rhs=Acur[:], start=True, stop=True)
                        return last
                    S.op("pe", [Acur, Ncur], [pn], f)
                    yield
                    if (lvl + r) % 2 == 0:
                        S.op("act", [pn], [Nnx], lambda e: e.activation(out=Nnx[:], in_=pn.ap, func=AF.Copy))
                        if lvl < 6:
                            S.op("act", [pa2], [Anx], lambda e: e.activation(out=Anx[:], in_=pa2.ap, func=AF.Copy))
                    else:
                        S.op("dve", [pn], [Nnx], lambda e: e.tensor_copy(out=Nnx[:], in_=pn.ap))
                        if lvl < 6:
                            S.op("dve", [pa2], [Anx], lambda e: e.tensor_copy(out=Anx[:], in_=pa2.ap))
                    yield
                    pp = pl.get()
                    S.op("pe", [Nnx, Pm[r]], [pp], lambda e: e.matmul(pp.ap, lhsT=Nnx[:], rhs=Pm[r][:], start=True, stop=True))
                    yield
                    S.op("dve", [pp, Pm[r]], [Pm[r]], lambda e: e.tensor_tensor(out=Pm[r][:], in0=pp.ap, in1=Pm[r][:], op=ALU.add))
                    yield
                    Ncur, Acur = Nnx, Anx
                pu, pw = pl.pair()
                def f(e):
                    e.matmul(pu.ap, lhsT=Pm[r][:], rhs=vb[r][:], start=True, stop=True)
                    return e.matmul(pw.ap, lhsT=kbg[r][:], rhs=Pm[r][:], start=True, stop=True)
                S.op("pe", [Pm[r], vb[r], kbg[r]], [pu], f)
                yield
                S.op("act", [pu], [U_[c]], lambda e: e.activation(out=U_[c][:], in_=pu.ap, func=AF.Copy))
                S.op("act", [pw], [WT[c]], lambda e: e.activation(out=WT[c][:], in_=pw.ap, func=AF.Copy))
                done1[c] = True

            def stage2(h=h, hi=hi, qTb=qTb, zg=zg, ot=ot, Sst=Sst, Sbfs=Sbfs, pool2=pool2, done1=done1):
                S.op("pool", [], [Sbfs[0]], lambda e: e.memset(Sbfs[0][:], 0.0))
                S.op("pool", [], [Sst], lambda e: e.memset(Sst[:], 0.0))
                col = slice(h, h + 1)
                for c in range(nchunks):
                    s2prog[hi] = c
                    while not done1[c]:
                        yield
                    cs = slice(c * C, (c + 1) * C)
                    r2 = c % 2
                    Sold, Snew = Sbfs[c % 2], Sbfs[(c + 1) % 2]
                    p1, p2a = pool2.get(), pool2.get()
                    S.op("pe", [WT[c], Sold], [p1], lambda e: e.matmul(p1.ap, lhsT=WT[c][:], rhs=Sold[:], start=True, stop=True))
                    S.op("pe", [qTb, Sold], [p2a], lambda e: e.matmul(p2a.ap, lhsT=qTb[:, cs], rhs=Sold[:], start=True, stop=True))
                    yield
                    S.op("dve", [U_[c], p1], [vnew[r2]], lambda e: e.tensor_tensor(out=vnew[r2][:], in0=U_[c][:], in1=p1.ap, op=ALU.subtract))
                    S.op("act", [p2a, egc], [o2b[r2]], lambda e: e.activation(out=o2b[r2][:], in_=p2a.ap, func=AF.Copy, scale=egc[:, c, col]))
                    yield
                    p3, p2b = pool2.get(), pool2.get()
                    S.op("pe", [KD[c], vnew[r2]], [p3], lambda e: e.matmul(p3.ap, lhsT=KD[c][:], rhs=vnew[r2][:], start=True, stop=True))
                    S.op("pe", [AT[c], vnew[r2]], [p2b], lambda e: e.matmul(p2b.ap, lhsT=AT[c][:], rhs=vnew[r2][:], start=True, stop=True))
                    yield
                    S.op("dve", [Sst, egl, p3], [Sst], lambda e: e.scalar_tensor_tensor(
                        out=Sst[:], in0=Sst[:], scalar=egl[:, c, col], in1=p3.ap, op0=ALU.mult, op1=ALU.add))
                    yield
                    S.op("act", [Sst], [Snew], lambda e: e.activation(out=Snew[:], in_=Sst[:], func=AF.Copy))
                    S.op("dve", [p2b, o2b[r2]], [osb[r2]], lambda e: e.tensor_tensor(out=osb[r2][:], in0=p2b.ap, in1=o2b[r2][:], op=ALU.add))
                    yield
                    S.op("act", [osb[r2]], [junk, ssq], lambda e: e.activation(out=junk[:], in_=osb[r2][:], func=AF.Square, accum_out=ssq[:, r2:r2 + 1]))
                    S.op("dve", [ssq], [ssq], lambda e: e.tensor_scalar(out=ssq[:, r2:r2 + 1], in0=ssq[:, r2:r2 + 1], scalar1=1.0 / 128, scalar2=EPS, op0=ALU.mult, op1=ALU.add))
                    S.op("act", [ssq], [ssq], lambda e: e.activation(out=ssq[:, r2:r2 + 1], in_=ssq[:, r2:r2 + 1], func=AF.Sqrt))
                    S.op("dve", [ssq], [ssq], lambda e: e.reciprocal(out=ssq[:, r2:r2 + 1], in_=ssq[:, r2:r2 + 1]))
                    S.op("dve", [osb[r2], ssq, zg], [ysb[r2]], lambda e: e.scalar_tensor_tensor(
                        out=ysb[r2][:], in0=osb[r2][:], scalar=ssq[:, r2:r2 + 1], in1=zg[:, c, :], op0=ALU.mult, op1=ALU.mult))
                    yield
                    py = pool2.get()
                    S.op("pe", [ysb[r2], I32m], [py], lambda e: e.transpose(out=py.ap, in_=ysb[r2][:], identity=I32m[:, :]))
                    S.op("act", [py], [ot], lambda e: e.activation(out=ot[:, cs], in_=py.ap, func=AF.Copy))
                    yield
                s2prog[hi] = nchunks
                S.dma([ot], [], OT[h * 128:(h + 1) * 128, :], ot[:])

            pending = list(range(nchunks))
            active = {}
            tails.append(stage2())
            while pending or active:
                while pending and len(active) < NR:
                    c = pending.pop(0)
                    slot = [k for k in range(NR) if k not in active][0]
                    active[slot] = stage1(c, slot)
                for slot in list(active.keys()):
                    try:
                        next(active[slot])
                    except StopIteration:
                        del active[slot]
                for g in list(tails):
                    try:
                        next(g)
                    except StopIteration:
                        tails.remove(g)
        while tails:
            for g in list(tails):
                try:
                    next(g)
                except StopIteration:
                    tails.remove(g)
        P.close()


def prep_shared(inp):
    perm = rope_partner_perm()
    w = np.asarray(inp["attn_w_in"], np.float32)
    w4 = np.concatenate([w, w[:, :, 0:512][:, :, perm], w[:, :, 512:1024][:, :, perm]], axis=2)
    return {
        "ada_w": np.ascontiguousarray(inp["ada_w"], np.float32),
        "attn_w_in": np.ascontiguousarray(w4),
        "attn_w_out": np.ascontiguousarray(inp["attn_w_out"], np.float32),
        "gdn_w_in": np.ascontiguousarray(inp["gdn_w_in"], np.float32),
        "gdn_w_out": np.ascontiguousarray(inp["gdn_w_out"], np.float32),
        "ffn_w_in": np.ascontiguousarray(inp["ffn_w_in"], np.float32),
        "ffn_w_out": np.ascontiguousarray(inp["ffn_w_out"], np.float32),
    }


def prep_core(inp, b, shared):
    m = dict(shared)
    m["xT"] = np.ascontiguousarray(np.asarray(inp["x"][b], np.float32).T)
    m["vecs"] = build_vecs(inp, b)
    m["pos"] = np.ascontiguousarray(np.asarray(inp["positions"][b], np.int32).reshape(1, T))
    return m


def build_full():
    pg = Prog()
    pg.defer_mod = True
    pg.phase0()
    for l in range(NL):
        pg.phase1(l)
        if l % 2 == 0:
            pg.phase_attn(mod_layers=((1, 2) if l == 0 else (3,)))
        else:
            pg.phase_gdn(l // 2)
        pg.phase34(l, final=(l == NL - 1))
    return pg


def kernel(**inputs):
    inp = {k: np.asarray(v) for k, v in inputs.items()}
    pg = build_full()
    shared = prep_shared(inp)
    in_maps = []
    for b in range(8):
        m = prep_core(inp, b, shared)
        in_maps.append({k: m[k] for k in pg.used_inputs})
    res = run_bass_kernel_spmd(pg.nc, in_maps, core_ids=list(range(8)))
    out = np.empty((8, T, D), np.float32)
    for b in range(8):
        out[b] = np.asarray(res.results[b]["outT"], np.float32).T
    return out
```

```python
import numpy as np
from contextlib import ExitStack
import concourse.bass as bass
import concourse.mybir as mybir
from concourse.bass_utils import run_bass_kernel_spmd

F32 = mybir.dt.float32
BF16 = mybir.dt.bfloat16
I32 = mybir.dt.int32
AF = mybir.ActivationFunctionType
ALU = mybir.AluOpType
AX = mybir.AxisListType

D = 1024
T = 4096
NL = 4
KC = 8
FH = 2816
NJ = 22
EPS = 1e-6
NDMA = 48
BIG = 30000.0
SAME_ENG_SYNC = False
GDN_NR = 4
S2_EVERY = 6
PSQ_SEPARATE = False
GDN_DBG = 0


class Buf:
    __slots__ = ("t", "w", "r", "name")

    def __init__(self, t, name=""):
        self.t = t
        self.w = None
        self.r = {}
        self.name = name

    def __getitem__(self, idx):
        return self.t[idx]


class PQ:
    __slots__ = ("ap", "b")

    def __init__(self, bank, q):
        self.b = bank
        self.ap = bank.t[:, q * 128:(q + 1) * 128]


class Sched:
    def __init__(self, nc):
        self.nc = nc
        self.E = {"pe": nc.tensor, "act": nc.scalar, "dve": nc.vector, "pool": nc.gpsimd, "sp": nc.sync}
        self.sem = {k: nc.alloc_semaphore("s_" + k) for k in ("pe", "act", "dve", "pool")}
        self.cnt = {k: 0 for k in self.sem}
        self.seen = {k: {} for k in self.E}
        self.dsem = [nc.alloc_semaphore("d%d" % i) for i in range(NDMA)]
        self.dcnt = [0] * NDMA
        self.drr = 0
        self.bufs = []
        self.ps_rr = 0

    def sbuf(self, name, shape, dtype):
        b = Buf(self.nc.alloc_sbuf_tensor(name, list(shape), dtype), name)
        self.bufs.append(b)
        return b

    def psum(self, name, shape, dtype=F32):
        b = Buf(self.nc.alloc_psum_tensor(name, list(shape), dtype), name)
        self.bufs.append(b)
        return b

    def dram(self, t, name=""):
        b = Buf(t, name)
        self.bufs.append(b)
        return b

    def _wait(self, eng, ev):
        if ev is None:
            return
        sem, val, key = ev
        if self.seen[eng].get(key, 0) >= val:
            return
        self.E[eng].wait_ge(sem, val)
        self.seen[eng][key] = val

    def _deps(self, eng, reads, writes):
        reads = [getattr(b, "b", b) for b in reads]
        writes = [getattr(b, "b", b) for b in writes]
        for b in reads:
            self._wait(eng, b.w)
        for b in writes:
            if b.w is not None and (SAME_ENG_SYNC or b.w[2] != eng):
                self._wait(eng, b.w)
            for ev in b.r.values():
                if SAME_ENG_SYNC or ev[2] != eng:
                    self._wait(eng, ev)

    def _done(self, ev, reads, writes):
        reads = [getattr(b, "b", b) for b in reads]
        writes = [getattr(b, "b", b) for b in writes]
        for b in reads:
            b.r[ev[2]] = ev
        for b in writes:
            b.w = ev
            b.r = {}

    def op(self, eng, reads, writes, fn):
        self._deps(eng, reads, writes)
        ins = fn(self.E[eng])
        self.cnt[eng] += 1
        ins.then_inc(self.sem[eng], 1)
        ev = (self.sem[eng], self.cnt[eng], eng)
        self._done(ev, reads, writes)
        return ev

    def seq(self, eng, buf, fns, reads=()):
        ev = None
        for k, fn in enumerate(fns):
            ev = self.op(eng, ([buf] if k > 0 else []) + list(reads), [buf], fn)
        return ev

    def dma(self, reads, writes, out, in_, eng="sp"):
        k = self.drr
        self.drr = (k + 1) % NDMA
        key = ("d", k)
        if self.dcnt[k] > 0:
            self._wait(eng, (self.dsem[k], self.dcnt[k], key))
        self._deps(eng, reads, writes)
        ins = self.E[eng].dma_start(out=out, in_=in_)
        self.dcnt[k] += 16
        ins.then_inc(self.dsem[k], 16)
        ev = (self.dsem[k], self.dcnt[k], key)
        self._done(ev, reads, writes)
        return ev

    def barrier(self):
        evs = []
        for k in self.sem:
            if self.cnt[k] > 0:
                evs.append((self.sem[k], self.cnt[k], k))
        for k in range(NDMA):
            if self.dcnt[k] > 0:
                evs.append((self.dsem[k], self.dcnt[k], ("d", k)))
        for eng in self.E:
            for ev in evs:
                self._wait(eng, ev)
        for b in self.bufs:
            b.w = None
            b.r = {}


class Phase:
    _uid = [0]

    def __init__(self, S):
        Phase._uid[0] += 1
        self.uid = Phase._uid[0]
        self.S = S
        self.nc = S.nc
        self.st = ExitStack()
        self.nbufs0 = len(S.bufs)

    def sbuf(self, name, shape, dtype):
        name = "%s_p%d" % (name, self.uid)
        t = self.st.enter_context(self.nc.sbuf_tensor(name, list(shape), dtype))
        b = Buf(t, name)
        self.S.bufs.append(b)
        return b

    def psum(self, name, shape, dtype=F32):
        name = "%s_p%d" % (name, self.uid)
        t = self.st.enter_context(self.nc.psum_tensor(name, list(shape), dtype))
        b = Buf(t, name)
        self.S.bufs.append(b)
        return b

    def close(self):
        self.S.barrier()
        self.st.close()
        del self.S.bufs[self.nbufs0:]


VOFF = {}


def _vec_layout():
    off = 0
    for name, n in [("gm", 32), ("gf", 32), ("fin", 8), ("adab", 192), ("conv", 192), ("gngb", 256),
                    ("alog", 16), ("dtb", 16), ("invf", 1), ("sinsign", 1), ("cT", 8)]:
        VOFF[name] = off
        off += n
    return off


NV = _vec_layout()


def build_vecs(inp, b):
    v = np.zeros((128, NV), np.float32)
    def chunked(a):
        a = np.asarray(a, np.float32).reshape(-1, 128)
        return a.T
    v[:, VOFF["gm"]:VOFF["gm"] + 32] = chunked(inp["norm_mix_g"])
    v[:, VOFF["gf"]:VOFF["gf"] + 32] = chunked(inp["norm_ffn_g"])
    v[:, VOFF["fin"]:VOFF["fin"] + 8] = chunked(inp["final_norm_g"])
    v[:, VOFF["adab"]:VOFF["adab"] + 192] = chunked(inp["ada_b"])
    v[:, VOFF["conv"]:VOFF["conv"] + 192] = chunked(inp["gdn_conv_w"])
    v[:, VOFF["gngb"]:VOFF["gngb"] + 256] = np.broadcast_to(np.asarray(inp["gdn_norm_g"], np.float32).reshape(1, 256), (128, 256))
    v[:, VOFF["alog"]:VOFF["alog"] + 16] = np.broadcast_to(np.asarray(inp["gdn_a_log"], np.float32).reshape(1, 16), (128, 16))
    v[:, VOFF["dtb"]:VOFF["dtb"] + 16] = np.broadcast_to(np.asarray(inp["gdn_dt_bias"], np.float32).reshape(1, 16), (128, 16))
    p = np.arange(128) % 64
    inv = (500000.0 ** (-np.arange(8, dtype=np.float32) * 2.0 / 16)).astype(np.float32)
    v[:, VOFF["invf"]] = np.where(p < 16, inv[p % 8], 0.0)
    v[:, VOFF["sinsign"]] = np.where(p < 8, -1.0, np.where(p < 16, 1.0, 0.0))
    v[:, VOFF["cT"]:VOFF["cT"] + 8] = chunked(inp["c"][b])
    return v


def rope_partner_perm():
    idx = np.arange(512)
    d = idx % 64
    return np.where(d < 8, idx + 8, np.where(d < 16, idx - 8, idx))


class LazyDram(dict):
    def __init__(self, prog):
        super().__init__()
        self.prog = prog

    def __missing__(self, name):
        pg = self.prog
        if name in pg.in_shapes:
            shape, dt = pg.in_shapes[name]
            kind = "ExternalInput"
            pg.used_inputs.append(name)
        else:
            shape, dt = pg.scr_shapes[name]
            kind = "Internal"
            if name in pg.ext_out or name == "outT":
                kind = "ExternalOutput"
            if name in pg.ext_in:
                kind = "ExternalInput"
                pg.used_inputs.append(name)
        ap = pg.nc.dram_tensor(name, list(shape), dt, kind=kind).ap()
        self[name] = ap
        return ap


class Prog:
    def __init__(self, ext_out=(), ext_in=()):
        self.nc = bass.Bass("TRN2", target_bir_lowering=False)
        nc = self.nc
        self.ext_out = set(ext_out)
        self.ext_in = set(ext_in)
        self.S = Sched(nc)
        S = self.S
        self.in_shapes = {
            "xT": ([D, T], F32), "vecs": ([128, NV], F32), "pos": ([1, T], I32),
            "ada_w": ([NL, D, 6 * D], F32), "attn_w_in": ([2, D, 4096], F32), "attn_w_out": ([2, D, D], F32),
            "gdn_w_in": ([2, D, 4112], F32), "gdn_w_out": ([2, D, D], F32),
            "ffn_w_in": ([NL, D, 2 * FH], F32), "ffn_w_out": ([NL, FH, D], F32),
        }
        self.used_inputs = []
        self.defer_mod = False
        self.dr = LazyDram(self)
        self.dbg_outs = []
        self.scr_shapes = {
            "XT": ([D, T], F32), "COS": ([128, T], F32), "SIN": ([128, T], F32),
            "QT": ([2048, T], BF16),
            "VTOK": ([T, 1024], BF16),
            "OT": ([D, T], BF16),
            "GT": ([3072, T], BF16),
            "AB": ([T, 16], F32),
            "outT": ([D, T], F32),
        }
        self.VEC = S.sbuf("VEC", [128, NV], F32)
        self.MOD = S.sbuf("MOD", [128, NL * 48], F32)
        self.AM = S.sbuf("AMc", [128, NL * 8], F32)
        self.AFc = S.sbuf("AFc", [128, NL * 8], F32)
        self.ones32 = S.sbuf("ones32", [128, 128], F32)
        self.ident32 = S.sbuf("ident32", [128, 128], F32)
        self.onesbf = S.sbuf("onesbf", [128, 128], BF16)
        self.condT = S.sbuf("condT", [128, 8], F32)
        self.PS = [S.psum("ps%d" % i, [128, 512], F32) for i in range(8)]
        self.ps_i = 0
        S.op("pool", [], [self.ones32], lambda e: e.memset(self.ones32[:], 1.0))
        S.op("pool", [], [self.onesbf], lambda e: e.memset(self.onesbf[:], 1.0))
        S.seq("pool", self.ident32, [
            lambda e: e.memset(self.ident32[:], 0.0),
            lambda e: e.affine_select(out=self.ident32[:], in_=self.ident32[:], compare_op=ALU.not_equal,
                                      fill=1.0, base=0, pattern=[[-1, 128]], channel_multiplier=1)])

    def dbg(self, name, buf, ap):
        shape = [int(x) for x in ap.shape]
        d = self.nc.dram_tensor(name, shape, buf.t.dtype, kind="ExternalOutput").ap()
        self.S.dma([buf], [], d, ap)
        self.dbg_outs.append(name)

    def ps(self):
        b = self.PS[self.ps_i]
        self.ps_i = (self.ps_i + 1) % 8
        return b

    def psqs(self, n):
        if PSQ_SEPARATE:
            return [PQ(self.ps(), 0) for q in range(n)]
        b = self.ps()
        return [PQ(b, q) for q in range(n)]

    def vcol(self, name, i=0, n=1):
        o = VOFF[name] + i
        return self.VEC[:, o:o + n]

    def phase0(self):
        S, nc = self.S, self.nc
        P = Phase(S)
        VEC, MOD = self.VEC, self.MOD
        S.dma([], [VEC], VEC[:], self.dr["vecs"])
        condT = self.condT
        S.op("act", [VEC], [condT], lambda e: e.activation(out=condT[:], in_=self.vcol("cT", 0, 8), func=AF.Silu))
        CB = 1536
        wst = [P.sbuf("adaw%d" % i, [128, KC, CB], F32) for i in range(2)]
        k = 0
        for l in range(1 if self.defer_mod else NL):
            pm = self.ps()
            for cb in range(4):
                w = wst[k % 2]
                k += 1
                src = self.dr["ada_w"][l, :, cb * CB:(cb + 1) * CB].rearrange("(c p) n -> p c n", p=128)
                S.dma([], [w], w[:], src)
                def mm(e, w=w, cb=cb, pm=pm):
                    last = None
                    for j in range(12):
                        col = cb * 12 + j
                        for kc in range(KC):
                            last = e.matmul(pm[:, col:col + 1], lhsT=w[:, kc, j * 128:(j + 1) * 128],
                                            rhs=condT[:, kc:kc + 1], start=(kc == 0), stop=(kc == KC - 1))
                    return last
                S.op("pe", [w, condT], [pm], mm)
            S.op("dve", [pm, VEC], [MOD], lambda e, l=l, pm=pm: e.tensor_tensor(
                out=MOD[:, l * 48:(l + 1) * 48], in0=pm[:, 0:48], in1=self.vcol("adab", l * 48, 48), op=ALU.add))
            S.op("dve", [MOD, VEC], [self.AM], lambda e, l=l: e.scalar_tensor_tensor(
                out=self.AM[:, l * 8:(l + 1) * 8], in0=MOD[:, l * 48 + 8:l * 48 + 16], scalar=1.0,
                in1=self.vcol("gm", l * 8, 8), op0=ALU.add, op1=ALU.mult))
            S.op("dve", [MOD, VEC], [self.AFc], lambda e, l=l: e.scalar_tensor_tensor(
                out=self.AFc[:, l * 8:(l + 1) * 8], in0=MOD[:, l * 48 + 32:l * 48 + 40], scalar=1.0,
                in1=self.vcol("gf", l * 8, 8), op0=ALU.add, op1=ALU.mult))
        posi = P.sbuf("posi", [1, T], I32)
        posf = P.sbuf("posf", [1, T], F32)
        S.dma([], [posi], posi[:], self.dr["pos"])
        S.op("dve", [posi], [posf], lambda e: e.tensor_copy(out=posf[:], in_=posi[:]))
        angs = [P.sbuf("ang%d" % k, [128, 512], F32) for k in range(2)]
        rbuf = {}
        for wh in ("SIN", "COS"):
            for k in range(2):
                rbuf[(wh, k)] = (P.sbuf("r%s%d" % (wh, k), [128, 512], F32), P.sbuf("xs%s%d" % (wh, k), [128, 512], F32),
                                 P.sbuf("ki%s%d" % (wh, k), [128, 512], I32))
        for tt in range(8):
            sl = slice(tt * 512, (tt + 1) * 512)
            pb = self.ps()
            S.op("pe", [posf, self.ones32], [pb], lambda e, pb=pb, sl=sl: e.matmul(
                pb[:, :], lhsT=self.ones32[0:1, :], rhs=posf[0:1, sl], start=True, stop=True))
            ang = angs[tt % 2]
            S.op("dve", [pb, VEC], [ang], lambda e, pb=pb, ang=ang: e.tensor_scalar(
                out=ang[:], in0=pb[:, :], scalar1=self.vcol("invf"), scalar2=None, op0=ALU.mult))
            for which, shift in (("SIN", 0.0), ("COS", 0.5 * np.pi)):
                r, xs, ki = rbuf[(which, tt % 2)]
                C1 = 6.28125
                C2 = float(2 * np.pi - 6.28125)
                S.op("dve", [ang], [xs], lambda e, xs=xs, ang=ang, shift=shift: e.tensor_scalar(
                    out=xs[:], in0=ang[:], scalar1=float(shift), scalar2=None, op0=ALU.add))
                S.op("dve", [xs], [r], lambda e, r=r, xs=xs: e.tensor_scalar(
                    out=r[:], in0=xs[:], scalar1=float(1.0 / (2 * np.pi)), scalar2=None, op0=ALU.mult))
                S.op("dve", [r], [ki], lambda e, r=r, ki=ki: e.tensor_copy(out=ki[:], in_=r[:]))
                S.op("dve", [ki], [r], lambda e, r=r, ki=ki: e.tensor_copy(out=r[:], in_=ki[:]))
                S.op("dve", [r, xs], [xs], lambda e, r=r, xs=xs: e.scalar_tensor_tensor(
                    out=xs[:], in0=r[:], scalar=-C1, in1=xs[:], op0=ALU.mult, op1=ALU.add))
                S.op("dve", [r, xs], [xs], lambda e, r=r, xs=xs: e.scalar_tensor_tensor(
                    out=xs[:], in0=r[:], scalar=-C2, in1=xs[:], op0=ALU.mult, op1=ALU.add))
                S.op("dve", [xs], [xs], lambda e, xs=xs: e.tensor_scalar(
                    out=xs[:], in0=xs[:], scalar1=-3.14159, scalar2=3.14159, op0=ALU.max, op1=ALU.min))
                S.op("act", [xs], [r], lambda e, r=r, xs=xs: e.activation(out=r[:], in_=xs[:], func=AF.Sin))
                if which == "SIN":
                    S.op("dve", [r, VEC], [r], lambda e, r=r: e.tensor_scalar(
                        out=r[:], in0=r[:], scalar1=self.vcol("sinsign"), scalar2=None, op0=ALU.mult))
                S.dma([r], [], self.dr[which][:, sl], r[:])
        P.close()

    def load_weight_bf16(self, P, Wb, src3, ncols, stages, col0=0):
        S = self.S
        nk = src3.shape[1]
        CB = stages[0].t.shape[2]
        c = 0
        i = 0
        while c < ncols:
            n = min(CB, ncols - c)
            st = stages[i % len(stages)]
            i += 1
            S.dma([], [st], st[:, 0:nk, 0:n], src3[:, :, c:c + n])
            S.op("pool", [st], [Wb], lambda e, st=st, c=c, n=n: e.tensor_copy(
                out=Wb[:, 0:nk, col0 + c:col0 + c + n], in_=st[:, 0:nk, 0:n]))
            c += n

    def norm_tile(self, P, X, A, B, h, sq, tmps, rstd, n=512):
        S = self.S
        S.op("act", [X], [sq], lambda e: e.activation(out=sq[:, :, 0:n], in_=X[:, :, 0:n], func=AF.Square))
        pss = self.ps()
        def mm(e):
            last = None
            for kc in range(KC):
                last = e.matmul(pss[:, 0:n], lhsT=(self.ones32 if sq.t.dtype == F32 else self.onesbf)[:, :], rhs=sq[:, kc, 0:n],
                                start=(kc == 0), stop=(kc == KC - 1))
            return last
        S.op("pe", [sq, self.ones32, self.onesbf], [pss], mm)
        S.op("dve", [pss], [rstd], lambda e: e.tensor_scalar(
            out=rstd[:, 0:n], in0=pss[:, 0:n], scalar1=1.0 / D, scalar2=EPS, op0=ALU.mult, op1=ALU.add))
        S.op("act", [rstd], [rstd], lambda e: e.activation(out=rstd[:, 0:n], in_=rstd[:, 0:n], func=AF.Sqrt))
        S.op("dve", [rstd], [rstd], lambda e: e.reciprocal(out=rstd[:, 0:n], in_=rstd[:, 0:n]))
        for kc in range(KC):
            tmp = tmps[kc % len(tmps)]
            S.op("dve", [X, rstd, self.AM, self.AFc], [tmp], lambda e, kc=kc, tmp=tmp: e.scalar_tensor_tensor(
                out=tmp[:, 0:n], in0=X[:, kc, 0:n], scalar=A[:, kc:kc + 1], in1=rstd[:, 0:n], op0=ALU.mult, op1=ALU.mult))
            S.op("act", [tmp, self.MOD], [h], lambda e, kc=kc, tmp=tmp: e.activation(
                out=h[:, kc, 0:n], in_=tmp[:, 0:n], func=AF.Identity, bias=B[:, kc:kc + 1], scale=1.0))

    def phase1(self, l):
        S = self.S
        P = Phase(S)
        attn = (l % 2 == 0)
        i = l // 2
        NC_ = 4096 if attn else 4112
        Wb = P.sbuf("W1", [128, KC, NC_], BF16)
        stages = [P.sbuf("wst%d" % k, [128, KC, 512], F32) for k in range(2)]
        wsrc = self.dr["attn_w_in" if attn else "gdn_w_in"][i].rearrange("(c p) n -> p c n", p=128)
        self.load_weight_bf16(P, Wb, wsrc, NC_, stages)
        Xb = [P.sbuf("X%d" % k, [128, KC, 512], F32) for k in range(2)]
        sq = P.sbuf("sq", [128, KC, 512], F32)
        tmps = [P.sbuf("tmp%d" % k, [128, 512], F32) for k in range(2)]
        rstd = P.sbuf("rstd", [128, 512], F32)
        h = P.sbuf("h", [128, KC, 512], BF16)
        stg = [P.sbuf("stg%d" % k, [128, 512], BF16) for k in range(4)]
        stg32 = [P.sbuf("stgf%d" % k, [128, 16], F32) for k in range(2)]
        rt = [P.sbuf("rt%d" % k, [128, 512], F32) for k in range(4)]
        cs = [[P.sbuf("cs%d_%d" % (a, k), [128, 512], F32) for k in range(2)] for a in range(2)]
        A = self.AM[:, l * 8:(l + 1) * 8]
        B = self.MOD[:, l * 48:l * 48 + 8]
        XT3 = self.dr["xT" if l == 0 else "XT"].rearrange("(c p) t -> p c t", p=128)
        NT = T // 512
        S.dma([], [Xb[0]], Xb[0][:], XT3[:, :, 0:512])
        sk = 0
        ev = 0
        for tt in range(NT):
            sl = slice(tt * 512, (tt + 1) * 512)
            X = Xb[tt % 2]
            if tt + 1 < NT:
                S.dma([], [Xb[(tt + 1) % 2]], Xb[(tt + 1) % 2][:], XT3[:, :, (tt + 1) * 512:(tt + 2) * 512])
            if attn:
                cosb, sinb = cs[0][tt % 2], cs[1][tt % 2]
                S.dma([], [cosb], cosb[:], self.dr["COS"][:, sl])
                S.dma([], [sinb], sinb[:], self.dr["SIN"][:, sl])
            self.norm_tile(P, X, A, B, h, sq, tmps, rstd)

            def proj_fm(col0):
                pp = self.ps()
                def mm(e, pp=pp, col0=col0):
                    last = None
                    for kc in range(KC):
                        last = e.matmul(pp[:, :], lhsT=Wb[:, kc, col0:col0 + 128], rhs=h[:, kc, :],
                                        start=(kc == 0), stop=(kc == KC - 1))
                    return last
                S.op("pe", [Wb, h], [pp], mm)
                return pp

            def evac(pp, dst_ap, scale, width=512):
                nonlocal sk, ev
                st = stg[sk % 4]
                sk += 1
                if ev % 2 == 0:
                    S.op("act", [pp], [st], lambda e: e.activation(out=st[:, 0:width], in_=pp[:, 0:width], func=AF.Copy, scale=float(scale)))
                else:
                    S.op("dve", [pp], [st], lambda e: e.tensor_scalar(out=st[:, 0:width], in0=pp[:, 0:width], scalar1=float(scale), scalar2=None, op0=ALU.mult))
                ev += 1
                S.dma([st], [], dst_ap, st[:, 0:width])

            def proj_tm(col0, ncols, dst, dcol0, f32out=False):
                nonlocal sk
                for s in range(4):
                    pp = self.ps()
                    def mm(e, pp=pp, s=s):
                        last = None
                        for kc in range(KC):
                            last = e.matmul(pp[:, 0:ncols], lhsT=h[:, kc, s * 128:(s + 1) * 128], rhs=Wb[:, kc, col0:col0 + ncols],
                                            start=(kc == 0), stop=(kc == KC - 1))
                        return last
                    S.op("pe", [Wb, h], [pp], mm)
                    r0 = tt * 512 + s * 128
                    if f32out:
                        st = stg32[s % 2]
                        S.op("dve", [pp], [st], lambda e, st=st, pp=pp: e.tensor_copy(out=st[:, 0:ncols], in_=pp[:, 0:ncols]))
                        S.dma([st], [], dst[r0:r0 + 128, dcol0:dcol0 + ncols], st[:, 0:ncols])
                    else:
                        evac(pp, dst[r0:r0 + 128, dcol0:dcol0 + ncols], 1.0, ncols)

            if attn:
                QT = self.dr["QT"]
                for grp, (c0, pc0, d0, scale) in enumerate([(0, 3072, 0, 0.125), (512, 3584, 512, 1.0)]):
                    for n in range(4):
                        pq = proj_fm(c0 + n * 128)
                        pp = proj_fm(pc0 + n * 128)
                        t1, t2 = rt[(2 * n) % 4], rt[(2 * n + 1) % 4]
                        st = stg[sk % 4]
                        sk += 1
                        S.op("dve", [pq, cosb], [t1], lambda e, pq=pq, t1=t1, scale=scale: e.scalar_tensor_tensor(
                            out=t1[:], in0=pq[:, :], scalar=float(scale), in1=cosb[:], op0=ALU.mult, op1=ALU.mult))
                        S.op("dve", [pp, sinb], [t2], lambda e, pp=pp, t2=t2, scale=scale: e.scalar_tensor_tensor(
                            out=t2[:], in0=pp[:, :], scalar=float(scale), in1=sinb[:], op0=ALU.mult, op1=ALU.mult))
                        S.op("pool", [t1, t2], [st], lambda e, st=st, t1=t1, t2=t2: e.tensor_tensor(
                            out=st[:], in0=t1[:], in1=t2[:], op=ALU.add))
                        S.dma([st], [], QT[d0 + n * 128:d0 + (n + 1) * 128, sl], st[:])
                for n in range(4):
                    evac(proj_fm(1536 + n * 128), QT[1024 + n * 128:1024 + (n + 1) * 128, sl], 0.125)
                for n in range(4):
                    evac(proj_fm(2048 + n * 128), QT[1536 + n * 128:1536 + (n + 1) * 128, sl], 1.0)
                proj_tm(1024, 512, self.dr["VTOK"], 0)
                proj_tm(2560, 512, self.dr["VTOK"], 512)
            else:
                GT = self.dr["GT"]
                for n in range(24):
                    evac(proj_fm(n * 128), GT[n * 128:(n + 1) * 128, sl], 1.0)
                proj_tm(3072, 512, self.dr["VTOK"], 0)
                proj_tm(3584, 512, self.dr["VTOK"], 512)
                proj_tm(4096, 16, self.dr["AB"], 0, f32out=True)
        P.close()

    def phase34(self, l, final=False, tiles=None):
        S = self.S
        P = Phase(S)
        i = l // 2
        TT = 256
        Wo = P.sbuf("Wo", [128, KC, D], BF16)
        Wi = P.sbuf("Wi", [128, KC, 2 * FH], BF16)
        W2 = P.sbuf("W2", [128, NJ, D], BF16)
        PW = Phase(S)
        stA = [PW.sbuf("stA%d" % k, [128, KC, 512], F32) for k in range(2)]
        stB = [PW.sbuf("stB%d" % k, [128, NJ, 128], F32) for k in range(2)]
        wo_src = self.dr["attn_w_out" if l % 2 == 0 else "gdn_w_out"][i].rearrange("(c p) n -> p c n", p=128)
        self.load_weight_bf16(PW, Wo, wo_src, D, stA)
        self.load_weight_bf16(PW, Wi, self.dr["ffn_w_in"][l].rearrange("(c p) n -> p c n", p=128), 2 * FH, stA)
        self.load_weight_bf16(PW, W2, self.dr["ffn_w_out"][l].rearrange("(c p) n -> p c n", p=128), D, stB)
        PW.close()
        Xb = [P.sbuf("X%d" % k, [128, KC, TT], F32) for k in range(2)]
        ob = [P.sbuf("oT%d" % k, [128, KC, TT], BF16) for k in range(2)]
        sq = P.sbuf("sq", [128, KC, TT], BF16)
        tmps = [P.sbuf("tmp%d" % k, [128, TT], F32) for k in range(2)]
        sgs = [P.sbuf("sg%d" % k, [128, TT], F32) for k in range(2)]
        rstds = [P.sbuf("rstd%d" % k, [128, TT], F32) for k in range(2)]
        hb = [P.sbuf("h%d" % k, [128, KC, TT], BF16) for k in range(2)]
        hid = P.sbuf("hid", [128, NJ, TT], BF16)
        GM = self.MOD[:, l * 48 + 16:l * 48 + 24]
        GF = self.MOD[:, l * 48 + 40:l * 48 + 48]
        A = self.AFc[:, l * 8:(l + 1) * 8]
        B = self.MOD[:, l * 48 + 24:l * 48 + 32]
        XS3 = self.dr["xT" if l == 0 else "XT"].rearrange("(c p) t -> p c t", p=128)
        XD3 = self.dr["XT"].rearrange("(c p) t -> p c t", p=128)
        OT3 = self.dr["OT"].rearrange("(c p) t -> p c t", p=128)
        if final:
            OUT3 = self.dr["outT"].rearrange("(c p) t -> p c t", p=128)
        NT = T // TT
        tl = list(range(NT)) if tiles is None else list(tiles)

        def front(ti):
            tt = tl[ti]
            sl = slice(tt * TT, (tt + 1) * TT)
            X, oT, h = Xb[ti % 2], ob[ti % 2], hb[ti % 2]
            S.dma([], [oT], oT[:], OT3[:, :, sl])
            S.dma([], [X], X[:], XS3[:, :, sl])
            for n in range(KC):
                pp = self.ps()
                def mm(e, pp=pp, n=n):
                    last = None
                    for kc in range(KC):
                        last = e.matmul(pp[:, 0:TT], lhsT=Wo[:, kc, n * 128:(n + 1) * 128], rhs=oT[:, kc, :],
                                        start=(kc == 0), stop=(kc == KC - 1))
                    return last
                S.op("pe", [Wo, oT], [pp], mm)
                S.op("dve", [pp, X, self.MOD], [X], lambda e, pp=pp, n=n: e.scalar_tensor_tensor(
                    out=X[:, n, :], in0=pp[:, 0:TT], scalar=GM[:, n:n + 1], in1=X[:, n, :], op0=ALU.mult, op1=ALU.add))
            self.norm_tile(P, X, A, B, h, sq, tmps, rstds[ti % 2], n=TT)

        def back(ti):
            tt = tl[ti]
            sl = slice(tt * TT, (tt + 1) * TT)
            X, h = Xb[ti % 2], hb[ti % 2]
            for j in range(NJ):
                pg = self.ps()
                pu = self.ps()
                def mmg(e, pg=pg, j=j):
                    last = None
                    for kc in range(KC):
                        last = e.matmul(pg[:, 0:TT], lhsT=Wi[:, kc, j * 128:(j + 1) * 128], rhs=h[:, kc, :],
                                        start=(kc == 0), stop=(kc == KC - 1))
                    return last
                def mmu(e, pu=pu, j=j):
                    last = None
                    for kc in range(KC):
                        last = e.matmul(pu[:, 0:TT], lhsT=Wi[:, kc, FH + j * 128:FH + (j + 1) * 128], rhs=h[:, kc, :],
                                        start=(kc == 0), stop=(kc == KC - 1))
                    return last
                S.op("pe", [Wi, h], [pg], mmg)
                S.op("pe", [Wi, h], [pu], mmu)
                sg = sgs[j % 2]
                S.op("act", [pg], [sg], lambda e, pg=pg, sg=sg: e.activation(out=sg[:], in_=pg[:, 0:TT], func=AF.Silu))
                S.op("dve", [sg, pu], [hid], lambda e, pu=pu, sg=sg, j=j: e.tensor_tensor(
                    out=hid[:, j, :], in0=pu[:, 0:TT], in1=sg[:], op=ALU.mult))
            for n in range(KC):
                pp = self.ps()
                def mm2(e, pp=pp, n=n):
                    last = None
                    for j in range(NJ):
                        last = e.matmul(pp[:, 0:TT], lhsT=W2[:, j, n * 128:(n + 1) * 128], rhs=hid[:, j, :],
                                        start=(j == 0), stop=(j == NJ - 1))
                    return last
                S.op("pe", [W2, hid], [pp], mm2)
                S.op("dve", [pp, X, self.MOD], [X], lambda e, pp=pp, n=n: e.scalar_tensor_tensor(
                    out=X[:, n, :], in0=pp[:, 0:TT], scalar=GF[:, n:n + 1], in1=X[:, n, :], op0=ALU.mult, op1=ALU.add))
            if final:
                self.final_norm_tile(X, X, sq, rstds[ti % 2], TT)
                S.dma([X], [], OUT3[:, :, sl], X[:])
            else:
                S.dma([X], [], XD3[:, :, sl], X[:])

        front(0)
        for ti in range(len(tl)):
            if ti + 1 < len(tl):
                front(ti + 1)
            back(ti)
        P.close()

    def final_norm_tile(self, X, yout, sq, rstd, n):
        S = self.S
        S.op("act", [X], [sq], lambda e: e.activation(out=sq[:, :, 0:n], in_=X[:, :, 0:n], func=AF.Square))
        pss = self.ps()
        def mm(e):
            last = None
            for kc in range(KC):
                last = e.matmul(pss[:, 0:n], lhsT=(self.ones32 if sq.t.dtype == F32 else self.onesbf)[:, :], rhs=sq[:, kc, 0:n],
                                start=(kc == 0), stop=(kc == KC - 1))
            return last
        S.op("pe", [sq, self.ones32, self.onesbf], [pss], mm)
        S.op("dve", [pss], [rstd], lambda e: e.tensor_scalar(
            out=rstd[:, 0:n], in0=pss[:, 0:n], scalar1=1.0 / D, scalar2=EPS, op0=ALU.mult, op1=ALU.add))
        S.op("act", [rstd], [rstd], lambda e: e.activation(out=rstd[:, 0:n], in_=rstd[:, 0:n], func=AF.Sqrt))
        S.op("dve", [rstd], [rstd], lambda e: e.reciprocal(out=rstd[:, 0:n], in_=rstd[:, 0:n]))
        for kc in range(KC):
            S.op("dve", [X, rstd, self.VEC], [yout], lambda e, kc=kc: e.scalar_tensor_tensor(
                out=yout[:, kc, 0:n], in0=X[:, kc, 0:n], scalar=self.vcol("fin", kc), in1=rstd[:, 0:n], op0=ALU.mult, op1=ALU.mult))

    def mod_gen(self, P, layers, psfn):
        S = self.S
        stg = [P.sbuf("mst%d" % k, [128, KC, 512], F32) for k in range(2)]
        condT, MOD = self.condT, self.MOD
        k = 0
        for l in layers:
            for cb in range(12):
                w = stg[k % 2]
                k += 1
                S.dma([], [w], w[:], self.dr["ada_w"][l, :, cb * 512:(cb + 1) * 512].rearrange("(c p) n -> p c n", p=128))
                yield
                pm = psfn()
                def mm(e, w=w, pm=pm):
                    last = None
                    for j in range(4):
                        for kc in range(KC):
                            last = e.matmul(pm[:, j:j + 1], lhsT=w[:, kc, j * 128:(j + 1) * 128], rhs=condT[:, kc:kc + 1],
                                            start=(kc == 0), stop=(kc == KC - 1))
                    return last
                S.op("pe", [w, condT], [pm], mm)
                c0 = l * 48 + cb * 4
                S.op("dve", [pm, self.VEC], [MOD], lambda e, pm=pm, c0=c0: e.tensor_tensor(
                    out=MOD[:, c0:c0 + 4], in0=pm[:, 0:4], in1=self.VEC[:, VOFF["adab"] + c0:VOFF["adab"] + c0 + 4], op=ALU.add))
                yield
            S.op("dve", [MOD, self.VEC], [self.AM], lambda e, l=l: e.scalar_tensor_tensor(
                out=self.AM[:, l * 8:(l + 1) * 8], in0=MOD[:, l * 48 + 8:l * 48 + 16], scalar=1.0,
                in1=self.vcol("gm", l * 8, 8), op0=ALU.add, op1=ALU.mult))
            S.op("dve", [MOD, self.VEC], [self.AFc], lambda e, l=l: e.scalar_tensor_tensor(
                out=self.AFc[:, l * 8:(l + 1) * 8], in0=MOD[:, l * 48 + 32:l * 48 + 40], scalar=1.0,
                in1=self.vcol("gf", l * 8, 8), op0=ALU.add, op1=ALU.mult))

    def phase_attn(self, heads_moba=range(8), heads_sb=range(8), qtiles=range(8), mod_layers=()):
        S = self.S
        P = Phase(S)
        QT, VTOK, OT = self.dr["QT"], self.dr["VTOK"], self.dr["OT"]
        cmask = [P.sbuf("cmask%d" % d, [128, 512], BF16) for d in range(4)]
        smask = [P.sbuf("smask%d" % d, [128, 512], BF16) for d in range(4)]
        for d in range(4):
            for mk, op in ((cmask[d], ALU.is_ge), (smask[d], ALU.is_gt)):
                S.seq("pool", mk, [
                    lambda e, mk=mk: e.memset(mk[:], 1.0),
                    lambda e, mk=mk, op=op, d=d: e.affine_select(out=mk[:], in_=mk[:], compare_op=op, fill=0.0, base=-128 * d,
                                                                 pattern=[[1, 512]], channel_multiplier=-1)])
        NUI = P.sbuf("NUI", [128, 128], BF16)
        S.seq("pool", NUI, [
            lambda e: e.memset(NUI[:], -1.0),
            lambda e: e.affine_select(out=NUI[:], in_=NUI[:], compare_op=ALU.is_ge, fill=0.0, base=0,
                                      pattern=[[-1, 128]], channel_multiplier=1)])
        onesb = P.sbuf("onesb", [128, 128], BF16)
        S.op("pool", [], [onesb], lambda e: e.memset(onesb[:], 1.0))
        past01 = P.sbuf("past01", [128, 32, 16], F32)
        pastb = P.sbuf("pastb", [128, 32, 16], F32)
        ownm1 = P.sbuf("ownm1", [128, 32, 16], F32)
        S.seq("pool", past01, [
            lambda e: e.memset(past01[:], 1.0),
            lambda e: e.affine_select(out=past01[:], in_=past01[:], compare_op=ALU.is_ge, fill=0.0, base=-2,
                                      pattern=[[1, 32], [-2, 16]], channel_multiplier=0)])
        S.seq("pool", pastb, [
            lambda e: e.memset(pastb[:], 0.0),
            lambda e: e.affine_select(out=pastb[:], in_=pastb[:], compare_op=ALU.is_ge, fill=-1e30, base=-2,
                                      pattern=[[1, 32], [-2, 16]], channel_multiplier=0)])
        S.seq("pool", ownm1, [
            lambda e: e.memset(ownm1[:], 0.0),
            lambda e: e.affine_select(out=ownm1[:], in_=ownm1[:], compare_op=ALU.is_ge, fill=-1.0, base=0,
                                      pattern=[[1, 32], [-2, 16]], channel_multiplier=0),
            lambda e: e.affine_select(out=ownm1[:], in_=ownm1[:], compare_op=ALU.is_ge, fill=-1.0, base=1,
                                      pattern=[[-1, 32], [2, 16]], channel_multiplier=0)])
        QAUG = [P.sbuf("QAUG%d" % k, [128, T], BF16) for k in range(2)]
        KAUG = [P.sbuf("KAUG%d" % k, [128, T], BF16) for k in range(2)]
        KS = [P.sbuf("KS%d" % k, [128, T], BF16) for k in range(2)]
        KAc = P.sbuf("KAc", [16, T], BF16)
        S.seq("pool", KAc, [
            lambda e: e.memset(KAc[:], BIG),
            lambda e: e.affine_select(out=KAc[:], in_=KAc[:], compare_op=ALU.is_ge, fill=0.0, base=0,
                                      pattern=[[1, T]], channel_multiplier=-256),
            lambda e: e.affine_select(out=KAc[:], in_=KAc[:], compare_op=ALU.is_ge, fill=0.0, base=255,
                                      pattern=[[-1, T]], channel_multiplier=256)])
        for k in range(2):
            S.op("pool", [], [QAUG[k]], lambda e, k=k: e.memset(QAUG[k][:], 0.0))
            S.op("pool", [], [KS[k]], lambda e, k=k: e.memset(KS[k][:], 0.0))
            S.seq("pool", KAUG[k], [
                lambda e, k=k: e.memset(KAUG[k][:], 0.0),
                lambda e, k=k: e.memset(KAUG[k][96:97, :], -1.0)])
            S.dma([KAc], [KAUG[k]], KAUG[k][64:80, :], KAc[:])
        Vb = [P.sbuf("V%d" % k, [128, 32, 65], BF16) for k in range(2)]
        for k in range(2):
            S.op("pool", [], [Vb[k]], lambda e, k=k: e.memset(Vb[k][:], 1.0))
        sqQ = P.sbuf("sqQ", [64, T], BF16)
        kmean = P.sbuf("kmean", [64, 16], F32)
        kmeanb = P.sbuf("kmeanb", [64, 16], BF16)
        gm = P.sbuf("gm", [128, 32, 16], F32)
        m8 = P.sbuf("m8", [128, 32, 8], F32)
        MBw = P.sbuf("MBw", [128, 32, 97], F32)
        S.op("pool", [], [MBw], lambda e: e.memset(MBw[:], 0.0))
        kmx = P.sbuf("kmx", [128, 8], F32)
        kmax2 = P.sbuf("kmax2", [128, 1], F32)
        Pt = [P.sbuf("Pt%d" % k, [128, 512], BF16) for k in range(5)]
        Eb = [P.sbuf("E%d" % k, [128, 512], F32) for k in range(2)]
        Lp = [P.sbuf("Lp%d" % k, [128, 512], BF16) for k in range(4)]
        exb = [P.sbuf("ex%d" % k, [128, 512], F32) for k in range(2)]
        rs = P.sbuf("rs", [128, 512], F32)
        Osb = P.sbuf("Osb", [65, 512], F32)
        rden = P.sbuf("rden", [128, 512], F32)
        S.op("pool", [], [rden], lambda e: e.memset(rden[:], 0.0))
        sel64 = P.sbuf("sel64", [128, 128], F32)
        S.seq("pool", sel64, [lambda e: e.memset(sel64[:], 0.0), lambda e: e.memset(sel64[64:65, :], 1.0)])
        ost = [P.sbuf("ost%d" % k, [64, 512], BF16) for k in range(2)]
        PSr = self.PS[0:6]
        PSo = self.PS[6:8]
        rr = [0]
        def psr():
            b = PSr[rr[0] % 6]
            rr[0] += 1
            return b
        jobs = [("m", h) for h in heads_moba] + [("s", h) for h in heads_sb]
        mg = self.mod_gen(P, list(mod_layers), psr) if mod_layers else None
        def mg_step():
            nonlocal mg
            if mg is not None:
                try:
                    next(mg)
                except StopIteration:
                    mg = None
        def load_head(idx):
            kind, h = jobs[idx]
            k = idx % 2
            qrow = (0 if kind == "m" else 1024) + h * 64
            krow = (512 if kind == "m" else 1536) + h * 64
            vcol = (0 if kind == "m" else 512) + h * 64
            kdst = KAUG[k] if kind == "m" else KS[k]
            S.dma([], [QAUG[k]], QAUG[k][0:64, :], QT[qrow:qrow + 64, :])
            S.dma([], [kdst], kdst[0:64, :], QT[krow:krow + 64, :])
            S.dma([], [Vb[k]], Vb[k][:, :, 0:64], VTOK[:, vcol:vcol + 64].rearrange("(j p) d -> p j d", p=128))
        load_head(0)
        oi = 0
        pre_done = set()
        for idx, (kind, h) in enumerate(jobs):
            Q, V = QAUG[idx % 2], Vb[idx % 2]
            Kt = KAUG[idx % 2] if kind == "m" else KS[idx % 2]
            if idx + 1 < len(jobs):
                load_head(idx + 1)
            if kind == "m":
                def prelude(Q, Kt):
                    S.op("dve", [Kt], [kmean], lambda e: e.tensor_reduce(
                        out=kmean[:], in_=Kt[0:64, :].rearrange("p (n k) -> p n k", k=256), axis=AX.X, op=ALU.add))
                    S.op("dve", [kmean], [kmeanb], lambda e: e.tensor_scalar(
                        out=kmeanb[:], in0=kmean[:], scalar1=1.0 / 256, scalar2=None, op0=ALU.mult))
                    yield
                    GP = psr()
                    def mm(e):
                        last = None
                        for tt in range(32):
                            last = e.matmul(GP[:, tt * 16:(tt + 1) * 16], lhsT=Q[0:64, tt * 128:(tt + 1) * 128], rhs=kmeanb[:, :],
                                            start=True, stop=True)
                        return last
                    S.op("pe", [Q, kmeanb], [GP], mm)
                    S.op("dve", [GP, pastb], [gm], lambda e: e.tensor_tensor(
                        out=gm[:].rearrange("p a b -> p (a b)"), in0=GP[:, :], in1=pastb[:].rearrange("p a b -> p (a b)"), op=ALU.add))
                    yield
                    for tt in range(32):
                        S.op("dve", [gm], [m8], lambda e, tt=tt: e.max(out=m8[:, tt, :], in_=gm[:, tt, :]))
                    yield
                    for tt in range(32):
                        S.op("dve", [gm, m8, past01], [MBw], lambda e, tt=tt: e.scalar_tensor_tensor(
                            out=MBw[:, tt, 64:80], in0=gm[:, tt, :], scalar=m8[:, tt, 2:3], in1=past01[:, tt, :], op0=ALU.is_ge, op1=ALU.mult))
                    S.op("dve", [MBw, ownm1], [MBw], lambda e: e.tensor_tensor(
                        out=MBw[:, :, 64:80], in0=MBw[:, :, 64:80], in1=ownm1[:], op=ALU.add))
                    yield
                    S.op("pool", [Kt], [sqQ], lambda e: e.tensor_tensor(out=sqQ[:], in0=Kt[0:64, :], in1=Kt[0:64, :], op=ALU.mult))
                    for g in range(8):
                        KP = psr()
                        S.op("pe", [sqQ, onesb], [KP], lambda e, KP=KP, g=g: e.matmul(
                            KP[:, :], lhsT=onesb[0:64, :], rhs=sqQ[:, g * 512:(g + 1) * 512], start=True, stop=True))
                        S.op("dve", [KP], [kmx], lambda e, KP=KP, g=g: e.tensor_reduce(
                            out=kmx[:, g:g + 1], in_=KP[:, :], axis=AX.X, op=ALU.max))
                    yield
                    S.op("dve", [kmx], [kmax2], lambda e: e.tensor_reduce(out=kmax2[:], in_=kmx[:], axis=AX.X, op=ALU.max))
                    S.op("pool", [Q], [sqQ], lambda e: e.tensor_tensor(out=sqQ[:], in0=Q[0:64, :], in1=Q[0:64, :], op=ALU.mult))
                    yield
                    QQ = psr()
                    def mmq(e):
                        last = None
                        for tt in range(32):
                            last = e.matmul(QQ[:, tt:tt + 1], lhsT=sqQ[:, tt * 128:(tt + 1) * 128], rhs=onesb[0:64, 0:1], start=True, stop=True)
                        return last
                    S.op("pe", [sqQ, onesb], [QQ], mmq)
                    S.op("act", [QQ, kmax2], [MBw], lambda e: e.activation(
                        out=MBw[:, :, 96], in_=QQ[:, 0:32], func=AF.Sqrt, scale=kmax2[:, 0:1]))
                    yield
                    for g in range(8):
                        TP = psr()
                        def tr(e, TP=TP, g=g):
                            last = None
                            for k in range(4):
                                last = e.transpose(out=TP[0:97, k * 128:(k + 1) * 128], in_=MBw[:, 4 * g + k, :], identity=self.ident32[:, :])
                            return last
                        S.op("pe", [MBw, self.ident32], [TP], tr)
                        S.op("dve", [TP], [Q], lambda e, TP=TP, g=g: e.tensor_copy(out=Q[64:97, g * 512:(g + 1) * 512], in_=TP[64:97, :]))

                if idx not in pre_done:
                    for _ in prelude(Q, Kt):
                        pass
                nxt = None
                if idx + 1 < len(jobs) and jobs[idx + 1][0] == "m":
                    nxt = prelude(QAUG[(idx + 1) % 2], KAUG[(idx + 1) % 2])
                    pre_done.add(idx + 1)
                pairs = [(i, j) for i in qtiles for j in range(4 * i + 4)]
                st = {}
                def stageA(p):
                    i, j = pairs[p]
                    sp = psr()
                    S.op("pe", [Kt, Q], [sp], lambda e: e.matmul(
                        sp[:, :], lhsT=Kt[:, j * 128:(j + 1) * 128], rhs=Q[:, i * 512:(i + 1) * 512], start=True, stop=True))
                    pt = Pt[p % 5]
                    S.op("act", [sp], [pt], lambda e: e.activation(out=pt[:], in_=sp[:, :], func=AF.Exp))
                    if j >= 4 * i:
                        S.op("dve", [pt, cmask[j - 4 * i]], [pt], lambda e: e.tensor_tensor(out=pt[:], in0=pt[:], in1=cmask[j - 4 * i][:], op=ALU.mult))
                    st[p] = pt
                def stageC(p):
                    nonlocal oi
                    i, j = pairs[p]
                    pt = st.pop(p)
                    last = 4 * i + 3
                    if j == 0:
                        oi += 1
                    OP = PSo[oi % 2]
                    S.op("pe", [V, pt], [OP], lambda e: e.matmul(OP[0:65, :], lhsT=V[:, j, :], rhs=pt[:], start=(j == 0), stop=(j == last)))
                    if j == last:
                        S.op("act", [OP], [Osb], lambda e: e.activation(out=Osb[:], in_=OP[0:65, :], func=AF.Copy))
                        if getattr(self, "debug", False) and i == 0:
                            self.dbg("dbgOsb", Osb, Osb[:])
                        S.op("dve", [Osb], [rden], lambda e: e.reciprocal(out=rden[64:65, :], in_=Osb[64:65, :]))
                        BP = psr()
                        S.op("pe", [rden, sel64], [BP], lambda e: e.matmul(
                            BP[:, :], lhsT=sel64[:, :], rhs=rden[:, :], start=True, stop=True))
                        o = ost[i % 2]
                        S.op("dve", [Osb, BP], [o], lambda e: e.tensor_tensor(out=o[:], in0=Osb[0:64, :], in1=BP[0:64, :], op=ALU.mult))
                        S.dma([o], [], OT[h * 64:(h + 1) * 64, i * 512:(i + 1) * 512], o[:])
                for p in range(len(pairs) + 4):
                    if p < len(pairs):
                        stageA(p)
                    if p >= 4:
                        stageC(p - 4)
                    if nxt is not None and p % 8 == 4:
                        try:
                            next(nxt)
                        except StopIteration:
                            nxt = None
                    if p % 16 == 9:
                        mg_step()
                if nxt is not None:
                    for _ in nxt:
                        pass
            else:
                pairs = [(i, j) for i in qtiles for j in range(4 * i + 3, -1, -1)]
                st = {}
                def stageA(p):
                    i, j = pairs[p]
                    z = psr()
                    S.op("pe", [Kt, Q], [z], lambda e: e.matmul(
                        z[:, :], lhsT=Kt[:, j * 128:(j + 1) * 128], rhs=Q[:, i * 512:(i + 1) * 512], start=True, stop=False))
                    E = Eb[p % 2]
                    lp = Lp[p % 4]
                    S.op("act", [z], [E], lambda e: e.activation(out=E[:], in_=z[:, :], func=AF.Exp))
                    S.op("act", [E], [lp], lambda e: e.activation(out=lp[:], in_=E[:], func=AF.Ln, bias=1.0, scale=1.0))
                    if j >= 4 * i:
                        S.op("pool", [lp, smask[j - 4 * i]], [lp], lambda e: e.tensor_tensor(out=lp[:], in0=lp[:], in1=smask[j - 4 * i][:], op=ALU.mult))
                    st[p] = (z, lp)
                def stageB(p):
                    i, j = pairs[p]
                    z, lp = st[p]
                    first = (j == 4 * i + 3)
                    S.op("pe", [NUI, lp], [z], lambda e: e.matmul(z[:, :], lhsT=NUI[:, :], rhs=lp[:], start=False, stop=True))
                    cs = psr()
                    S.op("pe", [onesb, lp], [cs], lambda e: e.matmul(cs[:, :], lhsT=onesb[:, :], rhs=lp[:], start=True, stop=True))
                    ex = exb[p % 2]
                    if first:
                        S.op("dve", [z], [ex], lambda e: e.tensor_copy(out=ex[:], in_=z[:, :]))
                        S.op("dve", [cs], [rs], lambda e: e.tensor_copy(out=rs[:], in_=cs[:, :]))
                    else:
                        S.op("dve", [z, rs], [ex], lambda e: e.tensor_tensor(out=ex[:], in0=z[:, :], in1=rs[:], op=ALU.subtract))
                        S.op("dve", [cs, rs], [rs], lambda e: e.tensor_tensor(out=rs[:], in0=cs[:, :], in1=rs[:], op=ALU.add))
                    w = Pt[p % 3]
                    S.op("act", [ex], [w], lambda e: e.activation(out=w[:], in_=ex[:], func=AF.Exp))
                    if j >= 4 * i:
                        S.op("pool", [w, smask[j - 4 * i]], [w], lambda e: e.tensor_tensor(out=w[:], in0=w[:], in1=smask[j - 4 * i][:], op=ALU.mult))
                    st[p] = w
                def stageC(p):
                    nonlocal oi
                    i, j = pairs[p]
                    w = st.pop(p)
                    first = (j == 4 * i + 3)
                    if first:
                        oi += 1
                    OP = PSo[oi % 2]
                    S.op("pe", [V, w], [OP], lambda e: e.matmul(OP[0:64, :], lhsT=V[:, j, 0:64], rhs=w[:], start=first, stop=(j == 0)))
                    if j == 0:
                        o = ost[i % 2]
                        S.op("act", [OP], [o], lambda e: e.activation(out=o[:], in_=OP[0:64, :], func=AF.Copy))
                        S.dma([o], [], OT[512 + h * 64:512 + (h + 1) * 64, i * 512:(i + 1) * 512], o[:])
                n = len(pairs)
                for p in range(n + 4):
                    if p % 16 == 9:
                        mg_step()
                    if p < n:
                        stageA(p)
                    if 2 <= p < n + 2:
                        stageB(p - 2)
                    if p >= 4:
                        stageC(p - 4)
        while mg is not None:
            mg_step()
        P.close()

    def phase_gdn(self, li, heads=range(8), nchunks=32):
        S = self.S
        P = Phase(S)
        GT, ZT, AB, OT = self.dr["GT"], self.dr["VTOK"], self.dr["AB"], self.dr["OT"]
        NCH = 32
        C = 128
        UT = P.sbuf("UT", [128, 128], F32)
        SU = P.sbuf("SU", [128, 128], F32)
        S.seq("pool", UT, [
            lambda e: e.memset(UT[:], 1.0),
            lambda e: e.affine_select(out=UT[:], in_=UT[:], compare_op=ALU.is_ge, fill=0.0, base=0,
                                      pattern=[[1, 128]], channel_multiplier=-1)])
        S.seq("pool", SU, [
            lambda e: e.memset(SU[:], 1.0),
            lambda e: e.affine_select(out=SU[:], in_=SU[:], compare_op=ALU.is_gt, fill=0.0, base=0,
                                      pattern=[[-1, 128]], channel_multiplier=1)])
        I32m = self.ident32
        ab = P.sbuf("ab", [128, NCH, 16], F32)
        S.dma([], [ab], ab[:], AB.rearrange("(c p) n -> p c n", p=128))
        graw = P.sbuf("graw", [128, NCH, 8], F32)
        beta = P.sbuf("beta", [128, NCH, 8], F32)
        nbeta = P.sbuf("nbeta", [128, NCH, 8], F32)
        negA = P.sbuf("negA", [128, 8], F32)
        tmpa = P.sbuf("tmpa", [128, NCH, 8], F32)
        S.op("act", [self.VEC], [negA], lambda e: e.activation(out=negA[:], in_=self.vcol("alog", li * 8, 8), func=AF.Exp))
        S.op("dve", [negA], [negA], lambda e: e.tensor_scalar(out=negA[:], in0=negA[:], scalar1=-1.0, scalar2=None, op0=ALU.mult))
        for c in range(NCH):
            S.op("dve", [ab, self.VEC], [tmpa], lambda e, c=c: e.tensor_tensor(
                out=tmpa[:, c, :], in0=ab[:, c, 0:8], in1=self.vcol("dtb", li * 8, 8), op=ALU.add))
        S.op("act", [tmpa], [tmpa], lambda e: e.activation(out=tmpa[:], in_=tmpa[:], func=AF.Exp))
        S.op("act", [tmpa], [tmpa], lambda e: e.activation(out=tmpa[:], in_=tmpa[:], func=AF.Ln, bias=1.0, scale=1.0))
        for c in range(NCH):
            S.op("dve", [tmpa, negA], [graw], lambda e, c=c: e.tensor_tensor(
                out=graw[:, c, :], in0=tmpa[:, c, :], in1=negA[:], op=ALU.mult))
        S.op("act", [ab], [beta], lambda e: e.activation(out=beta[:], in_=ab[:, :, 8:16], func=AF.Sigmoid))
        S.op("dve", [beta], [nbeta], lambda e: e.tensor_scalar(out=nbeta[:], in0=beta[:], scalar1=-1.0, scalar2=None, op0=ALU.mult))
        gc = P.sbuf("gc", [128, NCH, 8], F32)
        egc = P.sbuf("egc", [128, NCH, 8], F32)
        egl = P.sbuf("egl", [128, NCH, 8], F32)
        edec = P.sbuf("edec", [128, NCH, 8], F32)
        bgc = P.sbuf("bgc", [128, NCH, 8], F32)
        pgc = self.ps()
        S.op("pe", [UT, graw], [pgc], lambda e: e.matmul(pgc[:, 0:256], lhsT=UT[:, :], rhs=graw[:].rearrange("p a b -> p (a b)"), start=True, stop=True))
        pgl = self.ps()
        S.op("pe", [self.ones32, graw], [pgl], lambda e: e.matmul(pgl[:, 0:256], lhsT=self.ones32[:, :], rhs=graw[:].rearrange("p a b -> p (a b)"), start=True, stop=True))
        fl = lambda b: b[:].rearrange("p a b -> p (a b)")
        S.op("dve", [pgc], [gc], lambda e: e.tensor_copy(out=fl(gc), in_=pgc[:, 0:256]))
        S.op("act", [pgc], [egc], lambda e: e.activation(out=fl(egc), in_=pgc[:, 0:256], func=AF.Exp))
        S.op("act", [pgl], [egl], lambda e: e.activation(out=fl(egl), in_=pgl[:, 0:256], func=AF.Exp))
        S.op("dve", [pgl, gc], [edec], lambda e: e.tensor_tensor(out=fl(edec), in0=pgl[:, 0:256], in1=fl(gc), op=ALU.subtract))
        S.op("act", [edec], [edec], lambda e: e.activation(out=fl(edec), in_=fl(edec), func=AF.Exp))
        S.op("dve", [beta, egc], [bgc], lambda e: e.tensor_tensor(out=fl(bgc), in0=fl(beta), in1=fl(egc), op=ALU.mult))
        xin = [P.sbuf("xin%d" % k, [128, T + 3], BF16) for k in range(1)]
        for k in range(1):
            S.op("pool", [], [xin[k]], lambda e, k=k: e.memset(xin[k][:, 0:3], 0.0))
        xin_i = [0]
        identb = P.sbuf("identb", [128, 128], BF16)
        S.op("pool", [self.ident32], [identb], lambda e: e.tensor_copy(out=identb[:], in_=self.ident32[:]))
        dg = [[P.sbuf("dg%d_%d" % (a, j), [128, 128], BF16) for j in range(4)] for a in range(2)]
        sqb = [P.sbuf("sqb%d" % k, [128, 512], F32) for k in range(2)]
        ybuf = P.sbuf("ybuf", [128, T], F32)
        qTbs = [P.sbuf("qTb%d" % k, [128, T], BF16) for k in range(2)]
        kTf = P.sbuf("kTf", [128, T], F32)
        kTb = P.sbuf("kTb", [128, T], BF16)
        vTf = P.sbuf("vTf", [128, T], F32)
        rn = P.sbuf("rn", [128, 512], F32)
        zt = P.sbuf("zt", [128, NCH, 128], BF16)
        zgs = [P.sbuf("zg%d" % k, [128, NCH, 128], BF16) for k in range(2)]
        def chunked(name):
            big = P.sbuf(name, [128, NCH, 128], BF16)
            out = []
            for k in range(NCH):
                b = Buf(big.t[:, k, :], "%s%d" % (name, k))
                S.bufs.append(b)
                out.append(b)
            return out
        U_, WT, AT, KD = chunked("U_"), chunked("WT"), chunked("AT"), chunked("KD")
        ots = [P.sbuf("ot%d" % k, [128, T], BF16) for k in range(2)]
        Ssts = [P.sbuf("Sst%d" % k, [128, 128], F32) for k in range(2)]
        Sbfss = [[P.sbuf("Sbf%d_%d" % (a, k), [128, 128], BF16) for k in range(2)] for a in range(2)]
        tails = []
        s2prog = {}
        NR = GDN_NR
        def rot(name, dt=F32, n=NR):
            return [P.sbuf("%s%d" % (name, k), [128, 128], dt) for k in range(n)]
        ktok, vb, kbg, G1, dcs, dsc, Nm, Am, Pm, Nn, An = (rot(x) for x in ("ktok", "vb", "kbg", "G1", "dcs", "dsc", "Nm", "Am", "Pm", "Nn", "An"))
        vnew = rot("vnew", BF16, 2)
        o2b = rot("o2b", F32, 2)
        osb = rot("osb", F32, 2)
        ysb = rot("ysb", F32, 2)
        ssq = P.sbuf("ssq", [128, 2], F32)
        junk = P.sbuf("junk", [128, 128], F32)
        ew = [0]
        def evac_copy(src_ps, dst, dst_ap, src_ap, extra_reads=()):
            ew[0] += 1
            if ew[0] % 2 == 0:
                S.op("act", [src_ps] + list(extra_reads), [dst], lambda e: e.activation(out=dst_ap, in_=src_ap, func=AF.Copy))
            else:
                S.op("dve", [src_ps] + list(extra_reads), [dst], lambda e: e.tensor_copy(out=dst_ap, in_=src_ap))
        heads = list(heads)
        pre_rr = [0]
        def pre_ps():
            b = self.PS[pre_rr[0] % 4]
            pre_rr[0] += 1
            return b
        for hi, h in enumerate(heads):
            hp = hi % 2
            qTb, zg, ot, Sst, Sbfs = qTbs[hp], zgs[hp], ots[hp], Ssts[hp], Sbfss[hp]
            for wi, which in enumerate(("q", "k", "v")):
                xi = xin[0]
                xin_i[0] += 1
                row = wi * 1024 + h * 128
                S.dma([], [xi], xi[:, 3:], GT[row:row + 128, :])
                def wcol(j, wi=wi):
                    return self.vcol("conv", (li * 4 + j) * 24 + wi * 8 + h)
                dgs = dg[xin_i[0] % 2]
                for j in range(4):
                    S.op("dve", [identb, self.VEC], [dgs[j]], lambda e, j=j: e.tensor_scalar(
                        out=dgs[j][:], in0=identb[:], scalar1=wcol(j), scalar2=None, op0=ALU.mult))
                dst = vTf if which == "v" else ybuf
                for g in range(8):
                    pc = pre_ps()
                    def mmc(e, pc=pc, g=g):
                        last = None
                        for j in range(4):
                            last = e.matmul(pc[:, :], lhsT=dgs[j][:], rhs=xi[:, g * 512 + j:g * 512 + j + 512], start=(j == 0), stop=(j == 3))
                        return last
                    S.op("pe", [xi] + dgs, [pc], mmc)
                    S.op("act", [pc], [dst], lambda e, pc=pc, g=g: e.activation(out=dst[:, g * 512:(g + 1) * 512], in_=pc[:, :], func=AF.Silu))
                if which == "v":
                    continue
                for g in range(8):
                    sl = slice(g * 512, (g + 1) * 512)
                    sq_ = sqb[g % 2]
                    S.op("act", [ybuf], [sq_], lambda e, sq_=sq_, sl=sl: e.activation(out=sq_[:], in_=ybuf[:, sl], func=AF.Square))
                    pss = pre_ps()
                    S.op("pe", [sq_, self.ones32], [pss], lambda e, pss=pss, sq_=sq_: e.matmul(pss[:, :], lhsT=self.ones32[:, :], rhs=sq_[:], start=True, stop=True))
                    S.op("dve", [pss], [rn], lambda e, pss=pss: e.tensor_scalar(out=rn[:], in0=pss[:, :], scalar1=EPS, scalar2=None, op0=ALU.add))
                    S.op("act", [rn], [rn], lambda e: e.activation(out=rn[:], in_=rn[:], func=AF.Sqrt))
                    S.op("dve", [rn], [rn], lambda e: e.reciprocal(out=rn[:], in_=rn[:]))
                    if which == "q":
                        S.op("dve", [ybuf, rn], [qTb], lambda e, sl=sl: e.scalar_tensor_tensor(
                            out=qTb[:, sl], in0=ybuf[:, sl], scalar=float(128 ** -0.5), in1=rn[:], op0=ALU.mult, op1=ALU.mult))
                    else:
                        S.op("dve", [ybuf, rn], [kTf], lambda e, sl=sl: e.tensor_tensor(out=kTf[:, sl], in0=ybuf[:, sl], in1=rn[:], op=ALU.mult))
                        S.op("pool", [kTf], [kTb], lambda e, sl=sl: e.tensor_copy(out=kTb[:, sl], in_=kTf[:, sl]))
            S.dma([], [zt], zt[:], ZT[:, h * 128:(h + 1) * 128].rearrange("(c p) d -> p c d", p=128))
            S.op("act", [zt], [zt], lambda e: e.activation(out=zt[:], in_=zt[:], func=AF.Silu))
            for c in range(NCH):
                S.op("pool", [zt, self.VEC], [zg], lambda e, c=c: e.tensor_tensor(
                    out=zg[:, c, :], in0=zt[:, c, :], in1=self.vcol("gngb", li * 128, 128), op=ALU.mult))
            done1 = [False] * nchunks

            class Pool3:
                def __init__(self, banks):
                    self.banks = banks
                    self.i = 0

                def get(self):
                    b = self.banks[self.i % len(self.banks)]
                    self.i += 1
                    return PQ(b, 0)

                def pair(self):
                    b = self.banks[self.i % len(self.banks)]
                    self.i += 1
                    return PQ(b, 0), PQ(b, 1)

            class PoolH:
                def __init__(self, bank):
                    self.bank = bank
                    self.i = 0

                def pair(self):
                    hh = self.i % 2
                    self.i += 1
                    return PQ(self.bank, 2 * hh), PQ(self.bank, 2 * hh + 1)

                def get(self):
                    return self.pair()[0]
            pools = [PoolH(self.PS[k]) for k in range(4)]
            pool2 = Pool3(self.PS[4:6] if hp == 0 else self.PS[6:8])
            def evq(pq, dst, dst_ap, extra_reads=()):
                evac_copy(pq, dst, dst_ap, pq.ap, extra_reads)

            def stage1(c, r, h=h, hi=hi):
                while hi > 0 and s2prog.get(hi - 1, nchunks) <= c:
                    yield
                cs = slice(c * C, (c + 1) * C)
                col = slice(h, h + 1)
                pl = pools[r]
                pk, pv = pl.pair()
                def f(e):
                    e.transpose(out=pk.ap, in_=kTf[:, cs], identity=I32m[:, :])
                    return e.transpose(out=pv.ap, in_=vTf[:, cs], identity=I32m[:, :])
                S.op("pe", [kTf, vTf, I32m], [pk], f)
                S.op("dve", [SU, graw], [G1[r]], lambda e: e.tensor_scalar(out=G1[r][:], in0=SU[:], scalar1=graw[:, c, col], scalar2=None, op0=ALU.mult))
                yield
                S.op("dve", [pk], [ktok[r]], lambda e: e.tensor_copy(out=ktok[r][:], in_=pk.ap))
                S.op("dve", [pv, beta], [vb[r]], lambda e: e.tensor_scalar(out=vb[r][:], in0=pv.ap, scalar1=beta[:, c, col], scalar2=None, op0=ALU.mult))
                pd1, pd2 = pl.pair()
                def f(e):
                    e.matmul(pd1.ap, lhsT=UT[:, :], rhs=G1[r][:], start=True, stop=True)
                    return e.matmul(pd2.ap, lhsT=G1[r][:], rhs=UT[:, :], start=True, stop=True)
                S.op("pe", [UT, G1[r]], [pd1], f)
                yield
                S.op("act", [pd1], [dcs[r]], lambda e: e.activation(out=dcs[r][:], in_=pd1.ap, func=AF.Exp))
                S.op("act", [pd2], [dsc[r]], lambda e: e.activation(out=dsc[r][:], in_=pd2.ap, func=AF.Exp))
                S.op("act", [ktok[r], bgc], [kbg[r]], lambda e: e.activation(out=kbg[r][:], in_=ktok[r][:], func=AF.Copy, scale=bgc[:, c, col]))
                S.op("act", [ktok[r], edec], [KD[c]], lambda e: e.activation(out=KD[c][:], in_=ktok[r][:], func=AF.Copy, scale=edec[:, c, col]))
                pkk, pqk = pl.pair()
                def f(e):
                    e.matmul(pkk.ap, lhsT=kTf[:, cs], rhs=kTf[:, cs], start=True, stop=True)
                    return e.matmul(pqk.ap, lhsT=kTb[:, cs], rhs=qTb[:, cs], start=True, stop=True)
                S.op("pe", [kTf, kTb, qTb], [pkk], f)
                yield
                S.op("pool", [dcs[r], SU], [dcs[r]], lambda e: e.tensor_tensor(out=dcs[r][:], in0=dcs[r][:], in1=SU[:], op=ALU.mult))
                S.op("pool", [dsc[r], UT], [dsc[r]], lambda e: e.tensor_tensor(out=dsc[r][:], in0=dsc[r][:], in1=UT[:], op=ALU.mult))
                yield
                S.op("dve", [pkk, nbeta, dcs[r]], [Nm[r]], lambda e: e.scalar_tensor_tensor(
                    out=Nm[r][:], in0=pkk.ap, scalar=nbeta[:, c, col], in1=dcs[r][:], op0=ALU.mult, op1=ALU.mult))
                S.op("dve", [pqk, dsc[r]], [AT[c]], lambda e: e.tensor_tensor(out=AT[c][:], in0=pqk.ap, in1=dsc[r][:], op=ALU.mult))
                yield
                pa = pl.get()
                S.op("pe", [Nm[r], I32m], [pa], lambda e: e.transpose(out=pa.ap, in_=Nm[r][:], identity=I32m[:, :]))
                yield
                S.op("act", [pa], [Am[r]], lambda e: e.activation(out=Am[r][:], in_=pa.ap, func=AF.Copy))
                yield
                S.op("pool", [Am[r], I32m], [Pm[r]], lambda e: e.tensor_tensor(out=Pm[r][:], in0=Am[r][:], in1=I32m[:], op=ALU.add))
                Ncur, Acur = Nm[r], Am[r]
                for lvl in range(1, 7):
                    Nnx = Nn[r] if lvl % 2 == 1 else Nm[r]
                    Anx = An[r] if lvl % 2 == 1 else Am[r]
                    pn, pa2 = pl.pair()
                    def f(e):
                        last = e.matmul(pn.ap, lhsT=Acur[:], rhs=Ncur[:], start=True, stop=True)
                        if lvl < 6:
                            last = e.matmul(pa2.ap, lhsT=Ncur[:], rhs=Acur[:], start=True, stop=True)
                        return last
                    S.op("pe", [Acur, Ncur], [pn], f)
                    yield
                    if (lvl + r) % 2 == 0:
                        S.op("act", [pn], [Nnx], lambda e: e.activation(out=Nnx[:], in_=pn.ap, func=AF.Copy))
                        if lvl < 6:
                            S.op("act", [pa2], [Anx], lambda e: e.activation(out=Anx[:], in_=pa2.ap, func=AF.Copy))
                    else:
                        S.op("dve", [pn], [Nnx], lambda e: e.tensor_copy(out=Nnx[:], in_=pn.ap))
                        if lvl < 6:
                            S.op("dve", [pa2], [Anx], lambda e: e.tensor_copy(out=Anx[:], in_=pa2.ap))
                    yield
                    pp = pl.get()
                    S.op("pe", [Nnx, Pm[r]], [pp], lambda e: e.matmul(pp.ap, lhsT=Nnx[:], rhs=Pm[r][:], start=True, stop=True))
                    yield
                    S.op("dve", [pp, Pm[r]], [Pm[r]], lambda e: e.tensor_tensor(out=Pm[r][:], in0=pp.ap, in1=Pm[r][:], op=ALU.add))
                    yield
                    Ncur, Acur = Nnx, Anx
                pu, pw = pl.pair()
                def f(e):
                    e.matmul(pu.ap, lhsT=Pm[r][:], rhs=vb[r][:], start=True, stop=True)
                    return e.matmul(pw.ap, lhsT=kbg[r][:], rhs=Pm[r][:], start=True, stop=True)
                S.op("pe", [Pm[r], vb[r], kbg[r]], [pu], f)
                yield
                S.op("act", [pu], [U_[c]], lambda e: e.activation(out=U_[c][:], in_=pu.ap, func=AF.Copy))
                S.op("act", [pw], [WT[c]], lambda e: e.activation(out=WT[c][:], in_=pw.ap, func=AF.Copy))
                done1[c] = True

            def stage2(h=h, hi=hi, qTb=qTb, zg=zg, ot=ot, Sst=Sst, Sbfs=Sbfs, pool2=pool2, done1=done1):
                S.op("pool", [], [Sbfs[0]], lambda e: e.memset(Sbfs[0][:], 0.0))
                S.op("pool", [], [Sst], lambda e: e.memset(Sst[:], 0.0))
                col = slice(h, h + 1)
                for c in range(nchunks):
                    s2prog[hi] = c
                    while not done1[c]:
                        yield
                    cs = slice(c * C, (c + 1) * C)
                    r2 = c % 2
                    Sold, Snew = Sbfs[c % 2], Sbfs[(c + 1) % 2]
                    p1, p2a = pool2.get(), pool2.get()
                    S.op("pe", [WT[c], Sold], [p1], lambda e: e.matmul(p1.ap, lhsT=WT[c][:], rhs=Sold[:], start=True, stop=True))
                    S.op("pe", [qTb, Sold], [p2a], lambda e: e.matmul(p2a.ap, lhsT=qTb[:, cs], rhs=Sold[:], start=True, stop=True))
                    yield
                    S.op("dve", [U_[c], p1], [vnew[r2]], lambda e: e.tensor_tensor(out=vnew[r2][:], in0=U_[c][:], in1=p1.ap, op=ALU.subtract))
                    S.op("act", [p2a, egc], [o2b[r2]], lambda e: e.activation(out=o2b[r2][:], in_=p2a.ap, func=AF.Copy, scale=egc[:, c, col]))
                    yield
                    p3, p2b = pool2.get(), pool2.get()
                    S.op("pe", [KD[c], vnew[r2]], [p3], lambda e: e.matmul(p3.ap, lhsT=KD[c][:], rhs=vnew[r2][:], start=True, stop=True))
                    S.op("pe", [AT[c], vnew[r2]], [p2b], lambda e: e.matmul(p2b.ap, lhsT=AT[c][:], rhs=vnew[r2][:], start=True, stop=True))
                    yield
                    S.op("dve", [Sst, egl, p3], [Sst], lambda e: e.scalar_tensor_tensor(
                        out=Sst[:], in0=Sst[:], scalar=egl[:, c, col], in1=p3.ap, op0=ALU.mult, op1=ALU.add))
                    yield
                    S.op("act", [Sst], [Snew], lambda e: e.activation(out=Snew[:], in_=Sst[:], func=AF.Copy))
                    S.op("dve", [p2b, o2b[r2]], [osb[r2]], lambda e: e.tensor_tensor(out=osb[r2][:], in0=p2b.ap, in1=o2b[r2][:], op=ALU.add))
                    yield
                    S.op("act", [osb[r2]], [junk, ssq], lambda e: e.activation(out=junk[:], in_=osb[r2][:], func=AF.Square, accum_out=ssq[:, r2:r2 + 1]))
                    S.op("dve", [ssq], [ssq], lambda e: e.tensor_scalar(out=ssq[:, r2:r2 + 1], in0=ssq[:, r2:r2 + 1], scalar1=1.0 / 128, scalar2=EPS, op0=ALU.mult, op1=ALU.add))
                    S.op("act", [ssq], [ssq], lambda e: e.activation(out=ssq[:, r2:r2 + 1], in_=ssq[:, r2:r2 + 1], func=AF.Sqrt))
                    S.op("dve", [ssq], [ssq], lambda e: e.reciprocal(out=ssq[:, r2:r2 + 1], in_=ssq[:, r2:r2 + 1]))
                    S.op("dve", [osb[r2], ssq, zg], [ysb[r2]], lambda e: e.scalar_tensor_tensor(
                        out=ysb[r2][:], in0=osb[r2][:], scalar=ssq[:, r2:r2 + 1], in1=zg[:, c, :], op0=ALU.mult, op1=ALU.mult))
                    yield
                    py = pool2.get()
                    S.op("pe", [ysb[r2], I32m], [py], lambda e: e.transpose(out=py.ap, in_=ysb[r2][:], identity=I32m[:, :]))
                    S.op("act", [py], [ot], lambda e: e.activation(out=ot[:, cs], in_=py.ap, func=AF.Copy))
                    yield
                s2prog[hi] = nchunks
                S.dma([ot], [], OT[h * 128:(h + 1) * 128, :], ot[:])

            pending = list(range(nchunks))
            active = {}
            tails.append(stage2())
            while pending or active:
                while pending and len(active) < NR:
                    c = pending.pop(0)
                    slot = [k for k in range(NR) if k not in active][0]
                    active[slot] = stage1(c, slot)
                for slot in list(active.keys()):
                    try:
                        next(active[slot])
                    except StopIteration:
                        del active[slot]
                for g in list(tails):
                    try:
                        next(g)
                    except StopIteration:
                        tails.remove(g)
        while tails:
            for g in list(tails):
                try:
                    next(g)
                except StopIteration:
                    tails.remove(g)
        P.close()


def prep_shared(inp):
    perm = rope_partner_perm()
    w = np.asarray(inp["attn_w_in"], np.float32)
    w4 = np.concatenate([w, w[:, :, 0:512][:, :, perm], w[:, :, 512:1024][:, :, perm]], axis=2)
    return {
        "ada_w": np.ascontiguousarray(inp["ada_w"], np.float32),
        "attn_w_in": np.ascontiguousarray(w4),
        "attn_w_out": np.ascontiguousarray(inp["attn_w_out"], np.float32),
        "gdn_w_in": np.ascontiguousarray(inp["gdn_w_in"], np.float32),
        "gdn_w_out": np.ascontiguousarray(inp["gdn_w_out"], np.float32),
        "ffn_w_in": np.ascontiguousarray(inp["ffn_w_in"], np.float32),
        "ffn_w_out": np.ascontiguousarray(inp["ffn_w_out"], np.float32),
    }


def prep_core(inp, b, shared):
    m = dict(shared)
    m["xT"] = np.ascontiguousarray(np.asarray(inp["x"][b], np.float32).T)
    m["vecs"] = build_vecs(inp, b)
    m["pos"] = np.ascontiguousarray(np.asarray(inp["positions"][b], np.int32).reshape(1, T))
    return m


def build_full():
    pg = Prog()
    pg.defer_mod = True
    pg.phase0()
    for l in range(NL):
        pg.phase1(l)
        if l % 2 == 0:
            pg.phase_attn(mod_layers=((1, 2) if l == 0 else (3,)))
        else:
            pg.phase_gdn(l // 2)
        pg.phase34(l, final=(l == NL - 1))
    return pg


def kernel(**inputs):
    inp = {k: np.asarray(v) for k, v in inputs.items()}
    pg = build_full()
    shared = prep_shared(inp)
    in_maps = []
    for b in range(8):
        m = prep_core(inp, b, shared)
        in_maps.append({k: m[k] for k in pg.used_inputs})
    res = run_bass_kernel_spmd(pg.nc, in_maps, core_ids=list(range(8)))
    out = np.empty((8, T, D), np.float32)
    for b in range(8):
        out[b] = np.asarray(res.results[b]["outT"], np.float32).T
    return out
```

```python
import numpy as np
from contextlib import ExitStack
import concourse.bass as bass
import concourse.mybir as mybir
from concourse.bass_utils import run_bass_kernel_spmd

F32 = mybir.dt.float32
BF16 = mybir.dt.bfloat16
I32 = mybir.dt.int32
AF = mybir.ActivationFunctionType
ALU = mybir.AluOpType
AX = mybir.AxisListType

D = 1024
T = 4096
NL = 4
KC = 8
FH = 2816
NJ = 22
EPS = 1e-6
NDMA = 48
BIG = 30000.0
SAME_ENG_SYNC = False
GDN_NR = 4
S2_EVERY = 6
PSQ_SEPARATE = False
GDN_DBG = 0


class Buf:
    __slots__ = ("t", "w", "r", "name")

    def __init__(self, t, name=""):
        self.t = t
        self.w = None
        self.r = {}
        self.name = name

    def __getitem__(self, idx):
        return self.t[idx]


class PQ:
    __slots__ = ("ap", "b")

    def __init__(self, bank, q):
        self.b = bank
        self.ap = bank.t[:, q * 128:(q + 1) * 128]


class Sched:
    def __init__(self, nc):
        self.nc = nc
        self.E = {"pe": nc.tensor, "act": nc.scalar, "dve": nc.vector, "pool": nc.gpsimd, "sp": nc.sync}
        self.sem = {k: nc.alloc_semaphore("s_" + k) for k in ("pe", "act", "dve", "pool")}
        self.cnt = {k: 0 for k in self.sem}
        self.seen = {k: {} for k in self.E}
        self.dsem = [nc.alloc_semaphore("d%d" % i) for i in range(NDMA)]
        self.dcnt = [0] * NDMA
        self.drr = 0
        self.bufs = []
        self.ps_rr = 0

    def sbuf(self, name, shape, dtype):
        b = Buf(self.nc.alloc_sbuf_tensor(name, list(shape), dtype), name)
        self.bufs.append(b)
        return b

    def psum(self, name, shape, dtype=F32):
        b = Buf(self.nc.alloc_psum_tensor(name, list(shape), dtype), name)
        self.bufs.append(b)
        return b

    def dram(self, t, name=""):
        b = Buf(t, name)
        self.bufs.append(b)
        return b

    def _wait(self, eng, ev):
        if ev is None:
            return
        sem, val, key = ev
        if self.seen[eng].get(key, 0) >= val:
            return
        self.E[eng].wait_ge(sem, val)
        self.seen[eng][key] = val

    def _deps(self, eng, reads, writes):
        reads = [getattr(b, "b", b) for b in reads]
        writes = [getattr(b, "b", b) for b in writes]
        for b in reads:
            self._wait(eng, b.w)
        for b in writes:
            if b.w is not None and (SAME_ENG_SYNC or b.w[2] != eng):
                self._wait(eng, b.w)
            for ev in b.r.values():
                if SAME_ENG_SYNC or ev[2] != eng:
                    self._wait(eng, ev)

    def _done(self, ev, reads, writes):
        reads = [getattr(b, "b", b) for b in reads]
        writes = [getattr(b, "b", b) for b in writes]
        for b in reads:
            b.r[ev[2]] = ev
        for b in writes:
            b.w = ev
            b.r = {}

    def op(self, eng, reads, writes, fn):
        self._deps(eng, reads, writes)
        ins = fn(self.E[eng])
        self.cnt[eng] += 1
        ins.then_inc(self.sem[eng], 1)
        ev = (self.sem[eng], self.cnt[eng], eng)
        self._done(ev, reads, writes)
        return ev

    def seq(self, eng, buf, fns, reads=()):
        ev = None
        for k, fn in enumerate(fns):
            ev = self.op(eng, ([buf] if k > 0 else []) + list(reads), [buf], fn)
        return ev

    def dma(self, reads, writes, out, in_, eng="sp"):
        k = self.drr
        self.drr = (k + 1) % NDMA
        key = ("d", k)
        if self.dcnt[k] > 0:
            self._wait(eng, (self.dsem[k], self.dcnt[k], key))
        self._deps(eng, reads, writes)
        ins = self.E[eng].dma_start(out=out, in_=in_)
        self.dcnt[k] += 16
        ins.then_inc(self.dsem[k], 16)
        ev = (self.dsem[k], self.dcnt[k], key)
        self._done(ev, reads, writes)
        return ev

    def barrier(self):
        evs = []
        for k in self.sem:
            if self.cnt[k] > 0:
                evs.append((self.sem[k], self.cnt[k], k))
        for k in range(NDMA):
            if self.dcnt[k] > 0:
                evs.append((self.dsem[k], self.dcnt[k], ("d", k)))
        for eng in self.E:
            for ev in evs:
                self._wait(eng, ev)
        for b in self.bufs:
            b.w = None
            b.r = {}


class Phase:
    _uid = [0]

    def __init__(self, S):
        Phase._uid[0] += 1
        self.uid = Phase._uid[0]
        self.S = S
        self.nc = S.nc
        self.st = ExitStack()
        self.nbufs0 = len(S.bufs)

    def sbuf(self, name, shape, dtype):
        name = "%s_p%d" % (name, self.uid)
        t = self.st.enter_context(self.nc.sbuf_tensor(name, list(shape), dtype))
        b = Buf(t, name)
        self.S.bufs.append(b)
        return b

    def psum(self, name, shape, dtype=F32):
        name = "%s_p%d" % (name, self.uid)
        t = self.st.enter_context(self.nc.psum_tensor(name, list(shape), dtype))
        b = Buf(t, name)
        self.S.bufs.append(b)
        return b

    def close(self):
        self.S.barrier()
        self.st.close()
        del self.S.bufs[self.nbufs0:]


VOFF = {}


def _vec_layout():
    off = 0
    for name, n in [("gm", 32), ("gf", 32), ("fin", 8), ("adab", 192), ("conv", 192), ("gngb", 256),
                    ("alog", 16), ("dtb", 16), ("invf", 1), ("sinsign", 1), ("cT", 8)]:
        VOFF[name] = off
        off += n
    return off


NV = _vec_layout()


def build_vecs(inp, b):
    v = np.zeros((128, NV), np.float32)
    def chunked(a):
        a = np.asarray(a, np.float32).reshape(-1, 128)
        return a.T
    v[:, VOFF["gm"]:VOFF["gm"] + 32] = chunked(inp["norm_mix_g"])
    v[:, VOFF["gf"]:VOFF["gf"] + 32] = chunked(inp["norm_ffn_g"])
    v[:, VOFF["fin"]:VOFF["fin"] + 8] = chunked(inp["final_norm_g"])
    v[:, VOFF["adab"]:VOFF["adab"] + 192] = chunked(inp["ada_b"])
    v[:, VOFF["conv"]:VOFF["conv"] + 192] = chunked(inp["gdn_conv_w"])
    v[:, VOFF["gngb"]:VOFF["gngb"] + 256] = np.broadcast_to(np.asarray(inp["gdn_norm_g"], np.float32).reshape(1, 256), (128, 256))
    v[:, VOFF["alog"]:VOFF["alog"] + 16] = np.broadcast_to(np.asarray(inp["gdn_a_log"], np.float32).reshape(1, 16), (128, 16))
    v[:, VOFF["dtb"]:VOFF["dtb"] + 16] = np.broadcast_to(np.asarray(inp["gdn_dt_bias"], np.float32).reshape(1, 16), (128, 16))
    p = np.arange(128) % 64
    inv = (500000.0 ** (-np.arange(8, dtype=np.float32) * 2.0 / 16)).astype(np.float32)
    v[:, VOFF["invf"]] = np.where(p < 16, inv[p % 8], 0.0)
    v[:, VOFF["sinsign"]] = np.where(p < 8, -1.0, np.where(p < 16, 1.0, 0.0))
    v[:, VOFF["cT"]:VOFF["cT"] + 8] = chunked(inp["c"][b])
    return v


def rope_partner_perm():
    idx = np.arange(512)
    d = idx % 64
    return np.where(d < 8, idx + 8, np.where(d < 16, idx - 8, idx))


class LazyDram(dict):
    def __init__(self, prog):
        super().__init__()
        self.prog = prog

    def __missing__(self, name):
        pg = self.prog
        if name in pg.in_shapes:
            shape, dt = pg.in_shapes[name]
            kind = "ExternalInput"
            pg.used_inputs.append(name)
        else:
            shape, dt = pg.scr_shapes[name]
            kind = "Internal"
            if name in pg.ext_out or name == "outT":
                kind = "ExternalOutput"
            if name in pg.ext_in:
                kind = "ExternalInput"
                pg.used_inputs.append(name)
        ap = pg.nc.dram_tensor(name, list(shape), dt, kind=kind).ap()
        self[name] = ap
        return ap


class Prog:
    def __init__(self, ext_out=(), ext_in=()):
        self.nc = bass.Bass("TRN2", target_bir_lowering=False)
        nc = self.nc
        self.ext_out = set(ext_out)
        self.ext_in = set(ext_in)
        self.S = Sched(nc)
        S = self.S
        self.in_shapes = {
            "xT": ([D, T], F32), "vecs": ([128, NV], F32), "pos": ([1, T], I32),
            "ada_w": ([NL, D, 6 * D], F32), "attn_w_in": ([2, D, 4096], F32), "attn_w_out": ([2, D, D], F32),
            "gdn_w_in": ([2, D, 4112], F32), "gdn_w_out": ([2, D, D], F32),
            "ffn_w_in": ([NL, D, 2 * FH], F32), "ffn_w_out": ([NL, FH, D], F32),
        }
        self.used_inputs = []
        self.defer_mod = False
        self.dr = LazyDram(self)
        self.dbg_outs = []
        self.scr_shapes = {
            "XT": ([D, T], F32), "COS": ([128, T], F32), "SIN": ([128, T], F32),
            "QT": ([2048, T], BF16),
            "VTOK": ([T, 1024], BF16),
            "OT": ([D, T], BF16),
            "GT": ([3072, T], BF16),
            "AB": ([T, 16], F32),
            "outT": ([D, T], F32),
        }
        self.VEC = S.sbuf("VEC", [128, NV], F32)
        self.MOD = S.sbuf("MOD", [128, NL * 48], F32)
        self.AM = S.sbuf("AMc", [128, NL * 8], F32)
        self.AFc = S.sbuf("AFc", [128, NL * 8], F32)
        self.ones32 = S.sbuf("ones32", [128, 128], F32)
        self.ident32 = S.sbuf("ident32", [128, 128], F32)
        self.onesbf = S.sbuf("onesbf", [128, 128], BF16)
        self.condT = S.sbuf("condT", [128, 8], F32)
        self.PS = [S.psum("ps%d" % i, [128, 512], F32) for i in range(8)]
        self.ps_i = 0
        S.op("pool", [], [self.ones32], lambda e: e.memset(self.ones32[:], 1.0))
        S.op("pool", [], [self.onesbf], lambda e: e.memset(self.onesbf[:], 1.0))
        S.seq("pool", self.ident32, [
            lambda e: e.memset(self.ident32[:], 0.0),
            lambda e: e.affine_select(out=self.ident32[:], in_=self.ident32[:], compare_op=ALU.not_equal,
                                      fill=1.0, base=0, pattern=[[-1, 128]], channel_multiplier=1)])

    def dbg(self, name, buf, ap):
        shape = [int(x) for x in ap.shape]
        d = self.nc.dram_tensor(name, shape, buf.t.dtype, kind="ExternalOutput").ap()
        self.S.dma([buf], [], d, ap)
        self.dbg_outs.append(name)

    def ps(self):
        b = self.PS[self.ps_i]
        self.ps_i = (self.ps_i + 1) % 8
        return b

    def psqs(self, n):
        if PSQ_SEPARATE:
            return [PQ(self.ps(), 0) for q in range(n)]
        b = self.ps()
        return [PQ(b, q) for q in range(n)]

    def vcol(self, name, i=0, n=1):
        o = VOFF[name] + i
        return self.VEC[:, o:o + n]

    def phase0(self):
        S, nc = self.S, self.nc
        P = Phase(S)
        VEC, MOD = self.VEC, self.MOD
        S.dma([], [VEC], VEC[:], self.dr["vecs"])
        condT = self.condT
        S.op("act", [VEC], [condT], lambda e: e.activation(out=condT[:], in_=self.vcol("cT", 0, 8), func=AF.Silu))
        CB = 1536
        wst = [P.sbuf("adaw%d" % i, [128, KC, CB], F32) for i in range(2)]
        k = 0
        for l in range(1 if self.defer_mod else NL):
            pm = self.ps()
            for cb in range(4):
                w = wst[k % 2]
                k += 1
                src = self.dr["ada_w"][l, :, cb * CB:(cb + 1) * CB].rearrange("(c p) n -> p c n", p=128)
                S.dma([], [w], w[:], src)
                def mm(e, w=w, cb=cb, pm=pm):
                    last = None
                    for j in range(12):
                        col = cb * 12 + j
                        for kc in range(KC):
                            last = e.matmul(pm[:, col:col + 1], lhsT=w[:, kc, j * 128:(j + 1) * 128],
                                            rhs=condT[:, kc:kc + 1], start=(kc == 0), stop=(kc == KC - 1))
                    return last
                S.op("pe", [w, condT], [pm], mm)
            S.op("dve", [pm, VEC], [MOD], lambda e, l=l, pm=pm: e.tensor_tensor(
                out=MOD[:, l * 48:(l + 1) * 48], in0=pm[:, 0:48], in1=self.vcol("adab", l * 48, 48), op=ALU.add))
            S.op("dve", [MOD, VEC], [self.AM], lambda e, l=l: e.scalar_tensor_tensor(
                out=self.AM[:, l * 8:(l + 1) * 8], in0=MOD[:, l * 48 + 8:l * 48 + 16], scalar=1.0,
                in1=self.vcol("gm", l * 8, 8), op0=ALU.add, op1=ALU.mult))
            S.op("dve", [MOD, VEC], [self.AFc], lambda e, l=l: e.scalar_tensor_tensor(
                out=self.AFc[:, l * 8:(l + 1) * 8], in0=MOD[:, l * 48 + 32:l * 48 + 40], scalar=1.0,
                in1=self.vcol("gf", l * 8, 8), op0=ALU.add, op1=ALU.mult))
        posi = P.sbuf("posi", [1, T], I32)
        posf = P.sbuf("posf", [1, T], F32)
        S.dma([], [posi], posi[:], self.dr["pos"])
        S.op("dve", [posi], [posf], lambda e: e.tensor_copy(out=posf[:], in_=posi[:]))
        angs = [P.sbuf("ang%d" % k, [128, 512], F32) for k in range(2)]
        rbuf = {}
        for wh in ("SIN", "COS"):
            for k in range(2):
                rbuf[(wh, k)] = (P.sbuf("r%s%d" % (wh, k), [128, 512], F32), P.sbuf("xs%s%d" % (wh, k), [128, 512], F32),
                                 P.sbuf("ki%s%d" % (wh, k), [128, 512], I32))
        for tt in range(8):
            sl = slice(tt * 512, (tt + 1) * 512)
            pb = self.ps()
            S.op("pe", [posf, self.ones32], [pb], lambda e, pb=pb, sl=sl: e.matmul(
                pb[:, :], lhsT=self.ones32[0:1, :], rhs=posf[0:1, sl], start=True, stop=True))
            ang = angs[tt % 2]
            S.op("dve", [pb, VEC], [ang], lambda e, pb=pb, ang=ang: e.tensor_scalar(
                out=ang[:], in0=pb[:, :], scalar1=self.vcol("invf"), scalar2=None, op0=ALU.mult))
            for which, shift in (("SIN", 0.0), ("COS", 0.5 * np.pi)):
                r, xs, ki = rbuf[(which, tt % 2)]
                C1 = 6.28125
                C2 = float(2 * np.pi - 6.28125)
                S.op("dve", [ang], [xs], lambda e, xs=xs, ang=ang, shift=shift: e.tensor_scalar(
                    out=xs[:], in0=ang[:], scalar1=float(shift), scalar2=None, op0=ALU.add))
                S.op("dve", [xs], [r], lambda e, r=r, xs=xs: e.tensor_scalar(
                    out=r[:], in0=xs[:], scalar1=float(1.0 / (2 * np.pi)), scalar2=None, op0=ALU.mult))
                S.op("dve", [r], [ki], lambda e, r=r, ki=ki: e.tensor_copy(out=ki[:], in_=r[:]))
                S.op("dve", [ki], [r], lambda e, r=r, ki=ki: e.tensor_copy(out=r[:], in_=ki[:]))
                S.op("dve", [r, xs], [xs], lambda e, r=r, xs=xs: e.scalar_tensor_tensor(
                    out=xs[:], in0=r[:], scalar=-C1, in1=xs[:], op0=ALU.mult, op1=ALU.add))
                S.op("dve", [r, xs], [xs], lambda e, r=r, xs=xs: e.scalar_tensor_tensor(
                    out=xs[:], in0=r[:], scalar=-C2, in1=xs[:], op0=ALU.mult, op1=ALU.add))
                S.op("dve", [xs], [xs], lambda e, xs=xs: e.tensor_scalar(
                    out=xs[:], in0=xs[:], scalar1=-3.14159, scalar2=3.14159, op0=ALU.max, op1=ALU.min))
                S.op("act", [xs], [r], lambda e, r=r, xs=xs: e.activation(out=r[:], in_=xs[:], func=AF.Sin))
                if which == "SIN":
                    S.op("dve", [r, VEC], [r], lambda e, r=r: e.tensor_scalar(
                        out=r[:], in0=r[:], scalar1=self.vcol("sinsign"), scalar2=None, op0=ALU.mult))
                S.dma([r], [], self.dr[which][:, sl], r[:])
        P.close()

    def load_weight_bf16(self, P, Wb, src3, ncols, stages, col0=0):
        S = self.S
        nk = src3.shape[1]
        CB = stages[0].t.shape[2]
        c = 0
        i = 0
        while c < ncols:
            n = min(CB, ncols - c)
            st = stages[i % len(stages)]
            i += 1
            S.dma([], [st], st[:, 0:nk, 0:n], src3[:, :, c:c + n])
            S.op("pool", [st], [Wb], lambda e, st=st, c=c, n=n: e.tensor_copy(
                out=Wb[:, 0:nk, col0 + c:col0 + c + n], in_=st[:, 0:nk, 0:n]))
            c += n

    def norm_tile(self, P, X, A, B, h, sq, tmps, rstd, n=512):
        S = self.S
        S.op("act", [X], [sq], lambda e: e.activation(out=sq[:, :, 0:n], in_=X[:, :, 0:n], func=AF.Square))
        pss = self.ps()
        def mm(e):
            last = None
            for kc in range(KC):
                last = e.matmul(pss[:, 0:n], lhsT=(self.ones32 if sq.t.dtype == F32 else self.onesbf)[:, :], rhs=sq[:, kc, 0:n],
                                start=(kc == 0), stop=(kc == KC - 1))
            return last
        S.op("pe", [sq, self.ones32, self.onesbf], [pss], mm)
        S.op("dve", [pss], [rstd], lambda e: e.tensor_scalar(
            out=rstd[:, 0:n], in0=pss[:, 0:n], scalar1=1.0 / D, scalar2=EPS, op0=ALU.mult, op1=ALU.add))
        S.op("act", [rstd], [rstd], lambda e: e.activation(out=rstd[:, 0:n], in_=rstd[:, 0:n], func=AF.Sqrt))
        S.op("dve", [rstd], [rstd], lambda e: e.reciprocal(out=rstd[:, 0:n], in_=rstd[:, 0:n]))
        for kc in range(KC):
            tmp = tmps[kc % len(tmps)]
            S.op("dve", [X, rstd, self.AM, self.AFc], [tmp], lambda e, kc=kc, tmp=tmp: e.scalar_tensor_tensor(
                out=tmp[:, 0:n], in0=X[:, kc, 0:n], scalar=A[:, kc:kc + 1], in1=rstd[:, 0:n], op0=ALU.mult, op1=ALU.mult))
            S.op("act", [tmp, self.MOD], [h], lambda e, kc=kc, tmp=tmp: e.activation(
                out=h[:, kc, 0:n], in_=tmp[:, 0:n], func=AF.Identity, bias=B[:, kc:kc + 1], scale=1.0))

    def phase1(self, l):
        S = self.S
        P = Phase(S)
        attn = (l % 2 == 0)
        i = l // 2
        NC_ = 4096 if attn else 4112
        Wb = P.sbuf("W1", [128, KC, NC_], BF16)
        stages = [P.sbuf("wst%d" % k, [128, KC, 512], F32) for k in range(2)]
        wsrc = self.dr["attn_w_in" if attn else "gdn_w_in"][i].rearrange("(c p) n -> p c n", p=128)
        self.load_weight_bf16(P, Wb, wsrc, NC_, stages)
        Xb = [P.sbuf("X%d" % k, [128, KC, 512], F32) for k in range(2)]
        sq = P.sbuf("sq", [128, KC, 512], F32)
        tmps = [P.sbuf("tmp%d" % k, [128, 512], F32) for k in range(2)]
        rstd = P.sbuf("rstd", [128, 512], F32)
        h = P.sbuf("h", [128, KC, 512], BF16)
        stg = [P.sbuf("stg%d" % k, [128, 512], BF16) for k in range(4)]
        stg32 = [P.sbuf("stgf%d" % k, [128, 16], F32) for k in range(2)]
        rt = [P.sbuf("rt%d" % k, [128, 512], F32) for k in range(4)]
        cs = [[P.sbuf("cs%d_%d" % (a, k), [128, 512], F32) for k in range(2)] for a in range(2)]
        A = self.AM[:, l * 8:(l + 1) * 8]
        B = self.MOD[:, l * 48:l * 48 + 8]
        XT3 = self.dr["xT" if l == 0 else "XT"].rearrange("(c p) t -> p c t", p=128)
        NT = T // 512
        S.dma([], [Xb[0]], Xb[0][:], XT3[:, :, 0:512])
        sk = 0
        ev = 0
        for tt in range(NT):
            sl = slice(tt * 512, (tt + 1) * 512)
            X = Xb[tt % 2]
            if tt + 1 < NT:
                S.dma([], [Xb[(tt + 1) % 2]], Xb[(tt + 1) % 2][:], XT3[:, :, (tt + 1) * 512:(tt + 2) * 512])
            if attn:
                cosb, sinb = cs[0][tt % 2], cs[1][tt % 2]
                S.dma([], [cosb], cosb[:], self.dr["COS"][:, sl])
                S.dma([], [sinb], sinb[:], self.dr["SIN"][:, sl])
            self.norm_tile(P, X, A, B, h, sq, tmps, rstd)

            def proj_fm(col0):
                pp = self.ps()
                def mm(e, pp=pp, col0=col0):
                    last = None
                    for kc in range(KC):
                        last = e.matmul(pp[:, :], lhsT=Wb[:, kc, col0:col0 + 128], rhs=h[:, kc, :],
                                        start=(kc == 0), stop=(kc == KC - 1))
                    return last
                S.op("pe", [Wb, h], [pp], mm)
                return pp

            def evac(pp, dst_ap, scale, width=512):
                nonlocal sk, ev
                st = stg[sk % 4]
                sk += 1
                if ev % 2 == 0:
                    S.op("act", [pp], [st], lambda e: e.activation(out=st[:, 0:width], in_=pp[:, 0:width], func=AF.Copy, scale=float(scale)))
                else:
                    S.op("dve", [pp], [st], lambda e: e.tensor_scalar(out=st[:, 0:width], in0=pp[:, 0:width], scalar1=float(scale), scalar2=None, op0=ALU.mult))
                ev += 1
                S.dma([st], [], dst_ap, st[:, 0:width])

            def proj_tm(col0, ncols, dst, dcol0, f32out=False):
                nonlocal sk
                for s in range(4):
                    pp = self.ps()
                    def mm(e, pp=pp, s=s):
                        last = None
                        for kc in range(KC):
                            last = e.matmul(pp[:, 0:ncols], lhsT=h[:, kc, s * 128:(s + 1) * 128], rhs=Wb[:, kc, col0:col0 + ncols],
                                            start=(kc == 0), stop=(kc == KC - 1))
                        return last
                    S.op("pe", [Wb, h], [pp], mm)
                    r0 = tt * 512 + s * 128
                    if f32out:
                        st = stg32[s % 2]
                        S.op("dve", [pp], [st], lambda e, st=st, pp=pp: e.tensor_copy(out=st[:, 0:ncols], in_=pp[:, 0:ncols]))
                        S.dma([st], [], dst[r0:r0 + 128, dcol0:dcol0 + ncols], st[:, 0:ncols])
                    else:
                        evac(pp, dst[r0:r0 + 128, dcol0:dcol0 + ncols], 1.0, ncols)

            if attn:
                QT = self.dr["QT"]
                for grp, (c0, pc0, d0, scale) in enumerate([(0, 3072, 0, 0.125), (512, 3584, 512, 1.0)]):
                    for n in range(4):
                        pq = proj_fm(c0 + n * 128)
                        pp = proj_fm(pc0 + n * 128)
                        t1, t2 = rt[(2 * n) % 4], rt[(2 * n + 1) % 4]
                        st = stg[sk % 4]
                        sk += 1
                        S.op("dve", [pq, cosb], [t1], lambda e, pq=pq, t1=t1, scale=scale: e.scalar_tensor_tensor(
                            out=t1[:], in0=pq[:, :], scalar=float(scale), in1=cosb[:], op0=ALU.mult, op1=ALU.mult))
                        S.op("dve", [pp, sinb], [t2], lambda e, pp=pp, t2=t2, scale=scale: e.scalar_tensor_tensor(
                            out=t2[:], in0=pp[:, :], scalar=float(scale), in1=sinb[:], op0=ALU.mult, op1=ALU.mult))
                        S.op("pool", [t1, t2], [st], lambda e, st=st, t1=t1, t2=t2: e.tensor_tensor(
                            out=st[:], in0=t1[:], in1=t2[:], op=ALU.add))
                        S.dma([st], [], QT[d0 + n * 128:d0 + (n + 1) * 128, sl], st[:])
                for n in range(4):
                    evac(proj_fm(1536 + n * 128), QT[1024 + n * 128:1024 + (n + 1) * 128, sl], 0.125)
                for n in range(4):
                    evac(proj_fm(2048 + n * 128), QT[1536 + n * 128:1536 + (n + 1) * 128, sl], 1.0)
                proj_tm(1024, 512, self.dr["VTOK"], 0)
                proj_tm(2560, 512, self.dr["VTOK"], 512)
            else:
                GT = self.dr["GT"]
                for n in range(24):
                    evac(proj_fm(n * 128), GT[n * 128:(n + 1) * 128, sl], 1.0)
                proj_tm(3072, 512, self.dr["VTOK"], 0)
                proj_tm(3584, 512, self.dr["VTOK"], 512)
                proj_tm(4096, 16, self.dr["AB"], 0, f32out=True)
        P.close()

    def phase34(self, l, final=False, tiles=None):
        S = self.S
        P = Phase(S)
        i = l // 2
        TT = 256
        Wo = P.sbuf("Wo", [128, KC, D], BF16)
        Wi = P.sbuf("Wi", [128, KC, 2 * FH], BF16)
        W2 = P.sbuf("W2", [128, NJ, D], BF16)
        PW = Phase(S)
        stA = [PW.sbuf("stA%d" % k, [128, KC, 512], F32) for k in range(2)]
        stB = [PW.sbuf("stB%d" % k, [128, NJ, 128], F32) for k in range(2)]
        wo_src = self.dr["attn_w_out" if l % 2 == 0 else "gdn_w_out"][i].rearrange("(c p) n -> p c n", p=128)
        self.load_weight_bf16(PW, Wo, wo_src, D, stA)
        self.load_weight_bf16(PW, Wi, self.dr["ffn_w_in"][l].rearrange("(c p) n -> p c n", p=128), 2 * FH, stA)
        self.load_weight_bf16(PW, W2, self.dr["ffn_w_out"][l].rearrange("(c p) n -> p c n", p=128), D, stB)
        PW.close()
        Xb = [P.sbuf("X%d" % k, [128, KC, TT], F32) for k in range(2)]
        ob = [P.sbuf("oT%d" % k, [128, KC, TT], BF16) for k in range(2)]
        sq = P.sbuf("sq", [128, KC, TT], BF16)
        tmps = [P.sbuf("tmp%d" % k, [128, TT], F32) for k in range(2)]
        sgs = [P.sbuf("sg%d" % k, [128, TT], F32) for k in range(2)]
        rstds = [P.sbuf("rstd%d" % k, [128, TT], F32) for k in range(2)]
        hb = [P.sbuf("h%d" % k, [128, KC, TT], BF16) for k in range(2)]
        hid = P.sbuf("hid", [128, NJ, TT], BF16)
        GM = self.MOD[:, l * 48 + 16:l * 48 + 24]
        GF = self.MOD[:, l * 48 + 40:l * 48 + 48]
        A = self.AFc[:, l * 8:(l + 1) * 8]
        B = self.MOD[:, l * 48 + 24:l * 48 + 32]
        XS3 = self.dr["xT" if l == 0 else "XT"].rearrange("(c p) t -> p c t", p=128)
        XD3 = self.dr["XT"].rearrange("(c p) t -> p c t", p=128)
        OT3 = self.dr["OT"].rearrange("(c p) t -> p c t", p=128)
        if final:
            OUT3 = self.dr["outT"].rearrange("(c p) t -> p c t", p=128)
        NT = T // TT
        tl = list(range(NT)) if tiles is None else list(tiles)

        def front(ti):
            tt = tl[ti]
            sl = slice(tt * TT, (tt + 1) * TT)
            X, oT, h = Xb[ti % 2], ob[ti % 2], hb[ti % 2]
            S.dma([], [oT], oT[:], OT3[:, :, sl])
            S.dma([], [X], X[:], XS3[:, :, sl])
            for n in range(KC):
                pp = self.ps()
                def mm(e, pp=pp, n=n):
                    last = None
                    for kc in range(KC):
                        last = e.matmul(pp[:, 0:TT], lhsT=Wo[:, kc, n * 128:(n + 1) * 128], rhs=oT[:, kc, :],
                                        start=(kc == 0), stop=(kc == KC - 1))
                    return last
                S.op("pe", [Wo, oT], [pp], mm)
                S.op("dve", [pp, X, self.MOD], [X], lambda e, pp=pp, n=n: e.scalar_tensor_tensor(
                    out=X[:, n, :], in0=pp[:, 0:TT], scalar=GM[:, n:n + 1], in1=X[:, n, :], op0=ALU.mult, op1=ALU.add))
            self.norm_tile(P, X, A, B, h, sq, tmps, rstds[ti % 2], n=TT)

        def back(ti):
            tt = tl[ti]
            sl = slice(tt * TT, (tt + 1) * TT)
            X, h = Xb[ti % 2], hb[ti % 2]
            for j in range(NJ):
                pg = self.ps()
                pu = self.ps()
                def mmg(e, pg=pg, j=j):
                    last = None
                    for kc in range(KC):
                        last = e.matmul(pg[:, 0:TT], lhsT=Wi[:, kc, j * 128:(j + 1) * 128], rhs=h[:, kc, :],
                                        start=(kc == 0), stop=(kc == KC - 1))
                    return last
                def mmu(e, pu=pu, j=j):
                    last = None
                    for kc in range(KC):
                        last = e.matmul(pu[:, 0:TT], lhsT=Wi[:, kc, FH + j * 128:FH + (j + 1) * 128], rhs=h[:, kc, :],
                                        start=(kc == 0), stop=(kc == KC - 1))
                    return last
                S.op("pe", [Wi, h], [pg], mmg)
                S.op("pe", [Wi, h], [pu], mmu)
                sg = sgs[j % 2]
                S.op("act", [pg], [sg], lambda e, pg=pg, sg=sg: e.activation(out=sg[:], in_=pg[:, 0:TT], func=AF.Silu))
                S.op("dve", [sg, pu], [hid], lambda e, pu=pu, sg=sg, j=j: e.tensor_tensor(
                    out=hid[:, j, :], in0=pu[:, 0:TT], in1=sg[:], op=ALU.mult))
            for n in range(KC):
                pp = self.ps()
                def mm2(e, pp=pp, n=n):
                    last = None
                    for j in range(NJ):
                        last = e.matmul(pp[:, 0:TT], lhsT=W2[:, j, n * 128:(n + 1) * 128], rhs=hid[:, j, :],
                                        start=(j == 0), stop=(j == NJ - 1))
                    return last
                S.op("pe", [W2, hid], [pp], mm2)
                S.op("dve", [pp, X, self.MOD], [X], lambda e, pp=pp, n=n: e.scalar_tensor_tensor(
                    out=X[:, n, :], in0=pp[:, 0:TT], scalar=GF[:, n:n + 1], in1=X[:, n, :], op0=ALU.mult, op1=ALU.add))
            if final:
                self.final_norm_tile(X, X, sq, rstds[ti % 2], TT)
                S.dma([X], [], OUT3[:, :, sl], X[:])
            else:
                S.dma([X], [], XD3[:, :, sl], X[:])

        front(0)
        for ti in range(len(tl)):
            if ti + 1 < len(tl):
                front(ti + 1)
            back(ti)
        P.close()

    def final_norm_tile(self, X, yout, sq, rstd, n):
        S = self.S
        S.op("act", [X], [sq], lambda e: e.activation(out=sq[:, :, 0:n], in_=X[:, :, 0:n], func=AF.Square))
        pss = self.ps()
        def mm(e):
            last = None
            for kc in range(KC):
                last = e.matmul(pss[:, 0:n], lhsT=(self.ones32 if sq.t.dtype == F32 else self.onesbf)[:, :], rhs=sq[:, kc, 0:n],
                                start=(kc == 0), stop=(kc == KC - 1))
            return last
        S.op("pe", [sq, self.ones32, self.onesbf], [pss], mm)
        S.op("dve", [pss], [rstd], lambda e: e.tensor_scalar(
            out=rstd[:, 0:n], in0=pss[:, 0:n], scalar1=1.0 / D, scalar2=EPS, op0=ALU.mult, op1=ALU.add))
        S.op("act", [rstd], [rstd], lambda e: e.activation(out=rstd[:, 0:n], in_=rstd[:, 0:n], func=AF.Sqrt))
        S.op("dve", [rstd], [rstd], lambda e: e.reciprocal(out=rstd[:, 0:n], in_=rstd[:, 0:n]))
        for kc in range(KC):
            S.op("dve", [X, rstd, self.VEC], [yout], lambda e, kc=kc: e.scalar_tensor_tensor(
                out=yout[:, kc, 0:n], in0=X[:, kc, 0:n], scalar=self.vcol("fin", kc), in1=rstd[:, 0:n], op0=ALU.mult, op1=ALU.mult))

    def mod_gen(self, P, layers, psfn):
        S = self.S
        stg = [P.sbuf("mst%d" % k, [128, KC, 512], F32) for k in range(2)]
        condT, MOD = self.condT, self.MOD
        k = 0
        for l in layers:
            for cb in range(12):
                w = stg[k % 2]
                k += 1
                S.dma([], [w], w[:], self.dr["ada_w"][l, :, cb * 512:(cb + 1) * 512].rearrange("(c p) n -> p c n", p=128))
                yield
                pm = psfn()
                def mm(e, w=w, pm=pm):
                    last = None
                    for j in range(4):
                        for kc in range(KC):
                            last = e.matmul(pm[:, j:j + 1], lhsT=w[:, kc, j * 128:(j + 1) * 128], rhs=condT[:, kc:kc + 1],
                                            start=(kc == 0), stop=(kc == KC - 1))
                    return last
                S.op("pe", [w, condT], [pm], mm)
                c0 = l * 48 + cb * 4
                S.op("dve", [pm, self.VEC], [MOD], lambda e, pm=pm, c0=c0: e.tensor_tensor(
                    out=MOD[:, c0:c0 + 4], in0=pm[:, 0:4], in1=self.VEC[:, VOFF["adab"] + c0:VOFF["adab"] + c0 + 4], op=ALU.add))
                yield
            S.op("dve", [MOD, self.VEC], [self.AM], lambda e, l=l: e.scalar_tensor_tensor(
                out=self.AM[:, l * 8:(l + 1) * 8], in0=MOD[:, l * 48 + 8:l * 48 + 16], scalar=1.0,
                in1=self.vcol("gm", l * 8, 8), op0=ALU.add, op1=ALU.mult))
            S.op("dve", [MOD, self.VEC], [self.AFc], lambda e, l=l: e.scalar_tensor_tensor(
                out=self.AFc[:, l * 8:(l + 1) * 8], in0=MOD[:, l * 48 + 32:l * 48 + 40], scalar=1.0,
                in1=self.vcol("gf", l * 8, 8), op0=ALU.add, op1=ALU.mult))

    def phase_attn(self, heads_moba=range(8), heads_sb=range(8), qtiles=range(8), mod_layers=()):
        S = self.S
        P = Phase(S)
        QT, VTOK, OT = self.dr["QT"], self.dr["VTOK"], self.dr["OT"]
        cmask = [P.sbuf("cmask%d" % d, [128, 512], BF16) for d in range(4)]
        smask = [P.sbuf("smask%d" % d, [128, 512], BF16) for d in range(4)]
        for d in range(4):
            for mk, op in ((cmask[d], ALU.is_ge), (smask[d], ALU.is_gt)):
                S.seq("pool", mk, [
                    lambda e, mk=mk: e.memset(mk[:], 1.0),
                    lambda e, mk=mk, op=op, d=d: e.affine_select(out=mk[:], in_=mk[:], compare_op=op, fill=0.0, base=-128 * d,
                                                                 pattern=[[1, 512]], channel_multiplier=-1)])
        NUI = P.sbuf("NUI", [128, 128], BF16)
        S.seq("pool", NUI, [
            lambda e: e.memset(NUI[:], -1.0),
            lambda e: e.affine_select(out=NUI[:], in_=NUI[:], compare_op=ALU.is_ge, fill=0.0, base=0,
                                      pattern=[[-1, 128]], channel_multiplier=1)])
        onesb = P.sbuf("onesb", [128, 128], BF16)
        S.op("pool", [], [onesb], lambda e: e.memset(onesb[:], 1.0))
        past01 = P.sbuf("past01", [128, 32, 16], F32)
        pastb = P.sbuf("pastb", [128, 32, 16], F32)
        ownm1 = P.sbuf("ownm1", [128, 32, 16], F32)
        S.seq("pool", past01, [
            lambda e: e.memset(past01[:], 1.0),
            lambda e: e.affine_select(out=past01[:], in_=past01[:], compare_op=ALU.is_ge, fill=0.0, base=-2,
                                      pattern=[[1, 32], [-2, 16]], channel_multiplier=0)])
        S.seq("pool", pastb, [
            lambda e: e.memset(pastb[:], 0.0),
            lambda e: e.affine_select(out=pastb[:], in_=pastb[:], compare_op=ALU.is_ge, fill=-1e30, base=-2,
                                      pattern=[[1, 32], [-2, 16]], channel_multiplier=0)])
        S.seq("pool", ownm1, [
            lambda e: e.memset(ownm1[:], 0.0),
            lambda e: e.affine_select(out=ownm1[:], in_=ownm1[:], compare_op=ALU.is_ge, fill=-1.0, base=0,
                                      pattern=[[1, 32], [-2, 16]], channel_multiplier=0),
            lambda e: e.affine_select(out=ownm1[:], in_=ownm1[:], compare_op=ALU.is_ge, fill=-1.0, base=1,
                                      pattern=[[-1, 32], [2, 16]], channel_multiplier=0)])
        QAUG = [P.sbuf("QAUG%d" % k, [128, T], BF16) for k in range(2)]
        KAUG = [P.sbuf("KAUG%d" % k, [128, T], BF16) for k in range(2)]
        KS = [P.sbuf("KS%d" % k, [128, T], BF16) for k in range(2)]
        KAc = P.sbuf("KAc", [16, T], BF16)
        S.seq("pool", KAc, [
            lambda e: e.memset(KAc[:], BIG),
            lambda e: e.affine_select(out=KAc[:], in_=KAc[:], compare_op=ALU.is_ge, fill=0.0, base=0,
                                      pattern=[[1, T]], channel_multiplier=-256),
            lambda e: e.affine_select(out=KAc[:], in_=KAc[:], compare_op=ALU.is_ge, fill=0.0, base=255,
                                      pattern=[[-1, T]], channel_multiplier=256)])
        for k in range(2):
            S.op("pool", [], [QAUG[k]], lambda e, k=k: e.memset(QAUG[k][:], 0.0))
            S.op("pool", [], [KS[k]], lambda e, k=k: e.memset(KS[k][:], 0.0))
            S.seq("pool", KAUG[k], [
                lambda e, k=k: e.memset(KAUG[k][:], 0.0),
                lambda e, k=k: e.memset(KAUG[k][96:97, :], -1.0)])
            S.dma([KAc], [KAUG[k]], KAUG[k][64:80, :], KAc[:])
        Vb = [P.sbuf("V%d" % k, [128, 32, 65], BF16) for k in range(2)]
        for k in range(2):
            S.op("pool", [], [Vb[k]], lambda e, k=k: e.memset(Vb[k][:], 1.0))
        sqQ = P.sbuf("sqQ", [64, T], BF16)
        kmean = P.sbuf("kmean", [64, 16], F32)
        kmeanb = P.sbuf("kmeanb", [64, 16], BF16)
        gm = P.sbuf("gm", [128, 32, 16], F32)
        m8 = P.sbuf("m8", [128, 32, 8], F32)
        MBw = P.sbuf("MBw", [128, 32, 97], F32)
        S.op("pool", [], [MBw], lambda e: e.memset(MBw[:], 0.0))
        kmx = P.sbuf("kmx", [128, 8], F32)
        kmax2 = P.sbuf("kmax2", [128, 1], F32)
        Pt = [P.sbuf("Pt%d" % k, [128, 512], BF16) for k in range(5)]
        Eb = [P.sbuf("E%d" % k, [128, 512], F32) for k in range(2)]
        Lp = [P.sbuf("Lp%d" % k, [128, 512], BF16) for k in range(4)]
        exb = [P.sbuf("ex%d" % k, [128, 512], F32) for k in range(2)]
        rs = P.sbuf("rs", [128, 512], F32)
        Osb = P.sbuf("Osb", [65, 512], F32)
        rden = P.sbuf("rden", [128, 512], F32)
        S.op("pool", [], [rden], lambda e: e.memset(rden[:], 0.0))
        sel64 = P.sbuf("sel64", [128, 128], F32)
        S.seq("pool", sel64, [lambda e: e.memset(sel64[:], 0.0), lambda e: e.memset(sel64[64:65, :], 1.0)])
        ost = [P.sbuf("ost%d" % k, [64, 512], BF16) for k in range(2)]
        PSr = self.PS[0:6]
        PSo = self.PS[6:8]
        rr = [0]
        def psr():
            b = PSr[rr[0] % 6]
            rr[0] += 1
            return b
        jobs = [("m", h) for h in heads_moba] + [("s", h) for h in heads_sb]
        mg = self.mod_gen(P, list(mod_layers), psr) if mod_layers else None
        def mg_step():
            nonlocal mg
            if mg is not None:
                try:
                    next(mg)
                except StopIteration:
                    mg = None
        def load_head(idx):
            kind, h = jobs[idx]
            k = idx % 2
            qrow = (0 if kind == "m" else 1024) + h * 64
            krow = (512 if kind == "m" else 1536) + h * 64
            vcol = (0 if kind == "m" else 512) + h * 64
            kdst = KAUG[k] if kind == "m" else KS[k]
            S.dma([], [QAUG[k]], QAUG[k][0:64, :], QT[qrow:qrow + 64, :])
            S.dma([], [kdst], kdst[0:64, :], QT[krow:krow + 64, :])
            S.dma([], [Vb[k]], Vb[k][:, :, 0:64], VTOK[:, vcol:vcol + 64].rearrange("(j p) d -> p j d", p=128))
        load_head(0)
        oi = 0
        pre_done = set()
        for idx, (kind, h) in enumerate(jobs):
            Q, V = QAUG[idx % 2], Vb[idx % 2]
            Kt = KAUG[idx % 2] if kind == "m" else KS[idx % 2]
            if idx + 1 < len(jobs):
                load_head(idx + 1)
            if kind == "m":
                def prelude(Q, Kt):
                    S.op("dve", [Kt], [kmean], lambda e: e.tensor_reduce(
                        out=kmean[:], in_=Kt[0:64, :].rearrange("p (n k) -> p n k", k=256), axis=AX.X, op=ALU.add))
                    S.op("dve", [kmean], [kmeanb], lambda e: e.tensor_scalar(
                        out=kmeanb[:], in0=kmean[:], scalar1=1.0 / 256, scalar2=None, op0=ALU.mult))
                    yield
                    GP = psr()
                    def mm(e):
                        last = None
                        for tt in range(32):
                            last = e.matmul(GP[:, tt * 16:(tt + 1) * 16], lhsT=Q[0:64, tt * 128:(tt + 1) * 128], rhs=kmeanb[:, :],
                                            start=True, stop=True)
                        return last
                    S.op("pe", [Q, kmeanb], [GP], mm)
                    S.op("dve", [GP, pastb], [gm], lambda e: e.tensor_tensor(
                        out=gm[:].rearrange("p a b -> p (a b)"), in0=GP[:, :], in1=pastb[:].rearrange("p a b -> p (a b)"), op=ALU.add))
                    yield
                    for tt in range(32):
                        S.op("dve", [gm], [m8], lambda e, tt=tt: e.max(out=m8[:, tt, :], in_=gm[:, tt, :]))
                    yield
                    for tt in range(32):
                        S.op("dve", [gm, m8, past01], [MBw], lambda e, tt=tt: e.scalar_tensor_tensor(
                            out=MBw[:, tt, 64:80], in0=gm[:, tt, :], scalar=m8[:, tt, 2:3], in1=past01[:, tt, :], op0=ALU.is_ge, op1=ALU.mult))
                    S.op("dve", [MBw, ownm1], [MBw], lambda e: e.tensor_tensor(
                        out=MBw[:, :, 64:80], in0=MBw[:, :, 64:80], in1=ownm1[:], op=ALU.add))
                    yield
                    S.op("pool", [Kt], [sqQ], lambda e: e.tensor_tensor(out=sqQ[:], in0=Kt[0:64, :], in1=Kt[0:64, :], op=ALU.mult))
                    for g in range(8):
                        KP = psr()
                        S.op("pe", [sqQ, onesb], [KP], lambda e, KP=KP, g=g: e.matmul(
                            KP[:, :], lhsT=onesb[0:64, :], rhs=sqQ[:, g * 512:(g + 1) * 512], start=True, stop=True))
                        S.op("dve", [KP], [kmx], lambda e, KP=KP, g=g: e.tensor_reduce(
                            out=kmx[:, g:g + 1], in_=KP[:, :], axis=AX.X, op=ALU.max))
                    yield
                    S.op("dve", [kmx], [kmax2], lambda e: e.tensor_reduce(out=kmax2[:], in_=kmx[:], axis=AX.X, op=ALU.max))
                    S.op("pool", [Q], [sqQ], lambda e: e.tensor_tensor(out=sqQ[:], in0=Q[0:64, :], in1=Q[0:64, :], op=ALU.mult))
                    yield
                    QQ = psr()
                    def mmq(e):
                        last = None
                        for tt in range(32):
                            last = e.matmul(QQ[:, tt:tt + 1], lhsT=sqQ[:, tt * 128:(tt + 1) * 128], rhs=onesb[0:64, 0:1], start=True, stop=True)
                        return last
                    S.op("pe", [sqQ, onesb], [QQ], mmq)
                    S.op("act", [QQ, kmax2], [MBw], lambda e: e.activation(
                        out=MBw[:, :, 96], in_=QQ[:, 0:32], func=AF.Sqrt, scale=kmax2[:, 0:1]))
                    yield
                    for g in range(8):
                        TP = psr()
                        def tr(e, TP=TP, g=g):
                            last = None
                            for k in range(4):
                                last = e.transpose(out=TP[0:97, k * 128:(k + 1) * 128], in_=MBw[:, 4 * g + k, :], identity=self.ident32[:, :])
                            return last
                        S.op("pe", [MBw, self.ident32], [TP], tr)
                        S.op("dve", [TP], [Q], lambda e, TP=TP, g=g: e.tensor_copy(out=Q[64:97, g * 512:(g + 1) * 512], in_=TP[64:97, :]))

                if idx not in pre_done:
                    for _ in prelude(Q, Kt):
                        pass
                nxt = None
                if idx + 1 < len(jobs) and jobs[idx + 1][0] == "m":
                    nxt = prelude(QAUG[(idx + 1) % 2], KAUG[(idx + 1) % 2])
                    pre_done.add(idx + 1)
                pairs = [(i, j) for i in qtiles for j in range(4 * i + 4)]
                st = {}
                def stageA(p):
                    i, j = pairs[p]
                    sp = psr()
                    S.op("pe", [Kt, Q], [sp], lambda e: e.matmul(
                        sp[:, :], lhsT=Kt[:, j * 128:(j + 1) * 128], rhs=Q[:, i * 512:(i + 1) * 512], start=True, stop=True))
                    pt = Pt[p % 5]
                    S.op("act", [sp], [pt], lambda e: e.activation(out=pt[:], in_=sp[:, :], func=AF.Exp))
                    if j >= 4 * i:
                        S.op("dve", [pt, cmask[j - 4 * i]], [pt], lambda e: e.tensor_tensor(out=pt[:], in0=pt[:], in1=cmask[j - 4 * i][:], op=ALU.mult))
                    st[p] = pt
                def stageC(p):
                    nonlocal oi
                    i, j = pairs[p]
                    pt = st.pop(p)
                    last = 4 * i + 3
                    if j == 0:
                        oi += 1
                    OP = PSo[oi % 2]
                    S.op("pe", [V, pt], [OP], lambda e: e.matmul(OP[0:65, :], lhsT=V[:, j, :], rhs=pt[:], start=(j == 0), stop=(j == last)))
                    if j == last:
                        S.op("act", [OP], [Osb], lambda e: e.activation(out=Osb[:], in_=OP[0:65, :], func=AF.Copy))
                        if getattr(self, "debug", False) and i == 0:
                            self.dbg("dbgOsb", Osb, Osb[:])
                        S.op("dve", [Osb], [rden], lambda e: e.reciprocal(out=rden[64:65, :], in_=Osb[64:65, :]))
                        BP = psr()
                        S.op("pe", [rden, sel64], [BP], lambda e: e.matmul(
                            BP[:, :], lhsT=sel64[:, :], rhs=rden[:, :], start=True, stop=True))
                        o = ost[i % 2]
                        S.op("dve", [Osb, BP], [o], lambda e: e.tensor_tensor(out=o[:], in0=Osb[0:64, :], in1=BP[0:64, :], op=ALU.mult))
                        S.dma([o], [], OT[h * 64:(h + 1) * 64, i * 512:(i + 1) * 512], o[:])
                for p in range(len(pairs) + 4):
                    if p < len(pairs):
                        stageA(p)
                    if p >= 4:
                        stageC(p - 4)
                    if nxt is not None and p % 8 == 4:
                        try:
                            next(nxt)
                        except StopIteration:
                            nxt = None
                    if p % 16 == 9:
                        mg_step()
                if nxt is not None:
                    for _ in nxt:
                        pass
            else:
                pairs = [(i, j) for i in qtiles for j in range(4 * i + 3, -1, -1)]
                st = {}
                def stageA(p):
                    i, j = pairs[p]
                    z = psr()
                    S.op("pe", [Kt, Q], [z], lambda e: e.matmul(
                        z[:, :], lhsT=Kt[:, j * 128:(j + 1) * 128], rhs=Q[:, i * 512:(i + 1) * 512], start=True, stop=False))
                    E = Eb[p % 2]
                    lp = Lp[p % 4]
                    S.op("act", [z], [E], lambda e: e.activation(out=E[:], in_=z[:, :], func=AF.Exp))
                    S.op("act", [E], [lp], lambda e: e.activation(out=lp[:], in_=E[:], func=AF.Ln, bias=1.0, scale=1.0))
                    if j >= 4 * i:
                        S.op("pool", [lp, smask[j - 4 * i]], [lp], lambda e: e.tensor_tensor(out=lp[:], in0=lp[:], in1=smask[j - 4 * i][:], op=ALU.mult))
                    st[p] = (z, lp)
                def stageB(p):
                    i, j = pairs[p]
                    z, lp = st[p]
                    first = (j == 4 * i + 3)
                    S.op("pe", [NUI, lp], [z], lambda e: e.matmul(z[:, :], lhsT=NUI[:, :], rhs=lp[:], start=False, stop=True))
                    cs = psr()
                    S.op("pe", [onesb, lp], [cs], lambda e: e.matmul(cs[:, :], lhsT=onesb[:, :], rhs=lp[:], start=True, stop=True))
                    ex = exb[p % 2]
                    if first:
                        S.op("dve", [z], [ex], lambda e: e.tensor_copy(out=ex[:], in_=z[:, :]))
                        S.op("dve", [cs], [rs], lambda e: e.tensor_copy(out=rs[:], in_=cs[:, :]))
                    else:
                        S.op("dve", [z, rs], [ex], lambda e: e.tensor_tensor(out=ex[:], in0=z[:, :], in1=rs[:], op=ALU.subtract))
                        S.op("dve", [cs, rs], [rs], lambda e: e.tensor_tensor(out=rs[:], in0=cs[:, :], in1=rs[:], op=ALU.add))
                    w = Pt[p % 3]
                    S.op("act", [ex], [w], lambda e: e.activation(out=w[:], in_=ex[:], func=AF.Exp))
                    if j >= 4 * i:
                        S.op("pool", [w, smask[j - 4 * i]], [w], lambda e: e.tensor_tensor(out=w[:], in0=w[:], in1=smask[j - 4 * i][:], op=ALU.mult))
                    st[p] = w
                def stageC(p):
                    nonlocal oi
                    i, j = pairs[p]
                    w = st.pop(p)
                    first = (j == 4 * i + 3)
                    if first:
                        oi += 1
                    OP = PSo[oi % 2]
                    S.op("pe", [V, w], [OP], lambda e: e.matmul(OP[0:64, :], lhsT=V[:, j, 0:64], rhs=w[:], start=first, stop=(j == 0)))
                    if j == 0:
                        o = ost[i % 2]
                        S.op("act", [OP], [o], lambda e: e.activation(out=o[:], in_=OP[0:64, :], func=AF.Copy))
                        S.dma([o], [], OT[512 + h * 64:512 + (h + 1) * 64, i * 512:(i + 1) * 512], o[:])
                n = len(pairs)
                for p in range(n + 4):
                    if p % 16 == 9:
                        mg_step()
                    if p < n:
                        stageA(p)
                    if 2 <= p < n + 2:
                        stageB(p - 2)
                    if p >= 4:
                        stageC(p - 4)
        while mg is not None:
            mg_step()
        P.close()

    def phase_gdn(self, li, heads=range(8), nchunks=32):
        S = self.S
        P = Phase(S)
        GT, ZT, AB, OT = self.dr["GT"], self.dr["VTOK"], self.dr["AB"], self.dr["OT"]
        NCH = 32
        C = 128
        UT = P.sbuf("UT", [128, 128], F32)
        SU = P.sbuf("SU", [128, 128], F32)
        S.seq("pool", UT, [
            lambda e: e.memset(UT[:], 1.0),
            lambda e: e.affine_select(out=UT[:], in_=UT[:], compare_op=ALU.is_ge, fill=0.0, base=0,
                                      pattern=[[1, 128]], channel_multiplier=-1)])
        S.seq("pool", SU, [
            lambda e: e.memset(SU[:], 1.0),
            lambda e: e.affine_select(out=SU[:], in_=SU[:], compare_op=ALU.is_gt, fill=0.0, base=0,
                                      pattern=[[-1, 128]], channel_multiplier=1)])
        I32m = self.ident32
        ab = P.sbuf("ab", [128, NCH, 16], F32)
        S.dma([], [ab], ab[:], AB.rearrange("(c p) n -> p c n", p=128))
        graw = P.sbuf("graw", [128, NCH, 8], F32)
        beta = P.sbuf("beta", [128, NCH, 8], F32)
        nbeta = P.sbuf("nbeta", [128, NCH, 8], F32)
        negA = P.sbuf("negA", [128, 8], F32)
        tmpa = P.sbuf("tmpa", [128, NCH, 8], F32)
        S.op("act", [self.VEC], [negA], lambda e: e.activation(out=negA[:], in_=self.vcol("alog", li * 8, 8), func=AF.Exp))
        S.op("dve", [negA], [negA], lambda e: e.tensor_scalar(out=negA[:], in0=negA[:], scalar1=-1.0, scalar2=None, op0=ALU.mult))
        for c in range(NCH):
            S.op("dve", [ab, self.VEC], [tmpa], lambda e, c=c: e.tensor_tensor(
                out=tmpa[:, c, :], in0=ab[:, c, 0:8], in1=self.vcol("dtb", li * 8, 8), op=ALU.add))
        S.op("act", [tmpa], [tmpa], lambda e: e.activation(out=tmpa[:], in_=tmpa[:], func=AF.Exp))
        S.op("act", [tmpa], [tmpa], lambda e: e.activation(out=tmpa[:], in_=tmpa[:], func=AF.Ln, bias=1.0, scale=1.0))
        for c in range(NCH):
            S.op("dve", [tmpa, negA], [graw], lambda e, c=c: e.tensor_tensor(
                out=graw[:, c, :], in0=tmpa[:, c, :], in1=negA[:], op=ALU.mult))
        S.op("act", [ab], [beta], lambda e: e.activation(out=beta[:], in_=ab[:, :, 8:16], func=AF.Sigmoid))
        S.op("dve", [beta], [nbeta], lambda e: e.tensor_scalar(out=nbeta[:], in0=beta[:], scalar1=-1.0, scalar2=None, op0=ALU.mult))
        gc = P.sbuf("gc", [128, NCH, 8], F32)
        egc = P.sbuf("egc", [128, NCH, 8], F32)
        egl = P.sbuf("egl", [128, NCH, 8], F32)
        edec = P.sbuf("edec", [128, NCH, 8], F32)
        bgc = P.sbuf("bgc", [128, NCH, 8], F32)
        pgc = self.ps()
        S.op("pe", [UT, graw], [pgc], lambda e: e.matmul(pgc[:, 0:256], lhsT=UT[:, :], rhs=graw[:].rearrange("p a b -> p (a b)"), start=True, stop=True))
        pgl = self.ps()
        S.op("pe", [self.ones32, graw], [pgl], lambda e: e.matmul(pgl[:, 0:256], lhsT=self.ones32[:, :], rhs=graw[:].rearrange("p a b -> p (a b)"), start=True, stop=True))
        fl = lambda b: b[:].rearrange("p a b -> p (a b)")
        S.op("dve", [pgc], [gc], lambda e: e.tensor_copy(out=fl(gc), in_=pgc[:, 0:256]))
        S.op("act", [pgc], [egc], lambda e: e.activation(out=fl(egc), in_=pgc[:, 0:256], func=AF.Exp))
        S.op("act", [pgl], [egl], lambda e: e.activation(out=fl(egl), in_=pgl[:, 0:256], func=AF.Exp))
        S.op("dve", [pgl, gc], [edec], lambda e: e.tensor_tensor(out=fl(edec), in0=pgl[:, 0:256], in1=fl(gc), op=ALU.subtract))
        S.op("act", [edec], [edec], lambda e: e.activation(out=fl(edec), in_=fl(edec), func=AF.Exp))
        S.op("dve", [beta, egc], [bgc], lambda e: e.tensor_tensor(out=fl(bgc), in0=fl(beta), in1=fl(egc), op=ALU.mult))
        xin = [P.sbuf("xin%d" % k, [128, T + 3], BF16) for k in range(1)]
        for k in range(1):
            S.op("pool", [], [xin[k]], lambda e, k=k: e.memset(xin[k][:, 0:3], 0.0))
        xin_i = [0]
        identb = P.sbuf("identb", [128, 128], BF16)
        S.op("pool", [self.ident32], [identb], lambda e: e.tensor_copy(out=identb[:], in_=self.ident32[:]))
        dg = [[P.sbuf("dg%d_%d" % (a, j), [128, 128], BF16) for j in range(4)] for a in range(2)]
        sqb = [P.sbuf("sqb%d" % k, [128, 512], F32) for k in range(2)]
        ybuf = P.sbuf("ybuf", [128, T], F32)
        qTbs = [P.sbuf("qTb%d" % k, [128, T], BF16) for k in range(2)]
        kTf = P.sbuf("kTf", [128, T], F32)
        kTb = P.sbuf("kTb", [128, T], BF16)
        vTf = P.sbuf("vTf", [128, T], F32)
        rn = P.sbuf("rn", [128, 512], F32)
        zt = P.sbuf("zt", [128, NCH, 128], BF16)
        zgs = [P.sbuf("zg%d" % k, [128, NCH, 128], BF16) for k in range(2)]
        def chunked(name):
            big = P.sbuf(name, [128, NCH, 128], BF16)
            out = []
            for k in range(NCH):
                b = Buf(big.t[:, k, :], "%s%d" % (name, k))
                S.bufs.append(b)
                out.append(b)
            return out
        U_, WT, AT, KD = chunked("U_"), chunked("WT"), chunked("AT"), chunked("KD")
        ots = [P.sbuf("ot%d" % k, [128, T], BF16) for k in range(2)]
        Ssts = [P.sbuf("Sst%d" % k, [128, 128], F32) for k in range(2)]
        Sbfss = [[P.sbuf("Sbf%d_%d" % (a, k), [128, 128], BF16) for k in range(2)] for a in range(2)]
        tails = []
        s2prog = {}
        NR = GDN_NR
        def rot(name, dt=F32, n=NR):
            return [P.sbuf("%s%d" % (name, k), [128, 128], dt) for k in range(n)]
        ktok, vb, kbg, G1, dcs, dsc, Nm, Am, Pm, Nn, An = (rot(x) for x in ("ktok", "vb", "kbg", "G1", "dcs", "dsc", "Nm", "Am", "Pm", "Nn", "An"))
        vnew = rot("vnew", BF16, 2)
        o2b = rot("o2b", F32, 2)
        osb = rot("osb", F32, 2)
        ysb = rot("ysb", F32, 2)
        ssq = P.sbuf("ssq", [128, 2], F32)
        junk = P.sbuf("junk", [128, 128], F32)
        ew = [0]
        def evac_copy(src_ps, dst, dst_ap, src_ap, extra_reads=()):
            ew[0] += 1
            if ew[0] % 2 == 0:
                S.op("act", [src_ps] + list(extra_reads), [dst], lambda e: e.activation(out=dst_ap, in_=src_ap, func=AF.Copy))
            else:
                S.op("dve", [src_ps] + list(extra_reads), [dst], lambda e: e.tensor_copy(out=dst_ap, in_=src_ap))
        heads = list(heads)
        pre_rr = [0]
        def pre_ps():
            b = self.PS[pre_rr[0] % 4]
            pre_rr[0] += 1
            return b
        for hi, h in enumerate(heads):
            hp = hi % 2
            qTb, zg, ot, Sst, Sbfs = qTbs[hp], zgs[hp], ots[hp], Ssts[hp], Sbfss[hp]
            for wi, which in enumerate(("q", "k", "v")):
                xi = xin[0]
                xin_i[0] += 1
                row = wi * 1024 + h * 128
                S.dma([], [xi], xi[:, 3:], GT[row:row + 128, :])
                def wcol(j, wi=wi):
                    return self.vcol("conv", (li * 4 + j) * 24 + wi * 8 + h)
                dgs = dg[xin_i[0] % 2]
                for j in range(4):
                    S.op("dve", [identb, self.VEC], [dgs[j]], lambda e, j=j: e.tensor_scalar(
                        out=dgs[j][:], in0=identb[:], scalar1=wcol(j), scalar2=None, op0=ALU.mult))
                dst = vTf if which == "v" else ybuf
                for g in range(8):
                    pc = pre_ps()
                    def mmc(e, pc=pc, g=g):
                        last = None
                        for j in range(4):
                            last = e.matmul(pc[:, :], lhsT=dgs[j][:], rhs=xi[:, g * 512 + j:g * 512 + j + 512], start=(j == 0), stop=(j == 3))
                        return last
                    S.op("pe", [xi] + dgs, [pc], mmc)
                    S.op("act", [pc], [dst], lambda e, pc=pc, g=g: e.activation(out=dst[:, g * 512:(g + 1) * 512], in_=pc[:, :], func=AF.Silu))
                if which == "v":
                    continue
                for g in range(8):
                    sl = slice(g * 512, (g + 1) * 512)
                    sq_ = sqb[g % 2]
                    S.op("act", [ybuf], [sq_], lambda e, sq_=sq_, sl=sl: e.activation(out=sq_[:], in_=ybuf[:, sl], func=AF.Square))
                    pss = pre_ps()
                    S.op("pe", [sq_, self.ones32], [pss], lambda e, pss=pss, sq_=sq_: e.matmul(pss[:, :], lhsT=self.ones32[:, :], rhs=sq_[:], start=True, stop=True))
                    S.op("dve", [pss], [rn], lambda e, pss=pss: e.tensor_scalar(out=rn[:], in0=pss[:, :], scalar1=EPS, scalar2=None, op0=ALU.add))
                    S.op("act", [rn], [rn], lambda e: e.activation(out=rn[:], in_=rn[:], func=AF.Sqrt))
                    S.op("dve", [rn], [rn], lambda e: e.reciprocal(out=rn[:], in_=rn[:]))
                    if which == "q":
                        S.op("dve", [ybuf, rn], [qTb], lambda e, sl=sl: e.scalar_tensor_tensor(
                            out=qTb[:, sl], in0=ybuf[:, sl], scalar=float(128 ** -0.5), in1=rn[:], op0=ALU.mult, op1=ALU.mult))
                    else:
                        S.op("dve", [ybuf, rn], [kTf], lambda e, sl=sl: e.tensor_tensor(out=kTf[:, sl], in0=ybuf[:, sl], in1=rn[:], op=ALU.mult))
                        S.op("pool", [kTf], [kTb], lambda e, sl=sl: e.tensor_copy(out=kTb[:, sl], in_=kTf[:, sl]))
            S.dma([], [zt], zt[:], ZT[:, h * 128:(h + 1) * 128].rearrange("(c p) d -> p c d", p=128))
            S.op("act", [zt], [zt], lambda e: e.activation(out=zt[:], in_=zt[:], func=AF.Silu))
            for c in range(NCH):
                S.op("pool", [zt, self.VEC], [zg], lambda e, c=c: e.tensor_tensor(
                    out=zg[:, c, :], in0=zt[:, c, :], in1=self.vcol("gngb", li * 128, 128), op=ALU.mult))
            done1 = [False] * nchunks

            class Pool3:
                def __init__(self, banks):
                    self.banks = banks
                    self.i = 0

                def get(self):
                    b = self.banks[self.i % len(self.banks)]
                    self.i += 1
                    return PQ(b, 0)

                def pair(self):
                    b = self.banks[self.i % len(self.banks)]
                    self.i += 1
                    return PQ(b, 0), PQ(b, 1)

            class PoolH:
                def __init__(self, bank):
                    self.bank = bank
                    self.i = 0

                def pair(self):
                    hh = self.i % 2
                    self.i += 1
                    return PQ(self.bank, 2 * hh), PQ(self.bank, 2 * hh + 1)

                def get(self):
                    return self.pair()[0]
            pools = [PoolH(self.PS[k]) for k in range(4)]
            pool2 = Pool3(self.PS[4:6] if hp == 0 else self.PS[6:8])
            def evq(pq, dst, dst_ap, extra_reads=()):
                evac_copy(pq, dst, dst_ap, pq.ap, extra_reads)

            def stage1(c, r, h=h, hi=hi):
                while hi > 0 and s2prog.get(hi - 1, nchunks) <= c:
                    yield
                cs = slice(c * C, (c + 1) * C)
                col = slice(h, h + 1)
                pl = pools[r]
                pk, pv = pl.pair()
                def f(e):
                    e.transpose(out=pk.ap, in_=kTf[:, cs], identity=I32m[:, :])
                    return e.transpose(out=pv.ap, in_=vTf[:, cs], identity=I32m[:, :])
                S.op("pe", [kTf, vTf, I32m], [pk], f)
                S.op("dve", [SU, graw], [G1[r]], lambda e: e.tensor_scalar(out=G1[r][:], in0=SU[:], scalar1=graw[:, c, col], scalar2=None, op0=ALU.mult))
                yield
                S.op("dve", [pk], [ktok[r]], lambda e: e.tensor_copy(out=ktok[r][:], in_=pk.ap))
                S.op("dve", [pv, beta], [vb[r]], lambda e: e.tensor_scalar(out=vb[r][:], in0=pv.ap, scalar1=beta[:, c, col], scalar2=None, op0=ALU.mult))
                pd1, pd2 = pl.pair()
                def f(e):
                    e.matmul(pd1.ap, lhsT=UT[:, :], rhs=G1[r][:], start=True, stop=True)
                    return e.matmul(pd2.ap, lhsT=G1[r][:], rhs=UT[:, :], start=True, stop=True)
                S.op("pe", [UT, G1[r]], [pd1], f)
                yield
                S.op("act", [pd1], [dcs[r]], lambda e: e.activation(out=dcs[r][:], in_=pd1.ap, func=AF.Exp))
                S.op("act", [pd2], [dsc[r]], lambda e: e.activation(out=dsc[r][:], in_=pd2.ap, func=AF.Exp))
                S.op("act", [ktok[r], bgc], [kbg[r]], lambda e: e.activation(out=kbg[r][:], in_=ktok[r][:], func=AF.Copy, scale=bgc[:, c, col]))
                S.op("act", [ktok[r], edec], [KD[c]], lambda e: e.activation(out=KD[c][:], in_=ktok[r][:], func=AF.Copy, scale=edec[:, c, col]))
                pkk, pqk = pl.pair()
                def f(e):
                    e.matmul(pkk.ap, lhsT=kTb[:, cs], rhs=kTb[:, cs], start=True, stop=True)
                    return e.matmul(pqk.ap, lhsT=kTb[:, cs], rhs=qTb[:, cs], start=True, stop=True)
                S.op("pe", [kTf, kTb, qTb], [pkk], f)
                yield
                S.op("pool", [dcs[r], SU], [dcs[r]], lambda e: e.tensor_tensor(out=dcs[r][:], in0=dcs[r][:], in1=SU[:], op=ALU.mult))
                S.op("pool", [dsc[r], UT], [dsc[r]], lambda e: e.tensor_tensor(out=dsc[r][:], in0=dsc[r][:], in1=UT[:], op=ALU.mult))
                yield
                S.op("dve", [pkk, nbeta, dcs[r]], [Nm[r]], lambda e: e.scalar_tensor_tensor(
                    out=Nm[r][:], in0=pkk.ap, scalar=nbeta[:, c, col], in1=dcs[r][:], op0=ALU.mult, op1=ALU.mult))
                S.op("dve", [pqk, dsc[r]], [AT[c]], lambda e: e.tensor_tensor(out=AT[c][:], in0=pqk.ap, in1=dsc[r][:], op=ALU.mult))
                yield
                pa = pl.get()
                S.op("pe", [Nm[r], I32m], [pa], lambda e: e.transpose(out=pa.ap, in_=Nm[r][:], identity=I32m[:, :]))
                yield
                S.op("act", [pa], [Am[r]], lambda e: e.activation(out=Am[r][:], in_=pa.ap, func=AF.Copy))
                yield
                S.op("pool", [Am[r], I32m], [Pm[r]], lambda e: e.tensor_tensor(out=Pm[r][:], in0=Am[r][:], in1=I32m[:], op=ALU.add))
                Ncur, Acur = Nm[r], Am[r]
                for lvl in range(1, 7):
                    Nnx = Nn[r] if lvl % 2 == 1 else Nm[r]
                    Anx = An[r] if lvl % 2 == 1 else Am[r]
                    pn, pa2 = pl.pair()
                    def f(e):
                        last = e.matmul(pn.ap, lhsT=Acur[:], rhs=Ncur[:], start=True, stop=True)
                        if lvl < 6:
                            last = e.matmul(pa2.ap, lhsT=Ncur[:], rhs=Acur[:], start=True, stop=True)
                        return last
                    S.op("pe", [Acur, Ncur], [pn], f)
                    yield
                    if (lvl + r) % 2 == 0:
                        S.op("act", [pn], [Nnx], lambda e: e.activation(out=Nnx[:], in_=pn.ap, func=AF.Copy))
                        if lvl < 6:
                            S.op("act", [pa2], [Anx], lambda e: e.activation(out=Anx[:], in_=pa2.ap, func=AF.Copy))
                    else:
                        S.op("dve", [pn], [Nnx], lambda e: e.tensor_copy(out=Nnx[:], in_=pn.ap))
                        if lvl < 6:
                            S.op("dve", [pa2], [Anx], lambda e: e.tensor_copy(out=Anx[:], in_=pa2.ap))
                    yield
                    pp = pl.get()
                    S.op("pe", [Nnx, Pm[r]], [pp], lambda e: e.matmul(pp.ap, lhsT=Nnx[:], rhs=Pm[r][:], start=True, stop=True))
                    yield
                    S.op("dve", [pp, Pm[r]], [Pm[r]], lambda e: e.tensor_tensor(out=Pm[r][:], in0=pp.ap, in1=Pm[r][:], op=ALU.add))
                    yield
                    Ncur, Acur = Nnx, Anx
                pu, pw = pl.pair()
                def f(e):
                    e.matmul(pu.ap, lhsT=Pm[r][:], rhs=vb[r][:], start=True, stop=True)
                    return e.matmul(pw.ap, lhsT=kbg[r][:], rhs=Pm[r][:], start=True, stop=True)
                S.op("pe", [Pm[r], vb[r], kbg[r]], [pu], f)
                yield
                S.op("act", [pu], [U_[c]], lambda e: e.activation(out=U_[c][:], in_=pu.ap, func=AF.Copy))
                S.op("act", [pw], [WT[c]], lambda e: e.activation(out=WT[c][:], in_=pw.ap, func=AF.Copy))
                done1[c] = True

            def stage2(h=h, hi=hi, qTb=qTb, zg=zg, ot=ot, Sst=Sst, Sbfs=Sbfs, pool2=pool2, done1=done1):
                S.op("pool", [], [Sbfs[0]], lambda e: e.memset(Sbfs[0][:], 0.0))
                S.op("pool", [], [Sst], lambda e: e.memset(Sst[:], 0.0))
                col = slice(h, h + 1)
                for c in range(nchunks):
                    s2prog[hi] = c
                    while not done1[c]:
                        yield
                    cs = slice(c * C, (c + 1) * C)
                    r2 = c % 2
                    Sold, Snew = Sbfs[c % 2], Sbfs[(c + 1) % 2]
                    p1, p2a = pool2.get(), pool2.get()
                    S.op("pe", [WT[c], Sold], [p1], lambda e: e.matmul(p1.ap, lhsT=WT[c][:], rhs=Sold[:], start=True, stop=True))
                    S.op("pe", [qTb, Sold], [p2a], lambda e: e.matmul(p2a.ap, lhsT=qTb[:, cs], rhs=Sold[:], start=True, stop=True))
                    yield
                    S.op("dve", [U_[c], p1], [vnew[r2]], lambda e: e.tensor_tensor(out=vnew[r2][:], in0=U_[c][:], in1=p1.ap, op=ALU.subtract))
                    S.op("act", [p2a, egc], [o2b[r2]], lambda e: e.activation(out=o2b[r2][:], in_=p2a.ap, func=AF.Copy, scale=egc[:, c, col]))
                    yield
                    p3, p2b = pool2.get(), pool2.get()
                    S.op("pe", [KD[c], vnew[r2]], [p3], lambda e: e.matmul(p3.ap, lhsT=KD[c][:], rhs=vnew[r2][:], start=True, stop=True))
                    S.op("pe", [AT[c], vnew[r2]], [p2b], lambda e: e.matmul(p2b.ap, lhsT=AT[c][:], rhs=vnew[r2][:], start=True, stop=True))
                    yield
                    S.op("dve", [Sst, egl, p3], [Sst], lambda e: e.scalar_tensor_tensor(
                        out=Sst[:], in0=Sst[:], scalar=egl[:, c, col], in1=p3.ap, op0=ALU.mult, op1=ALU.add))
                    yield
                    S.op("act", [Sst], [Snew], lambda e: e.activation(out=Snew[:], in_=Sst[:], func=AF.Copy))
                    S.op("dve", [p2b, o2b[r2]], [osb[r2]], lambda e: e.tensor_tensor(out=osb[r2][:], in0=p2b.ap, in1=o2b[r2][:], op=ALU.add))
                    yield
                    S.op("act", [osb[r2]], [junk, ssq], lambda e: e.activation(out=junk[:], in_=osb[r2][:], func=AF.Square, accum_out=ssq[:, r2:r2 + 1]))
                    S.op("dve", [ssq], [ssq], lambda e: e.tensor_scalar(out=ssq[:, r2:r2 + 1], in0=ssq[:, r2:r2 + 1], scalar1=1.0 / 128, scalar2=EPS, op0=ALU.mult, op1=ALU.add))
                    S.op("act", [ssq], [ssq], lambda e: e.activation(out=ssq[:, r2:r2 + 1], in_=ssq[:, r2:r2 + 1], func=AF.Sqrt))
                    S.op("dve", [ssq], [ssq], lambda e: e.reciprocal(out=ssq[:, r2:r2 + 1], in_=ssq[:, r2:r2 + 1]))
                    S.op("dve", [osb[r2], ssq, zg], [ysb[r2]], lambda e: e.scalar_tensor_tensor(
                        out=ysb[r2][:], in0=osb[r2][:], scalar=ssq[:, r2:r2 + 1], in1=zg[:, c, :], op0=ALU.mult, op1=ALU.mult))
                    yield
                    py = pool2.get()
                    S.op("pe", [ysb[r2], I32m], [py], lambda e: e.transpose(out=py.ap, in_=ysb[r2][:], identity=I32m[:, :]))
                    S.op("act", [py], [ot], lambda e: e.activation(out=ot[:, cs], in_=py.ap, func=AF.Copy))
                    yield
                s2prog[hi] = nchunks
                S.dma([ot], [], OT[h * 128:(h + 1) * 128, :], ot[:])

            pending = list(range(nchunks))
            active = {}
            tails.append(stage2())
            while pending or active:
                while pending and len(active) < NR:
                    c = pending.pop(0)
                    slot = [k for k in range(NR) if k not in active][0]
                    active[slot] = stage1(c, slot)
                for slot in list(active.keys()):
                    try:
                        next(active[slot])
                    except StopIteration:
                        del active[slot]
                for g in list(tails):
                    try:
                        next(g)
                    except StopIteration:
                        tails.remove(g)
        while tails:
            for g in list(tails):
                try:
                    next(g)
                except StopIteration:
                    tails.remove(g)
        P.close()


def prep_shared(inp):
    perm = rope_partner_perm()
    w = np.asarray(inp["attn_w_in"], np.float32)
    w4 = np.concatenate([w, w[:, :, 0:512][:, :, perm], w[:, :, 512:1024][:, :, perm]], axis=2)
    return {
        "ada_w": np.ascontiguousarray(inp["ada_w"], np.float32),
        "attn_w_in": np.ascontiguousarray(w4),
        "attn_w_out": np.ascontiguousarray(inp["attn_w_out"], np.float32),
        "gdn_w_in": np.ascontiguousarray(inp["gdn_w_in"], np.float32),
        "gdn_w_out": np.ascontiguousarray(inp["gdn_w_out"], np.float32),
        "ffn_w_in": np.ascontiguousarray(inp["ffn_w_in"], np.float32),
        "ffn_w_out": np.ascontiguousarray(inp["ffn_w_out"], np.float32),
    }


def prep_core(inp, b, shared):
    m = dict(shared)
    m["xT"] = np.ascontiguousarray(np.asarray(inp["x"][b], np.float32).T)
    m["vecs"] = build_vecs(inp, b)
    m["pos"] = np.ascontiguousarray(np.asarray(inp["positions"][b], np.int32).reshape(1, T))
    return m


def build_full():
    pg = Prog()
    pg.defer_mod = True
    pg.phase0()
    for l in range(NL):
        pg.phase1(l)
        if l % 2 == 0:
            pg.phase_attn(mod_layers=((1, 2) if l == 0 else (3,)))
        else:
            pg.phase_gdn(l // 2)
        pg.phase34(l, final=(l == NL - 1))
    return pg


def kernel(**inputs):
    inp = {k: np.asarray(v) for k, v in inputs.items()}
    pg = build_full()
    shared = prep_shared(inp)
    in_maps = []
    for b in range(8):
        m = prep_core(inp, b, shared)
        in_maps.append({k: m[k] for k in pg.used_inputs})
    res = run_bass_kernel_spmd(pg.nc, in_maps, core_ids=list(range(8)))
    out = np.empty((8, T, D), np.float32)
    for b in range(8):
        out[b] = np.asarray(res.results[b]["outT"], np.float32).T
    return out
```
